# Optimizing a Trainium2 kernel written in Bass

```python
import jax, jax.numpy as jnp
from jax import lax
import numpy as np

D_MODEL = 1024
BATCH = 32
SEQ = 256
DEPTH = 1
DEC_BATCH = 4
DEC_SEQ = 2048
PAST_LEN = 256

GRID_W = 64
GLA_HEADS = 4
GLA_DK = D_MODEL // 16
GLA_DV = D_MODEL // 8
GLA_LOWRANK = 16
GLA_GATE_NORM = 16.0
HGRN_HEADS = 4
HGRN_EXPAND = D_MODEL // 8
HGRN_DV = D_MODEL // 8
GLA_KW = GLA_HEADS * GLA_DK
GLA_VW = GLA_HEADS * GLA_DV
HGRN_KW = HGRN_HEADS * HGRN_EXPAND
HGRN_VW = HGRN_HEADS * HGRN_DV
IN_SIZES = (GLA_KW, GLA_KW, GLA_VW, GLA_VW, 2 * GLA_LOWRANK, HGRN_KW, 2 * HGRN_KW, HGRN_VW, HGRN_VW)
IN_WIDTH = GLA_KW * 2 + GLA_VW * 2 + 2 * GLA_LOWRANK + HGRN_KW * 3 + HGRN_VW * 2
FFN_HIDDEN = ((8 * D_MODEL // 3 + 127) // 128) * 128
CONV_W = 3
CHUNK = 16
EPS = 1e-6

kernel_name = 'hybrid_gla_hgrn2_diffusion_step'


def rmsnorm(x, g):
    xf = x.astype(jnp.float32)
    y = xf * lax.rsqrt(jnp.mean(xf * xf, axis=-1, keepdims=True) + EPS)
    return (y * g.astype(jnp.float32)).astype(x.dtype)


def chunk_gated_linear(q, k, v, log_f, s0):
    B, H, T, DK = q.shape
    DV = v.shape[-1]
    N = T // CHUNK
    f32 = jnp.float32
    q = q.astype(f32).reshape(B, H, N, CHUNK, DK)
    k = k.astype(f32).reshape(B, H, N, CHUNK, DK)
    v = v.astype(f32).reshape(B, H, N, CHUNK, DV)
    b = jnp.cumsum(log_f.astype(f32).reshape(B, H, N, CHUNK, DK), axis=3)
    b_last = b[:, :, :, -1:, :]
    lower = jnp.tril(jnp.ones((CHUNK, CHUNK), dtype=bool))[:, :, None]
    rel = b[:, :, :, :, None, :] - b[:, :, :, None, :, :]
    decay = jnp.exp(jnp.where(lower, rel, -jnp.inf))
    scores = jnp.einsum('bhnid,bhnijd,bhnjd->bhnij', q, decay, k)
    o_intra = jnp.einsum('bhnij,bhnjv->bhniv', scores, v)
    q_in = q * jnp.exp(b)
    k_out = k * jnp.exp(b_last - b)
    f_chunk = jnp.exp(b_last[:, :, :, 0, :])

    def step(S, xs):
        qc, kc, vc, fc = xs
        o = jnp.einsum('bhcd,bhdv->bhcv', qc, S)
        S = fc[..., None] * S + jnp.einsum('bhcd,bhcv->bhdv', kc, vc)
        return S, o

    xs = (jnp.moveaxis(q_in, 2, 0), jnp.moveaxis(k_out, 2, 0), jnp.moveaxis(v, 2, 0), jnp.moveaxis(f_chunk, 2, 0))
    s_final, o_inter = lax.scan(step, s0.astype(f32), xs)
    o = o_intra + jnp.moveaxis(o_inter, 0, 2)
    return o.reshape(B, H, T, DV), s_final


def bidir_scan(q, k_fwd, k_bwd, v, logf_fwd, logf_bwd, s0_fwd, s0_bwd):
    flip = lambda t: jnp.flip(t, axis=2)
    o_f, s_f = chunk_gated_linear(q, k_fwd, v, logf_fwd, s0_fwd)
    o_b, s_b = chunk_gated_linear(flip(q), flip(k_bwd), flip(v), flip(logf_bwd), s0_bwd)
    return o_f + flip(o_b), s_f, s_b


def to_heads(t, n):
    B, T, W = t.shape
    return t.reshape(B, T, n, W // n).transpose(0, 2, 1, 3)


def hybrid_mixer(h, s_gla, s_hgrn, w_in, w_gla_up, b_gla, lb, gla_norm, hgrn_norm, w_out):
    B, T, _ = h.shape
    offsets = np.cumsum(IN_SIZES)[:-1].tolist()
    q_a, k_a, v_a, g_a, lr_a, q_b, f_b, i_b, g_b = jnp.split(h @ w_in, offsets, axis=-1)
    lr = lr_a.reshape(B, T, 2, GLA_LOWRANK)
    alpha_logit = jnp.einsum('btdr,drk->dbtk', lr, w_gla_up) + b_gla[:, None, None, :]
    log_alpha = jax.nn.log_sigmoid(alpha_logit.astype(jnp.float32)) / GLA_GATE_NORM
    qa = to_heads(q_a, GLA_HEADS) * (GLA_DK ** -0.5)
    ka = to_heads(k_a, GLA_HEADS)
    va = to_heads(v_a, GLA_HEADS)
    o_a, sa_f, sa_b = bidir_scan(qa, ka, ka, va, to_heads(log_alpha[0], GLA_HEADS), to_heads(log_alpha[1], GLA_HEADS), s_gla[:, 0], s_gla[:, 1])
    o_a = rmsnorm(o_a.transpose(0, 2, 1, 3).astype(h.dtype), gla_norm) * jax.nn.silu(g_a).reshape(B, T, GLA_HEADS, GLA_DV)
    f_raw = f_b.reshape(B, T, 2, HGRN_KW).astype(jnp.float32)
    log_f = jnp.logaddexp(jnp.log(lb), jnp.log1p(-lb) + jax.nn.log_sigmoid(f_raw))
    i_gate = -jnp.expm1(log_f)
    qb = to_heads(q_b, HGRN_HEADS)
    vb = to_heads(i_b, HGRN_HEADS)
    o_h, sb_f, sb_b = bidir_scan(qb, to_heads(i_gate[:, :, 0], HGRN_HEADS), to_heads(i_gate[:, :, 1], HGRN_HEADS), vb, to_heads(log_f[:, :, 0], HGRN_HEADS), to_heads(log_f[:, :, 1], HGRN_HEADS), s_hgrn[:, 0], s_hgrn[:, 1])
    o_h = rmsnorm(o_h.transpose(0, 2, 1, 3).astype(h.dtype), hgrn_norm) * jax.nn.silu(g_b).reshape(B, T, HGRN_HEADS, HGRN_DV)
    merged = jnp.concatenate([o_a.reshape(B, T, GLA_VW), o_h.reshape(B, T, HGRN_VW)], axis=-1)
    new_gla = jnp.stack([sa_f, sa_b], axis=1).astype(s_gla.dtype)
    new_hgrn = jnp.stack([sb_f, sb_b], axis=1).astype(s_hgrn.dtype)
    return merged @ w_out, new_gla, new_hgrn


def dw_conv(u, conv_w, conv_b, grid):
    B, T, C = u.shape
    if grid:
        rows = T // GRID_W
        img = u.reshape(B, rows, GRID_W, C)
        out = lax.conv_general_dilated(img, conv_w[:, :, None, :], (1, 1), 'SAME', dimension_numbers=('NHWC', 'HWIO', 'NHWC'), feature_group_count=C)
        out = out.reshape(B, T, C)
    else:
        out = lax.conv_general_dilated(u, conv_w[CONV_W // 2][:, None, :], (1,), 'SAME', dimension_numbers=('NWC', 'WIO', 'NWC'), feature_group_count=C)
    return out + conv_b


def conv_ffn(h, w_up, conv_w, conv_b, w_down, grid):
    u = dw_conv(h @ w_up, conv_w, conv_b, grid)
    gate, up = jnp.split(u, 2, axis=-1)
    return (jax.nn.silu(gate) * up) @ w_down


def trunk_layer(x, mod, s_gla, s_hgrn, norm1, norm2, w_in, w_gla_up, b_gla, lb, gla_norm, hgrn_norm, w_out, w_ffn_up, ffn_conv, b_ffn_conv, w_ffn_down, grid):
    shift1, scale1, gate1, shift2, scale2, gate2 = jnp.split(mod, 6, axis=-1)
    h = rmsnorm(x, norm1) * (1 + scale1) + shift1
    mix, new_gla, new_hgrn = hybrid_mixer(h, s_gla, s_hgrn, w_in, w_gla_up, b_gla, lb, gla_norm, hgrn_norm, w_out)
    x = x + gate1 * mix
    h = rmsnorm(x, norm2) * (1 + scale2) + shift2
    x = x + gate2 * conv_ffn(h, w_ffn_up, ffn_conv, b_ffn_conv, w_ffn_down, grid)
    return x, new_gla, new_hgrn


def setup_inputs(seed: int = 0) -> dict:
    key = jax.random.key(seed)
    ks = jax.random.split(key, 24)
    n = lambda i, shape: jax.random.normal(ks[i], shape, jnp.float32)
    return {
        'x_prompt': n(0, (BATCH, SEQ, D_MODEL)),
        'x_sample': n(1, (DEC_BATCH, DEC_SEQ, D_MODEL)),
        'state_gla': 0.5 * n(2, (DEC_BATCH, DEPTH, 2, GLA_HEADS, GLA_DK, GLA_DV)),
        'state_hgrn': 0.5 * n(3, (DEC_BATCH, DEPTH, 2, HGRN_HEADS, HGRN_EXPAND, HGRN_DV)),
        'c': n(4, (DEC_BATCH, D_MODEL)),
        'c_ctx': n(5, (D_MODEL,)),
        'w_ada': 0.5 * D_MODEL ** -0.5 * n(6, (DEPTH, D_MODEL, 6 * D_MODEL)),
        'b_ada': 0.02 * n(7, (DEPTH, 6 * D_MODEL)),
        'norm1': 1.0 + 0.02 * n(8, (DEPTH, D_MODEL)),
        'norm2': 1.0 + 0.02 * n(9, (DEPTH, D_MODEL)),
        'w_in': D_MODEL ** -0.5 * n(10, (DEPTH, D_MODEL, IN_WIDTH)),
        'w_gla_up': GLA_LOWRANK ** -0.5 * n(11, (DEPTH, 2, GLA_LOWRANK, GLA_KW)),
        'b_gla': 0.1 * n(12, (DEPTH, 2, GLA_KW)),
        'hgrn_lb': 0.5 * n(13, (DEPTH + 1, 2, HGRN_KW)),
        'gla_norm': 1.0 + 0.02 * n(14, (DEPTH, GLA_DV)),
        'hgrn_norm': 1.0 + 0.02 * n(15, (DEPTH, HGRN_DV)),
        'w_out': D_MODEL ** -0.5 * n(16, (DEPTH, D_MODEL, D_MODEL)),
        'w_ffn_up': D_MODEL ** -0.5 * n(17, (DEPTH, D_MODEL, 2 * FFN_HIDDEN)),
        'ffn_conv': (CONV_W * CONV_W) ** -0.5 * n(18, (DEPTH, CONV_W, CONV_W, 2 * FFN_HIDDEN)),
        'b_ffn_conv': 0.02 * n(19, (DEPTH, 2 * FFN_HIDDEN)),
        'w_ffn_down': FFN_HIDDEN ** -0.5 * n(20, (DEPTH, FFN_HIDDEN, D_MODEL)),
        'final_norm': 1.0 + 0.02 * n(21, (D_MODEL,)),
    }


def reference(x_prompt, x_sample, state_gla, state_hgrn, c, c_ctx, w_ada, b_ada, norm1, norm2, w_in, w_gla_up, b_gla, hgrn_lb, gla_norm, hgrn_norm, w_out, w_ffn_up, ffn_conv, b_ffn_conv, w_ffn_down, final_norm):
    lb_all = jnp.cumsum(jax.nn.softmax(hgrn_lb.astype(jnp.float32), axis=0), axis=0)
    bp = x_prompt.shape[0]
    zero_gla = jnp.zeros((bp, 2, GLA_HEADS, GLA_DK, GLA_DV), state_gla.dtype)
    zero_hgrn = jnp.zeros((bp, 2, HGRN_HEADS, HGRN_EXPAND, HGRN_DV), state_hgrn.dtype)
    xp, xs = x_prompt, x_sample
    gla_states, hgrn_states = [], []
    for l in range(DEPTH):
        mod_ctx = (jax.nn.silu(c_ctx) @ w_ada[l] + b_ada[l])[None, None, :]
        mod_lat = (jax.nn.silu(c) @ w_ada[l] + b_ada[l])[:, None, :]
        layer_w = (norm1[l], norm2[l], w_in[l], w_gla_up[l], b_gla[l], lb_all[l], gla_norm[l], hgrn_norm[l], w_out[l], w_ffn_up[l], ffn_conv[l], b_ffn_conv[l], w_ffn_down[l])
        xp, sg, sh = trunk_layer(xp, mod_ctx, zero_gla, zero_hgrn, *layer_w, grid=False)
        gla_states.append(sg)
        hgrn_states.append(sh)
        xs, _, _ = trunk_layer(xs, mod_lat, state_gla[:, l], state_hgrn[:, l], *layer_w, grid=True)
    y_prompt = rmsnorm(xp, final_norm)
    y_sample = rmsnorm(xs, final_norm)
    new_state_gla = jnp.stack(gla_states, axis=1)
    new_state_hgrn = jnp.stack(hgrn_states, axis=1)
    return (y_prompt, y_sample, new_state_gla, new_state_hgrn)
```

```python
import numpy as np
from contextlib import ExitStack
import concourse.bass as bass
import concourse.mybir as mybir
from concourse.bass_utils import run_bass_kernel_spmd

F32 = mybir.dt.float32
BF16 = mybir.dt.bfloat16
AF = mybir.ActivationFunctionType
ALU = mybir.AluOpType


class Tok:
    __slots__ = ("name", "w", "r", "rd", "excl", "acc")

    def __init__(self, name, excl=False):
        self.name = name
        self.w = None
        self.r = {}
        self.rd = []
        self.excl = excl
        self.acc = {}


class TT:
    def __init__(self, t, tok):
        self.t = t
        self.tok = tok


class _Op:
    __slots__ = ("idx", "eng", "fn", "deps", "is_dma", "sig", "semval", "slot", "is_out", "multi")


def _tok(x):
    return x.tok if isinstance(x, TT) else x


class Prog:
    NSLOT = {"sp": 24, "pool": 16, "act": 8}

    def __init__(self, nc, es):
        self.nc = nc
        self.es = es
        self.ops = []
        self.n_dma = {"sp": 0, "pool": 0, "act": 0}
        self._n = 0
        self.bar = set()

    def sb(self, name, shape, dtype):
        t = self.es.enter_context(self.nc.sbuf_tensor(name, list(shape), dtype))
        return TT(t, Tok(name))

    def ps(self, name):
        t = self.es.enter_context(self.nc.psum_tensor(name, [128, 512], F32))
        return TT(t, Tok(name))

    def tok(self, name):
        return Tok(name)

    def _record(self, eng, fn, reads, writes, is_dma, is_out=False, extra=(), multi=False):
        op = _Op()
        op.multi = multi
        op.idx = len(self.ops)
        op.eng = eng
        op.fn = fn
        op.is_dma = is_dma
        op.sig = False
        op.semval = 0
        op.slot = None
        op.is_out = is_out
        deps = set()
        reads = [_tok(x) for x in reads]
        writes = [_tok(x) for x in writes]

        def consider(pidx, kind):
            p = self.ops[pidx]
            if p.is_dma:
                deps.add(pidx)
                return
            if (not is_dma) and p.eng == eng:
                if eng == "pe":
                    return
                if kind != "raw":
                    return
            deps.add(pidx)

        for t in reads:
            if t.w is not None:
                consider(t.w, "raw")
        for t in writes:
            if t.w is not None:
                consider(t.w, "waw")
            for _, ridx in t.r.items():
                consider(ridx, "war")
            for ridx in t.rd:
                consider(ridx, "war")
        for t in reads + writes:
            if t.excl:
                for e2, aidx in t.acc.items():
                    if e2 != eng:
                        deps.add(aidx)
                t.acc[eng] = op.idx
        for x in extra:
            deps.add(x.idx)
        for pidx in self.bar:
            p = self.ops[pidx]
            if (not is_dma) and (not p.is_dma) and p.eng == eng:
                continue
            deps.add(pidx)
        op.deps = deps
        for t in reads:
            if is_dma:
                t.rd.append(op.idx)
            else:
                t.r[eng] = op.idx
        for t in writes:
            t.w = op.idx
            t.r = {}
            t.rd = []
        if is_dma:
            k = self.n_dma[eng]
            self.n_dma[eng] += 1
            ns = self.NSLOT[eng]
            op.slot = (eng, k % ns)
            op.semval = 16 * (k // ns + 1)
        self.ops.append(op)
        return op

    def op(self, eng, fn, reads, writes, extra=(), multi=False):
        return self._record(eng, fn, reads, writes, False, extra=extra, multi=multi)

    def barrier(self):
        last = {}
        for op in self.ops:
            if op.is_dma:
                last[("d",) + op.slot] = op.idx
            else:
                last[op.eng] = op.idx
        self.bar = set(last.values())

    def dma(self, eng, out_ap, in_ap, reads, writes, is_out=False, **kw):
        def fn(e, out_ap=out_ap, in_ap=in_ap, kw=kw):
            return e.dma_start(out=out_ap, in_=in_ap, **kw)
        return self._record(eng, fn, reads, writes, True, is_out)

    def finish(self):
        nc = self.nc
        es = self.es
        ops = self.ops
        for op in ops:
            for d in op.deps:
                ops[d].sig = True
        engs = ["pe", "act", "dve", "pool", "sp"]
        esem = {e: es.enter_context(nc.semaphore("s_" + e)) for e in engs}
        dsem = {}
        for e, ns in self.NSLOT.items():
            for i in range(min(ns, max(1, self.n_dma[e]))):
                dsem[(e, i)] = es.enter_context(nc.semaphore("d_%s%d" % (e, i)))
        cnt = {e: 0 for e in engs}
        for op in ops:
            if not op.is_dma and op.sig:
                cnt[op.eng] += 1
                op.semval = cnt[op.eng]
        slot_last = {}
        prev_on_slot = {}
        for op in ops:
            if op.is_dma:
                prev_on_slot[op.idx] = slot_last.get(op.slot)
                slot_last[op.slot] = op.idx
        by_eng = {e: [op for op in ops if op.eng == e] for e in engs}

        def sigof(p):
            if p.is_dma:
                return dsem[p.slot], p.semval
            return esem[p.eng], p.semval

        def emit(ename, e):
            waited = {}

            embed = ename in ("dve", "act", "pool")

            for op in by_eng[ename]:
                need = {}
                order = []
                cand = [sigof(ops[d]) for d in sorted(op.deps)]
                if op.is_dma:
                    pv = prev_on_slot[op.idx]
                    if pv is not None:
                        cand.append(sigof(ops[pv]))
                for s, v in cand:
                    key = id(s)
                    if waited.get(key, 0) < v and need.get(key, (None, 0))[1] < v:
                        if key not in need:
                            order.append(key)
                        need[key] = (s, v)
                pend = [need[k] for k in order]
                for s, v in pend:
                    waited[id(s)] = v
                fold = None
                if embed and pend and not op.is_dma and not op.multi:
                    fold = pend.pop()
                for s, v in pend:
                    e.wait_ge(s, v)
                ins = op.fn(e)
                if fold is not None:
                    ins._wait_ge(fold[0], fold[1])
                if op.is_dma:
                    ins.then_inc(dsem[op.slot], 16)
                elif op.sig:
                    ins.then_inc(esem[ename], 1)
            if ename == "sp":
                for slot, idx in slot_last.items():
                    s, v = sigof(ops[idx])
                    if waited.get(id(s), 0) < v:
                        e.wait_ge(s, v)
                        waited[id(s)] = v

        with nc.Block() as block:
            @block.tensor
            def _(e):
                emit("pe", e)

            @block.scalar
            def _(e):
                emit("act", e)

            @block.vector
            def _(e):
                emit("dve", e)

            @block.gpsimd
            def _(e):
                emit("pool", e)

            @block.sync
            def _(e):
                emit("sp", e)
        self.stats = {e: len(by_eng[e]) for e in engs}


D = 1024
T = 2048
NB = 16
EPS = 1e-6
IN_W = 4128
FFN_H = 2816
NCH = 44
NPAIR = 22
UPAD = 65
UW = UPAD + T + UPAD
GROUPS = [(0, 6), (6, 12), (12, 17), (17, 22)]
TS = 512
OFF_QA, OFF_KA, OFF_VA, OFF_GA, OFF_LR = 0, 256, 512, 1024, 1536
OFF_QB, OFF_FB, OFF_IB, OFF_GB = 1568, 2080, 3104, 3616


class Arena:
    def __init__(self, P, nf32):
        self.P = P
        self.tt = P.sb("arena", [128, nf32], F32)
        self.n = nf32
        self.top = 0
        self.end = nf32

    def alloc(self, name, free_shape, dtype, top=False):
        nel = int(np.prod(free_shape))
        nf = nel if dtype == F32 else (nel + 1) // 2
        nf = (nf + 3) // 4 * 4
        assert self.top + nf <= self.end, ("arena overflow", name, self.top, nf, self.end)
        if top:
            self.end -= nf
            ap = self.tt.t[:, self.end:self.end + nf]
        else:
            ap = self.tt.t[:, self.top:self.top + nf]
        if dtype != F32:
            ap = ap.bitcast(dtype)
        ap = ap[:, 0:nel]
        if len(free_shape) == 2:
            ap = ap.rearrange("p (a b) -> p a b", a=free_shape[0])
        elif len(free_shape) == 3:
            ap = ap.rearrange("p (a b c) -> p a b c", a=free_shape[0], b=free_shape[1])
        if not top:
            self.top += nf
        return TT(ap, Tok(name))

    def mark(self):
        return self.top

    def reset(self, m):
        self.top = m


def build_program(debug=False):
    import os as _os
    nc = bass.Bass("TRN2", target_bir_lowering=False)

    def din(name, shape):
        return nc.dram_tensor(name, list(shape), F32, kind="ExternalInput").ap()

    def dout(name, shape, dt=F32):
        return nc.dram_tensor(name, list(shape), dt, kind="ExternalOutput").ap()

    x_d = din("x", [T, D])
    cvT_d = din("cvT", [128, 8])
    wada_d = din("w_ada", [D, 6 * D])
    bada_d = din("b_ada", [1, 6 * D])
    badaT_d = din("b_adaT", [128, 48])
    norm2T_d = din("norm2T", [128, 8])
    norm1_d = din("norm1", [1, D])
    norm2_d = din("norm2", [1, D])
    fnorm_d = din("fnorm", [1, D])
    win_d = din("w_in", [D, IN_W])
    wgu_d = din("w_gla_up", [2, 16, 256])
    bglaT_d = din("b_glaT", [128, 4])
    lbT_d = din("lbT", [128, 16])
    gnorm_d = din("gnorm", [128, 2])
    wout_d = din("w_out", [D, D])
    wup_d = din("w_ffn_up", [D, 2 * FFN_H])
    w11T_d = din("w11T", [128, NCH * 11])
    bconvT_d = din("bconvT", [128, NCH])
    wdn_d = din("w_ffn_down", [FFN_H, D])
    sig_d = din("sinit_g", [2, 4, 64, 128])
    sih_d = din("sinit_h", [2, 4, 128, 128])
    mchain_d = din("mchain", [128, 64])
    identF_d = din("identF", [128, 128])
    maskT2_d = din("maskT2", [128, 256])
    scanmask_d = din("scanmask", [1, T])
    y_d = dout("y", [T, D])
    sog_d = dout("snew_g", [8, 2, 4, 64, 128])
    soh_d = dout("snew_h", [8, 2, 4, 128, 128])
    dbg = {}
    if debug:
        dbg["h1T"] = dout("dbg_h1T", [128, 8 * T], BF16)
        dbg["mod"] = dout("dbg_mod", [128, 6 * D])
        dbg["mergedT"] = dout("dbg_mergedT", [128, 8 * T], BF16)
        dbg["x1"] = dout("dbg_x1", [T, D])

    es = ExitStack()
    with es:
        P = Prog(nc, es)
        AR = Arena(P, 52800)
        dumps = {}

        def dump(name, tt, ncols, dt, parts=128):
            if not debug:
                return
            if name not in dumps:
                dumps[name] = nc.dram_tensor("dd_" + name, [128, ncols], dt, kind="ExternalOutput").ap()
            ap = tt.t
            if len(ap.shape) == 3:
                ap = ap.rearrange("p a b -> p (a b)")
            elif len(ap.shape) == 4:
                ap = ap.rearrange("p a b c -> p (a b c)")
            P.dma("sp", dumps[name][0:parts], ap[0:parts], [tt], [], is_out=True)
        psb = [P.ps("psum%d" % i) for i in range(8)]
        psq = [Tok("psbank%d" % b, excl=True) for b in range(8)]

        def PQ(b, c0, c1):
            return [psq[b]]

        def pst(b):
            return psb[b].t

        rr = {"n": 0}

        def evac_eng():
            rr["n"] += 1
            return "act" if rr["n"] % 2 else "dve"

        def copy_op(eng, out, in_, reads, writes, scale=None):
            if eng == "act":
                if scale is None:
                    P.op("act", lambda e: e.activation(out, in_, AF.Copy), reads, writes)
                else:
                    P.op("act", lambda e: e.activation(out, in_, AF.Copy, scale=scale), reads, writes)
            else:
                if scale is None:
                    P.op(eng, lambda e: e.tensor_copy(out, in_), reads, writes)
                else:
                    P.op(eng, lambda e: e.tensor_scalar(out, in_, scale, None, op0=ALU.mult), reads, writes)

        identF = AR.alloc("identF", [128], F32)
        identB = AR.alloc("identB", [128], BF16)
        maskT2 = AR.alloc("maskT2", [256], F32)
        scanmask = AR.alloc("scanmask", [TS], BF16)
        mchain = AR.alloc("mchain", [64], F32)
        smalls = AR.alloc("smalls", [64], F32)
        LB = smalls.t[:, 0:8]
        L1M = smalls.t[:, 8:16]
        NEGB = smalls.t[:, 16:24]
        GS = smalls.t[:, 24:26]
        SC = smalls.t[:, 32:40]
        TMPS = smalls.t[:, 40:64]
        MB = [None] * 6
        hT = AR.alloc("hT", [8, T], BF16)
        hT_tok = [Tok("hT_b%d" % b) for b in range(NB)]
        onesF = AR.alloc("onesF", [128], F32)
        ring = {"slots": [], "n": 0}

        def next_w():
            w = ring["slots"][ring["n"] % len(ring["slots"])]
            ring["n"] += 1
            return w

        ph1_mark = AR.mark()
        ring["slots"] = [AR.alloc("wringA%d" % i, [8 * 512], BF16) for i in range(2)]
        srep = AR.alloc("srep", [8, 128], BF16)
        brow = [AR.alloc("brow%d" % i, [512], F32) for i in range(2)]
        MB[0] = AR.alloc("mod0", [D], F32)
        MB[1] = AR.alloc("mod1", [D], F32)

        P.dma("sp", identF.t, identF_d, [], [identF])
        P.dma("sp", maskT2.t, maskT2_d, [], [maskT2])
        P.dma("pool", scanmask.t, scanmask_d[:, 0:TS].partition_broadcast(128), [], [scanmask])
        P.dma("sp", mchain.t, mchain_d, [], [mchain])
        P.op("dve", lambda e: e.tensor_copy(identB.t, identF.t), [identF], [identB])
        lbT = AR.alloc("lbT", [16], F32)
        cvT = AR.alloc("cvT", [8], F32)
        bgl = AR.alloc("bgl", [8], F32)
        gnm = AR.alloc("gnm", [2], F32)
        P.dma("sp", lbT.t, lbT_d, [], [lbT])
        P.dma("sp", cvT.t, cvT_d, [], [cvT])
        P.dma("sp", bgl.t[:, 0:4], bglaT_d, [], [bgl])
        P.dma("sp", gnm.t, gnorm_d, [], [gnm])
        DD = TMPS[:, 0:8]
        EE = TMPS[:, 8:16]
        P.op("dve", lambda e: e.tensor_tensor(DD, lbT.t[:, 8:16], lbT.t[:, 0:8], ALU.subtract), [lbT], [smalls])
        P.op("act", lambda e: e.activation(EE, DD, AF.Exp), [smalls], [smalls])
        P.op("act", lambda e: e.activation(EE, EE, AF.Ln, bias=1.0), [smalls], [smalls])
        P.op("act", lambda e: e.activation(LB, EE, AF.Exp, scale=-1.0), [smalls], [smalls])
        P.op("dve", lambda e: e.tensor_tensor(L1M, DD, EE, ALU.subtract), [smalls], [smalls])
        P.op("dve", lambda e: e.tensor_scalar(NEGB[:, 0:4], bgl.t[:, 0:4], -1.0, None, op0=ALU.mult), [bgl], [smalls])
        P.op("dve", lambda e: e.tensor_scalar(GS, gnm.t, float(np.sqrt(128.0)), None, op0=ALU.mult), [gnm], [smalls])
        E2 = TMPS[:, 16:24]
        P.op("act", lambda e: e.activation(E2, cvT.t, AF.Exp, scale=-1.0), [cvT, smalls], [smalls])
        P.op("dve", lambda e: e.tensor_scalar(E2, E2, 1.0, None, op0=ALU.add), [smalls], [smalls])
        P.op("dve", lambda e: e.reciprocal(E2, E2), [smalls], [smalls])
        P.op("dve", lambda e: e.tensor_tensor(SC, cvT.t, E2, ALU.mult), [cvT, smalls], [smalls])
        sc_b = bass.AP(smalls.t.tensor, smalls.t.offset + 32, [list(smalls.t.ap[0]), [1, 8], [0, 128]])
        P.op("dve", lambda e: e.tensor_copy(srep.t, sc_b), [smalls], [srep])
        nrm = AR.alloc("nrm_bc", [D], F32)
        P.op("pool", lambda e: e.memset(onesF.t, 1.0), [], [onesF])
        pbank = {"n": 0}

        def mod_compute(j):
            col = [0, 1, 2, 3, 4, 5][j]
            for n in range(2):
                c0 = col * D + n * 512
                w = next_w()
                wv = w.t[:, 0:8 * 512].rearrange("p (k n) -> p k n", k=8)
                P.dma("pool", wv, wada_d[:, c0:c0 + 512].rearrange("(k p) n -> p k n", p=128), [], [w])
                br = brow[(2 * j + n) % 2]
                P.dma("sp", br.t[0:1, :], bada_d[:, c0:c0 + 512], [], [br])
                b = pbank["n"] % 2
                pbank["n"] += 1
                for kc in range(8):
                    P.op("pe", lambda e, kc=kc, b=b, wv=wv: e.matmul(pst(b)[:, :], srep.t[:, kc, :], wv[:, kc, :], start=(kc == 0), stop=False),
                         [srep, w], PQ(b, 0, 512))
                P.op("pe", lambda e, b=b, br=br: e.matmul(pst(b)[:, :], onesF.t[0:1, :], br.t[0:1, :], start=False, stop=True),
                     [onesF, br], PQ(b, 0, 512))
                copy_op(evac_eng(), MB[j].t[:, n * 512:(n + 1) * 512], pst(b)[:, :], PQ(b, 0, 512), [MB[j]])

        mod_compute(0)
        mod_compute(1)
        P.dma("sp", nrm.t, norm1_d.partition_broadcast(128), [], [nrm])
        P.op("dve", lambda e: e.scalar_tensor_tensor(MB[1].t, MB[1].t, 1.0, nrm.t, op0=ALU.add, op1=ALU.mult), [MB[1], nrm], [MB[1]])

        xring = [AR.alloc("xring%d" % i, [D], F32) for i in range(3)]
        junk = AR.alloc("junk", [D], BF16)
        tmpf = AR.alloc("tmpf", [D], F32)
        hb = [AR.alloc("hb%d" % i, [D], BF16) for i in range(2)]
        stt = [AR.alloc("stt%d" % i, [4], F32) for i in range(3)]

        def norm_A(src_ap, src_toks, st):
            jk = junk
            P.op("act", lambda e: e.activation(jk.t, src_ap, AF.Square, accum_out=st.t[:, 0:1]), src_toks, [jk, st], multi=True)
            P.op("act", lambda e: e.activation(st.t[:, 1:2], st.t[:, 0:1], AF.Ln, scale=1.0 / D, bias=EPS), [st], [st])
            P.op("act", lambda e: e.activation(st.t[:, 2:3], st.t[:, 1:2], AF.Exp, scale=-0.5), [st], [st])

        def norm_B1(src_ap, src_toks, g_t, s_t, b, st):
            tf, h = tmpf, hb[b % 2]
            P.op("dve", lambda e: e.scalar_tensor_tensor(tf.t, src_ap, st.t[:, 2:3], g_t.t, op0=ALU.mult, op1=ALU.mult),
                 src_toks + [st, g_t], [tf])
            P.op("dve", lambda e: e.tensor_tensor(h.t, tf.t, s_t.t, ALU.add), [tf, s_t], [h])

        def norm_B2(b, pbk):
            h = hb[b % 2]
            pv = pst(pbk).bitcast(BF16).rearrange("p (k t) -> p k t", k=8)
            for kc in range(8):
                P.op("pe", lambda e, kc=kc: e.transpose(pv[:, kc, :], h.t[:, kc * 128:(kc + 1) * 128], identB.t),
                     [h, identB], PQ(pbk, 0, 512))
            copy_op("act", hT.t[:, :, b * 128:(b + 1) * 128], pv, PQ(pbk, 0, 512), [hT_tok[b]])

        for b in range(NB + 2):
            if b < NB:
                xt = xring[b % 3]
                P.dma("sp", xt.t, x_d[b * 128:(b + 1) * 128, :], [], [xt])
                norm_A(xt.t, [xt], stt[b % 3])
            if 1 <= b <= NB:
                xp = xring[(b - 1) % 3]
                norm_B1(xp.t, [xp], MB[1], MB[0], b - 1, stt[(b - 1) % 3])
            if b >= 2:
                norm_B2(b - 2, 2 + ((b - 2) % 2))
        if debug:
            P.dma("sp", dbg["h1T"], hT.t.rearrange("p k t -> p (k t)"), hT_tok, [], is_out=True)
            for j in range(2):
                P.dma("sp", dbg["mod"][:, j * D:(j + 1) * D], MB[j].t, [MB[j]], [], is_out=True)

        P.barrier()
        AR.reset(ph1_mark)
        BS = 64
        NBK = T // BS
        NG = T // 128
        BPS = TS // BS
        NSPAN = T // TS
        modT = AR.alloc("modT", [4, 8], F32, top=True)
        badaT = AR.alloc("badaT", [48], F32, top=True)
        scb = AR.alloc("scb", [8], BF16, top=True)
        top_keep = AR.end
        mergedT = AR.alloc("mergedT", [8, T], BF16, top=True)
        mT_tok = [Tok("mT_h%d" % i) for i in range(8)]
        ring["slots"] = [AR.alloc("wringB%d" % i, [8 * 640], BF16) for i in range(2)]
        QT = AR.alloc("QT", [T], F32)
        KT = AR.alloc("KT", [T], F32)
        QTt = [AR.alloc("QTt%d" % d, [T], BF16) for d in range(2)]
        KTt = [AR.alloc("KTt%d" % d, [T], BF16) for d in range(2)]
        KTM = [AR.alloc("KTM%d" % d, [NG, 128], BF16) for d in range(2)]
        LH = AR.alloc("LH", [2, NBK, 2], F32)
        EX = AR.alloc("EX", [2, NBK, 2], F32)
        SCL = AR.alloc("SCL", [2, NBK, 2], F32)
        Vt = [AR.alloc("Vt%d" % pb, [NG, 128], BF16) for pb in range(2)]
        GG = [AR.alloc("GG%d" % pb, [NG, 128], BF16) for pb in range(2)]
        NSET = 4
        SETS = [dict(X=TT(KT.t[:, i * TS:(i + 1) * TS], Tok("Xs%d" % i)), U=AR.alloc("Us%d" % i, [TS], F32), T2=AR.alloc("T2s%d" % i, [TS], F32),
                     B=AR.alloc("Bs%d" % i, [TS], F32), KH=AR.alloc("KHt%d" % i, [TS], BF16)) for i in range(NSET)]
        SQt = [AR.alloc("SQt%d" % d, [NBK, 128], BF16) for d in range(2)]
        ST = [[AR.alloc("ST%d_%d" % (d, i), [128], F32) for i in range(2)] for d in range(2)]
        SIL = AR.alloc("SIL", [256], F32)
        ATt = [AR.alloc("AT%d" % i, [256], BF16) for i in range(4)]
        MTM = [AR.alloc("MTM%d" % i, [128], BF16) for i in range(2)]
        sto = [AR.alloc("sto%d" % i, [4], F32) for i in range(4)]
        junk2 = AR.alloc("junk2", [128], BF16)
        LRT = AR.alloc("LRT", [2, T], BF16)
        WLR = AR.alloc("WLR", [8, 32], BF16)
        WGU = AR.alloc("WGU", [2, 256], BF16)

        def bcol(Bap, parts, col, nblk):
            pstep = Bap.ap[0][0]
            return bass.AP(Bap.tensor, Bap.offset + col, [[pstep, parts], [BS, nblk], [0, BS]])

        P.dma("pool", WLR.t, win_d[:, OFF_LR:OFF_LR + 32].rearrange("(k p) n -> p k n", p=128), [], [WLR])
        P.dma("pool", WGU.t[0:16], wgu_d.rearrange("d r k -> r d k"), [], [WGU])
        P.dma("sp", badaT.t, badaT_d, [], [badaT])
        P.op("dve", lambda e: e.tensor_copy(scb.t, SC), [smalls], [scb])
        fmb = {"n": 0}

        def fm_bank():
            fmb["n"] += 1
            return fmb["n"] % 2

        def head_cfg(hh):
            gla = hh < 4
            h = hh if gla else hh - 4
            if gla:
                p = h // 2
                owner = (h % 2 == 0)
                if owner:
                    groups = [(OFF_QA + 128 * p, 128, 0), (OFF_KA + 128 * p, 128, 128), (OFF_VA + 128 * h, 128, 256), (OFF_GA + 128 * h, 128, 384)]
                    c_vg = 256
                else:
                    groups = [(OFF_VA + 128 * h, 128, 0), (OFF_GA + 128 * h, 128, 128)]
                    c_vg = 0
                return dict(gla=True, h=h, p=p, owner=owner, dk=64, po=64 * (h % 2), dscale=-1.0 / 16.0, groups=groups, c_q=0, c_k=128, c_vg=c_vg, c_f=None)
            groups = [(OFF_QB + 128 * h, 128, 0), (OFF_FB + 128 * h, 128, 128), (OFF_FB + 512 + 128 * h, 128, 256),
                      (OFF_IB + 128 * h, 128, 384), (OFF_GB + 128 * h, 128, 512)]
            return dict(gla=False, h=h, p=0, owner=True, dk=128, po=0, dscale=1.0, groups=groups, c_q=0, c_k=None, c_vg=384, c_f=(128, 256))

        head_w = {}

        def load_head_weights(hh):
            cfg = head_cfg(hh)
            w = ring["slots"][hh % 2]
            wv = w.t[:, 0:8 * 640].rearrange("p (k n) -> p k n", k=8)
            for (c0, n, o) in cfg["groups"]:
                P.dma("pool", wv[:, :, o:o + n], win_d[:, c0:c0 + n].rearrange("(k p) n -> p k n", p=128), [], [w])
            head_w[hh] = (w, wv)

        def mod_computeT(j, w):
            b = fm_bank()
            for n in range(4):
                c0 = j * D + n * 256
                half = n % 2
                wv = w.t[:, half * 2048:(half + 1) * 2048].rearrange("p (k n) -> p k n", k=8)
                P.dma("pool", wv, wada_d[:, c0:c0 + 256].rearrange("(k p) n -> p k n", p=128), [], [w])
                for mb in range(2):
                    for kc in range(8):
                        P.op("pe", lambda e, kc=kc, mb=mb, n=n, wv=wv: e.matmul(pst(b)[:, 2 * n + mb:2 * n + mb + 1], wv[:, kc, mb * 128:(mb + 1) * 128], scb.t[:, kc:kc + 1],
                                                                              start=(kc == 0), stop=(kc == 7)), [w, scb], PQ(b, 0, 512))
                yield
            copy_op("dve", modT.t[:, j - 2, :], pst(b)[:, 0:8], PQ(b, 0, 512), [modT])
            P.op("dve", lambda e: e.tensor_tensor(modT.t[:, j - 2, :], modT.t[:, j - 2, :], badaT.t[:, j * 8:(j + 1) * 8], ALU.add), [modT, badaT], [modT])

        def stageAB1(hh, part):
            cfg = head_cfg(hh)
            gla, h, dscale, owner = cfg["gla"], cfg["h"], cfg["dscale"], cfg["owner"]
            dk = 128
            c_q, c_k, c_vg, c_f = cfg["c_q"], cfg["c_k"], cfg["c_vg"], cfg["c_f"]
            pb = hh % 2
            w, wv = head_w[hh]
            if part == "a" and hh + 1 < 8:
                load_head_weights(hh + 1)
            myVt, myGG = Vt[pb], GG[pb]

            def fm_proj(c0, M, tiles, dst_ap_fn, dst_toks, scale=None):
                for tt in tiles:
                    b = fm_bank()
                    for kc in range(8):
                        P.op("pe", lambda e, kc=kc, b=b, tt=tt: e.matmul(pst(b)[0:M, :], wv[:, kc, c0:c0 + M], hT.t[:, kc, tt * 512:(tt + 1) * 512],
                                                                       start=(kc == 0), stop=(kc == 7)),
                             [w] + hT_tok[4 * tt:4 * tt + 4], PQ(b, 0, 512))
                    yield
                    copy_op(evac_eng(), dst_ap_fn(tt), pst(b)[0:M, :], PQ(b, 0, 512), dst_toks, scale=scale)

            if part == "a" and owner:
                yield from fm_proj(c_q, dk, range(4), lambda tt: QT.t[0:dk, tt * 512:(tt + 1) * 512], [QT], scale=(0.125 if gla else None))
            if part == "a" and gla and owner:
                yield from fm_proj(c_k, dk, range(4), lambda tt: KT.t[0:dk, tt * 512:(tt + 1) * 512], [KT])
            if part == "a" and hh == 0:
                for d in range(2):
                    for tt in range(4):
                        b = fm_bank()
                        for kc in range(8):
                            P.op("pe", lambda e, kc=kc, b=b, tt=tt, d=d: e.matmul(pst(b)[0:16, :], WLR.t[:, kc, 16 * d:16 * d + 16], hT.t[:, kc, tt * 512:(tt + 1) * 512],
                                                                                  start=(kc == 0), stop=(kc == 7)),
                                 [WLR] + hT_tok[4 * tt:4 * tt + 4], PQ(b, 0, 512))
                        copy_op(evac_eng(), LRT.t[0:16, d, tt * 512:(tt + 1) * 512], pst(b)[0:16, :], PQ(b, 0, 512), [LRT])
                        yield
            if part == "a":
                return
            for bp in range(NG // 2):
                bk = [2, 6][bp % 2]
                for i in range(2):
                    blk = 2 * bp + i
                    for kc in range(8):
                        P.op("pe", lambda e, kc=kc, bk=bk, i=i, blk=blk: e.matmul(pst(bk)[:, i * 256:(i + 1) * 256], hT.t[:, kc, blk * 128:(blk + 1) * 128],
                                                                                wv[:, kc, c_vg:c_vg + 256], start=(kc == 0), stop=(kc == 7)),
                             [w, hT_tok[blk]], PQ(bk, i * 256, (i + 1) * 256))
                pv = pst(bk).rearrange("p (b c) -> p b c", b=2)
                yield
                ce = evac_eng()
                copy_op(ce, myVt.t[:, 2 * bp:2 * bp + 2, :], pv[:, :, 0:128], PQ(bk, 0, 512), [myVt])
                copy_op(ce, myGG.t[:, 2 * bp:2 * bp + 2, :], pv[:, :, 128:256], PQ(bk, 0, 512), [myGG])
                gsp = myGG.t[:, 2 * bp:2 * bp + 2, :].rearrange("p b c -> p (b c)")
                P.op("act", lambda e, gsp=gsp: e.activation(SIL.t, gsp, AF.Exp, scale=-1.0), [myGG], [SIL])
                P.op("act", lambda e: e.activation(SIL.t, SIL.t, AF.Ln, bias=1.0), [SIL], [SIL])
                P.op("act", lambda e: e.activation(SIL.t, SIL.t, AF.Exp, scale=-1.0), [SIL], [SIL])
                P.op("dve", lambda e, gsp=gsp: e.tensor_tensor(gsp, gsp, SIL.t, ALU.mult), [myGG, SIL], [myGG])
                yield

        def stageAB2(hh):
            cfg = head_cfg(hh)
            gla, h, dscale, owner, pr = cfg["gla"], cfg["h"], cfg["dscale"], cfg["owner"], cfg["p"]
            dk = 128
            c_f = cfg["c_f"]
            w, wv = head_w[hh]
            myQTt, myKTt, myKTM, myLH, myEX, mySCL = QTt, KTt, KTM, LH, EX, SCL
            if not owner:
                return

            def fm_proj(c0, M, tiles, dst_ap_fn, dst_toks, scale=None):
                for tt in tiles:
                    b = fm_bank()
                    for kc in range(8):
                        P.op("pe", lambda e, kc=kc, b=b, tt=tt: e.matmul(pst(b)[0:M, :], wv[:, kc, c0:c0 + M], hT.t[:, kc, tt * 512:(tt + 1) * 512],
                                                                       start=(kc == 0), stop=(kc == 7)),
                             [w] + hT_tok[4 * tt:4 * tt + 4], PQ(b, 0, 512))
                    copy_op("act", dst_ap_fn(tt), pst(b)[0:M, :], PQ(b, 0, 512), dst_toks, scale=scale)
                    yield

            v3 = lambda ap: ap.rearrange("p (b t) -> p b t", b=BPS)

            def decay(d, s_, tiles, sp0, sp1):
                st_ = SETS[(s_ % 2) * 2 + d]
                Xs, Us, T2s, Bs, KHt = st_["X"], st_["U"], st_["T2"], st_["B"], st_["KH"]
                X_, U_, T2_, B_, KH_ = Xs.t[0:dk], Us.t[0:dk], T2s.t[0:dk], Bs.t[0:dk], KHt.t[0:dk]
                col = (d * 2 + pr) if gla else (d * 4 + h)
                sel = d
                if gla:
                    for ti, tt in enumerate(tiles):
                        b = fm_bank()
                        P.op("pe", lambda e, b=b, tt=tt: e.matmul(pst(b)[0:128, :], WGU.t[0:16, d, 128 * pr:128 * pr + 128], LRT.t[0:16, d, tt * 512:(tt + 1) * 512],
                                                                start=True, stop=True), [WGU, LRT], PQ(b, 0, 512))
                        P.op("act", lambda e, b=b, ti=ti: e.activation(U_[:, ti * 512:(ti + 1) * 512], pst(b)[0:128, :], AF.Exp, scale=-1.0,
                                                                      bias=NEGB[:, col:col + 1]), PQ(b, 0, 512) + [smalls], [Us])
                    yield
                    P.op("act", lambda e: e.activation(T2_, U_, AF.Ln, bias=1.0), [Us], [T2s])
                    P.op("dve", lambda e: e.tensor_tensor_scan(B_, scanmask.t[0:dk, :], T2_, 0.0, ALU.mult, ALU.add), [scanmask, T2s], [Bs])
                else:
                    cf = c_f[d]
                    yield from fm_proj(cf, 128, tiles, lambda tt: X_[:, (tt - tiles[0]) * 512:(tt - tiles[0] + 1) * 512], [Xs])
                    P.op("act", lambda e: e.activation(U_, X_, AF.Exp, scale=-1.0), [Xs], [Us])
                    P.op("act", lambda e: e.activation(T2_, U_, AF.Ln, bias=1.0), [Us], [T2s])
                    P.op("act", lambda e: e.activation(U_, U_, AF.Ln, bias=1.0, scale=LB[:, col:col + 1]), [Us, smalls], [Us])
                    yield
                    P.op("dve", lambda e: e.tensor_tensor(U_, U_, T2_, ALU.subtract), [Us, T2s], [Us])
                    P.op("act", lambda e: e.activation(X_, X_, AF.Exp), [Xs], [Xs])
                    P.op("act", lambda e: e.activation(X_, X_, AF.Ln, bias=1.0), [Xs], [Xs])
                    P.op("dve", lambda e: e.tensor_tensor_scan(B_, scanmask.t[0:dk, :], U_, 0.0, ALU.mult, ALU.add), [scanmask, Us], [Bs])
                yield
                B3 = v3(B_)
                MID = BS // 2 - 1
                lh = myLH.t[0:dk, d, s_ * BPS:(s_ + 1) * BPS, :]
                ex = myEX.t[0:dk, d, s_ * BPS:(s_ + 1) * BPS, :]
                P.op("dve", lambda e: e.tensor_copy(lh[:, :, 0:1], B3[:, :, MID:MID + 1]), [Bs], [myLH])
                P.op("dve", lambda e: e.tensor_tensor(lh[:, :, 1:2], B3[:, :, BS - 1:BS], B3[:, :, MID:MID + 1], ALU.subtract), [Bs], [myLH])
                P.op("act", lambda e: e.activation(ex, lh, AF.Exp, scale=dscale), [myLH], [myEX])
                bmid = bcol(B_, dk, MID, BPS)
                ehb = bass.AP(myEX.t.tensor, myEX.t[0:dk, d, s_ * BPS:(s_ + 1) * BPS, 1 - sel].offset,
                              [[myEX.t.ap[0][0], dk], [2, BPS], [0, BS]])
                if gla:
                    if d == 0:
                        P.op("dve", lambda e: e.tensor_tensor(v3(U_), B3, bmid, ALU.subtract), [Bs], [Us])
                    else:
                        P.op("dve", lambda e: e.tensor_tensor(T2_, B_, T2_, ALU.subtract), [Bs, T2s], [T2s])
                        P.op("dve", lambda e: e.tensor_tensor(v3(U_), bmid, v3(T2_), ALU.subtract), [Bs, T2s], [Us])
                    P.op("dve", lambda e: e.tensor_scalar(U_, U_, 640.0, -640.0, op0=ALU.min, op1=ALU.max), [Us], [Us])
                    P.op("act", lambda e: e.activation(T2_, U_, AF.Exp, scale=dscale), [Us], [T2s])
                    P.op("pool", lambda e: e.tensor_tensor(myQTt[d].t[0:dk, sp0:sp1], QT.t[0:dk, sp0:sp1], T2_, ALU.mult), [QT, T2s], [myQTt[d]])
                    yield
                    P.op("act", lambda e: e.activation(T2_, U_, AF.Exp, scale=-dscale), [Us], [T2s])
                    P.op("dve", lambda e: e.tensor_tensor(myKTt[d].t[0:dk, sp0:sp1], KT.t[0:dk, sp0:sp1], T2_, ALU.mult), [KT, T2s], [myKTt[d]])
                else:
                    if d == 0:
                        P.op("dve", lambda e: e.tensor_tensor(v3(T2_), B3, bmid, ALU.subtract), [Bs], [T2s])
                    else:
                        P.op("dve", lambda e: e.tensor_tensor(U_, B_, U_, ALU.subtract), [Bs, Us], [Us])
                        P.op("dve", lambda e: e.tensor_tensor(v3(T2_), bmid, v3(U_), ALU.subtract), [Bs, Us], [T2s])
                    P.op("dve", lambda e: e.tensor_scalar(T2_, T2_, 40.0, -40.0, op0=ALU.min, op1=ALU.max), [T2s], [T2s])
                    P.op("act", lambda e: e.activation(U_, T2_, AF.Exp), [T2s], [Us])
                    P.op("pool", lambda e: e.tensor_tensor(myQTt[d].t[0:dk, sp0:sp1], QT.t[0:dk, sp0:sp1], U_, ALU.mult), [QT, Us], [myQTt[d]])
                    yield
                    P.op("dve", lambda e: e.tensor_tensor(X_, X_, T2_, ALU.add), [Xs, T2s], [Xs])
                    P.op("act", lambda e: e.activation(myKTt[d].t[0:dk, sp0:sp1], X_, AF.Exp, scale=-1.0, bias=L1M[:, col:col + 1]), [Xs, smalls], [myKTt[d]])
                yield
                P.op("dve", lambda e: e.tensor_tensor(v3(KH_), v3(myKTt[d].t[0:dk, sp0:sp1]), ehb, ALU.mult), [myKTt[d], myEX], [KHt])
                kb = fm_bank()
                pvk = pst(kb).bitcast(BF16).rearrange("p (b t) -> p b t", b=8)
                ng = TS // 128
                for i in range(ng):
                    P.op("pe", lambda e, i=i: e.transpose(pvk[:, i, 0:dk], KH_[:, i * 128:(i + 1) * 128], identB.t[0:dk, 0:dk]), [KHt, identB], PQ(kb, 0, 512))
                copy_op(evac_eng(), myKTM[d].t[:, s_ * ng:(s_ + 1) * ng, 0:dk], pvk[:, 0:ng, 0:dk], PQ(kb, 0, 512), [myKTM[d]])
                yield

            for s0_ in range(0, NSPAN, 2):
                gens = []
                for s_ in (s0_, s0_ + 1):
                    tiles = list(range(s_ * TS // 512, (s_ + 1) * TS // 512))
                    for d in range(2):
                        gens.append(decay(d, s_, tiles, s_ * TS, (s_ + 1) * TS))
                alive = [True] * len(gens)
                while any(alive):
                    for gi in range(len(gens)):
                        if alive[gi]:
                            try:
                                next(gens[gi])
                            except StopIteration:
                                alive[gi] = False
                    yield
            for d in range(2):
                sel = d
                mc = mchain.t[0:dk, d * NBK:(d + 1) * NBK]
                P.op("dve", lambda e, d=d, sel=sel, mc=mc: e.tensor_tensor(mySCL.t[0:dk, d, :, 0], myEX.t[0:dk, d, :, sel], mc, ALU.mult), [myEX, mchain], [mySCL])
                P.op("dve", lambda e, d=d, sel=sel: e.tensor_tensor(mySCL.t[0:dk, d, :, 1], mySCL.t[0:dk, d, :, 0], myEX.t[0:dk, d, :, 1 - sel], ALU.mult), [myEX, mySCL], [mySCL])
            yield

        def stageC(hh):
            cfg = head_cfg(hh)
            gla, h, dk, po = cfg["gla"], cfg["h"], cfg["dk"], cfg["po"]
            pq = slice(po, po + dk)
            pb = hh % 2
            myQTt, myKTt, myKTM, myVt, myGG, mySCL = QTt, KTt, KTM, Vt[pb], GG[pb], SCL
            kmt = {"n": 0}
            pv7 = pst(7).bitcast(BF16).rearrange("p (r s t) -> p r s t", r=2, s=4)
            grp_done = {}
            for d in range(2):
                src = (sig_d if gla else sih_d)[d, h]
                P.dma("sp", ST[d][0].t[pq], src, [], [ST[d][0]])
            state = {0: ST[0][0], 1: ST[1][0]}
            nxt = [1, 1]

            def chain_p(d, n, cb, after=()):
                g, hf = n // 2, n % 2
                return P.op("pe", lambda e: e.matmul(pst(cb)[pq, d * 128:(d + 1) * 128], myKTM[d].t[hf * 64:(hf + 1) * 64, g, pq], myVt.t[hf * 64:(hf + 1) * 64, g, :], start=True, stop=True),
                            [myKTM[d], myVt], PQ(cb, 0, 256), extra=after)

            def chain_step(d, n, cb):
                prev = state[d]
                new = ST[d][nxt[d]]
                nxt[d] = (nxt[d] + 1) % 2
                P.op("act", lambda e: e.activation(SQt[d].t[pq, n, :], prev.t[pq], AF.Copy, scale=mySCL.t[pq, d, n, 0:1]), [prev, mySCL], [SQt[d]])
                P.op("dve", lambda e: e.scalar_tensor_tensor(new.t[pq], prev.t[pq], mySCL.t[pq, d, n, 1:2], pst(cb)[pq, d * 128:(d + 1) * 128], op0=ALU.mult, op1=ALU.add),
                     [prev, mySCL] + PQ(cb, 0, 256), [new])
                state[d] = new
                if (d == 0 and n % 4 == 3) or (d == 1 and n % 4 == 0):
                    dst = (sog_d if gla else soh_d)[n // 4, d, h]
                    P.dma("sp", dst, new.t[pq], [new], [], is_out=True)

            gctr = {"n": 0}
            ginfo = {}

            def og_a1(g):
                i = gctr["n"]
                gctr["n"] += 1
                ginfo[g] = i
                blk = slice(g * 128, (g + 1) * 128)
                for d in range(2):
                    P.op("pe", lambda e, d=d: e.matmul(pst(5)[:, d * 128:(d + 1) * 128], myKTt[d].t[pq, blk], myQTt[d].t[pq, blk], start=True, stop=True),
                         [myKTt[d], myQTt[d]], PQ(5, 0, 256))

            def og_a2(g):
                at = ATt[ginfo[g] % 4]
                P.op("dve", lambda e: e.tensor_tensor(at.t, pst(5)[:, 0:256], maskT2.t, ALU.mult), PQ(5, 0, 256) + [maskT2], [at])

            def og_b(g):
                i = ginfo[g]
                at = ATt[i % 4]
                ob_ = [2, 6][i % 2]
                og = pst(ob_)[:, 0:128]
                otok = PQ(ob_, 0, 128)
                P.op("pe", lambda e: e.matmul(og, at.t[:, 0:128], myVt.t[:, g, :], start=True, stop=False), [at, myVt], otok)
                P.op("pe", lambda e: e.matmul(og, at.t[:, 128:256], myVt.t[:, g, :], start=False, stop=False), [at, myVt], otok)
                for hf in range(2):
                    for d in range(2):
                        last = (hf == 1 and d == 1)
                        c0 = g * 128 + hf * 64
                        P.op("pe", lambda e, hf=hf, d=d, last=last, c0=c0: e.matmul(pst(ob_)[hf * 64:(hf + 1) * 64, 0:128], myQTt[d].t[pq, c0:c0 + 64], SQt[d].t[pq, 2 * g + hf, :],
                                                                                   start=False, stop=last), [myQTt[d], SQt[d]], otok)

            def og_c(g):
                i = ginfo[g]
                ob_ = [2, 6][i % 2]
                og = pst(ob_)[:, 0:128]
                otok = PQ(ob_, 0, 128)
                so = sto[i % 4]
                P.op("act", lambda e: e.activation(junk2.t, og, AF.Square, accum_out=so.t[:, 0:1]), otok, [junk2, so], multi=True)
                P.op("act", lambda e: e.activation(so.t[:, 1:2], so.t[:, 0:1], AF.Ln, bias=128.0 * EPS), [so], [so])
                P.op("act", lambda e: e.activation(so.t[:, 2:3], so.t[:, 1:2], AF.Exp, scale=-0.5), [so], [so])

            def og_d(g):
                i = ginfo[g]
                ob_ = [2, 6][i % 2]
                og = pst(ob_)[:, 0:128]
                otok = PQ(ob_, 0, 128)
                so = sto[i % 4]
                mt = MTM[i % 2]
                P.op("dve", lambda e: e.scalar_tensor_tensor(mt.t, og, so.t[:, 2:3], myGG.t[:, g, :], op0=ALU.mult, op1=ALU.mult), otok + [so, myGG], [mt])
                grp = g // 4
                r = grp % 2
                rtok = PQ(7, r * 256, (r + 1) * 256)
                P.op("pe", lambda e: e.transpose(pv7[:, r, g % 4, :], mt.t, identB.t), [mt, identB], rtok)
                grp_done[grp] = grp_done.get(grp, 0) + 1
                if grp_done[grp] == 4:
                    copy_op(evac_eng(), mergedT.t[:, hh, grp * 512:(grp + 1) * 512], pv7[:, r].rearrange("p s t -> p (s t)"), rtok, [mT_tok[hh]])

            ready = {g: max(2 * g + 1, NBK - 1 - 2 * g) for g in range(NG)}
            p_ahead = int(_os.environ.get('DBG_PAHEAD', '1'))
            if p_ahead:
                o_ = chain_p(0, 0, 3)
                chain_p(1, NBK - 1, 3, after=(o_,))
            for s_ in range(NBK + 4):
                if s_ < NBK:
                    cb = 3 + (s_ % 2)
                    if not p_ahead:
                        o_ = chain_p(0, s_, cb)
                        chain_p(1, NBK - 1 - s_, cb, after=(o_,))
                    chain_step(0, s_, cb)
                    chain_step(1, NBK - 1 - s_, cb)
                    if p_ahead and s_ + 1 < NBK:
                        cbn = 3 + ((s_ + 1) % 2)
                        o_ = chain_p(0, s_ + 1, cbn)
                        chain_p(1, NBK - 2 - s_, cbn, after=(o_,))
                for g in range(NG):
                    if ready[g] == s_ - 3:
                        og_d(g)
                for g in range(NG):
                    if ready[g] == s_ - 2:
                        og_c(g)
                for g in range(NG):
                    if ready[g] == s_ - 1:
                        og_b(g)
                pair_now = [g for g in range(NG) if ready[g] == s_ + 3]
                pair_prev = [g for g in range(NG) if ready[g] == s_ + 2]
                pair_prev2 = [g for g in range(NG) if ready[g] == s_ + 1]
                if pair_prev2:
                    og_a2(pair_prev2[1])
                if pair_prev:
                    og_a2(pair_prev[0])
                    og_a1(pair_prev[1])
                if pair_now:
                    og_a1(pair_now[0])
                yield

        heads = [int(v) for v in _os.environ.get('DBG_HEADS', '0,1,2,3,4,5,6,7').split(',') if v != '']
        assert heads == list(range(8))
        def run_ab2_with_b(hh):
            g2 = stageAB2(hh)
            gb = stageAB1(hh, "b")
            a2, ab = True, True
            while a2 or ab:
                if a2:
                    try:
                        next(g2)
                    except StopIteration:
                        a2 = False
                if ab:
                    try:
                        next(gb)
                    except StopIteration:
                        ab = False
            if hh < 4:
                for _ in mod_computeT(2 + hh, head_w[hh][0]):
                    pass

        load_head_weights(0)
        for _ in stageAB1(0, "a"):
            pass
        run_ab2_with_b(0)
        ilv = int(_os.environ.get('DBG_ILV', '2'))
        for hh in range(8):
            cgen = stageC(hh)
            abgen = stageAB1(hh + 1, "a") if hh + 1 < 8 else None
            c_alive, ab_alive = True, abgen is not None
            step = 0
            while c_alive or ab_alive:
                if c_alive:
                    try:
                        next(cgen)
                    except StopIteration:
                        c_alive = False
                if ab_alive and (not c_alive or (ilv >= 1 and step % ilv == 0)):
                    try:
                        next(abgen)
                    except StopIteration:
                        ab_alive = False
                step += 1
            if hh + 1 < 8:
                run_ab2_with_b(hh + 1)
        if debug:
            P.dma("sp", dbg["mergedT"], mergedT.t.rearrange("p k t -> p (k t)"), mT_tok, [], is_out=True)
        P.barrier()
        AR.reset(ph1_mark)
        X1 = AR.alloc("X1", [NB, D], F32)
        X1_tok = [Tok("X1_b%d" % b) for b in range(NB)]
        ph3_keep = AR.mark()
        MB[2] = AR.alloc("gate1_bc", [D], F32)
        MB[3] = AR.alloc("shift2_bc", [D], F32)
        MB[4] = AR.alloc("g2_bc", [D], F32)
        WO = AR.alloc("WO", [8, D], BF16)
        wstage = [AR.alloc("wstage%d" % i, [D], F32) for i in range(2)]
        junk = AR.alloc("junk", [D], BF16)
        tmpf = AR.alloc("tmpf", [D], F32)
        hb = [AR.alloc("hb%d" % i, [D], BF16) for i in range(2)]
        stt = [AR.alloc("stt%d" % i, [4], F32) for i in range(3)]
        dgt = [AR.alloc("dgt%d" % i, [128], F32) for i in range(2)]
        n2T = AR.alloc("n2T", [8], F32)
        g2T = AR.alloc("g2T", [8], F32)
        xb = {"n": 0}

        def expand(vec_ap, vec_toks, dst):
            for half in range(2):
                b = xb["n"] % 2
                xb["n"] += 1
                for q in range(4):
                    kc = half * 4 + q
                    dg = dgt[kc % 2]
                    P.op("dve", lambda e, kc=kc, dg=dg: e.tensor_scalar(dg.t, identF.t, vec_ap[:, kc:kc + 1], None, op0=ALU.mult), [identF] + vec_toks, [dg])
                    P.op("pe", lambda e, q=q, b=b, dg=dg: e.matmul(pst(b)[:, q * 128:(q + 1) * 128], onesF.t, dg.t, start=True, stop=True), [onesF, dg], PQ(b, 0, 512))
                copy_op(evac_eng(), dst.t[:, half * 512:(half + 1) * 512], pst(b)[:, :], PQ(b, 0, 512), [dst])

        P.dma("sp", n2T.t, norm2T_d, [], [n2T])
        P.op("dve", lambda e: e.scalar_tensor_tensor(g2T.t, modT.t[:, 2, :], 1.0, n2T.t, op0=ALU.add, op1=ALU.mult), [modT, n2T], [g2T])
        expand(modT.t[:, 0, :], [modT], MB[2])
        expand(modT.t[:, 1, :], [modT], MB[3])
        expand(g2T.t, [g2T], MB[4])
        for kc in range(8):
            ws = wstage[kc % 2]
            P.dma("sp", ws.t, wout_d[kc * 128:(kc + 1) * 128, :], [], [ws])
            gcol = 0 if kc < 4 else 1
            P.op("dve", lambda e, kc=kc, ws=ws, gcol=gcol: e.scalar_tensor_tensor(WO.t[:, kc, :], ws.t, GS[:, gcol:gcol + 1], MB[2].t, op0=ALU.mult, op1=ALU.mult),
                 [ws, smalls, MB[2]], [WO])
        for b in range(NB):
            P.dma("sp", X1.t[:, b, :], x_d[b * 128:(b + 1) * 128, :], [], [X1_tok[b]])
        ob = {"n": 0}
        for b in range(NB):
            for half in range(2):
                bk = 2 + ob["n"] % 2
                ob["n"] += 1
                for kc in range(8):
                    P.op("pe", lambda e, kc=kc, bk=bk, b=b, half=half: e.matmul(pst(bk)[:, :], mergedT.t[:, kc, b * 128:(b + 1) * 128], WO.t[:, kc, half * 512:(half + 1) * 512],
                                                                              start=(kc == 0), stop=(kc == 7)), [mT_tok[kc], WO], PQ(bk, 0, 512))
                hs = slice(half * 512, (half + 1) * 512)
                P.op("dve", lambda e, bk=bk, b=b, hs=hs: e.tensor_tensor(X1.t[:, b, hs], pst(bk)[:, :], X1.t[:, b, hs], ALU.add), PQ(bk, 0, 512) + [X1_tok[b]], [X1_tok[b]])
            norm_A(X1.t[:, b, :], [X1_tok[b]], stt[b % 3])
            if b >= 1:
                norm_B1(X1.t[:, b - 1, :], [X1_tok[b - 1]], MB[4], MB[3], b - 1, stt[(b - 1) % 3])
            if b >= 2:
                norm_B2(b - 2, 4 + ((b - 2) % 2))
            if debug:
                P.dma("sp", dbg["x1"][b * 128:(b + 1) * 128, :], X1.t[:, b, :], [X1_tok[b]], [], is_out=True)
        norm_B1(X1.t[:, NB - 1, :], [X1_tok[NB - 1]], MB[4], MB[3], NB - 1, stt[(NB - 1) % 3])
        norm_B2(NB - 2, 4 + ((NB - 2) % 2))
        norm_B2(NB - 1, 4 + ((NB - 1) % 2))
        if debug:
            dump("h2T", TT(hT.t, hT_tok[0]), 8 * T, BF16)

        P.barrier()
        AR.reset(ph3_keep)
        AR.end = top_keep
        MB[5] = AR.alloc("gate2_bc", [D], F32)
        FN = AR.alloc("fnorm_bc", [D], F32)
        dgt = [AR.alloc("dgt%d" % i, [128], F32) for i in range(2)]
        GMAX = max(j1 - j0 for j0, j1 in GROUPS)
        HT = AR.alloc("HT", [GMAX, T], BF16)
        WD = AR.alloc("WD", [GMAX, D], BF16)
        wdst = [AR.alloc("wdst%d" % i, [D], F32) for i in range(2)]
        UB = [AR.alloc("UB%d" % i, [UW], BF16) for i in range(2)]
        SG = AR.alloc("SG", [T], F32)
        DG = [AR.alloc("DG%d" % i, [11, 128], BF16) for i in range(2)]
        w11T = AR.alloc("w11T", [NCH * 11], F32)
        bconvT = AR.alloc("bconvT", [NCH], F32)
        ring["slots"] = [AR.alloc("wringC%d" % i, [8 * 256], BF16) for i in range(3)]
        ring["n"] = 0
        yst = [AR.alloc("yst%d" % i, [D], F32) for i in range(2)]
        DACC = AR.alloc("DACC", [T], BF16)
        ctmp = yst
        junk = AR.alloc("junk", [D], BF16)
        stt = [AR.alloc("stt%d" % i, [4], F32) for i in range(3)]
        expand(modT.t[:, 3, :], [modT], MB[5])
        P.dma("sp", FN.t, fnorm_d.partition_broadcast(128), [], [FN])
        P.dma("sp", w11T.t, w11T_d, [], [w11T])
        P.dma("sp", bconvT.t, bconvT_d, [], [bconvT])
        for u in range(2):
            P.op("pool", lambda e, u=u: e.memset(UB[u].t, 0.0), [], [UB[u]])
        P.op("pool", lambda e: e.memset(DACC.t, 0.0), [], [DACC])
        identB_b11 = bass.AP(identB.t.tensor, identB.t.offset, [list(identB.t.ap[0]), [0, 11], [1, 128]])
        ucnt = {"n": 0}
        pair_w = {}

        def load_pair(j):
            w = next_w()
            wv = w.t[:, 0:8 * 256].rearrange("p (k n) -> p k n", k=8)
            P.dma("pool", wv[:, :, 0:128], wup_d[:, j * 128:(j + 1) * 128].rearrange("(k p) n -> p k n", p=128), [], [w])
            P.dma("pool", wv[:, :, 128:256], wup_d[:, FFN_H + j * 128:FFN_H + (j + 1) * 128].rearrange("(k p) n -> p k n", p=128), [], [w])
            pair_w[j] = (w, wv)

        def up_proj(j, is_up):
            w, wv = pair_w[j]
            off = 128 if is_up else 0
            u = ucnt["n"] % 2
            ucnt["n"] += 1
            for tt in range(4):
                for kc in range(8):
                    P.op("pe", lambda e, kc=kc, tt=tt: e.matmul(pst(tt)[:, :], wv[:, kc, off:off + 128], hT.t[:, kc, tt * 512:(tt + 1) * 512], start=(kc == 0), stop=(kc == 7)),
                         [w] + hT_tok[4 * tt:4 * tt + 4], PQ(tt, 0, 512))
                P.op("act", lambda e, tt=tt: e.activation(UB[u].t[:, UPAD + tt * 512:UPAD + (tt + 1) * 512], pst(tt)[:, :], AF.Copy), PQ(tt, 0, 512), [UB[u]])
            return u

        ctn = {"n": 0}

        def conv(j, jj, is_up, u):
            cc = (NPAIR + j) if is_up else j
            dg = DG[cc % 2]
            wb = bass.AP(w11T.t.tensor, w11T.t.offset + cc * 11, [list(w11T.t.ap[0]), [1, 11], [0, 128]])
            P.op("pool", lambda e: e.tensor_tensor(dg.t, identB_b11, wb, ALU.mult), [identB, w11T], [dg])
            ub = UB[u].t
            acc3 = DACC.t.rearrange("p (r c) -> p r c", c=64)[:, :, 0:63]
            for k_, dy in enumerate((0, -1, 1)):
                wi = (dy + 1) * 3 + 2
                src3 = ub[:, UPAD + 64 * dy + 1:UPAD + 64 * dy + 1 + T].rearrange("p (r c) -> p r c", c=64)[:, :, 0:63]
                wsc = w11T.t[:, cc * 11 + wi:cc * 11 + wi + 1]
                if k_ == 0:
                    P.op("dve", lambda e, src3=src3, wsc=wsc: e.tensor_scalar(acc3, src3, wsc, None, op0=ALU.mult), [UB[u], w11T], [DACC])
                else:
                    P.op("dve", lambda e, src3=src3, wsc=wsc: e.scalar_tensor_tensor(acc3, src3, wsc, acc3, op0=ALU.mult, op1=ALU.add), [UB[u], w11T, DACC], [DACC])
            for tt in range(4):
                base = UPAD + tt * 512
                pt = pst(4 + tt)
                taps = []
                for dy in (0, -1, 1):
                    taps.append(((dy + 1) * 3 + 1, pt[:, 0:512], ub[:, base + 64 * dy:base + 64 * dy + 512]))
                for dy in (0, -1, 1):
                    o3 = pt[:, 0:512].rearrange("p (r c) -> p r c", c=64)[:, :, 1:64]
                    r3 = ub[:, base + 64 * dy - 1:base + 64 * dy - 1 + 512].rearrange("p (r c) -> p r c", c=64)[:, :, 1:64]
                    taps.append(((dy + 1) * 3 + 0, o3, r3))
                o4 = pt[:, 0:512].rearrange("p (a r c) -> p a r c", a=2, r=4)[:, :, 1:4, 0]
                r4 = ub[:, base - 1:base - 1 + 512].rearrange("p (a r c) -> p a r c", a=2, r=4)[:, :, 1:4, 0]
                taps.append((9, o4, r4))
                o4 = pt[:, 0:512].rearrange("p (a r c) -> p a r c", a=2, r=4)[:, :, 0:3, 63]
                r4 = ub[:, base + 1:base + 1 + 512].rearrange("p (a r c) -> p a r c", a=2, r=4)[:, :, 0:3, 63]
                taps.append((10, o4, r4))
                for ti, (wi, oap, rap) in enumerate(taps):
                    P.op("pe", lambda e, wi=wi, oap=oap, rap=rap, ti=ti: e.matmul(oap, dg.t[:, wi, :], rap, start=(ti == 0), stop=(ti == len(taps) - 1)),
                         [dg, UB[u]], PQ(4 + tt, 0, 512))
                ts_ = slice(tt * 512, (tt + 1) * 512)
                ct = ctmp[ctn["n"] % 2]
                ctn["n"] += 1
                P.op("dve", lambda e, pt=pt, ts_=ts_, ct=ct: e.tensor_tensor(ct.t[:, 0:512], pt[:, 0:512], DACC.t[:, ts_], ALU.add), PQ(4 + tt, 0, 512) + [DACC], [ct])
                if not is_up:
                    P.op("act", lambda e, ts_=ts_, ct=ct: e.activation(SG.t[:, ts_], ct.t[:, 0:512], AF.Silu, bias=bconvT.t[:, cc:cc + 1]), [ct, bconvT], [SG])
                else:
                    P.op("dve", lambda e, ts_=ts_, ct=ct: e.scalar_tensor_tensor(HT.t[:, jj, ts_], ct.t[:, 0:512], bconvT.t[:, cc:cc + 1], SG.t[:, ts_], op0=ALU.add, op1=ALU.mult),
                         [ct, bconvT, SG], [HT])

        fin_pending = []

        def final_out(b):
            st = stt[b % 3]
            ys = yst[b % 2]
            xap = X1.t[:, b, :]
            P.op("dve", lambda e: e.scalar_tensor_tensor(ys.t, xap, st.t[:, 2:3], FN.t, op0=ALU.mult, op1=ALU.mult), [X1_tok[b], st, FN], [ys])
            P.dma("sp", y_d[b * 128:(b + 1) * 128, :], ys.t, [ys], [], is_out=True)

        load_pair(0)
        load_pair(1)
        dbk = {"n": 0}

        def wd_load(gi):
            j0, j1 = GROUPS[gi]
            for jj, j in enumerate(range(j0, j1)):
                wq = wdst[j % 2]
                P.dma("sp", wq.t, wdn_d[j * 128:(j + 1) * 128, :], [], [wq])
                P.op("dve", lambda e, jj=jj, wq=wq: e.tensor_tensor(WD.t[:, jj, :], wq.t, MB[5].t, ALU.mult), [wq, MB[5]], [WD])

        def down(gi):
            j0, j1 = GROUPS[gi]
            last_group = gi == len(GROUPS) - 1
            ng = j1 - j0
            for b in range(NB):
                for half in range(2):
                    bk = dbk["n"] % 4
                    dbk["n"] += 1
                    hs = slice(half * 512, (half + 1) * 512)
                    for jj in range(ng):
                        P.op("pe", lambda e, jj=jj, bk=bk, b=b, hs=hs: e.matmul(pst(bk)[:, :], HT.t[:, jj, b * 128:(b + 1) * 128], WD.t[:, jj, hs], start=(jj == 0), stop=(jj == ng - 1)),
                             [HT, WD], PQ(bk, 0, 512))
                    P.op("dve", lambda e, bk=bk, b=b, hs=hs: e.tensor_tensor(X1.t[:, b, hs], pst(bk)[:, :], X1.t[:, b, hs], ALU.add), PQ(bk, 0, 512) + [X1_tok[b]], [X1_tok[b]])
                if last_group:
                    st = stt[b % 3]
                    xap = X1.t[:, b, :]
                    P.op("act", lambda e, xap=xap, st=st, jk=junk: e.activation(jk.t, xap, AF.Square, accum_out=st.t[:, 0:1]), [X1_tok[b]], [junk, st], multi=True)
                    P.op("act", lambda e, st=st: e.activation(st.t[:, 1:2], st.t[:, 0:1], AF.Ln, scale=1.0 / D, bias=EPS), [st], [st])
                    P.op("act", lambda e, st=st: e.activation(st.t[:, 2:3], st.t[:, 1:2], AF.Exp, scale=-0.5), [st], [st])
                    fin_pending.append(b)
                    if len(fin_pending) > 1:
                        final_out(fin_pending.pop(0))
            if last_group:
                while fin_pending:
                    final_out(fin_pending.pop(0))

        wd_load(0)
        pend = None
        deferred = None
        for gi, (j0, j1) in enumerate(GROUPS):
            for j in range(j0, j1):
                for is_up in (False, True):
                    if (not is_up) and (j + 2 < NPAIR):
                        load_pair(j + 2)
                    u = up_proj(j, is_up)
                    if pend is not None:
                        if deferred is not None and pend[4] == gi and pend[2]:
                            down(deferred)
                            wd_load(gi)
                            deferred = None
                        conv(*pend[:4])
                    pend = (j, j - j0, is_up, u, gi)
            deferred = gi
        if deferred is not None and pend[4] == deferred:
            conv(*pend[:4])
            down(deferred)
        P.finish()
    return nc


def _consts():
    ident = np.eye(128, dtype=np.float32)
    j = np.arange(128)[:, None]
    i = np.arange(128)[None, :]
    same = (j // 64) == (i // 64)
    maskT2 = np.concatenate([(j <= i) & same, (j >= i) & same], axis=1).astype(np.float32)
    scanmask = np.ones((1, T), np.float32)
    scanmask[0, ::64] = 0.0
    return ident, maskT2, scanmask


def prep_core_inputs(inp):
    f32 = lambda a: np.ascontiguousarray(np.asarray(a, dtype=np.float32))
    ident, maskT2, scanmask = _consts()
    shared = {
        "w_ada": f32(inp["w_ada"][0]), "b_ada": f32(inp["b_ada"][0]).reshape(1, -1),
        "norm1": f32(inp["norm1"][0]).reshape(1, -1), "norm2": f32(inp["norm2"][0]).reshape(1, -1),
        "b_adaT": f32(np.asarray(inp["b_ada"][0]).reshape(48, 128).T), "norm2T": f32(np.asarray(inp["norm2"][0]).reshape(8, 128).T),
        "fnorm": f32(inp["final_norm"]).reshape(1, -1),
        "w_in": f32(inp["w_in"][0]), "w_gla_up": f32(inp["w_gla_up"][0]),
        "b_glaT": f32(np.asarray(inp["b_gla"][0]).reshape(2, 2, 128).transpose(2, 0, 1).reshape(128, 4)),
        "lbT": f32(np.asarray(inp["hgrn_lb"]).reshape(2, 2, 4, 128).transpose(3, 0, 1, 2).reshape(128, 16)),
        "gnorm": f32(np.stack([np.asarray(inp["gla_norm"][0]), np.asarray(inp["hgrn_norm"][0])], axis=1)),
        "w_out": f32(inp["w_out"][0]), "w_ffn_up": f32(inp["w_ffn_up"][0]),
        "bconvT": f32(np.asarray(inp["b_ffn_conv"][0]).reshape(NCH, 128).T),
        "w_ffn_down": f32(inp["w_ffn_down"][0]),
        "identF": ident, "maskT2": maskT2, "scanmask": scanmask,
    }
    conv = np.asarray(inp["ffn_conv"][0], dtype=np.float32).reshape(9, 2 * FFN_H)
    zero_row = np.zeros((1, 2 * FFN_H), np.float32)
    rows_s = np.concatenate([conv, zero_row, zero_row], axis=0)
    rows_p = np.concatenate([zero_row] * 3 + [conv[3:6]] + [zero_row] * 3 + [conv[3:4], conv[5:6]], axis=0)
    w11 = lambda rows: f32(rows.reshape(11, NCH, 128).transpose(2, 1, 0).reshape(128, NCH * 11))
    x_prompt = np.asarray(inp["x_prompt"], dtype=np.float32)
    x_sample = np.asarray(inp["x_sample"], dtype=np.float32)
    maps = []
    for c in range(8):
        m = dict(shared)
        if c < 4:
            m["x"] = f32(x_sample[c])
            m["cvT"] = f32(np.asarray(inp["c"][c]).reshape(8, 128).T)
            m["sinit_g"] = f32(inp["state_gla"][c, 0])
            m["sinit_h"] = f32(inp["state_hgrn"][c, 0])
            mf = np.ones(32, np.float32)
            mb = np.ones(32, np.float32)
            m["w11T"] = w11(rows_s)
        else:
            p = c - 4
            m["x"] = f32(x_prompt[8 * p:8 * p + 8].reshape(T, D))
            m["cvT"] = f32(np.asarray(inp["c_ctx"]).reshape(8, 128).T)
            m["sinit_g"] = np.zeros((2, 4, 64, 128), np.float32)
            m["sinit_h"] = np.zeros((2, 4, 128, 128), np.float32)
            mf = (np.arange(32) % 4 != 0).astype(np.float32)
            mb = (np.arange(32) % 4 != 3).astype(np.float32)
            m["w11T"] = w11(rows_p)
        m["mchain"] = f32(np.tile(np.concatenate([mf, mb])[None, :], (128, 1)))
        maps.append(m)
    return maps


_PROGRAM = {}


def kernel(**inputs):
    if "nc" not in _PROGRAM:
        _PROGRAM["nc"] = build_program(debug=False)
    nc = _PROGRAM["nc"]
    in_maps = prep_core_inputs(inputs)
    res = run_bass_kernel_spmd(nc, in_maps, core_ids=list(range(8)))
    r = res.results
    y_sample = np.stack([np.asarray(r[c]["y"], dtype=np.float32) for c in range(4)], axis=0)
    y_prompt = np.concatenate([np.asarray(r[c]["y"], dtype=np.float32).reshape(8, 256, D) for c in range(4, 8)], axis=0)
    sg = np.concatenate([np.asarray(r[c]["snew_g"], dtype=np.float32) for c in range(4, 8)], axis=0)[:, None]
    sh = np.concatenate([np.asarray(r[c]["snew_h"], dtype=np.float32) for c in range(4, 8)], axis=0)[:, None]
    return (y_prompt, y_sample, sg, sh)
```

```python
import numpy as np
from contextlib import ExitStack
import concourse.bass as bass
import concourse.mybir as mybir
from concourse.bass_utils import run_bass_kernel_spmd

F32 = mybir.dt.float32
BF16 = mybir.dt.bfloat16
AF = mybir.ActivationFunctionType
ALU = mybir.AluOpType


class Tok:
    __slots__ = ("name", "w", "r", "rd", "excl", "acc")

    def __init__(self, name, excl=False):
        self.name = name
        self.w = None
        self.r = {}
        self.rd = []
        self.excl = excl
        self.acc = {}


class TT:
    def __init__(self, t, tok):
        self.t = t
        self.tok = tok


class _Op:
    __slots__ = ("idx", "eng", "fn", "deps", "is_dma", "sig", "semval", "slot", "is_out", "multi")


def _tok(x):
    return x.tok if isinstance(x, TT) else x


class Prog:
    NSLOT = {"sp": 24, "pool": 16, "act": 8}

    def __init__(self, nc, es):
        self.nc = nc
        self.es = es
        self.ops = []
        self.n_dma = {"sp": 0, "pool": 0, "act": 0}
        self._n = 0
        self.bar = set()

    def sb(self, name, shape, dtype):
        t = self.es.enter_context(self.nc.sbuf_tensor(name, list(shape), dtype))
        return TT(t, Tok(name))

    def ps(self, name):
        t = self.es.enter_context(self.nc.psum_tensor(name, [128, 512], F32))
        return TT(t, Tok(name))

    def tok(self, name):
        return Tok(name)

    def _record(self, eng, fn, reads, writes, is_dma, is_out=False, extra=(), multi=False):
        op = _Op()
        op.multi = multi
        op.idx = len(self.ops)
        op.eng = eng
        op.fn = fn
        op.is_dma = is_dma
        op.sig = False
        op.semval = 0
        op.slot = None
        op.is_out = is_out
        deps = set()
        reads = [_tok(x) for x in reads]
        writes = [_tok(x) for x in writes]

        def consider(pidx, kind):
            p = self.ops[pidx]
            if p.is_dma:
                deps.add(pidx)
                return
            if (not is_dma) and p.eng == eng:
                if eng == "pe":
                    return
                if kind != "raw":
                    return
            deps.add(pidx)

        for t in reads:
            if t.w is not None:
                consider(t.w, "raw")
        for t in writes:
            if t.w is not None:
                consider(t.w, "waw")
            for _, ridx in t.r.items():
                consider(ridx, "war")
            for ridx in t.rd:
                consider(ridx, "war")
        for t in reads + writes:
            if t.excl:
                for e2, aidx in t.acc.items():
                    if e2 != eng:
                        deps.add(aidx)
                t.acc[eng] = op.idx
        for x in extra:
            deps.add(x.idx)
        for pidx in self.bar:
            p = self.ops[pidx]
            if (not is_dma) and (not p.is_dma) and p.eng == eng:
                continue
            deps.add(pidx)
        op.deps = deps
        for t in reads:
            if is_dma:
                t.rd.append(op.idx)
            else:
                t.r[eng] = op.idx
        for t in writes:
            t.w = op.idx
            t.r = {}
            t.rd = []
        if is_dma:
            k = self.n_dma[eng]
            self.n_dma[eng] += 1
            ns = self.NSLOT[eng]
            op.slot = (eng, k % ns)
            op.semval = 16 * (k // ns + 1)
        self.ops.append(op)
        return op

    def op(self, eng, fn, reads, writes, extra=(), multi=False):
        return self._record(eng, fn, reads, writes, False, extra=extra, multi=multi)

    def barrier(self):
        last = {}
        for op in self.ops:
            if op.is_dma:
                last[("d",) + op.slot] = op.idx
            else:
                last[op.eng] = op.idx
        self.bar = set(last.values())

    def dma(self, eng, out_ap, in_ap, reads, writes, is_out=False, **kw):
        def fn(e, out_ap=out_ap, in_ap=in_ap, kw=kw):
            return e.dma_start(out=out_ap, in_=in_ap, **kw)
        return self._record(eng, fn, reads, writes, True, is_out)

    def finish(self):
        nc = self.nc
        es = self.es
        ops = self.ops
        for op in ops:
            for d in op.deps:
                ops[d].sig = True
        engs = ["pe", "act", "dve", "pool", "sp"]
        esem = {e: es.enter_context(nc.semaphore("s_" + e)) for e in engs}
        dsem = {}
        for e, ns in self.NSLOT.items():
            for i in range(min(ns, max(1, self.n_dma[e]))):
                dsem[(e, i)] = es.enter_context(nc.semaphore("d_%s%d" % (e, i)))
        cnt = {e: 0 for e in engs}
        for op in ops:
            if not op.is_dma and op.sig:
                cnt[op.eng] += 1
                op.semval = cnt[op.eng]
        slot_last = {}
        prev_on_slot = {}
        for op in ops:
            if op.is_dma:
                prev_on_slot[op.idx] = slot_last.get(op.slot)
                slot_last[op.slot] = op.idx
        by_eng = {e: [op for op in ops if op.eng == e] for e in engs}

        def sigof(p):
            if p.is_dma:
                return dsem[p.slot], p.semval
            return esem[p.eng], p.semval

        def emit(ename, e):
            waited = {}

            embed = ename in ("dve", "act", "pool")

            for op in by_eng[ename]:
                need = {}
                order = []
                cand = [sigof(ops[d]) for d in sorted(op.deps)]
                if op.is_dma:
                    pv = prev_on_slot[op.idx]
                    if pv is not None:
                        cand.append(sigof(ops[pv]))
                for s, v in cand:
                    key = id(s)
                    if waited.get(key, 0) < v and need.get(key, (None, 0))[1] < v:
                        if key not in need:
                            order.append(key)
                        need[key] = (s, v)
                pend = [need[k] for k in order]
                for s, v in pend:
                    waited[id(s)] = v
                fold = None
                if embed and pend and not op.is_dma and not op.multi:
                    fold = pend.pop()
                for s, v in pend:
                    e.wait_ge(s, v)
                ins = op.fn(e)
                if fold is not None:
                    ins._wait_ge(fold[0], fold[1])
                if op.is_dma:
                    ins.then_inc(dsem[op.slot], 16)
                elif op.sig:
                    ins.then_inc(esem[ename], 1)
            if ename == "sp":
                for slot, idx in slot_last.items():
                    s, v = sigof(ops[idx])
                    if waited.get(id(s), 0) < v:
                        e.wait_ge(s, v)
                        waited[id(s)] = v

        with nc.Block() as block:
            @block.tensor
            def _(e):
                emit("pe", e)

            @block.scalar
            def _(e):
                emit("act", e)

            @block.vector
            def _(e):
                emit("dve", e)

            @block.gpsimd
            def _(e):
                emit("pool", e)

            @block.sync
            def _(e):
                emit("sp", e)
        self.stats = {e: len(by_eng[e]) for e in engs}


D = 1024
T = 2048
NB = 16
EPS = 1e-6
IN_W = 4128
FFN_H = 2816
NCH = 44
NPAIR = 22
UPAD = 65
UW = UPAD + T + UPAD
GROUPS = [(0, 6), (6, 12), (12, 17), (17, 22)]
TS = 512
OFF_QA, OFF_KA, OFF_VA, OFF_GA, OFF_LR = 0, 256, 512, 1024, 1536
OFF_QB, OFF_FB, OFF_IB, OFF_GB = 1568, 2080, 3104, 3616


class Arena:
    def __init__(self, P, nf32):
        self.P = P
        self.tt = P.sb("arena", [128, nf32], F32)
        self.n = nf32
        self.top = 0
        self.end = nf32

    def alloc(self, name, free_shape, dtype, top=False):
        nel = int(np.prod(free_shape))
        nf = nel if dtype == F32 else (nel + 1) // 2
        nf = (nf + 3) // 4 * 4
        assert self.top + nf <= self.end, ("arena overflow", name, self.top, nf, self.end)
        if top:
            self.end -= nf
            ap = self.tt.t[:, self.end:self.end + nf]
        else:
            ap = self.tt.t[:, self.top:self.top + nf]
        if dtype != F32:
            ap = ap.bitcast(dtype)
        ap = ap[:, 0:nel]
        if len(free_shape) == 2:
            ap = ap.rearrange("p (a b) -> p a b", a=free_shape[0])
        elif len(free_shape) == 3:
            ap = ap.rearrange("p (a b c) -> p a b c", a=free_shape[0], b=free_shape[1])
        if not top:
            self.top += nf
        return TT(ap, Tok(name))

    def mark(self):
        return self.top

    def reset(self, m):
        self.top = m


def build_program(debug=False):
    import os as _os
    nc = bass.Bass("TRN2", target_bir_lowering=False)

    def din(name, shape):
        return nc.dram_tensor(name, list(shape), F32, kind="ExternalInput").ap()

    def dout(name, shape, dt=F32):
        return nc.dram_tensor(name, list(shape), dt, kind="ExternalOutput").ap()

    x_d = din("x", [T, D])
    cvT_d = din("cvT", [128, 8])
    wada_d = din("w_ada", [D, 6 * D])
    bada_d = din("b_ada", [1, 6 * D])
    badaT_d = din("b_adaT", [128, 48])
    norm2T_d = din("norm2T", [128, 8])
    norm1_d = din("norm1", [1, D])
    norm2_d = din("norm2", [1, D])
    fnorm_d = din("fnorm", [1, D])
    win_d = din("w_in", [D, IN_W])
    wgu_d = din("w_gla_up", [2, 16, 256])
    bglaT_d = din("b_glaT", [128, 4])
    lbT_d = din("lbT", [128, 16])
    gnorm_d = din("gnorm", [128, 2])
    wout_d = din("w_out", [D, D])
    wup_d = din("w_ffn_up", [D, 2 * FFN_H])
    w11T_d = din("w11T", [128, NCH * 11])
    bconvT_d = din("bconvT", [128, NCH])
    wdn_d = din("w_ffn_down", [FFN_H, D])
    sig_d = din("sinit_g", [2, 4, 64, 128])
    sih_d = din("sinit_h", [2, 4, 128, 128])
    mchain_d = din("mchain", [128, 64])
    identF_d = din("identF", [128, 128])
    maskT2_d = din("maskT2", [128, 256])
    scanmask_d = din("scanmask", [1, T])
    y_d = dout("y", [T, D])
    sog_d = dout("snew_g", [8, 2, 4, 64, 128])
    soh_d = dout("snew_h", [8, 2, 4, 128, 128])
    dbg = {}
    if debug:
        dbg["h1T"] = dout("dbg_h1T", [128, 8 * T], BF16)
        dbg["mod"] = dout("dbg_mod", [128, 6 * D])
        dbg["mergedT"] = dout("dbg_mergedT", [128, 8 * T], BF16)
        dbg["x1"] = dout("dbg_x1", [T, D])

    es = ExitStack()
    with es:
        P = Prog(nc, es)
        AR = Arena(P, 52800)
        dumps = {}

        def dump(name, tt, ncols, dt, parts=128):
            if not debug:
                return
            if name not in dumps:
                dumps[name] = nc.dram_tensor("dd_" + name, [128, ncols], dt, kind="ExternalOutput").ap()
            ap = tt.t
            if len(ap.shape) == 3:
                ap = ap.rearrange("p a b -> p (a b)")
            elif len(ap.shape) == 4:
                ap = ap.rearrange("p a b c -> p (a b c)")
            P.dma("sp", dumps[name][0:parts], ap[0:parts], [tt], [], is_out=True)
        psb = [P.ps("psum%d" % i) for i in range(8)]
        psq = [Tok("psbank%d" % b, excl=True) for b in range(8)]

        def PQ(b, c0, c1):
            return [psq[b]]

        def pst(b):
            return psb[b].t

        rr = {"n": 0}

        def evac_eng():
            rr["n"] += 1
            return "act" if rr["n"] % 2 else "dve"

        def copy_op(eng, out, in_, reads, writes, scale=None):
            if eng == "act":
                if scale is None:
                    P.op("act", lambda e: e.activation(out, in_, AF.Copy), reads, writes)
                else:
                    P.op("act", lambda e: e.activation(out, in_, AF.Copy, scale=scale), reads, writes)
            else:
                if scale is None:
                    P.op(eng, lambda e: e.tensor_copy(out, in_), reads, writes)
                else:
                    P.op(eng, lambda e: e.tensor_scalar(out, in_, scale, None, op0=ALU.mult), reads, writes)

        identF = AR.alloc("identF", [128], F32)
        identB = AR.alloc("identB", [128], BF16)
        maskT2 = AR.alloc("maskT2", [256], F32)
        scanmask = AR.alloc("scanmask", [TS], BF16)
        mchain = AR.alloc("mchain", [64], F32)
        smalls = AR.alloc("smalls", [64], F32)
        LB = smalls.t[:, 0:8]
        L1M = smalls.t[:, 8:16]
        NEGB = smalls.t[:, 16:24]
        GS = smalls.t[:, 24:26]
        SC = smalls.t[:, 32:40]
        TMPS = smalls.t[:, 40:64]
        MB = [None] * 6
        hT = AR.alloc("hT", [8, T], BF16)
        hT_tok = [Tok("hT_b%d" % b) for b in range(NB)]
        onesF = AR.alloc("onesF", [128], F32)
        ring = {"slots": [], "n": 0}

        def next_w():
            w = ring["slots"][ring["n"] % len(ring["slots"])]
            ring["n"] += 1
            return w

        ph1_mark = AR.mark()
        ring["slots"] = [AR.alloc("wringA%d" % i, [8 * 512], BF16) for i in range(2)]
        srep = AR.alloc("srep", [8, 128], BF16)
        brow = [AR.alloc("brow%d" % i, [512], F32) for i in range(2)]
        MB[0] = AR.alloc("mod0", [D], F32)
        MB[1] = AR.alloc("mod1", [D], F32)

        P.dma("sp", identF.t, identF_d, [], [identF])
        P.dma("sp", maskT2.t, maskT2_d, [], [maskT2])
        P.dma("pool", scanmask.t, scanmask_d[:, 0:TS].partition_broadcast(128), [], [scanmask])
        P.dma("sp", mchain.t, mchain_d, [], [mchain])
        P.op("dve", lambda e: e.tensor_copy(identB.t, identF.t), [identF], [identB])
        lbT = AR.alloc("lbT", [16], F32)
        cvT = AR.alloc("cvT", [8], F32)
        bgl = AR.alloc("bgl", [8], F32)
        gnm = AR.alloc("gnm", [2], F32)
        P.dma("sp", lbT.t, lbT_d, [], [lbT])
        P.dma("sp", cvT.t, cvT_d, [], [cvT])
        P.dma("sp", bgl.t[:, 0:4], bglaT_d, [], [bgl])
        P.dma("sp", gnm.t, gnorm_d, [], [gnm])
        DD = TMPS[:, 0:8]
        EE = TMPS[:, 8:16]
        P.op("dve", lambda e: e.tensor_tensor(DD, lbT.t[:, 8:16], lbT.t[:, 0:8], ALU.subtract), [lbT], [smalls])
        P.op("act", lambda e: e.activation(EE, DD, AF.Exp), [smalls], [smalls])
        P.op("act", lambda e: e.activation(EE, EE, AF.Ln, bias=1.0), [smalls], [smalls])
        P.op("act", lambda e: e.activation(LB, EE, AF.Exp, scale=-1.0), [smalls], [smalls])
        P.op("dve", lambda e: e.tensor_tensor(L1M, DD, EE, ALU.subtract), [smalls], [smalls])
        P.op("dve", lambda e: e.tensor_scalar(NEGB[:, 0:4], bgl.t[:, 0:4], -1.0, None, op0=ALU.mult), [bgl], [smalls])
        P.op("dve", lambda e: e.tensor_scalar(GS, gnm.t, float(np.sqrt(128.0)), None, op0=ALU.mult), [gnm], [smalls])
        E2 = TMPS[:, 16:24]
        P.op("act", lambda e: e.activation(E2, cvT.t, AF.Exp, scale=-1.0), [cvT, smalls], [smalls])
        P.op("dve", lambda e: e.tensor_scalar(E2, E2, 1.0, None, op0=ALU.add), [smalls], [smalls])
        P.op("dve", lambda e: e.reciprocal(E2, E2), [smalls], [smalls])
        P.op("dve", lambda e: e.tensor_tensor(SC, cvT.t, E2, ALU.mult), [cvT, smalls], [smalls])
        sc_b = bass.AP(smalls.t.tensor, smalls.t.offset + 32, [list(smalls.t.ap[0]), [1, 8], [0, 128]])
        P.op("dve", lambda e: e.tensor_copy(srep.t, sc_b), [smalls], [srep])
        nrm = AR.alloc("nrm_bc", [D], F32)
        P.op("pool", lambda e: e.memset(onesF.t, 1.0), [], [onesF])
        pbank = {"n": 0}

        def mod_compute(j):
            col = [0, 1, 2, 3, 4, 5][j]
            for n in range(2):
                c0 = col * D + n * 512
                w = next_w()
                wv = w.t[:, 0:8 * 512].rearrange("p (k n) -> p k n", k=8)
                P.dma("pool", wv, wada_d[:, c0:c0 + 512].rearrange("(k p) n -> p k n", p=128), [], [w])
                br = brow[(2 * j + n) % 2]
                P.dma("sp", br.t[0:1, :], bada_d[:, c0:c0 + 512], [], [br])
                b = pbank["n"] % 2
                pbank["n"] += 1
                for kc in range(8):
                    P.op("pe", lambda e, kc=kc, b=b, wv=wv: e.matmul(pst(b)[:, :], srep.t[:, kc, :], wv[:, kc, :], start=(kc == 0), stop=False),
                         [srep, w], PQ(b, 0, 512))
                P.op("pe", lambda e, b=b, br=br: e.matmul(pst(b)[:, :], onesF.t[0:1, :], br.t[0:1, :], start=False, stop=True),
                     [onesF, br], PQ(b, 0, 512))
                copy_op(evac_eng(), MB[j].t[:, n * 512:(n + 1) * 512], pst(b)[:, :], PQ(b, 0, 512), [MB[j]])

        mod_compute(0)
        mod_compute(1)
        P.dma("sp", nrm.t, norm1_d.partition_broadcast(128), [], [nrm])
        P.op("dve", lambda e: e.scalar_tensor_tensor(MB[1].t, MB[1].t, 1.0, nrm.t, op0=ALU.add, op1=ALU.mult), [MB[1], nrm], [MB[1]])

        xring = [AR.alloc("xring%d" % i, [D], F32) for i in range(3)]
        junk = AR.alloc("junk", [D], BF16)
        tmpf = AR.alloc("tmpf", [D], F32)
        hb = [AR.alloc("hb%d" % i, [D], BF16) for i in range(2)]
        stt = [AR.alloc("stt%d" % i, [4], F32) for i in range(3)]

        def norm_A(src_ap, src_toks, st):
            jk = junk
            P.op("act", lambda e: e.activation(jk.t, src_ap, AF.Square, accum_out=st.t[:, 0:1]), src_toks, [jk, st], multi=True)
            P.op("act", lambda e: e.activation(st.t[:, 1:2], st.t[:, 0:1], AF.Ln, scale=1.0 / D, bias=EPS), [st], [st])
            P.op("act", lambda e: e.activation(st.t[:, 2:3], st.t[:, 1:2], AF.Exp, scale=-0.5), [st], [st])

        def norm_B1(src_ap, src_toks, g_t, s_t, b, st):
            tf, h = tmpf, hb[b % 2]
            P.op("dve", lambda e: e.scalar_tensor_tensor(tf.t, src_ap, st.t[:, 2:3], g_t.t, op0=ALU.mult, op1=ALU.mult),
                 src_toks + [st, g_t], [tf])
            P.op("dve", lambda e: e.tensor_tensor(h.t, tf.t, s_t.t, ALU.add), [tf, s_t], [h])

        def norm_B2(b, pbk):
            h = hb[b % 2]
            pv = pst(pbk).bitcast(BF16).rearrange("p (k t) -> p k t", k=8)
            for kc in range(8):
                P.op("pe", lambda e, kc=kc: e.transpose(pv[:, kc, :], h.t[:, kc * 128:(kc + 1) * 128], identB.t),
                     [h, identB], PQ(pbk, 0, 512))
            copy_op("act", hT.t[:, :, b * 128:(b + 1) * 128], pv, PQ(pbk, 0, 512), [hT_tok[b]])

        for b in range(NB + 2):
            if b < NB:
                xt = xring[b % 3]
                P.dma("sp", xt.t, x_d[b * 128:(b + 1) * 128, :], [], [xt])
                norm_A(xt.t, [xt], stt[b % 3])
            if 1 <= b <= NB:
                xp = xring[(b - 1) % 3]
                norm_B1(xp.t, [xp], MB[1], MB[0], b - 1, stt[(b - 1) % 3])
            if b >= 2:
                norm_B2(b - 2, 2 + ((b - 2) % 2))
        if debug:
            P.dma("sp", dbg["h1T"], hT.t.rearrange("p k t -> p (k t)"), hT_tok, [], is_out=True)
            for j in range(2):
                P.dma("sp", dbg["mod"][:, j * D:(j + 1) * D], MB[j].t, [MB[j]], [], is_out=True)

        P.barrier()
        AR.reset(ph1_mark)
        BS = 64
        NBK = T // BS
        NG = T // 128
        BPS = TS // BS
        NSPAN = T // TS
        modT = AR.alloc("modT", [4, 8], F32, top=True)
        badaT = AR.alloc("badaT", [48], F32, top=True)
        scb = AR.alloc("scb", [8], BF16, top=True)
        top_keep = AR.end
        mergedT = AR.alloc("mergedT", [8, T], BF16, top=True)
        mT_tok = [Tok("mT_h%d" % i) for i in range(8)]
        ring["slots"] = [AR.alloc("wringB%d" % i, [8 * 640], BF16) for i in range(2)]
        QT = AR.alloc("QT", [T], F32)
        KT = AR.alloc("KT", [T], F32)
        QTt = [AR.alloc("QTt%d" % d, [T], BF16) for d in range(2)]
        KTt = [AR.alloc("KTt%d" % d, [T], BF16) for d in range(2)]
        KTM = [AR.alloc("KTM%d" % d, [NG, 128], BF16) for d in range(2)]
        LH = AR.alloc("LH", [2, NBK, 2], F32)
        EX = AR.alloc("EX", [2, NBK, 2], F32)
        SCL = AR.alloc("SCL", [2, NBK, 2], F32)
        Vt = [AR.alloc("Vt%d" % pb, [NG, 128], BF16) for pb in range(2)]
        GG = [AR.alloc("GG%d" % pb, [NG, 128], BF16) for pb in range(2)]
        NSET = 4
        SETS = [dict(X=TT(KT.t[:, i * TS:(i + 1) * TS], Tok("Xs%d" % i)), U=AR.alloc("Us%d" % i, [TS], F32), T2=AR.alloc("T2s%d" % i, [TS], F32),
                     B=AR.alloc("Bs%d" % i, [TS], F32), KH=AR.alloc("KHt%d" % i, [TS], BF16)) for i in range(NSET)]
        SQt = [AR.alloc("SQt%d" % d, [NBK, 128], BF16) for d in range(2)]
        ST = [[AR.alloc("ST%d_%d" % (d, i), [128], F32) for i in range(3)] for d in range(2)]
        ATt = [AR.alloc("AT%d" % i, [256], BF16) for i in range(4)]
        MTM = [AR.alloc("MTM%d" % i, [128], BF16) for i in range(2)]
        sto = [AR.alloc("sto%d" % i, [4], F32) for i in range(4)]
        junk2 = AR.alloc("junk2", [128], BF16)
        LRT = AR.alloc("LRT", [2, T], BF16)
        WLR = AR.alloc("WLR", [8, 32], BF16)
        WGU = AR.alloc("WGU", [2, 256], BF16)

        def bcol(Bap, parts, col, nblk):
            pstep = Bap.ap[0][0]
            return bass.AP(Bap.tensor, Bap.offset + col, [[pstep, parts], [BS, nblk], [0, BS]])

        P.dma("pool", WLR.t, win_d[:, OFF_LR:OFF_LR + 32].rearrange("(k p) n -> p k n", p=128), [], [WLR])
        P.dma("pool", WGU.t[0:16], wgu_d.rearrange("d r k -> r d k"), [], [WGU])
        P.dma("sp", badaT.t, badaT_d, [], [badaT])
        P.op("dve", lambda e: e.tensor_copy(scb.t, SC), [smalls], [scb])
        fmb = {"n": 0}

        def fm_bank():
            fmb["n"] += 1
            return fmb["n"] % 2

        def head_cfg(hh):
            gla = hh < 4
            h = hh if gla else hh - 4
            if gla:
                p = h // 2
                owner = (h % 2 == 0)
                if owner:
                    groups = [(OFF_QA + 128 * p, 128, 0), (OFF_KA + 128 * p, 128, 128), (OFF_VA + 128 * h, 128, 256), (OFF_GA + 128 * h, 128, 384)]
                    c_vg = 256
                else:
                    groups = [(OFF_VA + 128 * h, 128, 0), (OFF_GA + 128 * h, 128, 128)]
                    c_vg = 0
                return dict(gla=True, h=h, p=p, owner=owner, dk=64, po=64 * (h % 2), dscale=-1.0 / 16.0, groups=groups, c_q=0, c_k=128, c_vg=c_vg, c_f=None)
            groups = [(OFF_QB + 128 * h, 128, 0), (OFF_FB + 128 * h, 128, 128), (OFF_FB + 512 + 128 * h, 128, 256),
                      (OFF_IB + 128 * h, 128, 384), (OFF_GB + 128 * h, 128, 512)]
            return dict(gla=False, h=h, p=0, owner=True, dk=128, po=0, dscale=1.0, groups=groups, c_q=0, c_k=None, c_vg=384, c_f=(128, 256))

        head_w = {}

        def load_head_weights(hh):
            cfg = head_cfg(hh)
            w = ring["slots"][hh % 2]
            wv = w.t[:, 0:8 * 640].rearrange("p (k n) -> p k n", k=8)
            for (c0, n, o) in cfg["groups"]:
                P.dma("pool", wv[:, :, o:o + n], win_d[:, c0:c0 + n].rearrange("(k p) n -> p k n", p=128), [], [w])
            head_w[hh] = (w, wv)

        def mod_computeT(j, w):
            b = fm_bank()
            for n in range(4):
                c0 = j * D + n * 256
                half = n % 2
                wv = w.t[:, half * 2048:(half + 1) * 2048].rearrange("p (k n) -> p k n", k=8)
                P.dma("pool", wv, wada_d[:, c0:c0 + 256].rearrange("(k p) n -> p k n", p=128), [], [w])
                for mb in range(2):
                    for kc in range(8):
                        P.op("pe", lambda e, kc=kc, mb=mb, n=n, wv=wv: e.matmul(pst(b)[:, 2 * n + mb:2 * n + mb + 1], wv[:, kc, mb * 128:(mb + 1) * 128], scb.t[:, kc:kc + 1],
                                                                              start=(kc == 0), stop=(kc == 7)), [w, scb], PQ(b, 0, 512))
                yield
            copy_op("dve", modT.t[:, j - 2, :], pst(b)[:, 0:8], PQ(b, 0, 512), [modT])
            P.op("dve", lambda e: e.tensor_tensor(modT.t[:, j - 2, :], modT.t[:, j - 2, :], badaT.t[:, j * 8:(j + 1) * 8], ALU.add), [modT, badaT], [modT])

        def stageAB1(hh):
            cfg = head_cfg(hh)
            gla, h, dscale, owner = cfg["gla"], cfg["h"], cfg["dscale"], cfg["owner"]
            dk = 128
            c_q, c_k, c_vg, c_f = cfg["c_q"], cfg["c_k"], cfg["c_vg"], cfg["c_f"]
            pb = hh % 2
            w, wv = head_w[hh]
            if hh + 1 < 8:
                load_head_weights(hh + 1)
            myVt, myGG = Vt[pb], GG[pb]
            Us = SETS[0]["U"]
            _ev = _os.environ.get('DBG_EV', 'dve')
            ev = (lambda: _ev) if _ev else evac_eng

            def fm_proj(c0, M, tiles, dst_ap_fn, dst_toks, scale=None):
                for tt in tiles:
                    b = fm_bank()
                    for kc in range(8):
                        P.op("pe", lambda e, kc=kc, b=b, tt=tt: e.matmul(pst(b)[0:M, :], wv[:, kc, c0:c0 + M], hT.t[:, kc, tt * 512:(tt + 1) * 512],
                                                                       start=(kc == 0), stop=(kc == 7)),
                             [w] + hT_tok[4 * tt:4 * tt + 4], PQ(b, 0, 512))
                    yield
                    copy_op(ev(), dst_ap_fn(tt), pst(b)[0:M, :], PQ(b, 0, 512), dst_toks, scale=scale)

            if owner:
                yield from fm_proj(c_q, dk, range(4), lambda tt: QT.t[0:dk, tt * 512:(tt + 1) * 512], [QT], scale=(0.125 if gla else None))
            if gla and owner:
                yield from fm_proj(c_k, dk, range(4), lambda tt: KT.t[0:dk, tt * 512:(tt + 1) * 512], [KT])
            if hh == 0:
                for d in range(2):
                    for tt in range(4):
                        b = fm_bank()
                        for kc in range(8):
                            P.op("pe", lambda e, kc=kc, b=b, tt=tt, d=d: e.matmul(pst(b)[0:16, :], WLR.t[:, kc, 16 * d:16 * d + 16], hT.t[:, kc, tt * 512:(tt + 1) * 512],
                                                                                  start=(kc == 0), stop=(kc == 7)),
                                 [WLR] + hT_tok[4 * tt:4 * tt + 4], PQ(b, 0, 512))
                        copy_op(ev(), LRT.t[0:16, d, tt * 512:(tt + 1) * 512], pst(b)[0:16, :], PQ(b, 0, 512), [LRT])
                        yield
            for bp in range(NG // 2):
                bk = fm_bank()
                for i in range(2):
                    blk = 2 * bp + i
                    for kc in range(8):
                        P.op("pe", lambda e, kc=kc, bk=bk, i=i, blk=blk: e.matmul(pst(bk)[:, i * 256:(i + 1) * 256], hT.t[:, kc, blk * 128:(blk + 1) * 128],
                                                                                wv[:, kc, c_vg:c_vg + 256], start=(kc == 0), stop=(kc == 7)),
                             [w, hT_tok[blk]], PQ(bk, i * 256, (i + 1) * 256))
                pv = pst(bk).rearrange("p (b c) -> p b c", b=2)
                yield
                ce = ev()
                copy_op(ce, myVt.t[:, 2 * bp:2 * bp + 2, :], pv[:, :, 0:128], PQ(bk, 0, 512), [myVt])
                copy_op(ce, myGG.t[:, 2 * bp:2 * bp + 2, :], pv[:, :, 128:256], PQ(bk, 0, 512), [myGG])
            GPS = TS // 128
            for s_ in range(NSPAN):
                gsp = myGG.t[:, s_ * GPS:(s_ + 1) * GPS, :].rearrange("p b c -> p (b c)")
                P.op("act", lambda e, gsp=gsp: e.activation(Us.t, gsp, AF.Exp, scale=-1.0), [myGG], [Us])
                P.op("act", lambda e: e.activation(Us.t, Us.t, AF.Ln, bias=1.0), [Us], [Us])
                P.op("act", lambda e: e.activation(Us.t, Us.t, AF.Exp, scale=-1.0), [Us], [Us])
                P.op("dve", lambda e, gsp=gsp: e.tensor_tensor(gsp, gsp, Us.t, ALU.mult), [myGG, Us], [myGG])
                yield

        def stageAB2(hh):
            cfg = head_cfg(hh)
            gla, h, dscale, owner, pr = cfg["gla"], cfg["h"], cfg["dscale"], cfg["owner"], cfg["p"]
            dk = 128
            c_f = cfg["c_f"]
            w, wv = head_w[hh]
            myQTt, myKTt, myKTM, myLH, myEX, mySCL = QTt, KTt, KTM, LH, EX, SCL
            if not owner:
                if hh < 4:
                    yield from mod_computeT(2 + hh, w)
                return

            def fm_proj(c0, M, tiles, dst_ap_fn, dst_toks, scale=None):
                for tt in tiles:
                    b = fm_bank()
                    for kc in range(8):
                        P.op("pe", lambda e, kc=kc, b=b, tt=tt: e.matmul(pst(b)[0:M, :], wv[:, kc, c0:c0 + M], hT.t[:, kc, tt * 512:(tt + 1) * 512],
                                                                       start=(kc == 0), stop=(kc == 7)),
                             [w] + hT_tok[4 * tt:4 * tt + 4], PQ(b, 0, 512))
                    copy_op("act", dst_ap_fn(tt), pst(b)[0:M, :], PQ(b, 0, 512), dst_toks, scale=scale)
                    yield

            v3 = lambda ap: ap.rearrange("p (b t) -> p b t", b=BPS)

            def decay(d, s_, tiles, sp0, sp1):
                st_ = SETS[(s_ % 2) * 2 + d]
                Xs, Us, T2s, Bs, KHt = st_["X"], st_["U"], st_["T2"], st_["B"], st_["KH"]
                X_, U_, T2_, B_, KH_ = Xs.t[0:dk], Us.t[0:dk], T2s.t[0:dk], Bs.t[0:dk], KHt.t[0:dk]
                col = (d * 2 + pr) if gla else (d * 4 + h)
                sel = d
                if gla:
                    for ti, tt in enumerate(tiles):
                        b = fm_bank()
                        P.op("pe", lambda e, b=b, tt=tt: e.matmul(pst(b)[0:128, :], WGU.t[0:16, d, 128 * pr:128 * pr + 128], LRT.t[0:16, d, tt * 512:(tt + 1) * 512],
                                                                start=True, stop=True), [WGU, LRT], PQ(b, 0, 512))
                        P.op("act", lambda e, b=b, ti=ti: e.activation(U_[:, ti * 512:(ti + 1) * 512], pst(b)[0:128, :], AF.Exp, scale=-1.0,
                                                                      bias=NEGB[:, col:col + 1]), PQ(b, 0, 512) + [smalls], [Us])
                    yield
                    P.op("act", lambda e: e.activation(T2_, U_, AF.Ln, bias=1.0), [Us], [T2s])
                    P.op("dve", lambda e: e.tensor_tensor_scan(B_, scanmask.t[0:dk, :], T2_, 0.0, ALU.mult, ALU.add), [scanmask, T2s], [Bs])
                else:
                    cf = c_f[d]
                    yield from fm_proj(cf, 128, tiles, lambda tt: X_[:, (tt - tiles[0]) * 512:(tt - tiles[0] + 1) * 512], [Xs])
                    P.op("act", lambda e: e.activation(U_, X_, AF.Exp, scale=-1.0), [Xs], [Us])
                    P.op("act", lambda e: e.activation(T2_, U_, AF.Ln, bias=1.0), [Us], [T2s])
                    P.op("act", lambda e: e.activation(U_, U_, AF.Ln, bias=1.0, scale=LB[:, col:col + 1]), [Us, smalls], [Us])
                    yield
                    P.op("dve", lambda e: e.tensor_tensor(U_, U_, T2_, ALU.subtract), [Us, T2s], [Us])
                    P.op("act", lambda e: e.activation(X_, X_, AF.Exp), [Xs], [Xs])
                    P.op("act", lambda e: e.activation(X_, X_, AF.Ln, bias=1.0), [Xs], [Xs])
                    P.op("dve", lambda e: e.tensor_tensor_scan(B_, scanmask.t[0:dk, :], U_, 0.0, ALU.mult, ALU.add), [scanmask, Us], [Bs])
                yield
                B3 = v3(B_)
                MID = BS // 2 - 1
                lh = myLH.t[0:dk, d, s_ * BPS:(s_ + 1) * BPS, :]
                ex = myEX.t[0:dk, d, s_ * BPS:(s_ + 1) * BPS, :]
                P.op("dve", lambda e: e.tensor_copy(lh[:, :, 0:1], B3[:, :, MID:MID + 1]), [Bs], [myLH])
                P.op("dve", lambda e: e.tensor_tensor(lh[:, :, 1:2], B3[:, :, BS - 1:BS], B3[:, :, MID:MID + 1], ALU.subtract), [Bs], [myLH])
                P.op("act", lambda e: e.activation(ex, lh, AF.Exp, scale=dscale), [myLH], [myEX])
                bmid = bcol(B_, dk, MID, BPS)
                ehb = bass.AP(myEX.t.tensor, myEX.t[0:dk, d, s_ * BPS:(s_ + 1) * BPS, 1 - sel].offset,
                              [[myEX.t.ap[0][0], dk], [2, BPS], [0, BS]])
                if gla:
                    if d == 0:
                        P.op("dve", lambda e: e.tensor_tensor(v3(U_), B3, bmid, ALU.subtract), [Bs], [Us])
                    else:
                        P.op("dve", lambda e: e.tensor_tensor(T2_, B_, T2_, ALU.subtract), [Bs, T2s], [T2s])
                        P.op("dve", lambda e: e.tensor_tensor(v3(U_), bmid, v3(T2_), ALU.subtract), [Bs, T2s], [Us])
                    P.op("dve", lambda e: e.tensor_scalar(U_, U_, 640.0, -640.0, op0=ALU.min, op1=ALU.max), [Us], [Us])
                    P.op("act", lambda e: e.activation(T2_, U_, AF.Exp, scale=dscale), [Us], [T2s])
                    P.op("pool", lambda e: e.tensor_tensor(myQTt[d].t[0:dk, sp0:sp1], QT.t[0:dk, sp0:sp1], T2_, ALU.mult), [QT, T2s], [myQTt[d]])
                    yield
                    P.op("act", lambda e: e.activation(T2_, U_, AF.Exp, scale=-dscale), [Us], [T2s])
                    P.op("dve", lambda e: e.tensor_tensor(myKTt[d].t[0:dk, sp0:sp1], KT.t[0:dk, sp0:sp1], T2_, ALU.mult), [KT, T2s], [myKTt[d]])
                else:
                    if d == 0:
                        P.op("dve", lambda e: e.tensor_tensor(v3(T2_), B3, bmid, ALU.subtract), [Bs], [T2s])
                    else:
                        P.op("dve", lambda e: e.tensor_tensor(U_, B_, U_, ALU.subtract), [Bs, Us], [Us])
                        P.op("dve", lambda e: e.tensor_tensor(v3(T2_), bmid, v3(U_), ALU.subtract), [Bs, Us], [T2s])
                    P.op("dve", lambda e: e.tensor_scalar(T2_, T2_, 40.0, -40.0, op0=ALU.min, op1=ALU.max), [T2s], [T2s])
                    P.op("act", lambda e: e.activation(U_, T2_, AF.Exp), [T2s], [Us])
                    P.op("pool", lambda e: e.tensor_tensor(myQTt[d].t[0:dk, sp0:sp1], QT.t[0:dk, sp0:sp1], U_, ALU.mult), [QT, Us], [myQTt[d]])
                    yield
                    P.op("dve", lambda e: e.tensor_tensor(X_, X_, T2_, ALU.add), [Xs, T2s], [Xs])
                    P.op("act", lambda e: e.activation(myKTt[d].t[0:dk, sp0:sp1], X_, AF.Exp, scale=-1.0, bias=L1M[:, col:col + 1]), [Xs, smalls], [myKTt[d]])
                yield
                P.op("dve", lambda e: e.tensor_tensor(v3(KH_), v3(myKTt[d].t[0:dk, sp0:sp1]), ehb, ALU.mult), [myKTt[d], myEX], [KHt])
                kb = fm_bank()
                pvk = pst(kb).bitcast(BF16).rearrange("p (b t) -> p b t", b=8)
                ng = TS // 128
                for i in range(ng):
                    P.op("pe", lambda e, i=i: e.transpose(pvk[:, i, 0:dk], KH_[:, i * 128:(i + 1) * 128], identB.t[0:dk, 0:dk]), [KHt, identB], PQ(kb, 0, 512))
                copy_op(evac_eng(), myKTM[d].t[:, s_ * ng:(s_ + 1) * ng, 0:dk], pvk[:, 0:ng, 0:dk], PQ(kb, 0, 512), [myKTM[d]])
                yield

            for s0_ in range(0, NSPAN, 2):
                gens = []
                for s_ in (s0_, s0_ + 1):
                    tiles = list(range(s_ * TS // 512, (s_ + 1) * TS // 512))
                    for d in range(2):
                        gens.append(decay(d, s_, tiles, s_ * TS, (s_ + 1) * TS))
                alive = [True] * len(gens)
                while any(alive):
                    for gi in range(len(gens)):
                        if alive[gi]:
                            try:
                                next(gens[gi])
                            except StopIteration:
                                alive[gi] = False
                    yield
            for d in range(2):
                sel = d
                mc = mchain.t[0:dk, d * NBK:(d + 1) * NBK]
                P.op("dve", lambda e, d=d, sel=sel, mc=mc: e.tensor_tensor(mySCL.t[0:dk, d, :, 0], myEX.t[0:dk, d, :, sel], mc, ALU.mult), [myEX, mchain], [mySCL])
                P.op("dve", lambda e, d=d, sel=sel: e.tensor_tensor(mySCL.t[0:dk, d, :, 1], mySCL.t[0:dk, d, :, 0], myEX.t[0:dk, d, :, 1 - sel], ALU.mult), [myEX, mySCL], [mySCL])
            yield
            if hh < 4:
                yield from mod_computeT(2 + hh, w)

        def stageC(hh):
            cfg = head_cfg(hh)
            gla, h, dk, po = cfg["gla"], cfg["h"], cfg["dk"], cfg["po"]
            pq = slice(po, po + dk)
            pb = hh % 2
            myQTt, myKTt, myKTM, myVt, myGG, mySCL = QTt, KTt, KTM, Vt[pb], GG[pb], SCL
            kmt = {"n": 0}
            pv7 = pst(7).bitcast(BF16).rearrange("p (r s t) -> p r s t", r=2, s=4)
            grp_done = {}
            for d in range(2):
                src = (sig_d if gla else sih_d)[d, h]
                P.dma("sp", ST[d][0].t[pq], src, [], [ST[d][0]])
            state = {0: ST[0][0], 1: ST[1][0]}
            nxt = [1, 1]

            def chain_p(d, n, cb, after=()):
                g, hf = n // 2, n % 2
                return P.op("pe", lambda e: e.matmul(pst(cb)[pq, d * 128:(d + 1) * 128], myKTM[d].t[hf * 64:(hf + 1) * 64, g, pq], myVt.t[hf * 64:(hf + 1) * 64, g, :], start=True, stop=True),
                            [myKTM[d], myVt], PQ(cb, 0, 256), extra=after)

            def chain_step(d, n, cb):
                prev = state[d]
                new = ST[d][nxt[d]]
                nxt[d] = (nxt[d] + 1) % int(_os.environ.get('DBG_TRI', '3'))
                P.op("act", lambda e: e.activation(SQt[d].t[pq, n, :], prev.t[pq], AF.Copy, scale=mySCL.t[pq, d, n, 0:1]), [prev, mySCL], [SQt[d]])
                P.op("dve", lambda e: e.scalar_tensor_tensor(new.t[pq], prev.t[pq], mySCL.t[pq, d, n, 1:2], pst(cb)[pq, d * 128:(d + 1) * 128], op0=ALU.mult, op1=ALU.add),
                     [prev, mySCL] + PQ(cb, 0, 256), [new])
                state[d] = new
                if (d == 0 and n % 4 == 3) or (d == 1 and n % 4 == 0):
                    dst = (sog_d if gla else soh_d)[n // 4, d, h]
                    P.dma("sp", dst, new.t[pq], [new], [], is_out=True)

            gctr = {"n": 0}
            ginfo = {}

            def og_a1(g):
                i = gctr["n"]
                gctr["n"] += 1
                ginfo[g] = i
                blk = slice(g * 128, (g + 1) * 128)
                for d in range(2):
                    P.op("pe", lambda e, d=d: e.matmul(pst(5)[:, d * 128:(d + 1) * 128], myKTt[d].t[pq, blk], myQTt[d].t[pq, blk], start=True, stop=True),
                         [myKTt[d], myQTt[d]], PQ(5, 0, 256))

            def og_a2(g):
                at = ATt[ginfo[g] % 4]
                P.op("dve", lambda e: e.tensor_tensor(at.t, pst(5)[:, 0:256], maskT2.t, ALU.mult), PQ(5, 0, 256) + [maskT2], [at])

            def og_b(g):
                i = ginfo[g]
                at = ATt[i % 4]
                ob_ = [2, 6][i % 2]
                og = pst(ob_)[:, 0:128]
                otok = PQ(ob_, 0, 128)
                P.op("pe", lambda e: e.matmul(og, at.t[:, 0:128], myVt.t[:, g, :], start=True, stop=False), [at, myVt], otok)
                P.op("pe", lambda e: e.matmul(og, at.t[:, 128:256], myVt.t[:, g, :], start=False, stop=False), [at, myVt], otok)
                for hf in range(2):
                    for d in range(2):
                        last = (hf == 1 and d == 1)
                        c0 = g * 128 + hf * 64
                        P.op("pe", lambda e, hf=hf, d=d, last=last, c0=c0: e.matmul(pst(ob_)[hf * 64:(hf + 1) * 64, 0:128], myQTt[d].t[pq, c0:c0 + 64], SQt[d].t[pq, 2 * g + hf, :],
                                                                                   start=False, stop=last), [myQTt[d], SQt[d]], otok)

            def og_c(g):
                i = ginfo[g]
                ob_ = [2, 6][i % 2]
                og = pst(ob_)[:, 0:128]
                otok = PQ(ob_, 0, 128)
                so = sto[i % 4]
                P.op("act", lambda e: e.activation(junk2.t, og, AF.Square, accum_out=so.t[:, 0:1]), otok, [junk2, so], multi=True)
                P.op("act", lambda e: e.activation(so.t[:, 1:2], so.t[:, 0:1], AF.Ln, bias=128.0 * EPS), [so], [so])
                P.op("act", lambda e: e.activation(so.t[:, 2:3], so.t[:, 1:2], AF.Exp, scale=-0.5), [so], [so])

            def og_d(g):
                i = ginfo[g]
                ob_ = [2, 6][i % 2]
                og = pst(ob_)[:, 0:128]
                otok = PQ(ob_, 0, 128)
                so = sto[i % 4]
                mt = MTM[i % 2]
                P.op("dve", lambda e: e.scalar_tensor_tensor(mt.t, og, so.t[:, 2:3], myGG.t[:, g, :], op0=ALU.mult, op1=ALU.mult), otok + [so, myGG], [mt])
                grp = g // 4
                r = grp % 2
                rtok = PQ(7, r * 256, (r + 1) * 256)
                P.op("pe", lambda e: e.transpose(pv7[:, r, g % 4, :], mt.t, identB.t), [mt, identB], rtok)
                grp_done[grp] = grp_done.get(grp, 0) + 1
                if grp_done[grp] == 4:
                    copy_op(evac_eng(), mergedT.t[:, hh, grp * 512:(grp + 1) * 512], pv7[:, r].rearrange("p s t -> p (s t)"), rtok, [mT_tok[hh]])

            ready = {g: max(2 * g + 1, NBK - 1 - 2 * g) for g in range(NG)}
            p_ahead = int(_os.environ.get('DBG_PAHEAD', '1'))
            if p_ahead:
                o_ = chain_p(0, 0, 3)
                chain_p(1, NBK - 1, 3, after=(o_,))
            for s_ in range(NBK + 4):
                if s_ < NBK:
                    cb = 3 + (s_ % 2)
                    if not p_ahead:
                        o_ = chain_p(0, s_, cb)
                        chain_p(1, NBK - 1 - s_, cb, after=(o_,))
                    chain_step(0, s_, cb)
                    chain_step(1, NBK - 1 - s_, cb)
                    if p_ahead and s_ + 1 < NBK:
                        cbn = 3 + ((s_ + 1) % 2)
                        o_ = chain_p(0, s_ + 1, cbn)
                        chain_p(1, NBK - 2 - s_, cbn, after=(o_,))
                for g in range(NG):
                    if ready[g] == s_ - 3:
                        og_d(g)
                for g in range(NG):
                    if ready[g] == s_ - 2:
                        og_c(g)
                for g in range(NG):
                    if ready[g] == s_ - 1:
                        og_b(g)
                pair_now = [g for g in range(NG) if ready[g] == s_ + 3]
                pair_prev = [g for g in range(NG) if ready[g] == s_ + 2]
                pair_prev2 = [g for g in range(NG) if ready[g] == s_ + 1]
                if pair_prev2:
                    og_a2(pair_prev2[1])
                if pair_prev:
                    og_a2(pair_prev[0])
                    og_a1(pair_prev[1])
                if pair_now:
                    og_a1(pair_now[0])
                yield

        heads = [int(v) for v in _os.environ.get('DBG_HEADS', '0,1,2,3,4,5,6,7').split(',') if v != '']
        assert heads == list(range(8))
        load_head_weights(0)
        for _ in stageAB1(0):
            pass
        for _ in stageAB2(0):
            pass
        ilv = int(_os.environ.get('DBG_ILV', '2'))
        for hh in range(8):
            cgen = stageC(hh)
            abgen = stageAB1(hh + 1) if hh + 1 < 8 else None
            c_alive, ab_alive = True, abgen is not None
            step = 0
            while c_alive or ab_alive:
                if c_alive:
                    try:
                        next(cgen)
                    except StopIteration:
                        c_alive = False
                if ab_alive and (not c_alive or (ilv >= 1 and step % ilv == 0)):
                    try:
                        next(abgen)
                    except StopIteration:
                        ab_alive = False
                step += 1
            if hh + 1 < 8:
                for _ in stageAB2(hh + 1):
                    pass
        if debug:
            P.dma("sp", dbg["mergedT"], mergedT.t.rearrange("p k t -> p (k t)"), mT_tok, [], is_out=True)
        P.barrier()
        AR.reset(ph1_mark)
        X1 = AR.alloc("X1", [NB, D], F32)
        X1_tok = [Tok("X1_b%d" % b) for b in range(NB)]
        ph3_keep = AR.mark()
        MB[2] = AR.alloc("gate1_bc", [D], F32)
        MB[3] = AR.alloc("shift2_bc", [D], F32)
        MB[4] = AR.alloc("g2_bc", [D], F32)
        WO = AR.alloc("WO", [8, D], BF16)
        wstage = [AR.alloc("wstage%d" % i, [D], F32) for i in range(2)]
        junk = AR.alloc("junk", [D], BF16)
        tmpf = AR.alloc("tmpf", [D], F32)
        hb = [AR.alloc("hb%d" % i, [D], BF16) for i in range(2)]
        stt = [AR.alloc("stt%d" % i, [4], F32) for i in range(3)]
        dgt = [AR.alloc("dgt%d" % i, [128], F32) for i in range(2)]
        n2T = AR.alloc("n2T", [8], F32)
        g2T = AR.alloc("g2T", [8], F32)
        xb = {"n": 0}

        def expand(vec_ap, vec_toks, dst):
            for half in range(2):
                b = xb["n"] % 2
                xb["n"] += 1
                for q in range(4):
                    kc = half * 4 + q
                    dg = dgt[kc % 2]
                    P.op("dve", lambda e, kc=kc, dg=dg: e.tensor_scalar(dg.t, identF.t, vec_ap[:, kc:kc + 1], None, op0=ALU.mult), [identF] + vec_toks, [dg])
                    P.op("pe", lambda e, q=q, b=b, dg=dg: e.matmul(pst(b)[:, q * 128:(q + 1) * 128], onesF.t, dg.t, start=True, stop=True), [onesF, dg], PQ(b, 0, 512))
                copy_op(evac_eng(), dst.t[:, half * 512:(half + 1) * 512], pst(b)[:, :], PQ(b, 0, 512), [dst])

        P.dma("sp", n2T.t, norm2T_d, [], [n2T])
        P.op("dve", lambda e: e.scalar_tensor_tensor(g2T.t, modT.t[:, 2, :], 1.0, n2T.t, op0=ALU.add, op1=ALU.mult), [modT, n2T], [g2T])
        expand(modT.t[:, 0, :], [modT], MB[2])
        expand(modT.t[:, 1, :], [modT], MB[3])
        expand(g2T.t, [g2T], MB[4])
        for kc in range(8):
            ws = wstage[kc % 2]
            P.dma("sp", ws.t, wout_d[kc * 128:(kc + 1) * 128, :], [], [ws])
            gcol = 0 if kc < 4 else 1
            P.op("dve", lambda e, kc=kc, ws=ws, gcol=gcol: e.scalar_tensor_tensor(WO.t[:, kc, :], ws.t, GS[:, gcol:gcol + 1], MB[2].t, op0=ALU.mult, op1=ALU.mult),
                 [ws, smalls, MB[2]], [WO])
        for b in range(NB):
            P.dma("sp", X1.t[:, b, :], x_d[b * 128:(b + 1) * 128, :], [], [X1_tok[b]])
        ob = {"n": 0}
        for b in range(NB):
            for half in range(2):
                bk = 2 + ob["n"] % 2
                ob["n"] += 1
                for kc in range(8):
                    P.op("pe", lambda e, kc=kc, bk=bk, b=b, half=half: e.matmul(pst(bk)[:, :], mergedT.t[:, kc, b * 128:(b + 1) * 128], WO.t[:, kc, half * 512:(half + 1) * 512],
                                                                              start=(kc == 0), stop=(kc == 7)), [mT_tok[kc], WO], PQ(bk, 0, 512))
                hs = slice(half * 512, (half + 1) * 512)
                P.op("dve", lambda e, bk=bk, b=b, hs=hs: e.tensor_tensor(X1.t[:, b, hs], pst(bk)[:, :], X1.t[:, b, hs], ALU.add), PQ(bk, 0, 512) + [X1_tok[b]], [X1_tok[b]])
            norm_A(X1.t[:, b, :], [X1_tok[b]], stt[b % 3])
            if b >= 1:
                norm_B1(X1.t[:, b - 1, :], [X1_tok[b - 1]], MB[4], MB[3], b - 1, stt[(b - 1) % 3])
            if b >= 2:
                norm_B2(b - 2, 4 + ((b - 2) % 2))
            if debug:
                P.dma("sp", dbg["x1"][b * 128:(b + 1) * 128, :], X1.t[:, b, :], [X1_tok[b]], [], is_out=True)
        norm_B1(X1.t[:, NB - 1, :], [X1_tok[NB - 1]], MB[4], MB[3], NB - 1, stt[(NB - 1) % 3])
        norm_B2(NB - 2, 4 + ((NB - 2) % 2))
        norm_B2(NB - 1, 4 + ((NB - 1) % 2))
        if debug:
            dump("h2T", TT(hT.t, hT_tok[0]), 8 * T, BF16)

        P.barrier()
        AR.reset(ph3_keep)
        AR.end = top_keep
        MB[5] = AR.alloc("gate2_bc", [D], F32)
        FN = AR.alloc("fnorm_bc", [D], F32)
        dgt = [AR.alloc("dgt%d" % i, [128], F32) for i in range(2)]
        GMAX = max(j1 - j0 for j0, j1 in GROUPS)
        HT = AR.alloc("HT", [GMAX, T], BF16)
        WD = AR.alloc("WD", [GMAX, D], BF16)
        wdst = [AR.alloc("wdst%d" % i, [D], F32) for i in range(2)]
        UB = [AR.alloc("UB%d" % i, [UW], BF16) for i in range(2)]
        SG = AR.alloc("SG", [T], F32)
        DG = [AR.alloc("DG%d" % i, [11, 128], BF16) for i in range(2)]
        w11T = AR.alloc("w11T", [NCH * 11], F32)
        bconvT = AR.alloc("bconvT", [NCH], F32)
        ring["slots"] = [AR.alloc("wringC%d" % i, [8 * 256], BF16) for i in range(3)]
        ring["n"] = 0
        yst = [AR.alloc("yst%d" % i, [D], F32) for i in range(2)]
        DACC = AR.alloc("DACC", [T], BF16)
        ctmp = yst
        junk = AR.alloc("junk", [D], BF16)
        stt = [AR.alloc("stt%d" % i, [4], F32) for i in range(3)]
        expand(modT.t[:, 3, :], [modT], MB[5])
        P.dma("sp", FN.t, fnorm_d.partition_broadcast(128), [], [FN])
        P.dma("sp", w11T.t, w11T_d, [], [w11T])
        P.dma("sp", bconvT.t, bconvT_d, [], [bconvT])
        for u in range(2):
            P.op("pool", lambda e, u=u: e.memset(UB[u].t, 0.0), [], [UB[u]])
        P.op("pool", lambda e: e.memset(DACC.t, 0.0), [], [DACC])
        identB_b11 = bass.AP(identB.t.tensor, identB.t.offset, [list(identB.t.ap[0]), [0, 11], [1, 128]])
        ucnt = {"n": 0}
        pair_w = {}

        def load_pair(j):
            w = next_w()
            wv = w.t[:, 0:8 * 256].rearrange("p (k n) -> p k n", k=8)
            P.dma("pool", wv[:, :, 0:128], wup_d[:, j * 128:(j + 1) * 128].rearrange("(k p) n -> p k n", p=128), [], [w])
            P.dma("pool", wv[:, :, 128:256], wup_d[:, FFN_H + j * 128:FFN_H + (j + 1) * 128].rearrange("(k p) n -> p k n", p=128), [], [w])
            pair_w[j] = (w, wv)

        def up_proj(j, is_up):
            w, wv = pair_w[j]
            off = 128 if is_up else 0
            u = ucnt["n"] % 2
            ucnt["n"] += 1
            for tt in range(4):
                for kc in range(8):
                    P.op("pe", lambda e, kc=kc, tt=tt: e.matmul(pst(tt)[:, :], wv[:, kc, off:off + 128], hT.t[:, kc, tt * 512:(tt + 1) * 512], start=(kc == 0), stop=(kc == 7)),
                         [w] + hT_tok[4 * tt:4 * tt + 4], PQ(tt, 0, 512))
                P.op("act", lambda e, tt=tt: e.activation(UB[u].t[:, UPAD + tt * 512:UPAD + (tt + 1) * 512], pst(tt)[:, :], AF.Copy), PQ(tt, 0, 512), [UB[u]])
            return u

        ctn = {"n": 0}

        def conv(j, jj, is_up, u):
            cc = (NPAIR + j) if is_up else j
            dg = DG[cc % 2]
            wb = bass.AP(w11T.t.tensor, w11T.t.offset + cc * 11, [list(w11T.t.ap[0]), [1, 11], [0, 128]])
            P.op("pool", lambda e: e.tensor_tensor(dg.t, identB_b11, wb, ALU.mult), [identB, w11T], [dg])
            ub = UB[u].t
            acc3 = DACC.t.rearrange("p (r c) -> p r c", c=64)[:, :, 0:63]
            for k_, dy in enumerate((0, -1, 1)):
                wi = (dy + 1) * 3 + 2
                src3 = ub[:, UPAD + 64 * dy + 1:UPAD + 64 * dy + 1 + T].rearrange("p (r c) -> p r c", c=64)[:, :, 0:63]
                wsc = w11T.t[:, cc * 11 + wi:cc * 11 + wi + 1]
                if k_ == 0:
                    P.op("dve", lambda e, src3=src3, wsc=wsc: e.tensor_scalar(acc3, src3, wsc, None, op0=ALU.mult), [UB[u], w11T], [DACC])
                else:
                    P.op("dve", lambda e, src3=src3, wsc=wsc: e.scalar_tensor_tensor(acc3, src3, wsc, acc3, op0=ALU.mult, op1=ALU.add), [UB[u], w11T, DACC], [DACC])
            for tt in range(4):
                base = UPAD + tt * 512
                pt = pst(4 + tt)
                taps = []
                for dy in (0, -1, 1):
                    taps.append(((dy + 1) * 3 + 1, pt[:, 0:512], ub[:, base + 64 * dy:base + 64 * dy + 512]))
                for dy in (0, -1, 1):
                    o3 = pt[:, 0:512].rearrange("p (r c) -> p r c", c=64)[:, :, 1:64]
                    r3 = ub[:, base + 64 * dy - 1:base + 64 * dy - 1 + 512].rearrange("p (r c) -> p r c", c=64)[:, :, 1:64]
                    taps.append(((dy + 1) * 3 + 0, o3, r3))
                o4 = pt[:, 0:512].rearrange("p (a r c) -> p a r c", a=2, r=4)[:, :, 1:4, 0]
                r4 = ub[:, base - 1:base - 1 + 512].rearrange("p (a r c) -> p a r c", a=2, r=4)[:, :, 1:4, 0]
                taps.append((9, o4, r4))
                o4 = pt[:, 0:512].rearrange("p (a r c) -> p a r c", a=2, r=4)[:, :, 0:3, 63]
                r4 = ub[:, base + 1:base + 1 + 512].rearrange("p (a r c) -> p a r c", a=2, r=4)[:, :, 0:3, 63]
                taps.append((10, o4, r4))
                for ti, (wi, oap, rap) in enumerate(taps):
                    P.op("pe", lambda e, wi=wi, oap=oap, rap=rap, ti=ti: e.matmul(oap, dg.t[:, wi, :], rap, start=(ti == 0), stop=(ti == len(taps) - 1)),
                         [dg, UB[u]], PQ(4 + tt, 0, 512))
                ts_ = slice(tt * 512, (tt + 1) * 512)
                ct = ctmp[ctn["n"] % 2]
                ctn["n"] += 1
                P.op("dve", lambda e, pt=pt, ts_=ts_, ct=ct: e.tensor_tensor(ct.t[:, 0:512], pt[:, 0:512], DACC.t[:, ts_], ALU.add), PQ(4 + tt, 0, 512) + [DACC], [ct])
                if not is_up:
                    P.op("act", lambda e, ts_=ts_, ct=ct: e.activation(SG.t[:, ts_], ct.t[:, 0:512], AF.Silu, bias=bconvT.t[:, cc:cc + 1]), [ct, bconvT], [SG])
                else:
                    P.op("dve", lambda e, ts_=ts_, ct=ct: e.scalar_tensor_tensor(HT.t[:, jj, ts_], ct.t[:, 0:512], bconvT.t[:, cc:cc + 1], SG.t[:, ts_], op0=ALU.add, op1=ALU.mult),
                         [ct, bconvT, SG], [HT])

        fin_pending = []

        def final_out(b):
            st = stt[b % 3]
            ys = yst[b % 2]
            xap = X1.t[:, b, :]
            P.op("dve", lambda e: e.scalar_tensor_tensor(ys.t, xap, st.t[:, 2:3], FN.t, op0=ALU.mult, op1=ALU.mult), [X1_tok[b], st, FN], [ys])
            P.dma("sp", y_d[b * 128:(b + 1) * 128, :], ys.t, [ys], [], is_out=True)

        load_pair(0)
        load_pair(1)
        dbk = {"n": 0}

        def wd_load(gi):
            j0, j1 = GROUPS[gi]
            for jj, j in enumerate(range(j0, j1)):
                wq = wdst[j % 2]
                P.dma("sp", wq.t, wdn_d[j * 128:(j + 1) * 128, :], [], [wq])
                P.op("dve", lambda e, jj=jj, wq=wq: e.tensor_tensor(WD.t[:, jj, :], wq.t, MB[5].t, ALU.mult), [wq, MB[5]], [WD])

        def down(gi):
            j0, j1 = GROUPS[gi]
            last_group = gi == len(GROUPS) - 1
            ng = j1 - j0
            for b in range(NB):
                for half in range(2):
                    bk = dbk["n"] % 4
                    dbk["n"] += 1
                    hs = slice(half * 512, (half + 1) * 512)
                    for jj in range(ng):
                        P.op("pe", lambda e, jj=jj, bk=bk, b=b, hs=hs: e.matmul(pst(bk)[:, :], HT.t[:, jj, b * 128:(b + 1) * 128], WD.t[:, jj, hs], start=(jj == 0), stop=(jj == ng - 1)),
                             [HT, WD], PQ(bk, 0, 512))
                    P.op("dve", lambda e, bk=bk, b=b, hs=hs: e.tensor_tensor(X1.t[:, b, hs], pst(bk)[:, :], X1.t[:, b, hs], ALU.add), PQ(bk, 0, 512) + [X1_tok[b]], [X1_tok[b]])
                if last_group:
                    st = stt[b % 3]
                    xap = X1.t[:, b, :]
                    P.op("act", lambda e, xap=xap, st=st, jk=junk: e.activation(jk.t, xap, AF.Square, accum_out=st.t[:, 0:1]), [X1_tok[b]], [junk, st], multi=True)
                    P.op("act", lambda e, st=st: e.activation(st.t[:, 1:2], st.t[:, 0:1], AF.Ln, scale=1.0 / D, bias=EPS), [st], [st])
                    P.op("act", lambda e, st=st: e.activation(st.t[:, 2:3], st.t[:, 1:2], AF.Exp, scale=-0.5), [st], [st])
                    fin_pending.append(b)
                    if len(fin_pending) > 1:
                        final_out(fin_pending.pop(0))
            if last_group:
                while fin_pending:
                    final_out(fin_pending.pop(0))

        wd_load(0)
        pend = None
        deferred = None
        for gi, (j0, j1) in enumerate(GROUPS):
            for j in range(j0, j1):
                for is_up in (False, True):
                    if (not is_up) and (j + 2 < NPAIR):
                        load_pair(j + 2)
                    u = up_proj(j, is_up)
                    if pend is not None:
                        if deferred is not None and pend[4] == gi and pend[2]:
                            down(deferred)
                            wd_load(gi)
                            deferred = None
                        conv(*pend[:4])
                    pend = (j, j - j0, is_up, u, gi)
            deferred = gi
        if deferred is not None and pend[4] == deferred:
            conv(*pend[:4])
            down(deferred)
        P.finish()
    return nc


def _consts():
    ident = np.eye(128, dtype=np.float32)
    j = np.arange(128)[:, None]
    i = np.arange(128)[None, :]
    same = (j // 64) == (i // 64)
    maskT2 = np.concatenate([(j <= i) & same, (j >= i) & same], axis=1).astype(np.float32)
    scanmask = np.ones((1, T), np.float32)
    scanmask[0, ::64] = 0.0
    return ident, maskT2, scanmask


def prep_core_inputs(inp):
    f32 = lambda a: np.ascontiguousarray(np.asarray(a, dtype=np.float32))
    ident, maskT2, scanmask = _consts()
    shared = {
        "w_ada": f32(inp["w_ada"][0]), "b_ada": f32(inp["b_ada"][0]).reshape(1, -1),
        "norm1": f32(inp["norm1"][0]).reshape(1, -1), "norm2": f32(inp["norm2"][0]).reshape(1, -1),
        "b_adaT": f32(np.asarray(inp["b_ada"][0]).reshape(48, 128).T), "norm2T": f32(np.asarray(inp["norm2"][0]).reshape(8, 128).T),
        "fnorm": f32(inp["final_norm"]).reshape(1, -1),
        "w_in": f32(inp["w_in"][0]), "w_gla_up": f32(inp["w_gla_up"][0]),
        "b_glaT": f32(np.asarray(inp["b_gla"][0]).reshape(2, 2, 128).transpose(2, 0, 1).reshape(128, 4)),
        "lbT": f32(np.asarray(inp["hgrn_lb"]).reshape(2, 2, 4, 128).transpose(3, 0, 1, 2).reshape(128, 16)),
        "gnorm": f32(np.stack([np.asarray(inp["gla_norm"][0]), np.asarray(inp["hgrn_norm"][0])], axis=1)),
        "w_out": f32(inp["w_out"][0]), "w_ffn_up": f32(inp["w_ffn_up"][0]),
        "bconvT": f32(np.asarray(inp["b_ffn_conv"][0]).reshape(NCH, 128).T),
        "w_ffn_down": f32(inp["w_ffn_down"][0]),
        "identF": ident, "maskT2": maskT2, "scanmask": scanmask,
    }
    conv = np.asarray(inp["ffn_conv"][0], dtype=np.float32).reshape(9, 2 * FFN_H)
    zero_row = np.zeros((1, 2 * FFN_H), np.float32)
    rows_s = np.concatenate([conv, zero_row, zero_row], axis=0)
    rows_p = np.concatenate([zero_row] * 3 + [conv[3:6]] + [zero_row] * 3 + [conv[3:4], conv[5:6]], axis=0)
    w11 = lambda rows: f32(rows.reshape(11, NCH, 128).transpose(2, 1, 0).reshape(128, NCH * 11))
    x_prompt = np.asarray(inp["x_prompt"], dtype=np.float32)
    x_sample = np.asarray(inp["x_sample"], dtype=np.float32)
    maps = []
    for c in range(8):
        m = dict(shared)
        if c < 4:
            m["x"] = f32(x_sample[c])
            m["cvT"] = f32(np.asarray(inp["c"][c]).reshape(8, 128).T)
            m["sinit_g"] = f32(inp["state_gla"][c, 0])
            m["sinit_h"] = f32(inp["state_hgrn"][c, 0])
            mf = np.ones(32, np.float32)
            mb = np.ones(32, np.float32)
            m["w11T"] = w11(rows_s)
        else:
            p = c - 4
            m["x"] = f32(x_prompt[8 * p:8 * p + 8].reshape(T, D))
            m["cvT"] = f32(np.asarray(inp["c_ctx"]).reshape(8, 128).T)
            m["sinit_g"] = np.zeros((2, 4, 64, 128), np.float32)
            m["sinit_h"] = np.zeros((2, 4, 128, 128), np.float32)
            mf = (np.arange(32) % 4 != 0).astype(np.float32)
            mb = (np.arange(32) % 4 != 3).astype(np.float32)
            m["w11T"] = w11(rows_p)
        m["mchain"] = f32(np.tile(np.concatenate([mf, mb])[None, :], (128, 1)))
        maps.append(m)
    return maps


_PROGRAM = {}


def kernel(**inputs):
    if "nc" not in _PROGRAM:
        _PROGRAM["nc"] = build_program(debug=False)
    nc = _PROGRAM["nc"]
    in_maps = prep_core_inputs(inputs)
    res = run_bass_kernel_spmd(nc, in_maps, core_ids=list(range(8)))
    r = res.results
    y_sample = np.stack([np.asarray(r[c]["y"], dtype=np.float32) for c in range(4)], axis=0)
    y_prompt = np.concatenate([np.asarray(r[c]["y"], dtype=np.float32).reshape(8, 256, D) for c in range(4, 8)], axis=0)
    sg = np.concatenate([np.asarray(r[c]["snew_g"], dtype=np.float32) for c in range(4, 8)], axis=0)[:, None]
    sh = np.concatenate([np.asarray(r[c]["snew_h"], dtype=np.float32) for c in range(4, 8)], axis=0)[:, None]
    return (y_prompt, y_sample, sg, sh)
```

```python
import numpy as np
from contextlib import ExitStack
import concourse.bass as bass
import concourse.mybir as mybir
from concourse.bass_utils import run_bass_kernel_spmd

F32 = mybir.dt.float32
BF16 = mybir.dt.bfloat16
AF = mybir.ActivationFunctionType
ALU = mybir.AluOpType


class Tok:
    __slots__ = ("name", "w", "r", "rd", "excl", "acc")

    def __init__(self, name, excl=False):
        self.name = name
        self.w = None
        self.r = {}
        self.rd = []
        self.excl = excl
        self.acc = {}


class TT:
    def __init__(self, t, tok):
        self.t = t
        self.tok = tok


class _Op:
    __slots__ = ("idx", "eng", "fn", "deps", "is_dma", "sig", "semval", "slot", "is_out", "multi")


def _tok(x):
    return x.tok if isinstance(x, TT) else x


class Prog:
    NSLOT = {"sp": 24, "pool": 16, "act": 8}

    def __init__(self, nc, es):
        self.nc = nc
        self.es = es
        self.ops = []
        self.n_dma = {"sp": 0, "pool": 0, "act": 0}
        self._n = 0
        self.bar = set()

    def sb(self, name, shape, dtype):
        t = self.es.enter_context(self.nc.sbuf_tensor(name, list(shape), dtype))
        return TT(t, Tok(name))

    def ps(self, name):
        t = self.es.enter_context(self.nc.psum_tensor(name, [128, 512], F32))
        return TT(t, Tok(name))

    def tok(self, name):
        return Tok(name)

    def _record(self, eng, fn, reads, writes, is_dma, is_out=False, extra=(), multi=False):
        op = _Op()
        op.multi = multi
        op.idx = len(self.ops)
        op.eng = eng
        op.fn = fn
        op.is_dma = is_dma
        op.sig = False
        op.semval = 0
        op.slot = None
        op.is_out = is_out
        deps = set()
        reads = [_tok(x) for x in reads]
        writes = [_tok(x) for x in writes]

        def consider(pidx, kind):
            p = self.ops[pidx]
            if p.is_dma:
                deps.add(pidx)
                return
            if (not is_dma) and p.eng == eng:
                if eng == "pe":
                    return
                if kind != "raw":
                    return
            deps.add(pidx)

        for t in reads:
            if t.w is not None:
                consider(t.w, "raw")
        for t in writes:
            if t.w is not None:
                consider(t.w, "waw")
            for _, ridx in t.r.items():
                consider(ridx, "war")
            for ridx in t.rd:
                consider(ridx, "war")
        for t in reads + writes:
            if t.excl:
                for e2, aidx in t.acc.items():
                    if e2 != eng:
                        deps.add(aidx)
                t.acc[eng] = op.idx
        for x in extra:
            deps.add(x.idx)
        for pidx in self.bar:
            p = self.ops[pidx]
            if (not is_dma) and (not p.is_dma) and p.eng == eng:
                continue
            deps.add(pidx)
        op.deps = deps
        for t in reads:
            if is_dma:
                t.rd.append(op.idx)
            else:
                t.r[eng] = op.idx
        for t in writes:
            t.w = op.idx
            t.r = {}
            t.rd = []
        if is_dma:
            k = self.n_dma[eng]
            self.n_dma[eng] += 1
            ns = self.NSLOT[eng]
            op.slot = (eng, k % ns)
            op.semval = 16 * (k // ns + 1)
        self.ops.append(op)
        return op

    def op(self, eng, fn, reads, writes, extra=(), multi=False):
        return self._record(eng, fn, reads, writes, False, extra=extra, multi=multi)

    def barrier(self):
        last = {}
        for op in self.ops:
            if op.is_dma:
                last[("d",) + op.slot] = op.idx
            else:
                last[op.eng] = op.idx
        self.bar = set(last.values())

    def dma(self, eng, out_ap, in_ap, reads, writes, is_out=False, **kw):
        def fn(e, out_ap=out_ap, in_ap=in_ap, kw=kw):
            return e.dma_start(out=out_ap, in_=in_ap, **kw)
        return self._record(eng, fn, reads, writes, True, is_out)

    def finish(self):
        nc = self.nc
        es = self.es
        ops = self.ops
        for op in ops:
            for d in op.deps:
                ops[d].sig = True
        engs = ["pe", "act", "dve", "pool", "sp"]
        esem = {e: es.enter_context(nc.semaphore("s_" + e)) for e in engs}
        dsem = {}
        for e, ns in self.NSLOT.items():
            for i in range(min(ns, max(1, self.n_dma[e]))):
                dsem[(e, i)] = es.enter_context(nc.semaphore("d_%s%d" % (e, i)))
        cnt = {e: 0 for e in engs}
        for op in ops:
            if not op.is_dma and op.sig:
                cnt[op.eng] += 1
                op.semval = cnt[op.eng]
        slot_last = {}
        prev_on_slot = {}
        for op in ops:
            if op.is_dma:
                prev_on_slot[op.idx] = slot_last.get(op.slot)
                slot_last[op.slot] = op.idx
        by_eng = {e: [op for op in ops if op.eng == e] for e in engs}

        def sigof(p):
            if p.is_dma:
                return dsem[p.slot], p.semval
            return esem[p.eng], p.semval

        def emit(ename, e):
            waited = {}

            embed = ename in ("dve", "act", "pool")

            for op in by_eng[ename]:
                need = {}
                order = []
                cand = [sigof(ops[d]) for d in sorted(op.deps)]
                if op.is_dma:
                    pv = prev_on_slot[op.idx]
                    if pv is not None:
                        cand.append(sigof(ops[pv]))
                for s, v in cand:
                    key = id(s)
                    if waited.get(key, 0) < v and need.get(key, (None, 0))[1] < v:
                        if key not in need:
                            order.append(key)
                        need[key] = (s, v)
                pend = [need[k] for k in order]
                for s, v in pend:
                    waited[id(s)] = v
                fold = None
                if embed and pend and not op.is_dma and not op.multi:
                    fold = pend.pop()
                for s, v in pend:
                    e.wait_ge(s, v)
                ins = op.fn(e)
                if fold is not None:
                    ins._wait_ge(fold[0], fold[1])
                if op.is_dma:
                    ins.then_inc(dsem[op.slot], 16)
                elif op.sig:
                    ins.then_inc(esem[ename], 1)
            if ename == "sp":
                for slot, idx in slot_last.items():
                    s, v = sigof(ops[idx])
                    if waited.get(id(s), 0) < v:
                        e.wait_ge(s, v)
                        waited[id(s)] = v

        with nc.Block() as block:
            @block.tensor
            def _(e):
                emit("pe", e)

            @block.scalar
            def _(e):
                emit("act", e)

            @block.vector
            def _(e):
                emit("dve", e)

            @block.gpsimd
            def _(e):
                emit("pool", e)

            @block.sync
            def _(e):
                emit("sp", e)
        self.stats = {e: len(by_eng[e]) for e in engs}


D = 1024
T = 2048
NB = 16
EPS = 1e-6
IN_W = 4128
FFN_H = 2816
NCH = 44
NPAIR = 22
UPAD = 65
UW = UPAD + T + UPAD
GROUPS = [(0, 6), (6, 12), (12, 17), (17, 22)]
TS = 512
OFF_QA, OFF_KA, OFF_VA, OFF_GA, OFF_LR = 0, 256, 512, 1024, 1536
OFF_QB, OFF_FB, OFF_IB, OFF_GB = 1568, 2080, 3104, 3616


class Arena:
    def __init__(self, P, nf32):
        self.P = P
        self.tt = P.sb("arena", [128, nf32], F32)
        self.n = nf32
        self.top = 0
        self.end = nf32

    def alloc(self, name, free_shape, dtype, top=False):
        nel = int(np.prod(free_shape))
        nf = nel if dtype == F32 else (nel + 1) // 2
        nf = (nf + 3) // 4 * 4
        assert self.top + nf <= self.end, ("arena overflow", name, self.top, nf, self.end)
        if top:
            self.end -= nf
            ap = self.tt.t[:, self.end:self.end + nf]
        else:
            ap = self.tt.t[:, self.top:self.top + nf]
        if dtype != F32:
            ap = ap.bitcast(dtype)
        ap = ap[:, 0:nel]
        if len(free_shape) == 2:
            ap = ap.rearrange("p (a b) -> p a b", a=free_shape[0])
        elif len(free_shape) == 3:
            ap = ap.rearrange("p (a b c) -> p a b c", a=free_shape[0], b=free_shape[1])
        if not top:
            self.top += nf
        return TT(ap, Tok(name))

    def mark(self):
        return self.top

    def reset(self, m):
        self.top = m


def build_program(debug=False):
    import os as _os
    nc = bass.Bass("TRN2", target_bir_lowering=False)

    def din(name, shape):
        return nc.dram_tensor(name, list(shape), F32, kind="ExternalInput").ap()

    def dout(name, shape, dt=F32):
        return nc.dram_tensor(name, list(shape), dt, kind="ExternalOutput").ap()

    x_d = din("x", [T, D])
    cvT_d = din("cvT", [128, 8])
    wada_d = din("w_ada", [D, 6 * D])
    bada_d = din("b_ada", [1, 6 * D])
    badaT_d = din("b_adaT", [128, 48])
    norm2T_d = din("norm2T", [128, 8])
    norm1_d = din("norm1", [1, D])
    norm2_d = din("norm2", [1, D])
    fnorm_d = din("fnorm", [1, D])
    win_d = din("w_in", [D, IN_W])
    wgu_d = din("w_gla_up", [2, 16, 256])
    bglaT_d = din("b_glaT", [128, 4])
    lbT_d = din("lbT", [128, 16])
    gnorm_d = din("gnorm", [128, 2])
    wout_d = din("w_out", [D, D])
    wup_d = din("w_ffn_up", [D, 2 * FFN_H])
    w11T_d = din("w11T", [128, NCH * 11])
    bconvT_d = din("bconvT", [128, NCH])
    wdn_d = din("w_ffn_down", [FFN_H, D])
    sig_d = din("sinit_g", [2, 4, 64, 128])
    sih_d = din("sinit_h", [2, 4, 128, 128])
    mchain_d = din("mchain", [128, 64])
    identF_d = din("identF", [128, 128])
    maskT2_d = din("maskT2", [128, 256])
    scanmask_d = din("scanmask", [1, T])
    y_d = dout("y", [T, D])
    sog_d = dout("snew_g", [8, 2, 4, 64, 128])
    soh_d = dout("snew_h", [8, 2, 4, 128, 128])
    dbg = {}
    if debug:
        dbg["h1T"] = dout("dbg_h1T", [128, 8 * T], BF16)
        dbg["mod"] = dout("dbg_mod", [128, 6 * D])
        dbg["mergedT"] = dout("dbg_mergedT", [128, 8 * T], BF16)
        dbg["x1"] = dout("dbg_x1", [T, D])

    es = ExitStack()
    with es:
        P = Prog(nc, es)
        AR = Arena(P, 52800)
        dumps = {}

        def dump(name, tt, ncols, dt, parts=128):
            if not debug:
                return
            if name not in dumps:
                dumps[name] = nc.dram_tensor("dd_" + name, [128, ncols], dt, kind="ExternalOutput").ap()
            ap = tt.t
            if len(ap.shape) == 3:
                ap = ap.rearrange("p a b -> p (a b)")
            elif len(ap.shape) == 4:
                ap = ap.rearrange("p a b c -> p (a b c)")
            P.dma("sp", dumps[name][0:parts], ap[0:parts], [tt], [], is_out=True)
        psb = [P.ps("psum%d" % i) for i in range(8)]
        psq = [Tok("psbank%d" % b, excl=True) for b in range(8)]

        def PQ(b, c0, c1):
            return [psq[b]]

        def pst(b):
            return psb[b].t

        rr = {"n": 0}

        def evac_eng():
            rr["n"] += 1
            return "act" if rr["n"] % 2 else "dve"

        def copy_op(eng, out, in_, reads, writes, scale=None):
            if eng == "act":
                if scale is None:
                    P.op("act", lambda e: e.activation(out, in_, AF.Copy), reads, writes)
                else:
                    P.op("act", lambda e: e.activation(out, in_, AF.Copy, scale=scale), reads, writes)
            else:
                if scale is None:
                    P.op(eng, lambda e: e.tensor_copy(out, in_), reads, writes)
                else:
                    P.op(eng, lambda e: e.tensor_scalar(out, in_, scale, None, op0=ALU.mult), reads, writes)

        identF = AR.alloc("identF", [128], F32)
        identB = AR.alloc("identB", [128], BF16)
        maskT2 = AR.alloc("maskT2", [256], F32)
        scanmask = AR.alloc("scanmask", [TS], BF16)
        mchain = AR.alloc("mchain", [64], F32)
        smalls = AR.alloc("smalls", [64], F32)
        LB = smalls.t[:, 0:8]
        L1M = smalls.t[:, 8:16]
        NEGB = smalls.t[:, 16:24]
        GS = smalls.t[:, 24:26]
        SC = smalls.t[:, 32:40]
        TMPS = smalls.t[:, 40:64]
        MB = [None] * 6
        hT = AR.alloc("hT", [8, T], BF16)
        hT_tok = [Tok("hT_b%d" % b) for b in range(NB)]
        onesF = AR.alloc("onesF", [128], F32)
        ring = {"slots": [], "n": 0}

        def next_w():
            w = ring["slots"][ring["n"] % len(ring["slots"])]
            ring["n"] += 1
            return w

        ph1_mark = AR.mark()
        ring["slots"] = [AR.alloc("wringA%d" % i, [8 * 512], BF16) for i in range(2)]
        srep = AR.alloc("srep", [8, 128], BF16)
        brow = [AR.alloc("brow%d" % i, [512], F32) for i in range(2)]
        MB[0] = AR.alloc("mod0", [D], F32)
        MB[1] = AR.alloc("mod1", [D], F32)

        P.dma("sp", identF.t, identF_d, [], [identF])
        P.dma("sp", maskT2.t, maskT2_d, [], [maskT2])
        P.dma("pool", scanmask.t, scanmask_d[:, 0:TS].partition_broadcast(128), [], [scanmask])
        P.dma("sp", mchain.t, mchain_d, [], [mchain])
        P.op("dve", lambda e: e.tensor_copy(identB.t, identF.t), [identF], [identB])
        lbT = AR.alloc("lbT", [16], F32)
        cvT = AR.alloc("cvT", [8], F32)
        bgl = AR.alloc("bgl", [8], F32)
        gnm = AR.alloc("gnm", [2], F32)
        P.dma("sp", lbT.t, lbT_d, [], [lbT])
        P.dma("sp", cvT.t, cvT_d, [], [cvT])
        P.dma("sp", bgl.t[:, 0:4], bglaT_d, [], [bgl])
        P.dma("sp", gnm.t, gnorm_d, [], [gnm])
        DD = TMPS[:, 0:8]
        EE = TMPS[:, 8:16]
        P.op("dve", lambda e: e.tensor_tensor(DD, lbT.t[:, 8:16], lbT.t[:, 0:8], ALU.subtract), [lbT], [smalls])
        P.op("act", lambda e: e.activation(EE, DD, AF.Exp), [smalls], [smalls])
        P.op("act", lambda e: e.activation(EE, EE, AF.Ln, bias=1.0), [smalls], [smalls])
        P.op("act", lambda e: e.activation(LB, EE, AF.Exp, scale=-1.0), [smalls], [smalls])
        P.op("dve", lambda e: e.tensor_tensor(L1M, DD, EE, ALU.subtract), [smalls], [smalls])
        P.op("dve", lambda e: e.tensor_scalar(NEGB[:, 0:4], bgl.t[:, 0:4], -1.0, None, op0=ALU.mult), [bgl], [smalls])
        P.op("dve", lambda e: e.tensor_scalar(GS, gnm.t, float(np.sqrt(128.0)), None, op0=ALU.mult), [gnm], [smalls])
        E2 = TMPS[:, 16:24]
        P.op("act", lambda e: e.activation(E2, cvT.t, AF.Exp, scale=-1.0), [cvT, smalls], [smalls])
        P.op("dve", lambda e: e.tensor_scalar(E2, E2, 1.0, None, op0=ALU.add), [smalls], [smalls])
        P.op("dve", lambda e: e.reciprocal(E2, E2), [smalls], [smalls])
        P.op("dve", lambda e: e.tensor_tensor(SC, cvT.t, E2, ALU.mult), [cvT, smalls], [smalls])
        sc_b = bass.AP(smalls.t.tensor, smalls.t.offset + 32, [list(smalls.t.ap[0]), [1, 8], [0, 128]])
        P.op("dve", lambda e: e.tensor_copy(srep.t, sc_b), [smalls], [srep])
        srepF = AR.alloc("srepF", [8, 128], F32)
        P.op("dve", lambda e: e.tensor_copy(srepF.t, sc_b), [smalls], [srepF])
        wF32 = [AR.alloc("wF32_%d" % i, [8, 512], F32) for i in range(2)]
        nrm = AR.alloc("nrm_bc", [D], F32)
        P.op("pool", lambda e: e.memset(onesF.t, 1.0), [], [onesF])
        pbank = {"n": 0}

        def mod_compute(j):
            col = [0, 1, 2, 3, 4, 5][j]
            for n in range(2):
                c0 = col * D + n * 512
                if n == 0:
                    w = next_w()
                    wv = w.t[:, 0:8 * 512].rearrange("p (k n) -> p k n", k=8)
                    P.dma("pool", wv, wada_d[:, c0:c0 + 512].rearrange("(k p) n -> p k n", p=128), [], [w])
                    sr = srep
                else:
                    w = wF32[j % 2]
                    wv = w.t
                    P.dma("sp", wv, wada_d[:, c0:c0 + 512].rearrange("(k p) n -> p k n", p=128), [], [w])
                    sr = srepF
                br = brow[(2 * j + n) % 2]
                P.dma("sp", br.t[0:1, :], bada_d[:, c0:c0 + 512], [], [br])
                b = pbank["n"] % 2
                pbank["n"] += 1
                for kc in range(8):
                    P.op("pe", lambda e, kc=kc, b=b, wv=wv, sr=sr: e.matmul(pst(b)[:, :], sr.t[:, kc, :], wv[:, kc, :], start=(kc == 0), stop=False),
                         [sr, w], PQ(b, 0, 512))
                P.op("pe", lambda e, b=b, br=br: e.matmul(pst(b)[:, :], onesF.t[0:1, :], br.t[0:1, :], start=False, stop=True),
                     [onesF, br], PQ(b, 0, 512))
                copy_op(evac_eng(), MB[j].t[:, n * 512:(n + 1) * 512], pst(b)[:, :], PQ(b, 0, 512), [MB[j]])

        mod_compute(0)
        mod_compute(1)
        P.dma("sp", nrm.t, norm1_d.partition_broadcast(128), [], [nrm])
        P.op("dve", lambda e: e.scalar_tensor_tensor(MB[1].t, MB[1].t, 1.0, nrm.t, op0=ALU.add, op1=ALU.mult), [MB[1], nrm], [MB[1]])

        xring = [AR.alloc("xring%d" % i, [D], F32) for i in range(3)]
        junk = AR.alloc("junk", [D], BF16)
        tmpf = AR.alloc("tmpf", [D], F32)
        hb = [AR.alloc("hb%d" % i, [D], BF16) for i in range(2)]
        stt = [AR.alloc("stt%d" % i, [4], F32) for i in range(3)]

        def norm_A(src_ap, src_toks, st):
            jk = junk
            P.op("act", lambda e: e.activation(jk.t, src_ap, AF.Square, accum_out=st.t[:, 0:1]), src_toks, [jk, st], multi=True)
            P.op("act", lambda e: e.activation(st.t[:, 1:2], st.t[:, 0:1], AF.Ln, scale=1.0 / D, bias=EPS), [st], [st])
            P.op("act", lambda e: e.activation(st.t[:, 2:3], st.t[:, 1:2], AF.Exp, scale=-0.5), [st], [st])

        def norm_B1(src_ap, src_toks, g_t, s_t, b, st):
            tf, h = tmpf, hb[b % 2]
            P.op("dve", lambda e: e.scalar_tensor_tensor(tf.t, src_ap, st.t[:, 2:3], g_t.t, op0=ALU.mult, op1=ALU.mult),
                 src_toks + [st, g_t], [tf])
            P.op("dve", lambda e: e.tensor_tensor(h.t, tf.t, s_t.t, ALU.add), [tf, s_t], [h])

        def norm_B2(b, pbk):
            h = hb[b % 2]
            pv = pst(pbk).bitcast(BF16).rearrange("p (k t) -> p k t", k=8)
            for kc in range(8):
                P.op("pe", lambda e, kc=kc: e.transpose(pv[:, kc, :], h.t[:, kc * 128:(kc + 1) * 128], identB.t),
                     [h, identB], PQ(pbk, 0, 512))
            copy_op("act", hT.t[:, :, b * 128:(b + 1) * 128], pv, PQ(pbk, 0, 512), [hT_tok[b]])

        for b in range(NB + 2):
            if b < NB:
                xt = xring[b % 3]
                P.dma("sp", xt.t, x_d[b * 128:(b + 1) * 128, :], [], [xt])
                norm_A(xt.t, [xt], stt[b % 3])
            if 1 <= b <= NB:
                xp = xring[(b - 1) % 3]
                norm_B1(xp.t, [xp], MB[1], MB[0], b - 1, stt[(b - 1) % 3])
            if b >= 2:
                norm_B2(b - 2, 2 + ((b - 2) % 2))
        if debug:
            P.dma("sp", dbg["h1T"], hT.t.rearrange("p k t -> p (k t)"), hT_tok, [], is_out=True)
            for j in range(2):
                P.dma("sp", dbg["mod"][:, j * D:(j + 1) * D], MB[j].t, [MB[j]], [], is_out=True)

        P.barrier()
        AR.reset(ph1_mark)
        BS = 64
        NBK = T // BS
        NG = T // 128
        BPS = TS // BS
        NSPAN = T // TS
        modT = AR.alloc("modT", [4, 8], F32, top=True)
        badaT = AR.alloc("badaT", [48], F32, top=True)
        scb = AR.alloc("scb", [8], BF16, top=True)
        top_keep = AR.end
        mergedT = AR.alloc("mergedT", [8, T], BF16, top=True)
        mT_tok = [Tok("mT_h%d" % i) for i in range(8)]
        ring["slots"] = [AR.alloc("wringB%d" % i, [8 * 640], BF16) for i in range(2)]
        QT = AR.alloc("QT", [T], F32)
        KT = AR.alloc("KT", [T], F32)
        QTt = [AR.alloc("QTt%d" % d, [T], BF16) for d in range(2)]
        KTt = [AR.alloc("KTt%d" % d, [T], BF16) for d in range(2)]
        KTM = [AR.alloc("KTM%d" % d, [NG, 128], BF16) for d in range(2)]
        LH = AR.alloc("LH", [2, NBK, 2], F32)
        EX = AR.alloc("EX", [2, NBK, 2], F32)
        SCL = AR.alloc("SCL", [2, NBK, 2], F32)
        Vt = [AR.alloc("Vt%d" % pb, [NG, 128], BF16) for pb in range(2)]
        GG = [AR.alloc("GG%d" % pb, [NG, 128], BF16) for pb in range(2)]
        NSET = 4
        SETS = [dict(X=TT(KT.t[:, i * TS:(i + 1) * TS], Tok("Xs%d" % i)), U=AR.alloc("Us%d" % i, [TS], F32), T2=AR.alloc("T2s%d" % i, [TS], F32),
                     B=AR.alloc("Bs%d" % i, [TS], F32), KH=AR.alloc("KHt%d" % i, [TS], BF16)) for i in range(NSET)]
        SQt = [AR.alloc("SQt%d" % d, [NBK, 128], BF16) for d in range(2)]
        ST = [[AR.alloc("ST%d_%d" % (d, i), [128], F32) for i in range(3)] for d in range(2)]
        ATt = [AR.alloc("AT%d" % i, [256], BF16) for i in range(4)]
        MTM = [AR.alloc("MTM%d" % i, [128], BF16) for i in range(2)]
        sto = [AR.alloc("sto%d" % i, [4], F32) for i in range(4)]
        junk2 = AR.alloc("junk2", [128], BF16)
        LRT = AR.alloc("LRT", [2, T], BF16)
        WLR = AR.alloc("WLR", [8, 32], BF16)
        WGU = AR.alloc("WGU", [2, 256], BF16)

        def bcol(Bap, parts, col, nblk):
            pstep = Bap.ap[0][0]
            return bass.AP(Bap.tensor, Bap.offset + col, [[pstep, parts], [BS, nblk], [0, BS]])

        P.dma("pool", WLR.t, win_d[:, OFF_LR:OFF_LR + 32].rearrange("(k p) n -> p k n", p=128), [], [WLR])
        P.dma("pool", WGU.t[0:16], wgu_d.rearrange("d r k -> r d k"), [], [WGU])
        P.dma("sp", badaT.t, badaT_d, [], [badaT])
        P.op("dve", lambda e: e.tensor_copy(scb.t, SC), [smalls], [scb])
        fmb = {"n": 0}

        def fm_bank():
            fmb["n"] += 1
            return fmb["n"] % 2

        def head_cfg(hh):
            gla = hh < 4
            h = hh if gla else hh - 4
            if gla:
                p = h // 2
                owner = (h % 2 == 0)
                if owner:
                    groups = [(OFF_QA + 128 * p, 128, 0), (OFF_KA + 128 * p, 128, 128), (OFF_VA + 128 * h, 128, 256), (OFF_GA + 128 * h, 128, 384)]
                    c_vg = 256
                else:
                    groups = [(OFF_VA + 128 * h, 128, 0), (OFF_GA + 128 * h, 128, 128)]
                    c_vg = 0
                return dict(gla=True, h=h, p=p, owner=owner, dk=64, po=64 * (h % 2), dscale=-1.0 / 16.0, groups=groups, c_q=0, c_k=128, c_vg=c_vg, c_f=None)
            groups = [(OFF_QB + 128 * h, 128, 0), (OFF_FB + 128 * h, 128, 128), (OFF_FB + 512 + 128 * h, 128, 256),
                      (OFF_IB + 128 * h, 128, 384), (OFF_GB + 128 * h, 128, 512)]
            return dict(gla=False, h=h, p=0, owner=True, dk=128, po=0, dscale=1.0, groups=groups, c_q=0, c_k=None, c_vg=384, c_f=(128, 256))

        head_w = {}

        def load_head_weights(hh):
            cfg = head_cfg(hh)
            w = ring["slots"][hh % 2]
            wv = w.t[:, 0:8 * 640].rearrange("p (k n) -> p k n", k=8)
            for (c0, n, o) in cfg["groups"]:
                P.dma("pool", wv[:, :, o:o + n], win_d[:, c0:c0 + n].rearrange("(k p) n -> p k n", p=128), [], [w])
            head_w[hh] = (w, wv)

        def mod_computeT(j, w):
            b = fm_bank()
            for n in range(4):
                c0 = j * D + n * 256
                half = n % 2
                wv = w.t[:, half * 2048:(half + 1) * 2048].rearrange("p (k n) -> p k n", k=8)
                P.dma("pool", wv, wada_d[:, c0:c0 + 256].rearrange("(k p) n -> p k n", p=128), [], [w])
                for mb in range(2):
                    for kc in range(8):
                        P.op("pe", lambda e, kc=kc, mb=mb, n=n, wv=wv: e.matmul(pst(b)[:, 2 * n + mb:2 * n + mb + 1], wv[:, kc, mb * 128:(mb + 1) * 128], scb.t[:, kc:kc + 1],
                                                                              start=(kc == 0), stop=(kc == 7)), [w, scb], PQ(b, 0, 512))
                yield
            copy_op("dve", modT.t[:, j - 2, :], pst(b)[:, 0:8], PQ(b, 0, 512), [modT])
            P.op("dve", lambda e: e.tensor_tensor(modT.t[:, j - 2, :], modT.t[:, j - 2, :], badaT.t[:, j * 8:(j + 1) * 8], ALU.add), [modT, badaT], [modT])

        def stageAB1(hh):
            cfg = head_cfg(hh)
            gla, h, dscale, owner = cfg["gla"], cfg["h"], cfg["dscale"], cfg["owner"]
            dk = 128
            c_q, c_k, c_vg, c_f = cfg["c_q"], cfg["c_k"], cfg["c_vg"], cfg["c_f"]
            pb = hh % 2
            w, wv = head_w[hh]
            if hh + 1 < 8:
                load_head_weights(hh + 1)
            myVt, myGG = Vt[pb], GG[pb]
            Us = SETS[0]["U"]

            def fm_proj(c0, M, tiles, dst_ap_fn, dst_toks, scale=None):
                for tt in tiles:
                    b = fm_bank()
                    for kc in range(8):
                        P.op("pe", lambda e, kc=kc, b=b, tt=tt: e.matmul(pst(b)[0:M, :], wv[:, kc, c0:c0 + M], hT.t[:, kc, tt * 512:(tt + 1) * 512],
                                                                       start=(kc == 0), stop=(kc == 7)),
                             [w] + hT_tok[4 * tt:4 * tt + 4], PQ(b, 0, 512))
                    yield
                    copy_op(evac_eng(), dst_ap_fn(tt), pst(b)[0:M, :], PQ(b, 0, 512), dst_toks, scale=scale)

            if owner:
                yield from fm_proj(c_q, dk, range(4), lambda tt: QT.t[0:dk, tt * 512:(tt + 1) * 512], [QT], scale=(0.125 if gla else None))
            if gla and owner:
                yield from fm_proj(c_k, dk, range(4), lambda tt: KT.t[0:dk, tt * 512:(tt + 1) * 512], [KT])
            if hh == 0:
                for d in range(2):
                    for tt in range(4):
                        b = fm_bank()
                        for kc in range(8):
                            P.op("pe", lambda e, kc=kc, b=b, tt=tt, d=d: e.matmul(pst(b)[0:16, :], WLR.t[:, kc, 16 * d:16 * d + 16], hT.t[:, kc, tt * 512:(tt + 1) * 512],
                                                                                  start=(kc == 0), stop=(kc == 7)),
                                 [WLR] + hT_tok[4 * tt:4 * tt + 4], PQ(b, 0, 512))
                        copy_op(evac_eng(), LRT.t[0:16, d, tt * 512:(tt + 1) * 512], pst(b)[0:16, :], PQ(b, 0, 512), [LRT])
                        yield
            for bp in range(NG // 2):
                bk = fm_bank()
                for i in range(2):
                    blk = 2 * bp + i
                    for kc in range(8):
                        P.op("pe", lambda e, kc=kc, bk=bk, i=i, blk=blk: e.matmul(pst(bk)[:, i * 256:(i + 1) * 256], hT.t[:, kc, blk * 128:(blk + 1) * 128],
                                                                                wv[:, kc, c_vg:c_vg + 256], start=(kc == 0), stop=(kc == 7)),
                             [w, hT_tok[blk]], PQ(bk, i * 256, (i + 1) * 256))
                pv = pst(bk).rearrange("p (b c) -> p b c", b=2)
                yield
                ce = evac_eng()
                copy_op(ce, myVt.t[:, 2 * bp:2 * bp + 2, :], pv[:, :, 0:128], PQ(bk, 0, 512), [myVt])
                copy_op(ce, myGG.t[:, 2 * bp:2 * bp + 2, :], pv[:, :, 128:256], PQ(bk, 0, 512), [myGG])
            GPS = TS // 128
            for s_ in range(NSPAN):
                gsp = myGG.t[:, s_ * GPS:(s_ + 1) * GPS, :].rearrange("p b c -> p (b c)")
                P.op("act", lambda e, gsp=gsp: e.activation(Us.t, gsp, AF.Exp, scale=-1.0), [myGG], [Us])
                P.op("act", lambda e: e.activation(Us.t, Us.t, AF.Ln, bias=1.0), [Us], [Us])
                P.op("act", lambda e: e.activation(Us.t, Us.t, AF.Exp, scale=-1.0), [Us], [Us])
                P.op("dve", lambda e, gsp=gsp: e.tensor_tensor(gsp, gsp, Us.t, ALU.mult), [myGG, Us], [myGG])
                yield

        def stageAB2(hh):
            cfg = head_cfg(hh)
            gla, h, dscale, owner, pr = cfg["gla"], cfg["h"], cfg["dscale"], cfg["owner"], cfg["p"]
            dk = 128
            c_f = cfg["c_f"]
            w, wv = head_w[hh]
            myQTt, myKTt, myKTM, myLH, myEX, mySCL = QTt, KTt, KTM, LH, EX, SCL
            if not owner:
                if hh < 4:
                    yield from mod_computeT(2 + hh, w)
                return

            def fm_proj(c0, M, tiles, dst_ap_fn, dst_toks, scale=None):
                for tt in tiles:
                    b = fm_bank()
                    for kc in range(8):
                        P.op("pe", lambda e, kc=kc, b=b, tt=tt: e.matmul(pst(b)[0:M, :], wv[:, kc, c0:c0 + M], hT.t[:, kc, tt * 512:(tt + 1) * 512],
                                                                       start=(kc == 0), stop=(kc == 7)),
                             [w] + hT_tok[4 * tt:4 * tt + 4], PQ(b, 0, 512))
                    copy_op("act", dst_ap_fn(tt), pst(b)[0:M, :], PQ(b, 0, 512), dst_toks, scale=scale)
                    yield

            v3 = lambda ap: ap.rearrange("p (b t) -> p b t", b=BPS)

            def decay(d, s_, tiles, sp0, sp1):
                st_ = SETS[(s_ % 2) * 2 + d]
                Xs, Us, T2s, Bs, KHt = st_["X"], st_["U"], st_["T2"], st_["B"], st_["KH"]
                X_, U_, T2_, B_, KH_ = Xs.t[0:dk], Us.t[0:dk], T2s.t[0:dk], Bs.t[0:dk], KHt.t[0:dk]
                col = (d * 2 + pr) if gla else (d * 4 + h)
                sel = d
                if gla:
                    for ti, tt in enumerate(tiles):
                        b = fm_bank()
                        P.op("pe", lambda e, b=b, tt=tt: e.matmul(pst(b)[0:128, :], WGU.t[0:16, d, 128 * pr:128 * pr + 128], LRT.t[0:16, d, tt * 512:(tt + 1) * 512],
                                                                start=True, stop=True), [WGU, LRT], PQ(b, 0, 512))
                        P.op("act", lambda e, b=b, ti=ti: e.activation(U_[:, ti * 512:(ti + 1) * 512], pst(b)[0:128, :], AF.Exp, scale=-1.0,
                                                                      bias=NEGB[:, col:col + 1]), PQ(b, 0, 512) + [smalls], [Us])
                    yield
                    P.op("act", lambda e: e.activation(T2_, U_, AF.Ln, bias=1.0), [Us], [T2s])
                    P.op("dve", lambda e: e.tensor_tensor_scan(B_, scanmask.t[0:dk, :], T2_, 0.0, ALU.mult, ALU.add), [scanmask, T2s], [Bs])
                else:
                    cf = c_f[d]
                    yield from fm_proj(cf, 128, tiles, lambda tt: X_[:, (tt - tiles[0]) * 512:(tt - tiles[0] + 1) * 512], [Xs])
                    P.op("act", lambda e: e.activation(U_, X_, AF.Exp, scale=-1.0), [Xs], [Us])
                    P.op("act", lambda e: e.activation(T2_, U_, AF.Ln, bias=1.0), [Us], [T2s])
                    P.op("act", lambda e: e.activation(U_, U_, AF.Ln, bias=1.0, scale=LB[:, col:col + 1]), [Us, smalls], [Us])
                    yield
                    P.op("dve", lambda e: e.tensor_tensor(U_, U_, T2_, ALU.subtract), [Us, T2s], [Us])
                    P.op("act", lambda e: e.activation(X_, X_, AF.Exp), [Xs], [Xs])
                    P.op("act", lambda e: e.activation(X_, X_, AF.Ln, bias=1.0), [Xs], [Xs])
                    P.op("dve", lambda e: e.tensor_tensor_scan(B_, scanmask.t[0:dk, :], U_, 0.0, ALU.mult, ALU.add), [scanmask, Us], [Bs])
                yield
                B3 = v3(B_)
                MID = BS // 2 - 1
                lh = myLH.t[0:dk, d, s_ * BPS:(s_ + 1) * BPS, :]
                ex = myEX.t[0:dk, d, s_ * BPS:(s_ + 1) * BPS, :]
                P.op("dve", lambda e: e.tensor_copy(lh[:, :, 0:1], B3[:, :, MID:MID + 1]), [Bs], [myLH])
                P.op("dve", lambda e: e.tensor_tensor(lh[:, :, 1:2], B3[:, :, BS - 1:BS], B3[:, :, MID:MID + 1], ALU.subtract), [Bs], [myLH])
                P.op("act", lambda e: e.activation(ex, lh, AF.Exp, scale=dscale), [myLH], [myEX])
                bmid = bcol(B_, dk, MID, BPS)
                ehb = bass.AP(myEX.t.tensor, myEX.t[0:dk, d, s_ * BPS:(s_ + 1) * BPS, 1 - sel].offset,
                              [[myEX.t.ap[0][0], dk], [2, BPS], [0, BS]])
                if gla:
                    if d == 0:
                        P.op("dve", lambda e: e.tensor_tensor(v3(U_), B3, bmid, ALU.subtract), [Bs], [Us])
                    else:
                        P.op("dve", lambda e: e.tensor_tensor(T2_, B_, T2_, ALU.subtract), [Bs, T2s], [T2s])
                        P.op("dve", lambda e: e.tensor_tensor(v3(U_), bmid, v3(T2_), ALU.subtract), [Bs, T2s], [Us])
                    P.op("dve", lambda e: e.tensor_scalar(U_, U_, 640.0, -640.0, op0=ALU.min, op1=ALU.max), [Us], [Us])
                    P.op("act", lambda e: e.activation(T2_, U_, AF.Exp, scale=dscale), [Us], [T2s])
                    P.op("pool", lambda e: e.tensor_tensor(myQTt[d].t[0:dk, sp0:sp1], QT.t[0:dk, sp0:sp1], T2_, ALU.mult), [QT, T2s], [myQTt[d]])
                    yield
                    P.op("act", lambda e: e.activation(T2_, U_, AF.Exp, scale=-dscale), [Us], [T2s])
                    P.op("dve", lambda e: e.tensor_tensor(myKTt[d].t[0:dk, sp0:sp1], KT.t[0:dk, sp0:sp1], T2_, ALU.mult), [KT, T2s], [myKTt[d]])
                else:
                    if d == 0:
                        P.op("dve", lambda e: e.tensor_tensor(v3(T2_), B3, bmid, ALU.subtract), [Bs], [T2s])
                    else:
                        P.op("dve", lambda e: e.tensor_tensor(U_, B_, U_, ALU.subtract), [Bs, Us], [Us])
                        P.op("dve", lambda e: e.tensor_tensor(v3(T2_), bmid, v3(U_), ALU.subtract), [Bs, Us], [T2s])
                    P.op("dve", lambda e: e.tensor_scalar(T2_, T2_, 40.0, -40.0, op0=ALU.min, op1=ALU.max), [T2s], [T2s])
                    P.op("act", lambda e: e.activation(U_, T2_, AF.Exp), [T2s], [Us])
                    P.op("pool", lambda e: e.tensor_tensor(myQTt[d].t[0:dk, sp0:sp1], QT.t[0:dk, sp0:sp1], U_, ALU.mult), [QT, Us], [myQTt[d]])
                    yield
                    P.op("dve", lambda e: e.tensor_tensor(X_, X_, T2_, ALU.add), [Xs, T2s], [Xs])
                    P.op("act", lambda e: e.activation(myKTt[d].t[0:dk, sp0:sp1], X_, AF.Exp, scale=-1.0, bias=L1M[:, col:col + 1]), [Xs, smalls], [myKTt[d]])
                yield
                P.op("dve", lambda e: e.tensor_tensor(v3(KH_), v3(myKTt[d].t[0:dk, sp0:sp1]), ehb, ALU.mult), [myKTt[d], myEX], [KHt])
                kb = fm_bank()
                pvk = pst(kb).bitcast(BF16).rearrange("p (b t) -> p b t", b=8)
                ng = TS // 128
                for i in range(ng):
                    P.op("pe", lambda e, i=i: e.transpose(pvk[:, i, 0:dk], KH_[:, i * 128:(i + 1) * 128], identB.t[0:dk, 0:dk]), [KHt, identB], PQ(kb, 0, 512))
                copy_op(evac_eng(), myKTM[d].t[:, s_ * ng:(s_ + 1) * ng, 0:dk], pvk[:, 0:ng, 0:dk], PQ(kb, 0, 512), [myKTM[d]])
                yield

            for s0_ in range(0, NSPAN, 2):
                gens = []
                for s_ in (s0_, s0_ + 1):
                    tiles = list(range(s_ * TS // 512, (s_ + 1) * TS // 512))
                    for d in range(2):
                        gens.append(decay(d, s_, tiles, s_ * TS, (s_ + 1) * TS))
                alive = [True] * len(gens)
                while any(alive):
                    for gi in range(len(gens)):
                        if alive[gi]:
                            try:
                                next(gens[gi])
                            except StopIteration:
                                alive[gi] = False
                    yield
            for d in range(2):
                sel = d
                mc = mchain.t[0:dk, d * NBK:(d + 1) * NBK]
                P.op("dve", lambda e, d=d, sel=sel, mc=mc: e.tensor_tensor(mySCL.t[0:dk, d, :, 0], myEX.t[0:dk, d, :, sel], mc, ALU.mult), [myEX, mchain], [mySCL])
                P.op("dve", lambda e, d=d, sel=sel: e.tensor_tensor(mySCL.t[0:dk, d, :, 1], mySCL.t[0:dk, d, :, 0], myEX.t[0:dk, d, :, 1 - sel], ALU.mult), [myEX, mySCL], [mySCL])
            yield
            if hh < 4:
                yield from mod_computeT(2 + hh, w)

        def stageC(hh):
            cfg = head_cfg(hh)
            gla, h, dk, po = cfg["gla"], cfg["h"], cfg["dk"], cfg["po"]
            pq = slice(po, po + dk)
            pb = hh % 2
            myQTt, myKTt, myKTM, myVt, myGG, mySCL = QTt, KTt, KTM, Vt[pb], GG[pb], SCL
            kmt = {"n": 0}
            pv7 = pst(7).bitcast(BF16).rearrange("p (r s t) -> p r s t", r=2, s=4)
            grp_done = {}
            for d in range(2):
                src = (sig_d if gla else sih_d)[d, h]
                P.dma("sp", ST[d][0].t[pq], src, [], [ST[d][0]])
            state = {0: ST[0][0], 1: ST[1][0]}
            nxt = [1, 1]

            def chain_p(d, n, cb, after=()):
                g, hf = n // 2, n % 2
                return P.op("pe", lambda e: e.matmul(pst(cb)[pq, d * 128:(d + 1) * 128], myKTM[d].t[hf * 64:(hf + 1) * 64, g, pq], myVt.t[hf * 64:(hf + 1) * 64, g, :], start=True, stop=True),
                            [myKTM[d], myVt], PQ(cb, 0, 256), extra=after)

            def chain_step(d, n, cb):
                prev = state[d]
                new = ST[d][nxt[d]]
                nxt[d] = (nxt[d] + 1) % int(_os.environ.get('DBG_TRI', '3'))
                P.op("act", lambda e: e.activation(SQt[d].t[pq, n, :], prev.t[pq], AF.Copy, scale=mySCL.t[pq, d, n, 0:1]), [prev, mySCL], [SQt[d]])
                P.op("dve", lambda e: e.scalar_tensor_tensor(new.t[pq], prev.t[pq], mySCL.t[pq, d, n, 1:2], pst(cb)[pq, d * 128:(d + 1) * 128], op0=ALU.mult, op1=ALU.add),
                     [prev, mySCL] + PQ(cb, 0, 256), [new])
                state[d] = new
                if (d == 0 and n % 4 == 3) or (d == 1 and n % 4 == 0):
                    dst = (sog_d if gla else soh_d)[n // 4, d, h]
                    P.dma("sp", dst, new.t[pq], [new], [], is_out=True)

            gctr = {"n": 0}
            ginfo = {}

            def og_a1(g):
                i = gctr["n"]
                gctr["n"] += 1
                ginfo[g] = i
                blk = slice(g * 128, (g + 1) * 128)
                for d in range(2):
                    P.op("pe", lambda e, d=d: e.matmul(pst(5)[:, d * 128:(d + 1) * 128], myKTt[d].t[pq, blk], myQTt[d].t[pq, blk], start=True, stop=True),
                         [myKTt[d], myQTt[d]], PQ(5, 0, 256))

            def og_a2(g):
                at = ATt[ginfo[g] % 4]
                P.op("dve", lambda e: e.tensor_tensor(at.t, pst(5)[:, 0:256], maskT2.t, ALU.mult), PQ(5, 0, 256) + [maskT2], [at])

            def og_b(g):
                i = ginfo[g]
                at = ATt[i % 4]
                ob_ = [2, 6][i % 2]
                og = pst(ob_)[:, 0:128]
                otok = PQ(ob_, 0, 128)
                P.op("pe", lambda e: e.matmul(og, at.t[:, 0:128], myVt.t[:, g, :], start=True, stop=False), [at, myVt], otok)
                P.op("pe", lambda e: e.matmul(og, at.t[:, 128:256], myVt.t[:, g, :], start=False, stop=False), [at, myVt], otok)
                for hf in range(2):
                    for d in range(2):
                        last = (hf == 1 and d == 1)
                        c0 = g * 128 + hf * 64
                        P.op("pe", lambda e, hf=hf, d=d, last=last, c0=c0: e.matmul(pst(ob_)[hf * 64:(hf + 1) * 64, 0:128], myQTt[d].t[pq, c0:c0 + 64], SQt[d].t[pq, 2 * g + hf, :],
                                                                                   start=False, stop=last), [myQTt[d], SQt[d]], otok)

            def og_c(g):
                i = ginfo[g]
                ob_ = [2, 6][i % 2]
                og = pst(ob_)[:, 0:128]
                otok = PQ(ob_, 0, 128)
                so = sto[i % 4]
                P.op("act", lambda e: e.activation(junk2.t, og, AF.Square, accum_out=so.t[:, 0:1]), otok, [junk2, so], multi=True)
                P.op("act", lambda e: e.activation(so.t[:, 1:2], so.t[:, 0:1], AF.Ln, bias=128.0 * EPS), [so], [so])
                P.op("act", lambda e: e.activation(so.t[:, 2:3], so.t[:, 1:2], AF.Exp, scale=-0.5), [so], [so])

            def og_d(g):
                i = ginfo[g]
                ob_ = [2, 6][i % 2]
                og = pst(ob_)[:, 0:128]
                otok = PQ(ob_, 0, 128)
                so = sto[i % 4]
                mt = MTM[i % 2]
                P.op("dve", lambda e: e.scalar_tensor_tensor(mt.t, og, so.t[:, 2:3], myGG.t[:, g, :], op0=ALU.mult, op1=ALU.mult), otok + [so, myGG], [mt])
                grp = g // 4
                r = grp % 2
                rtok = PQ(7, r * 256, (r + 1) * 256)
                P.op("pe", lambda e: e.transpose(pv7[:, r, g % 4, :], mt.t, identB.t), [mt, identB], rtok)
                grp_done[grp] = grp_done.get(grp, 0) + 1
                if grp_done[grp] == 4:
                    copy_op(evac_eng(), mergedT.t[:, hh, grp * 512:(grp + 1) * 512], pv7[:, r].rearrange("p s t -> p (s t)"), rtok, [mT_tok[hh]])

            ready = {g: max(2 * g + 1, NBK - 1 - 2 * g) for g in range(NG)}
            p_ahead = int(_os.environ.get('DBG_PAHEAD', '1'))
            if p_ahead:
                o_ = chain_p(0, 0, 3)
                chain_p(1, NBK - 1, 3, after=(o_,))
            for s_ in range(NBK + 4):
                if s_ < NBK:
                    cb = 3 + (s_ % 2)
                    if not p_ahead:
                        o_ = chain_p(0, s_, cb)
                        chain_p(1, NBK - 1 - s_, cb, after=(o_,))
                    chain_step(0, s_, cb)
                    chain_step(1, NBK - 1 - s_, cb)
                    if p_ahead and s_ + 1 < NBK:
                        cbn = 3 + ((s_ + 1) % 2)
                        o_ = chain_p(0, s_ + 1, cbn)
                        chain_p(1, NBK - 2 - s_, cbn, after=(o_,))
                for g in range(NG):
                    if ready[g] == s_ - 3:
                        og_d(g)
                for g in range(NG):
                    if ready[g] == s_ - 2:
                        og_c(g)
                for g in range(NG):
                    if ready[g] == s_ - 1:
                        og_b(g)
                pair_now = [g for g in range(NG) if ready[g] == s_ + 3]
                pair_prev = [g for g in range(NG) if ready[g] == s_ + 2]
                pair_prev2 = [g for g in range(NG) if ready[g] == s_ + 1]
                if pair_prev2:
                    og_a2(pair_prev2[1])
                if pair_prev:
                    og_a2(pair_prev[0])
                    og_a1(pair_prev[1])
                if pair_now:
                    og_a1(pair_now[0])
                yield

        heads = [int(v) for v in _os.environ.get('DBG_HEADS', '0,1,2,3,4,5,6,7').split(',') if v != '']
        assert heads == list(range(8))
        load_head_weights(0)
        for _ in stageAB1(0):
            pass
        for _ in stageAB2(0):
            pass
        ilv = int(_os.environ.get('DBG_ILV', '2'))
        for hh in range(8):
            cgen = stageC(hh)
            abgen = stageAB1(hh + 1) if hh + 1 < 8 else None
            c_alive, ab_alive = True, abgen is not None
            step = 0
            while c_alive or ab_alive:
                if c_alive:
                    try:
                        next(cgen)
                    except StopIteration:
                        c_alive = False
                if ab_alive and (not c_alive or (ilv >= 1 and step % ilv == 0)):
                    try:
                        next(abgen)
                    except StopIteration:
                        ab_alive = False
                step += 1
            if hh + 1 < 8:
                for _ in stageAB2(hh + 1):
                    pass
        if debug:
            P.dma("sp", dbg["mergedT"], mergedT.t.rearrange("p k t -> p (k t)"), mT_tok, [], is_out=True)
        P.barrier()
        AR.reset(ph1_mark)
        X1 = AR.alloc("X1", [NB, D], F32)
        X1_tok = [Tok("X1_b%d" % b) for b in range(NB)]
        ph3_keep = AR.mark()
        MB[2] = AR.alloc("gate1_bc", [D], F32)
        MB[3] = AR.alloc("shift2_bc", [D], F32)
        MB[4] = AR.alloc("g2_bc", [D], F32)
        WO = AR.alloc("WO", [8, D], BF16)
        wstage = [AR.alloc("wstage%d" % i, [D], F32) for i in range(2)]
        junk = AR.alloc("junk", [D], BF16)
        tmpf = AR.alloc("tmpf", [D], F32)
        hb = [AR.alloc("hb%d" % i, [D], BF16) for i in range(2)]
        stt = [AR.alloc("stt%d" % i, [4], F32) for i in range(3)]
        dgt = [AR.alloc("dgt%d" % i, [128], F32) for i in range(2)]
        n2T = AR.alloc("n2T", [8], F32)
        g2T = AR.alloc("g2T", [8], F32)
        xb = {"n": 0}

        def expand(vec_ap, vec_toks, dst):
            for half in range(2):
                b = xb["n"] % 2
                xb["n"] += 1
                for q in range(4):
                    kc = half * 4 + q
                    dg = dgt[kc % 2]
                    P.op("dve", lambda e, kc=kc, dg=dg: e.tensor_scalar(dg.t, identF.t, vec_ap[:, kc:kc + 1], None, op0=ALU.mult), [identF] + vec_toks, [dg])
                    P.op("pe", lambda e, q=q, b=b, dg=dg: e.matmul(pst(b)[:, q * 128:(q + 1) * 128], onesF.t, dg.t, start=True, stop=True), [onesF, dg], PQ(b, 0, 512))
                copy_op(evac_eng(), dst.t[:, half * 512:(half + 1) * 512], pst(b)[:, :], PQ(b, 0, 512), [dst])

        P.dma("sp", n2T.t, norm2T_d, [], [n2T])
        P.op("dve", lambda e: e.scalar_tensor_tensor(g2T.t, modT.t[:, 2, :], 1.0, n2T.t, op0=ALU.add, op1=ALU.mult), [modT, n2T], [g2T])
        expand(modT.t[:, 0, :], [modT], MB[2])
        expand(modT.t[:, 1, :], [modT], MB[3])
        expand(g2T.t, [g2T], MB[4])
        for kc in range(8):
            ws = wstage[kc % 2]
            P.dma("sp", ws.t, wout_d[kc * 128:(kc + 1) * 128, :], [], [ws])
            gcol = 0 if kc < 4 else 1
            P.op("dve", lambda e, kc=kc, ws=ws, gcol=gcol: e.scalar_tensor_tensor(WO.t[:, kc, :], ws.t, GS[:, gcol:gcol + 1], MB[2].t, op0=ALU.mult, op1=ALU.mult),
                 [ws, smalls, MB[2]], [WO])
        for b in range(NB):
            P.dma("sp", X1.t[:, b, :], x_d[b * 128:(b + 1) * 128, :], [], [X1_tok[b]])
        ob = {"n": 0}
        for b in range(NB):
            for half in range(2):
                bk = 2 + ob["n"] % 2
                ob["n"] += 1
                for kc in range(8):
                    P.op("pe", lambda e, kc=kc, bk=bk, b=b, half=half: e.matmul(pst(bk)[:, :], mergedT.t[:, kc, b * 128:(b + 1) * 128], WO.t[:, kc, half * 512:(half + 1) * 512],
                                                                              start=(kc == 0), stop=(kc == 7)), [mT_tok[kc], WO], PQ(bk, 0, 512))
                hs = slice(half * 512, (half + 1) * 512)
                P.op("dve", lambda e, bk=bk, b=b, hs=hs: e.tensor_tensor(X1.t[:, b, hs], pst(bk)[:, :], X1.t[:, b, hs], ALU.add), PQ(bk, 0, 512) + [X1_tok[b]], [X1_tok[b]])
            norm_A(X1.t[:, b, :], [X1_tok[b]], stt[b % 3])
            if b >= 1:
                norm_B1(X1.t[:, b - 1, :], [X1_tok[b - 1]], MB[4], MB[3], b - 1, stt[(b - 1) % 3])
            if b >= 2:
                norm_B2(b - 2, 4 + ((b - 2) % 2))
            if debug:
                P.dma("sp", dbg["x1"][b * 128:(b + 1) * 128, :], X1.t[:, b, :], [X1_tok[b]], [], is_out=True)
        norm_B1(X1.t[:, NB - 1, :], [X1_tok[NB - 1]], MB[4], MB[3], NB - 1, stt[(NB - 1) % 3])
        norm_B2(NB - 2, 4 + ((NB - 2) % 2))
        norm_B2(NB - 1, 4 + ((NB - 1) % 2))
        if debug:
            dump("h2T", TT(hT.t, hT_tok[0]), 8 * T, BF16)

        P.barrier()
        AR.reset(ph3_keep)
        AR.end = top_keep
        MB[5] = AR.alloc("gate2_bc", [D], F32)
        FN = AR.alloc("fnorm_bc", [D], F32)
        dgt = [AR.alloc("dgt%d" % i, [128], F32) for i in range(2)]
        GMAX = max(j1 - j0 for j0, j1 in GROUPS)
        HT = AR.alloc("HT", [GMAX, T], BF16)
        WD = AR.alloc("WD", [GMAX, D], BF16)
        wdst = [AR.alloc("wdst%d" % i, [D], F32) for i in range(2)]
        UB = [AR.alloc("UB%d" % i, [UW], BF16) for i in range(2)]
        SG = AR.alloc("SG", [T], F32)
        DG = [AR.alloc("DG%d" % i, [11, 128], BF16) for i in range(2)]
        w11T = AR.alloc("w11T", [NCH * 11], F32)
        bconvT = AR.alloc("bconvT", [NCH], F32)
        ring["slots"] = [AR.alloc("wringC%d" % i, [8 * 256], BF16) for i in range(3)]
        ring["n"] = 0
        yst = [AR.alloc("yst%d" % i, [D], F32) for i in range(2)]
        DACC = AR.alloc("DACC", [T], BF16)
        ctmp = yst
        junk = AR.alloc("junk", [D], BF16)
        stt = [AR.alloc("stt%d" % i, [4], F32) for i in range(3)]
        expand(modT.t[:, 3, :], [modT], MB[5])
        P.dma("sp", FN.t, fnorm_d.partition_broadcast(128), [], [FN])
        P.dma("sp", w11T.t, w11T_d, [], [w11T])
        P.dma("sp", bconvT.t, bconvT_d, [], [bconvT])
        for u in range(2):
            P.op("pool", lambda e, u=u: e.memset(UB[u].t, 0.0), [], [UB[u]])
        P.op("pool", lambda e: e.memset(DACC.t, 0.0), [], [DACC])
        identB_b11 = bass.AP(identB.t.tensor, identB.t.offset, [list(identB.t.ap[0]), [0, 11], [1, 128]])
        ucnt = {"n": 0}
        pair_w = {}

        def load_pair(j):
            w = next_w()
            wv = w.t[:, 0:8 * 256].rearrange("p (k n) -> p k n", k=8)
            P.dma("pool", wv[:, :, 0:128], wup_d[:, j * 128:(j + 1) * 128].rearrange("(k p) n -> p k n", p=128), [], [w])
            P.dma("pool", wv[:, :, 128:256], wup_d[:, FFN_H + j * 128:FFN_H + (j + 1) * 128].rearrange("(k p) n -> p k n", p=128), [], [w])
            pair_w[j] = (w, wv)

        def up_proj(j, is_up):
            w, wv = pair_w[j]
            off = 128 if is_up else 0
            u = ucnt["n"] % 2
            ucnt["n"] += 1
            for tt in range(4):
                for kc in range(8):
                    P.op("pe", lambda e, kc=kc, tt=tt: e.matmul(pst(tt)[:, :], wv[:, kc, off:off + 128], hT.t[:, kc, tt * 512:(tt + 1) * 512], start=(kc == 0), stop=(kc == 7)),
                         [w] + hT_tok[4 * tt:4 * tt + 4], PQ(tt, 0, 512))
                P.op("act", lambda e, tt=tt: e.activation(UB[u].t[:, UPAD + tt * 512:UPAD + (tt + 1) * 512], pst(tt)[:, :], AF.Copy), PQ(tt, 0, 512), [UB[u]])
            return u

        ctn = {"n": 0}

        def conv(j, jj, is_up, u):
            cc = (NPAIR + j) if is_up else j
            dg = DG[cc % 2]
            wb = bass.AP(w11T.t.tensor, w11T.t.offset + cc * 11, [list(w11T.t.ap[0]), [1, 11], [0, 128]])
            P.op("pool", lambda e: e.tensor_tensor(dg.t, identB_b11, wb, ALU.mult), [identB, w11T], [dg])
            ub = UB[u].t
            acc3 = DACC.t.rearrange("p (r c) -> p r c", c=64)[:, :, 0:63]
            for k_, dy in enumerate((0, -1, 1)):
                wi = (dy + 1) * 3 + 2
                src3 = ub[:, UPAD + 64 * dy + 1:UPAD + 64 * dy + 1 + T].rearrange("p (r c) -> p r c", c=64)[:, :, 0:63]
                wsc = w11T.t[:, cc * 11 + wi:cc * 11 + wi + 1]
                if k_ == 0:
                    P.op("dve", lambda e, src3=src3, wsc=wsc: e.tensor_scalar(acc3, src3, wsc, None, op0=ALU.mult), [UB[u], w11T], [DACC])
                else:
                    P.op("dve", lambda e, src3=src3, wsc=wsc: e.scalar_tensor_tensor(acc3, src3, wsc, acc3, op0=ALU.mult, op1=ALU.add), [UB[u], w11T, DACC], [DACC])
            for tt in range(4):
                base = UPAD + tt * 512
                pt = pst(4 + tt)
                taps = []
                for dy in (0, -1, 1):
                    taps.append(((dy + 1) * 3 + 1, pt[:, 0:512], ub[:, base + 64 * dy:base + 64 * dy + 512]))
                for dy in (0, -1, 1):
                    o3 = pt[:, 0:512].rearrange("p (r c) -> p r c", c=64)[:, :, 1:64]
                    r3 = ub[:, base + 64 * dy - 1:base + 64 * dy - 1 + 512].rearrange("p (r c) -> p r c", c=64)[:, :, 1:64]
                    taps.append(((dy + 1) * 3 + 0, o3, r3))
                o4 = pt[:, 0:512].rearrange("p (a r c) -> p a r c", a=2, r=4)[:, :, 1:4, 0]
                r4 = ub[:, base - 1:base - 1 + 512].rearrange("p (a r c) -> p a r c", a=2, r=4)[:, :, 1:4, 0]
                taps.append((9, o4, r4))
                o4 = pt[:, 0:512].rearrange("p (a r c) -> p a r c", a=2, r=4)[:, :, 0:3, 63]
                r4 = ub[:, base + 1:base + 1 + 512].rearrange("p (a r c) -> p a r c", a=2, r=4)[:, :, 0:3, 63]
                taps.append((10, o4, r4))
                for ti, (wi, oap, rap) in enumerate(taps):
                    P.op("pe", lambda e, wi=wi, oap=oap, rap=rap, ti=ti: e.matmul(oap, dg.t[:, wi, :], rap, start=(ti == 0), stop=(ti == len(taps) - 1)),
                         [dg, UB[u]], PQ(4 + tt, 0, 512))
                ts_ = slice(tt * 512, (tt + 1) * 512)
                ct = ctmp[ctn["n"] % 2]
                ctn["n"] += 1
                P.op("dve", lambda e, pt=pt, ts_=ts_, ct=ct: e.tensor_tensor(ct.t[:, 0:512], pt[:, 0:512], DACC.t[:, ts_], ALU.add), PQ(4 + tt, 0, 512) + [DACC], [ct])
                if not is_up:
                    P.op("act", lambda e, ts_=ts_, ct=ct: e.activation(SG.t[:, ts_], ct.t[:, 0:512], AF.Silu, bias=bconvT.t[:, cc:cc + 1]), [ct, bconvT], [SG])
                else:
                    P.op("dve", lambda e, ts_=ts_, ct=ct: e.scalar_tensor_tensor(HT.t[:, jj, ts_], ct.t[:, 0:512], bconvT.t[:, cc:cc + 1], SG.t[:, ts_], op0=ALU.add, op1=ALU.mult),
                         [ct, bconvT, SG], [HT])

        fin_pending = []

        def final_out(b):
            st = stt[b % 3]
            ys = yst[b % 2]
            xap = X1.t[:, b, :]
            P.op("dve", lambda e: e.scalar_tensor_tensor(ys.t, xap, st.t[:, 2:3], FN.t, op0=ALU.mult, op1=ALU.mult), [X1_tok[b], st, FN], [ys])
            P.dma("sp", y_d[b * 128:(b + 1) * 128, :], ys.t, [ys], [], is_out=True)

        load_pair(0)
        load_pair(1)
        dbk = {"n": 0}

        def wd_load(gi):
            j0, j1 = GROUPS[gi]
            for jj, j in enumerate(range(j0, j1)):
                wq = wdst[j % 2]
                P.dma("sp", wq.t, wdn_d[j * 128:(j + 1) * 128, :], [], [wq])
                P.op("dve", lambda e, jj=jj, wq=wq: e.tensor_tensor(WD.t[:, jj, :], wq.t, MB[5].t, ALU.mult), [wq, MB[5]], [WD])

        def down(gi):
            j0, j1 = GROUPS[gi]
            last_group = gi == len(GROUPS) - 1
            ng = j1 - j0
            for b in range(NB):
                for half in range(2):
                    bk = dbk["n"] % 4
                    dbk["n"] += 1
                    hs = slice(half * 512, (half + 1) * 512)
                    for jj in range(ng):
                        P.op("pe", lambda e, jj=jj, bk=bk, b=b, hs=hs: e.matmul(pst(bk)[:, :], HT.t[:, jj, b * 128:(b + 1) * 128], WD.t[:, jj, hs], start=(jj == 0), stop=(jj == ng - 1)),
                             [HT, WD], PQ(bk, 0, 512))
                    P.op("dve", lambda e, bk=bk, b=b, hs=hs: e.tensor_tensor(X1.t[:, b, hs], pst(bk)[:, :], X1.t[:, b, hs], ALU.add), PQ(bk, 0, 512) + [X1_tok[b]], [X1_tok[b]])
                if last_group:
                    st = stt[b % 3]
                    xap = X1.t[:, b, :]
                    P.op("act", lambda e, xap=xap, st=st, jk=junk: e.activation(jk.t, xap, AF.Square, accum_out=st.t[:, 0:1]), [X1_tok[b]], [junk, st], multi=True)
                    P.op("act", lambda e, st=st: e.activation(st.t[:, 1:2], st.t[:, 0:1], AF.Ln, scale=1.0 / D, bias=EPS), [st], [st])
                    P.op("act", lambda e, st=st: e.activation(st.t[:, 2:3], st.t[:, 1:2], AF.Exp, scale=-0.5), [st], [st])
                    fin_pending.append(b)
                    if len(fin_pending) > 1:
                        final_out(fin_pending.pop(0))
            if last_group:
                while fin_pending:
                    final_out(fin_pending.pop(0))

        wd_load(0)
        pend = None
        deferred = None
        for gi, (j0, j1) in enumerate(GROUPS):
            for j in range(j0, j1):
                for is_up in (False, True):
                    if (not is_up) and (j + 2 < NPAIR):
                        load_pair(j + 2)
                    u = up_proj(j, is_up)
                    if pend is not None:
                        if deferred is not None and pend[4] == gi and pend[2]:
                            down(deferred)
                            wd_load(gi)
                            deferred = None
                        conv(*pend[:4])
                    pend = (j, j - j0, is_up, u, gi)
            deferred = gi
        if deferred is not None and pend[4] == deferred:
            conv(*pend[:4])
            down(deferred)
        P.finish()
    return nc


def _consts():
    ident = np.eye(128, dtype=np.float32)
    j = np.arange(128)[:, None]
    i = np.arange(128)[None, :]
    same = (j // 64) == (i // 64)
    maskT2 = np.concatenate([(j <= i) & same, (j >= i) & same], axis=1).astype(np.float32)
    scanmask = np.ones((1, T), np.float32)
    scanmask[0, ::64] = 0.0
    return ident, maskT2, scanmask


def prep_core_inputs(inp):
    f32 = lambda a: np.ascontiguousarray(np.asarray(a, dtype=np.float32))
    ident, maskT2, scanmask = _consts()
    shared = {
        "w_ada": f32(inp["w_ada"][0]), "b_ada": f32(inp["b_ada"][0]).reshape(1, -1),
        "norm1": f32(inp["norm1"][0]).reshape(1, -1), "norm2": f32(inp["norm2"][0]).reshape(1, -1),
        "b_adaT": f32(np.asarray(inp["b_ada"][0]).reshape(48, 128).T), "norm2T": f32(np.asarray(inp["norm2"][0]).reshape(8, 128).T),
        "fnorm": f32(inp["final_norm"]).reshape(1, -1),
        "w_in": f32(inp["w_in"][0]), "w_gla_up": f32(inp["w_gla_up"][0]),
        "b_glaT": f32(np.asarray(inp["b_gla"][0]).reshape(2, 2, 128).transpose(2, 0, 1).reshape(128, 4)),
        "lbT": f32(np.asarray(inp["hgrn_lb"]).reshape(2, 2, 4, 128).transpose(3, 0, 1, 2).reshape(128, 16)),
        "gnorm": f32(np.stack([np.asarray(inp["gla_norm"][0]), np.asarray(inp["hgrn_norm"][0])], axis=1)),
        "w_out": f32(inp["w_out"][0]), "w_ffn_up": f32(inp["w_ffn_up"][0]),
        "bconvT": f32(np.asarray(inp["b_ffn_conv"][0]).reshape(NCH, 128).T),
        "w_ffn_down": f32(inp["w_ffn_down"][0]),
        "identF": ident, "maskT2": maskT2, "scanmask": scanmask,
    }
    conv = np.asarray(inp["ffn_conv"][0], dtype=np.float32).reshape(9, 2 * FFN_H)
    zero_row = np.zeros((1, 2 * FFN_H), np.float32)
    rows_s = np.concatenate([conv, zero_row, zero_row], axis=0)
    rows_p = np.concatenate([zero_row] * 3 + [conv[3:6]] + [zero_row] * 3 + [conv[3:4], conv[5:6]], axis=0)
    w11 = lambda rows: f32(rows.reshape(11, NCH, 128).transpose(2, 1, 0).reshape(128, NCH * 11))
    x_prompt = np.asarray(inp["x_prompt"], dtype=np.float32)
    x_sample = np.asarray(inp["x_sample"], dtype=np.float32)
    maps = []
    for c in range(8):
        m = dict(shared)
        if c < 4:
            m["x"] = f32(x_sample[c])
            m["cvT"] = f32(np.asarray(inp["c"][c]).reshape(8, 128).T)
            m["sinit_g"] = f32(inp["state_gla"][c, 0])
            m["sinit_h"] = f32(inp["state_hgrn"][c, 0])
            mf = np.ones(32, np.float32)
            mb = np.ones(32, np.float32)
            m["w11T"] = w11(rows_s)
        else:
            p = c - 4
            m["x"] = f32(x_prompt[8 * p:8 * p + 8].reshape(T, D))
            m["cvT"] = f32(np.asarray(inp["c_ctx"]).reshape(8, 128).T)
            m["sinit_g"] = np.zeros((2, 4, 64, 128), np.float32)
            m["sinit_h"] = np.zeros((2, 4, 128, 128), np.float32)
            mf = (np.arange(32) % 4 != 0).astype(np.float32)
            mb = (np.arange(32) % 4 != 3).astype(np.float32)
            m["w11T"] = w11(rows_p)
        m["mchain"] = f32(np.tile(np.concatenate([mf, mb])[None, :], (128, 1)))
        maps.append(m)
    return maps


_PROGRAM = {}


def kernel(**inputs):
    if "nc" not in _PROGRAM:
        _PROGRAM["nc"] = build_program(debug=False)
    nc = _PROGRAM["nc"]
    in_maps = prep_core_inputs(inputs)
    res = run_bass_kernel_spmd(nc, in_maps, core_ids=list(range(8)))
    r = res.results
    y_sample = np.stack([np.asarray(r[c]["y"], dtype=np.float32) for c in range(4)], axis=0)
    y_prompt = np.concatenate([np.asarray(r[c]["y"], dtype=np.float32).reshape(8, 256, D) for c in range(4, 8)], axis=0)
    sg = np.concatenate([np.asarray(r[c]["snew_g"], dtype=np.float32) for c in range(4, 8)], axis=0)[:, None]
    sh = np.concatenate([np.asarray(r[c]["snew_h"], dtype=np.float32) for c in range(4, 8)], axis=0)[:, None]
    return (y_prompt, y_sample, sg, sh)
```

```python
import numpy as np
from contextlib import ExitStack
import concourse.bass as bass
import concourse.mybir as mybir
from concourse.bass_utils import run_bass_kernel_spmd

F32 = mybir.dt.float32
BF16 = mybir.dt.bfloat16
AF = mybir.ActivationFunctionType
ALU = mybir.AluOpType


class Tok:
    __slots__ = ("name", "w", "r", "rd", "excl", "acc")

    def __init__(self, name, excl=False):
        self.name = name
        self.w = None
        self.r = {}
        self.rd = []
        self.excl = excl
        self.acc = {}


class TT:
    def __init__(self, t, tok):
        self.t = t
        self.tok = tok


class _Op:
    __slots__ = ("idx", "eng", "fn", "deps", "is_dma", "sig", "semval", "slot", "is_out", "multi")


def _tok(x):
    return x.tok if isinstance(x, TT) else x


class Prog:
    NSLOT = {"sp": 24, "pool": 16, "act": 8}

    def __init__(self, nc, es):
        self.nc = nc
        self.es = es
        self.ops = []
        self.n_dma = {"sp": 0, "pool": 0, "act": 0}
        self._n = 0
        self.bar = set()

    def sb(self, name, shape, dtype):
        t = self.es.enter_context(self.nc.sbuf_tensor(name, list(shape), dtype))
        return TT(t, Tok(name))

    def ps(self, name):
        t = self.es.enter_context(self.nc.psum_tensor(name, [128, 512], F32))
        return TT(t, Tok(name))

    def tok(self, name):
        return Tok(name)

    def _record(self, eng, fn, reads, writes, is_dma, is_out=False, extra=(), multi=False):
        op = _Op()
        op.multi = multi
        op.idx = len(self.ops)
        op.eng = eng
        op.fn = fn
        op.is_dma = is_dma
        op.sig = False
        op.semval = 0
        op.slot = None
        op.is_out = is_out
        deps = set()
        reads = [_tok(x) for x in reads]
        writes = [_tok(x) for x in writes]

        def consider(pidx, kind):
            p = self.ops[pidx]
            if p.is_dma:
                deps.add(pidx)
                return
            if (not is_dma) and p.eng == eng:
                if eng == "pe":
                    return
                if kind != "raw":
                    return
            deps.add(pidx)

        for t in reads:
            if t.w is not None:
                consider(t.w, "raw")
        for t in writes:
            if t.w is not None:
                consider(t.w, "waw")
            for _, ridx in t.r.items():
                consider(ridx, "war")
            for ridx in t.rd:
                consider(ridx, "war")
        for t in reads + writes:
            if t.excl:
                for e2, aidx in t.acc.items():
                    if e2 != eng:
                        deps.add(aidx)
                t.acc[eng] = op.idx
        for x in extra:
            deps.add(x.idx)
        for pidx in self.bar:
            p = self.ops[pidx]
            if (not is_dma) and (not p.is_dma) and p.eng == eng:
                continue
            deps.add(pidx)
        op.deps = deps
        for t in reads:
            if is_dma:
                t.rd.append(op.idx)
            else:
                t.r[eng] = op.idx
        for t in writes:
            t.w = op.idx
            t.r = {}
            t.rd = []
        if is_dma:
            k = self.n_dma[eng]
            self.n_dma[eng] += 1
            ns = self.NSLOT[eng]
            op.slot = (eng, k % ns)
            op.semval = 16 * (k // ns + 1)
        self.ops.append(op)
        return op

    def op(self, eng, fn, reads, writes, extra=(), multi=False):
        return self._record(eng, fn, reads, writes, False, extra=extra, multi=multi)

    def barrier(self):
        last = {}
        for op in self.ops:
            if op.is_dma:
                last[("d",) + op.slot] = op.idx
            else:
                last[op.eng] = op.idx
        self.bar = set(last.values())

    def dma(self, eng, out_ap, in_ap, reads, writes, is_out=False, **kw):
        def fn(e, out_ap=out_ap, in_ap=in_ap, kw=kw):
            return e.dma_start(out=out_ap, in_=in_ap, **kw)
        return self._record(eng, fn, reads, writes, True, is_out)

    def finish(self):
        nc = self.nc
        es = self.es
        ops = self.ops
        for op in ops:
            for d in op.deps:
                ops[d].sig = True
        engs = ["pe", "act", "dve", "pool", "sp"]
        esem = {e: es.enter_context(nc.semaphore("s_" + e)) for e in engs}
        dsem = {}
        for e, ns in self.NSLOT.items():
            for i in range(min(ns, max(1, self.n_dma[e]))):
                dsem[(e, i)] = es.enter_context(nc.semaphore("d_%s%d" % (e, i)))
        cnt = {e: 0 for e in engs}
        for op in ops:
            if not op.is_dma and op.sig:
                cnt[op.eng] += 1
                op.semval = cnt[op.eng]
        slot_last = {}
        prev_on_slot = {}
        for op in ops:
            if op.is_dma:
                prev_on_slot[op.idx] = slot_last.get(op.slot)
                slot_last[op.slot] = op.idx
        by_eng = {e: [op for op in ops if op.eng == e] for e in engs}

        def sigof(p):
            if p.is_dma:
                return dsem[p.slot], p.semval
            return esem[p.eng], p.semval

        def emit(ename, e):
            waited = {}

            embed = ename in ("dve", "act", "pool")

            for op in by_eng[ename]:
                need = {}
                order = []
                cand = [sigof(ops[d]) for d in sorted(op.deps)]
                if op.is_dma:
                    pv = prev_on_slot[op.idx]
                    if pv is not None:
                        cand.append(sigof(ops[pv]))
                for s, v in cand:
                    key = id(s)
                    if waited.get(key, 0) < v and need.get(key, (None, 0))[1] < v:
                        if key not in need:
                            order.append(key)
                        need[key] = (s, v)
                pend = [need[k] for k in order]
                for s, v in pend:
                    waited[id(s)] = v
                fold = None
                if embed and pend and not op.is_dma and not op.multi:
                    fold = pend.pop()
                for s, v in pend:
                    e.wait_ge(s, v)
                ins = op.fn(e)
                if fold is not None:
                    ins._wait_ge(fold[0], fold[1])
                if op.is_dma:
                    ins.then_inc(dsem[op.slot], 16)
                elif op.sig:
                    ins.then_inc(esem[ename], 1)
            if ename == "sp":
                for slot, idx in slot_last.items():
                    s, v = sigof(ops[idx])
                    if waited.get(id(s), 0) < v:
                        e.wait_ge(s, v)
                        waited[id(s)] = v

        with nc.Block() as block:
            @block.tensor
            def _(e):
                emit("pe", e)

            @block.scalar
            def _(e):
                emit("act", e)

            @block.vector
            def _(e):
                emit("dve", e)

            @block.gpsimd
            def _(e):
                emit("pool", e)

            @block.sync
            def _(e):
                emit("sp", e)
        self.stats = {e: len(by_eng[e]) for e in engs}


D = 1024
T = 2048
NB = 16
EPS = 1e-6
IN_W = 4128
FFN_H = 2816
NCH = 44
NPAIR = 22
UPAD = 65
UW = UPAD + T + UPAD
GROUPS = [(0, 6), (6, 12), (12, 17), (17, 22)]
TS = 512
OFF_QA, OFF_KA, OFF_VA, OFF_GA, OFF_LR = 0, 256, 512, 1024, 1536
OFF_QB, OFF_FB, OFF_IB, OFF_GB = 1568, 2080, 3104, 3616


class Arena:
    def __init__(self, P, nf32):
        self.P = P
        self.tt = P.sb("arena", [128, nf32], F32)
        self.n = nf32
        self.top = 0
        self.end = nf32

    def alloc(self, name, free_shape, dtype, top=False):
        nel = int(np.prod(free_shape))
        nf = nel if dtype == F32 else (nel + 1) // 2
        nf = (nf + 3) // 4 * 4
        assert self.top + nf <= self.end, ("arena overflow", name, self.top, nf, self.end)
        if top:
            self.end -= nf
            ap = self.tt.t[:, self.end:self.end + nf]
        else:
            ap = self.tt.t[:, self.top:self.top + nf]
        if dtype != F32:
            ap = ap.bitcast(dtype)
        ap = ap[:, 0:nel]
        if len(free_shape) == 2:
            ap = ap.rearrange("p (a b) -> p a b", a=free_shape[0])
        elif len(free_shape) == 3:
            ap = ap.rearrange("p (a b c) -> p a b c", a=free_shape[0], b=free_shape[1])
        if not top:
            self.top += nf
        return TT(ap, Tok(name))

    def mark(self):
        return self.top

    def reset(self, m):
        self.top = m


def build_program(debug=False):
    import os as _os
    nc = bass.Bass("TRN2", target_bir_lowering=False)

    def din(name, shape):
        return nc.dram_tensor(name, list(shape), F32, kind="ExternalInput").ap()

    def dout(name, shape, dt=F32):
        return nc.dram_tensor(name, list(shape), dt, kind="ExternalOutput").ap()

    x_d = din("x", [T, D])
    cvT_d = din("cvT", [128, 8])
    wada_d = din("w_ada", [D, 6 * D])
    bada_d = din("b_ada", [1, 6 * D])
    badaT_d = din("b_adaT", [128, 48])
    norm2T_d = din("norm2T", [128, 8])
    norm1_d = din("norm1", [1, D])
    norm2_d = din("norm2", [1, D])
    fnorm_d = din("fnorm", [1, D])
    win_d = din("w_in", [D, IN_W])
    wgu_d = din("w_gla_up", [2, 16, 256])
    bglaT_d = din("b_glaT", [128, 4])
    lbT_d = din("lbT", [128, 16])
    gnorm_d = din("gnorm", [128, 2])
    wout_d = din("w_out", [D, D])
    wup_d = din("w_ffn_up", [D, 2 * FFN_H])
    w11T_d = din("w11T", [128, NCH * 11])
    bconvT_d = din("bconvT", [128, NCH])
    wdn_d = din("w_ffn_down", [FFN_H, D])
    sig_d = din("sinit_g", [2, 4, 64, 128])
    sih_d = din("sinit_h", [2, 4, 128, 128])
    mchain_d = din("mchain", [128, 64])
    identF_d = din("identF", [128, 128])
    maskT2_d = din("maskT2", [128, 256])
    scanmask_d = din("scanmask", [1, T])
    y_d = dout("y", [T, D])
    sog_d = dout("snew_g", [8, 2, 4, 64, 128])
    soh_d = dout("snew_h", [8, 2, 4, 128, 128])
    dbg = {}
    if debug:
        dbg["h1T"] = dout("dbg_h1T", [128, 8 * T], BF16)
        dbg["mod"] = dout("dbg_mod", [128, 6 * D])
        dbg["mergedT"] = dout("dbg_mergedT", [128, 8 * T], BF16)
        dbg["x1"] = dout("dbg_x1", [T, D])

    es = ExitStack()
    with es:
        P = Prog(nc, es)
        AR = Arena(P, 52800)
        dumps = {}

        def dump(name, tt, ncols, dt, parts=128):
            if not debug:
                return
            if name not in dumps:
                dumps[name] = nc.dram_tensor("dd_" + name, [128, ncols], dt, kind="ExternalOutput").ap()
            ap = tt.t
            if len(ap.shape) == 3:
                ap = ap.rearrange("p a b -> p (a b)")
            elif len(ap.shape) == 4:
                ap = ap.rearrange("p a b c -> p (a b c)")
            P.dma("sp", dumps[name][0:parts], ap[0:parts], [tt], [], is_out=True)
        psb = [P.ps("psum%d" % i) for i in range(8)]
        psq = [Tok("psbank%d" % b, excl=True) for b in range(8)]

        def PQ(b, c0, c1):
            return [psq[b]]

        def pst(b):
            return psb[b].t

        rr = {"n": 0}

        def evac_eng():
            rr["n"] += 1
            return "act" if rr["n"] % 2 else "dve"

        def copy_op(eng, out, in_, reads, writes, scale=None):
            if eng == "act":
                if scale is None:
                    P.op("act", lambda e: e.activation(out, in_, AF.Copy), reads, writes)
                else:
                    P.op("act", lambda e: e.activation(out, in_, AF.Copy, scale=scale), reads, writes)
            else:
                if scale is None:
                    P.op(eng, lambda e: e.tensor_copy(out, in_), reads, writes)
                else:
                    P.op(eng, lambda e: e.tensor_scalar(out, in_, scale, None, op0=ALU.mult), reads, writes)

        identF = AR.alloc("identF", [128], F32)
        identB = AR.alloc("identB", [128], BF16)
        maskT2 = AR.alloc("maskT2", [256], F32)
        scanmask = AR.alloc("scanmask", [TS], BF16)
        mchain = AR.alloc("mchain", [64], F32)
        smalls = AR.alloc("smalls", [64], F32)
        LB = smalls.t[:, 0:8]
        L1M = smalls.t[:, 8:16]
        NEGB = smalls.t[:, 16:24]
        GS = smalls.t[:, 24:26]
        SC = smalls.t[:, 32:40]
        TMPS = smalls.t[:, 40:64]
        MB = [None] * 6
        hT = AR.alloc("hT", [8, T], BF16)
        hT_tok = [Tok("hT_b%d" % b) for b in range(NB)]
        onesF = AR.alloc("onesF", [128], F32)
        ring = {"slots": [], "n": 0}

        def next_w():
            w = ring["slots"][ring["n"] % len(ring["slots"])]
            ring["n"] += 1
            return w

        ph1_mark = AR.mark()
        ringB = [AR.alloc("wringB%d" % i, [8 * 640], BF16) for i in range(2)]
        ring["slots"] = [TT(ringB[i].t[:, 0:8 * 512], ringB[i].tok) for i in range(2)]
        srep = AR.alloc("srep", [8, 128], BF16)
        brow = [AR.alloc("brow%d" % i, [512], F32) for i in range(2)]
        MB[0] = AR.alloc("mod0", [D], F32)
        MB[1] = AR.alloc("mod1", [D], F32)

        P.dma("sp", identF.t, identF_d, [], [identF])
        P.dma("sp", maskT2.t, maskT2_d, [], [maskT2])
        P.dma("pool", scanmask.t, scanmask_d[:, 0:TS].partition_broadcast(128), [], [scanmask])
        P.dma("sp", mchain.t, mchain_d, [], [mchain])
        P.op("dve", lambda e: e.tensor_copy(identB.t, identF.t), [identF], [identB])
        lbT = AR.alloc("lbT", [16], F32)
        cvT = AR.alloc("cvT", [8], F32)
        bgl = AR.alloc("bgl", [8], F32)
        gnm = AR.alloc("gnm", [2], F32)
        P.dma("sp", lbT.t, lbT_d, [], [lbT])
        P.dma("sp", cvT.t, cvT_d, [], [cvT])
        P.dma("sp", bgl.t[:, 0:4], bglaT_d, [], [bgl])
        P.dma("sp", gnm.t, gnorm_d, [], [gnm])
        DD = TMPS[:, 0:8]
        EE = TMPS[:, 8:16]
        P.op("dve", lambda e: e.tensor_tensor(DD, lbT.t[:, 8:16], lbT.t[:, 0:8], ALU.subtract), [lbT], [smalls])
        P.op("act", lambda e: e.activation(EE, DD, AF.Exp), [smalls], [smalls])
        P.op("act", lambda e: e.activation(EE, EE, AF.Ln, bias=1.0), [smalls], [smalls])
        P.op("act", lambda e: e.activation(LB, EE, AF.Exp, scale=-1.0), [smalls], [smalls])
        P.op("dve", lambda e: e.tensor_tensor(L1M, DD, EE, ALU.subtract), [smalls], [smalls])
        P.op("dve", lambda e: e.tensor_scalar(NEGB[:, 0:4], bgl.t[:, 0:4], -1.0, None, op0=ALU.mult), [bgl], [smalls])
        P.op("dve", lambda e: e.tensor_scalar(GS, gnm.t, float(np.sqrt(128.0)), None, op0=ALU.mult), [gnm], [smalls])
        E2 = TMPS[:, 16:24]
        P.op("act", lambda e: e.activation(E2, cvT.t, AF.Exp, scale=-1.0), [cvT, smalls], [smalls])
        P.op("dve", lambda e: e.tensor_scalar(E2, E2, 1.0, None, op0=ALU.add), [smalls], [smalls])
        P.op("dve", lambda e: e.reciprocal(E2, E2), [smalls], [smalls])
        P.op("dve", lambda e: e.tensor_tensor(SC, cvT.t, E2, ALU.mult), [cvT, smalls], [smalls])
        sc_b = bass.AP(smalls.t.tensor, smalls.t.offset + 32, [list(smalls.t.ap[0]), [1, 8], [0, 128]])
        P.op("dve", lambda e: e.tensor_copy(srep.t, sc_b), [smalls], [srep])
        srepF = AR.alloc("srepF", [8, 128], F32)
        P.op("dve", lambda e: e.tensor_copy(srepF.t, sc_b), [smalls], [srepF])
        wF32 = [AR.alloc("wF32_%d" % i, [8, 512], F32) for i in range(2)]
        nrm = AR.alloc("nrm_bc", [D], F32)
        P.op("pool", lambda e: e.memset(onesF.t, 1.0), [], [onesF])
        pbank = {"n": 0}

        def mod_compute(j):
            col = [0, 1, 2, 3, 4, 5][j]
            for n in range(2):
                c0 = col * D + n * 512
                if n == 0:
                    w = next_w()
                    wv = w.t[:, 0:8 * 512].rearrange("p (k n) -> p k n", k=8)
                    P.dma("pool", wv, wada_d[:, c0:c0 + 512].rearrange("(k p) n -> p k n", p=128), [], [w])
                    sr = srep
                else:
                    w = wF32[j % 2]
                    wv = w.t
                    P.dma("sp", wv, wada_d[:, c0:c0 + 512].rearrange("(k p) n -> p k n", p=128), [], [w])
                    sr = srepF
                br = brow[(2 * j + n) % 2]
                P.dma("sp", br.t[0:1, :], bada_d[:, c0:c0 + 512], [], [br])
                b = pbank["n"] % 2
                pbank["n"] += 1
                for kc in range(8):
                    P.op("pe", lambda e, kc=kc, b=b, wv=wv, sr=sr: e.matmul(pst(b)[:, :], sr.t[:, kc, :], wv[:, kc, :], start=(kc == 0), stop=False),
                         [sr, w], PQ(b, 0, 512))
                P.op("pe", lambda e, b=b, br=br: e.matmul(pst(b)[:, :], onesF.t[0:1, :], br.t[0:1, :], start=False, stop=True),
                     [onesF, br], PQ(b, 0, 512))
                copy_op(evac_eng(), MB[j].t[:, n * 512:(n + 1) * 512], pst(b)[:, :], PQ(b, 0, 512), [MB[j]])

        mod_compute(0)
        mod_compute(1)
        P.dma("sp", nrm.t, norm1_d.partition_broadcast(128), [], [nrm])
        P.op("dve", lambda e: e.scalar_tensor_tensor(MB[1].t, MB[1].t, 1.0, nrm.t, op0=ALU.add, op1=ALU.mult), [MB[1], nrm], [MB[1]])

        xring = [AR.alloc("xring%d" % i, [D], F32) for i in range(3)]
        junk = AR.alloc("junk", [D], BF16)
        tmpf = AR.alloc("tmpf", [D], F32)
        hb = [AR.alloc("hb%d" % i, [D], BF16) for i in range(2)]
        stt = [AR.alloc("stt%d" % i, [4], F32) for i in range(3)]

        def norm_A(src_ap, src_toks, st):
            jk = junk
            P.op("act", lambda e: e.activation(jk.t, src_ap, AF.Square, accum_out=st.t[:, 0:1]), src_toks, [jk, st], multi=True)
            P.op("act", lambda e: e.activation(st.t[:, 1:2], st.t[:, 0:1], AF.Ln, scale=1.0 / D, bias=EPS), [st], [st])
            P.op("act", lambda e: e.activation(st.t[:, 2:3], st.t[:, 1:2], AF.Exp, scale=-0.5), [st], [st])

        def norm_B1(src_ap, src_toks, g_t, s_t, b, st):
            tf, h = tmpf, hb[b % 2]
            P.op("dve", lambda e: e.scalar_tensor_tensor(tf.t, src_ap, st.t[:, 2:3], g_t.t, op0=ALU.mult, op1=ALU.mult),
                 src_toks + [st, g_t], [tf])
            P.op("dve", lambda e: e.tensor_tensor(h.t, tf.t, s_t.t, ALU.add), [tf, s_t], [h])

        def norm_B2(b, pbk):
            h = hb[b % 2]
            pv = pst(pbk).bitcast(BF16).rearrange("p (k t) -> p k t", k=8)
            for kc in range(8):
                P.op("pe", lambda e, kc=kc: e.transpose(pv[:, kc, :], h.t[:, kc * 128:(kc + 1) * 128], identB.t),
                     [h, identB], PQ(pbk, 0, 512))
            copy_op("act", hT.t[:, :, b * 128:(b + 1) * 128], pv, PQ(pbk, 0, 512), [hT_tok[b]])

        _w0 = ringB[0]
        _wv0 = _w0.t[:, 0:8 * 640].rearrange("p (k n) -> p k n", k=8)
        for (c0_, n_, o_) in [(OFF_QA, 128, 0), (OFF_KA, 128, 128), (OFF_VA, 128, 256), (OFF_GA, 128, 384)]:
            P.dma("pool", _wv0[:, :, o_:o_ + n_], win_d[:, c0_:c0_ + n_].rearrange("(k p) n -> p k n", p=128), [], [_w0])
        for b in range(NB + 2):
            if b < NB:
                xt = xring[b % 3]
                P.dma("sp", xt.t, x_d[b * 128:(b + 1) * 128, :], [], [xt])
                norm_A(xt.t, [xt], stt[b % 3])
            if 1 <= b <= NB:
                xp = xring[(b - 1) % 3]
                norm_B1(xp.t, [xp], MB[1], MB[0], b - 1, stt[(b - 1) % 3])
            if b >= 2:
                norm_B2(b - 2, 2 + ((b - 2) % 2))
        if debug:
            P.dma("sp", dbg["h1T"], hT.t.rearrange("p k t -> p (k t)"), hT_tok, [], is_out=True)
            for j in range(2):
                P.dma("sp", dbg["mod"][:, j * D:(j + 1) * D], MB[j].t, [MB[j]], [], is_out=True)

        P.barrier()
        AR.reset(ph1_mark)
        BS = 64
        NBK = T // BS
        NG = T // 128
        BPS = TS // BS
        NSPAN = T // TS
        modT = AR.alloc("modT", [4, 8], F32, top=True)
        badaT = AR.alloc("badaT", [48], F32, top=True)
        scb = AR.alloc("scb", [8], BF16, top=True)
        top_keep = AR.end
        mergedT = AR.alloc("mergedT", [8, T], BF16, top=True)
        mT_tok = [Tok("mT_h%d" % i) for i in range(8)]
        for i in range(2):
            AR.alloc("wringB_again%d" % i, [8 * 640], BF16)
        ring["slots"] = ringB
        QT = AR.alloc("QT", [T], F32)
        KT = AR.alloc("KT", [T], F32)
        QTt = [AR.alloc("QTt%d" % d, [T], BF16) for d in range(2)]
        KTt = [AR.alloc("KTt%d" % d, [T], BF16) for d in range(2)]
        KTM = [AR.alloc("KTM%d" % d, [NG, 128], BF16) for d in range(2)]
        LH = AR.alloc("LH", [2, NBK, 2], F32)
        EX = AR.alloc("EX", [2, NBK, 2], F32)
        SCL = AR.alloc("SCL", [2, NBK, 2], F32)
        Vt = [AR.alloc("Vt%d" % pb, [NG, 128], BF16) for pb in range(2)]
        GG = [AR.alloc("GG%d" % pb, [NG, 128], BF16) for pb in range(2)]
        NSET = 4
        SETS = [dict(X=TT(KT.t[:, i * TS:(i + 1) * TS], Tok("Xs%d" % i)), U=AR.alloc("Us%d" % i, [TS], F32), T2=AR.alloc("T2s%d" % i, [TS], F32),
                     B=AR.alloc("Bs%d" % i, [TS], F32), KH=AR.alloc("KHt%d" % i, [TS], BF16)) for i in range(NSET)]
        SQt = [AR.alloc("SQt%d" % d, [NBK, 128], BF16) for d in range(2)]
        ST = [[AR.alloc("ST%d_%d" % (d, i), [128], F32) for i in range(3)] for d in range(2)]
        ATt = [AR.alloc("AT%d" % i, [256], BF16) for i in range(4)]
        MTM = [AR.alloc("MTM%d" % i, [128], BF16) for i in range(2)]
        sto = [AR.alloc("sto%d" % i, [4], F32) for i in range(4)]
        junk2 = AR.alloc("junk2", [128], BF16)
        LRT = AR.alloc("LRT", [2, T], BF16)
        WLR = AR.alloc("WLR", [8, 32], BF16)
        WGU = AR.alloc("WGU", [2, 256], BF16)

        def bcol(Bap, parts, col, nblk):
            pstep = Bap.ap[0][0]
            return bass.AP(Bap.tensor, Bap.offset + col, [[pstep, parts], [BS, nblk], [0, BS]])

        P.dma("pool", WLR.t, win_d[:, OFF_LR:OFF_LR + 32].rearrange("(k p) n -> p k n", p=128), [], [WLR])
        P.dma("pool", WGU.t[0:16], wgu_d.rearrange("d r k -> r d k"), [], [WGU])
        P.dma("sp", badaT.t, badaT_d, [], [badaT])
        P.op("dve", lambda e: e.tensor_copy(scb.t, SC), [smalls], [scb])
        fmb = {"n": 0}

        def fm_bank():
            fmb["n"] += 1
            return fmb["n"] % 2

        def head_cfg(hh):
            gla = hh < 4
            h = hh if gla else hh - 4
            if gla:
                p = h // 2
                owner = (h % 2 == 0)
                if owner:
                    groups = [(OFF_QA + 128 * p, 128, 0), (OFF_KA + 128 * p, 128, 128), (OFF_VA + 128 * h, 128, 256), (OFF_GA + 128 * h, 128, 384)]
                    c_vg = 256
                else:
                    groups = [(OFF_VA + 128 * h, 128, 0), (OFF_GA + 128 * h, 128, 128)]
                    c_vg = 0
                return dict(gla=True, h=h, p=p, owner=owner, dk=64, po=64 * (h % 2), dscale=-1.0 / 16.0, groups=groups, c_q=0, c_k=128, c_vg=c_vg, c_f=None)
            groups = [(OFF_QB + 128 * h, 128, 0), (OFF_FB + 128 * h, 128, 128), (OFF_FB + 512 + 128 * h, 128, 256),
                      (OFF_IB + 128 * h, 128, 384), (OFF_GB + 128 * h, 128, 512)]
            return dict(gla=False, h=h, p=0, owner=True, dk=128, po=0, dscale=1.0, groups=groups, c_q=0, c_k=None, c_vg=384, c_f=(128, 256))

        head_w = {}

        def load_head_weights(hh, dma=True):
            cfg = head_cfg(hh)
            w = ring["slots"][hh % 2]
            wv = w.t[:, 0:8 * 640].rearrange("p (k n) -> p k n", k=8)
            if dma:
                for (c0, n, o) in cfg["groups"]:
                    P.dma("pool", wv[:, :, o:o + n], win_d[:, c0:c0 + n].rearrange("(k p) n -> p k n", p=128), [], [w])
            head_w[hh] = (w, wv)

        def mod_computeT(j, w):
            b = fm_bank()
            for n in range(4):
                c0 = j * D + n * 256
                half = n % 2
                wv = w.t[:, half * 2048:(half + 1) * 2048].rearrange("p (k n) -> p k n", k=8)
                P.dma("pool", wv, wada_d[:, c0:c0 + 256].rearrange("(k p) n -> p k n", p=128), [], [w])
                for mb in range(2):
                    for kc in range(8):
                        P.op("pe", lambda e, kc=kc, mb=mb, n=n, wv=wv: e.matmul(pst(b)[:, 2 * n + mb:2 * n + mb + 1], wv[:, kc, mb * 128:(mb + 1) * 128], scb.t[:, kc:kc + 1],
                                                                              start=(kc == 0), stop=(kc == 7)), [w, scb], PQ(b, 0, 512))
                yield
            copy_op("dve", modT.t[:, j - 2, :], pst(b)[:, 0:8], PQ(b, 0, 512), [modT])
            P.op("dve", lambda e: e.tensor_tensor(modT.t[:, j - 2, :], modT.t[:, j - 2, :], badaT.t[:, j * 8:(j + 1) * 8], ALU.add), [modT, badaT], [modT])

        def stageAB1(hh):
            cfg = head_cfg(hh)
            gla, h, dscale, owner = cfg["gla"], cfg["h"], cfg["dscale"], cfg["owner"]
            dk = 128
            c_q, c_k, c_vg, c_f = cfg["c_q"], cfg["c_k"], cfg["c_vg"], cfg["c_f"]
            pb = hh % 2
            w, wv = head_w[hh]
            if hh + 1 < 8:
                load_head_weights(hh + 1)
            myVt, myGG = Vt[pb], GG[pb]
            Us = SETS[0]["U"]

            def fm_proj(c0, M, tiles, dst_ap_fn, dst_toks, scale=None):
                for tt in tiles:
                    b = fm_bank()
                    for kc in range(8):
                        P.op("pe", lambda e, kc=kc, b=b, tt=tt: e.matmul(pst(b)[0:M, :], wv[:, kc, c0:c0 + M], hT.t[:, kc, tt * 512:(tt + 1) * 512],
                                                                       start=(kc == 0), stop=(kc == 7)),
                             [w] + hT_tok[4 * tt:4 * tt + 4], PQ(b, 0, 512))
                    yield
                    copy_op(evac_eng(), dst_ap_fn(tt), pst(b)[0:M, :], PQ(b, 0, 512), dst_toks, scale=scale)

            if owner:
                yield from fm_proj(c_q, dk, range(4), lambda tt: QT.t[0:dk, tt * 512:(tt + 1) * 512], [QT], scale=(0.125 if gla else None))
            if gla and owner:
                yield from fm_proj(c_k, dk, range(4), lambda tt: KT.t[0:dk, tt * 512:(tt + 1) * 512], [KT])
            if hh == 0:
                for d in range(2):
                    for tt in range(4):
                        b = fm_bank()
                        for kc in range(8):
                            P.op("pe", lambda e, kc=kc, b=b, tt=tt, d=d: e.matmul(pst(b)[0:16, :], WLR.t[:, kc, 16 * d:16 * d + 16], hT.t[:, kc, tt * 512:(tt + 1) * 512],
                                                                                  start=(kc == 0), stop=(kc == 7)),
                                 [WLR] + hT_tok[4 * tt:4 * tt + 4], PQ(b, 0, 512))
                        copy_op(evac_eng(), LRT.t[0:16, d, tt * 512:(tt + 1) * 512], pst(b)[0:16, :], PQ(b, 0, 512), [LRT])
                        yield
            for bp in range(NG // 2):
                bk = fm_bank()
                for i in range(2):
                    blk = 2 * bp + i
                    for kc in range(8):
                        P.op("pe", lambda e, kc=kc, bk=bk, i=i, blk=blk: e.matmul(pst(bk)[:, i * 256:(i + 1) * 256], hT.t[:, kc, blk * 128:(blk + 1) * 128],
                                                                                wv[:, kc, c_vg:c_vg + 256], start=(kc == 0), stop=(kc == 7)),
                             [w, hT_tok[blk]], PQ(bk, i * 256, (i + 1) * 256))
                pv = pst(bk).rearrange("p (b c) -> p b c", b=2)
                yield
                ce = evac_eng()
                copy_op(ce, myVt.t[:, 2 * bp:2 * bp + 2, :], pv[:, :, 0:128], PQ(bk, 0, 512), [myVt])
                copy_op(ce, myGG.t[:, 2 * bp:2 * bp + 2, :], pv[:, :, 128:256], PQ(bk, 0, 512), [myGG])
            GPS = TS // 128
            for s_ in range(NSPAN):
                gsp = myGG.t[:, s_ * GPS:(s_ + 1) * GPS, :].rearrange("p b c -> p (b c)")
                P.op("act", lambda e, gsp=gsp: e.activation(Us.t, gsp, AF.Exp, scale=-1.0), [myGG], [Us])
                P.op("act", lambda e: e.activation(Us.t, Us.t, AF.Ln, bias=1.0), [Us], [Us])
                P.op("act", lambda e: e.activation(Us.t, Us.t, AF.Exp, scale=-1.0), [Us], [Us])
                P.op("dve", lambda e, gsp=gsp: e.tensor_tensor(gsp, gsp, Us.t, ALU.mult), [myGG, Us], [myGG])
                yield

        def stageAB2(hh):
            cfg = head_cfg(hh)
            gla, h, dscale, owner, pr = cfg["gla"], cfg["h"], cfg["dscale"], cfg["owner"], cfg["p"]
            dk = 128
            c_f = cfg["c_f"]
            w, wv = head_w[hh]
            myQTt, myKTt, myKTM, myLH, myEX, mySCL = QTt, KTt, KTM, LH, EX, SCL
            if not owner:
                if hh < 4:
                    yield from mod_computeT(2 + hh, w)
                return

            def fm_proj(c0, M, tiles, dst_ap_fn, dst_toks, scale=None):
                for tt in tiles:
                    b = fm_bank()
                    for kc in range(8):
                        P.op("pe", lambda e, kc=kc, b=b, tt=tt: e.matmul(pst(b)[0:M, :], wv[:, kc, c0:c0 + M], hT.t[:, kc, tt * 512:(tt + 1) * 512],
                                                                       start=(kc == 0), stop=(kc == 7)),
                             [w] + hT_tok[4 * tt:4 * tt + 4], PQ(b, 0, 512))
                    copy_op("act", dst_ap_fn(tt), pst(b)[0:M, :], PQ(b, 0, 512), dst_toks, scale=scale)
                    yield

            v3 = lambda ap: ap.rearrange("p (b t) -> p b t", b=BPS)

            def decay(d, s_, tiles, sp0, sp1):
                st_ = SETS[(s_ % 2) * 2 + d]
                Xs, Us, T2s, Bs, KHt = st_["X"], st_["U"], st_["T2"], st_["B"], st_["KH"]
                X_, U_, T2_, B_, KH_ = Xs.t[0:dk], Us.t[0:dk], T2s.t[0:dk], Bs.t[0:dk], KHt.t[0:dk]
                col = (d * 2 + pr) if gla else (d * 4 + h)
                sel = d
                if gla:
                    for ti, tt in enumerate(tiles):
                        b = fm_bank()
                        P.op("pe", lambda e, b=b, tt=tt: e.matmul(pst(b)[0:128, :], WGU.t[0:16, d, 128 * pr:128 * pr + 128], LRT.t[0:16, d, tt * 512:(tt + 1) * 512],
                                                                start=True, stop=True), [WGU, LRT], PQ(b, 0, 512))
                        P.op("act", lambda e, b=b, ti=ti: e.activation(U_[:, ti * 512:(ti + 1) * 512], pst(b)[0:128, :], AF.Exp, scale=-1.0,
                                                                      bias=NEGB[:, col:col + 1]), PQ(b, 0, 512) + [smalls], [Us])
                    yield
                    P.op("act", lambda e: e.activation(T2_, U_, AF.Ln, bias=1.0), [Us], [T2s])
                    P.op("dve", lambda e: e.tensor_tensor_scan(B_, scanmask.t[0:dk, :], T2_, 0.0, ALU.mult, ALU.add), [scanmask, T2s], [Bs])
                else:
                    cf = c_f[d]
                    yield from fm_proj(cf, 128, tiles, lambda tt: X_[:, (tt - tiles[0]) * 512:(tt - tiles[0] + 1) * 512], [Xs])
                    P.op("act", lambda e: e.activation(U_, X_, AF.Exp, scale=-1.0), [Xs], [Us])
                    P.op("act", lambda e: e.activation(T2_, U_, AF.Ln, bias=1.0), [Us], [T2s])
                    P.op("act", lambda e: e.activation(U_, U_, AF.Ln, bias=1.0, scale=LB[:, col:col + 1]), [Us, smalls], [Us])
                    yield
                    P.op("dve", lambda e: e.tensor_tensor(U_, U_, T2_, ALU.subtract), [Us, T2s], [Us])
                    P.op("act", lambda e: e.activation(X_, X_, AF.Exp), [Xs], [Xs])
                    P.op("act", lambda e: e.activation(X_, X_, AF.Ln, bias=1.0), [Xs], [Xs])
                    P.op("dve", lambda e: e.tensor_tensor_scan(B_, scanmask.t[0:dk, :], U_, 0.0, ALU.mult, ALU.add), [scanmask, Us], [Bs])
                yield
                B3 = v3(B_)
                MID = BS // 2 - 1
                lh = myLH.t[0:dk, d, s_ * BPS:(s_ + 1) * BPS, :]
                ex = myEX.t[0:dk, d, s_ * BPS:(s_ + 1) * BPS, :]
                P.op("dve", lambda e: e.tensor_copy(lh[:, :, 0:1], B3[:, :, MID:MID + 1]), [Bs], [myLH])
                P.op("dve", lambda e: e.tensor_tensor(lh[:, :, 1:2], B3[:, :, BS - 1:BS], B3[:, :, MID:MID + 1], ALU.subtract), [Bs], [myLH])
                P.op("act", lambda e: e.activation(ex, lh, AF.Exp, scale=dscale), [myLH], [myEX])
                bmid = bcol(B_, dk, MID, BPS)
                ehb = bass.AP(myEX.t.tensor, myEX.t[0:dk, d, s_ * BPS:(s_ + 1) * BPS, 1 - sel].offset,
                              [[myEX.t.ap[0][0], dk], [2, BPS], [0, BS]])
                if gla:
                    if d == 0:
                        P.op("dve", lambda e: e.tensor_tensor(v3(U_), B3, bmid, ALU.subtract), [Bs], [Us])
                    else:
                        P.op("dve", lambda e: e.tensor_tensor(T2_, B_, T2_, ALU.subtract), [Bs, T2s], [T2s])
                        P.op("dve", lambda e: e.tensor_tensor(v3(U_), bmid, v3(T2_), ALU.subtract), [Bs, T2s], [Us])
                    P.op("dve", lambda e: e.tensor_scalar(U_, U_, 640.0, -640.0, op0=ALU.min, op1=ALU.max), [Us], [Us])
                    P.op("act", lambda e: e.activation(T2_, U_, AF.Exp, scale=dscale), [Us], [T2s])
                    P.op("pool", lambda e: e.tensor_tensor(myQTt[d].t[0:dk, sp0:sp1], QT.t[0:dk, sp0:sp1], T2_, ALU.mult), [QT, T2s], [myQTt[d]])
                    yield
                    P.op("act", lambda e: e.activation(T2_, U_, AF.Exp, scale=-dscale), [Us], [T2s])
                    P.op("dve", lambda e: e.tensor_tensor(myKTt[d].t[0:dk, sp0:sp1], KT.t[0:dk, sp0:sp1], T2_, ALU.mult), [KT, T2s], [myKTt[d]])
                else:
                    if d == 0:
                        P.op("dve", lambda e: e.tensor_tensor(v3(T2_), B3, bmid, ALU.subtract), [Bs], [T2s])
                    else:
                        P.op("dve", lambda e: e.tensor_tensor(U_, B_, U_, ALU.subtract), [Bs, Us], [Us])
                        P.op("dve", lambda e: e.tensor_tensor(v3(T2_), bmid, v3(U_), ALU.subtract), [Bs, Us], [T2s])
                    P.op("dve", lambda e: e.tensor_scalar(T2_, T2_, 40.0, -40.0, op0=ALU.min, op1=ALU.max), [T2s], [T2s])
                    P.op("act", lambda e: e.activation(U_, T2_, AF.Exp), [T2s], [Us])
                    P.op("pool", lambda e: e.tensor_tensor(myQTt[d].t[0:dk, sp0:sp1], QT.t[0:dk, sp0:sp1], U_, ALU.mult), [QT, Us], [myQTt[d]])
                    yield
                    P.op("dve", lambda e: e.tensor_tensor(X_, X_, T2_, ALU.add), [Xs, T2s], [Xs])
                    P.op("act", lambda e: e.activation(myKTt[d].t[0:dk, sp0:sp1], X_, AF.Exp, scale=-1.0, bias=L1M[:, col:col + 1]), [Xs, smalls], [myKTt[d]])
                yield
                P.op("dve", lambda e: e.tensor_tensor(v3(KH_), v3(myKTt[d].t[0:dk, sp0:sp1]), ehb, ALU.mult), [myKTt[d], myEX], [KHt])
                kb = fm_bank()
                pvk = pst(kb).bitcast(BF16).rearrange("p (b t) -> p b t", b=8)
                ng = TS // 128
                for i in range(ng):
                    P.op("pe", lambda e, i=i: e.transpose(pvk[:, i, 0:dk], KH_[:, i * 128:(i + 1) * 128], identB.t[0:dk, 0:dk]), [KHt, identB], PQ(kb, 0, 512))
                copy_op(evac_eng(), myKTM[d].t[:, s_ * ng:(s_ + 1) * ng, 0:dk], pvk[:, 0:ng, 0:dk], PQ(kb, 0, 512), [myKTM[d]])
                yield

            for s0_ in range(0, NSPAN, 2):
                gens = []
                for s_ in (s0_, s0_ + 1):
                    tiles = list(range(s_ * TS // 512, (s_ + 1) * TS // 512))
                    for d in range(2):
                        gens.append(decay(d, s_, tiles, s_ * TS, (s_ + 1) * TS))
                alive = [True] * len(gens)
                while any(alive):
                    for gi in range(len(gens)):
                        if alive[gi]:
                            try:
                                next(gens[gi])
                            except StopIteration:
                                alive[gi] = False
                    yield
            for d in range(2):
                sel = d
                mc = mchain.t[0:dk, d * NBK:(d + 1) * NBK]
                P.op("dve", lambda e, d=d, sel=sel, mc=mc: e.tensor_tensor(mySCL.t[0:dk, d, :, 0], myEX.t[0:dk, d, :, sel], mc, ALU.mult), [myEX, mchain], [mySCL])
                P.op("dve", lambda e, d=d, sel=sel: e.tensor_tensor(mySCL.t[0:dk, d, :, 1], mySCL.t[0:dk, d, :, 0], myEX.t[0:dk, d, :, 1 - sel], ALU.mult), [myEX, mySCL], [mySCL])
            yield
            if hh < 4:
                yield from mod_computeT(2 + hh, w)

        def stageC(hh):
            cfg = head_cfg(hh)
            gla, h, dk, po = cfg["gla"], cfg["h"], cfg["dk"], cfg["po"]
            pq = slice(po, po + dk)
            pb = hh % 2
            myQTt, myKTt, myKTM, myVt, myGG, mySCL = QTt, KTt, KTM, Vt[pb], GG[pb], SCL
            kmt = {"n": 0}
            pv7 = pst(7).bitcast(BF16).rearrange("p (r s t) -> p r s t", r=2, s=4)
            grp_done = {}
            for d in range(2):
                src = (sig_d if gla else sih_d)[d, h]
                P.dma("sp", ST[d][0].t[pq], src, [], [ST[d][0]])
            state = {0: ST[0][0], 1: ST[1][0]}
            nxt = [1, 1]

            def chain_p(d, n, cb, after=()):
                g, hf = n // 2, n % 2
                return P.op("pe", lambda e: e.matmul(pst(cb)[pq, d * 128:(d + 1) * 128], myKTM[d].t[hf * 64:(hf + 1) * 64, g, pq], myVt.t[hf * 64:(hf + 1) * 64, g, :], start=True, stop=True),
                            [myKTM[d], myVt], PQ(cb, 0, 256), extra=after)

            def chain_step(d, n, cb):
                prev = state[d]
                new = ST[d][nxt[d]]
                nxt[d] = (nxt[d] + 1) % int(_os.environ.get('DBG_TRI', '3'))
                P.op("act", lambda e: e.activation(SQt[d].t[pq, n, :], prev.t[pq], AF.Copy, scale=mySCL.t[pq, d, n, 0:1]), [prev, mySCL], [SQt[d]])
                P.op("dve", lambda e: e.scalar_tensor_tensor(new.t[pq], prev.t[pq], mySCL.t[pq, d, n, 1:2], pst(cb)[pq, d * 128:(d + 1) * 128], op0=ALU.mult, op1=ALU.add),
                     [prev, mySCL] + PQ(cb, 0, 256), [new])
                state[d] = new
                if (d == 0 and n % 4 == 3) or (d == 1 and n % 4 == 0):
                    dst = (sog_d if gla else soh_d)[n // 4, d, h]
                    P.dma("sp", dst, new.t[pq], [new], [], is_out=True)

            gctr = {"n": 0}
            ginfo = {}

            def og_a1(g):
                i = gctr["n"]
                gctr["n"] += 1
                ginfo[g] = i
                blk = slice(g * 128, (g + 1) * 128)
                for d in range(2):
                    P.op("pe", lambda e, d=d: e.matmul(pst(5)[:, d * 128:(d + 1) * 128], myKTt[d].t[pq, blk], myQTt[d].t[pq, blk], start=True, stop=True),
                         [myKTt[d], myQTt[d]], PQ(5, 0, 256))

            def og_a2(g):
                at = ATt[ginfo[g] % 4]
                P.op("dve", lambda e: e.tensor_tensor(at.t, pst(5)[:, 0:256], maskT2.t, ALU.mult), PQ(5, 0, 256) + [maskT2], [at])

            def og_b(g):
                i = ginfo[g]
                at = ATt[i % 4]
                ob_ = [2, 6][i % 2]
                og = pst(ob_)[:, 0:128]
                otok = PQ(ob_, 0, 128)
                P.op("pe", lambda e: e.matmul(og, at.t[:, 0:128], myVt.t[:, g, :], start=True, stop=False), [at, myVt], otok)
                P.op("pe", lambda e: e.matmul(og, at.t[:, 128:256], myVt.t[:, g, :], start=False, stop=False), [at, myVt], otok)
                for hf in range(2):
                    for d in range(2):
                        last = (hf == 1 and d == 1)
                        c0 = g * 128 + hf * 64
                        P.op("pe", lambda e, hf=hf, d=d, last=last, c0=c0: e.matmul(pst(ob_)[hf * 64:(hf + 1) * 64, 0:128], myQTt[d].t[pq, c0:c0 + 64], SQt[d].t[pq, 2 * g + hf, :],
                                                                                   start=False, stop=last), [myQTt[d], SQt[d]], otok)

            def og_c(g):
                i = ginfo[g]
                ob_ = [2, 6][i % 2]
                og = pst(ob_)[:, 0:128]
                otok = PQ(ob_, 0, 128)
                so = sto[i % 4]
                P.op("act", lambda e: e.activation(junk2.t, og, AF.Square, accum_out=so.t[:, 0:1]), otok, [junk2, so], multi=True)
                P.op("act", lambda e: e.activation(so.t[:, 1:2], so.t[:, 0:1], AF.Ln, bias=128.0 * EPS), [so], [so])
                P.op("act", lambda e: e.activation(so.t[:, 2:3], so.t[:, 1:2], AF.Exp, scale=-0.5), [so], [so])

            def og_d(g):
                i = ginfo[g]
                ob_ = [2, 6][i % 2]
                og = pst(ob_)[:, 0:128]
                otok = PQ(ob_, 0, 128)
                so = sto[i % 4]
                mt = MTM[i % 2]
                P.op("dve", lambda e: e.scalar_tensor_tensor(mt.t, og, so.t[:, 2:3], myGG.t[:, g, :], op0=ALU.mult, op1=ALU.mult), otok + [so, myGG], [mt])
                grp = g // 4
                r = grp % 2
                rtok = PQ(7, r * 256, (r + 1) * 256)
                P.op("pe", lambda e: e.transpose(pv7[:, r, g % 4, :], mt.t, identB.t), [mt, identB], rtok)
                grp_done[grp] = grp_done.get(grp, 0) + 1
                if grp_done[grp] == 4:
                    copy_op(evac_eng(), mergedT.t[:, hh, grp * 512:(grp + 1) * 512], pv7[:, r].rearrange("p s t -> p (s t)"), rtok, [mT_tok[hh]])

            ready = {g: max(2 * g + 1, NBK - 1 - 2 * g) for g in range(NG)}
            p_ahead = int(_os.environ.get('DBG_PAHEAD', '1'))
            if p_ahead:
                o_ = chain_p(0, 0, 3)
                chain_p(1, NBK - 1, 3, after=(o_,))
            for s_ in range(NBK + 4):
                if s_ < NBK:
                    cb = 3 + (s_ % 2)
                    if not p_ahead:
                        o_ = chain_p(0, s_, cb)
                        chain_p(1, NBK - 1 - s_, cb, after=(o_,))
                    chain_step(0, s_, cb)
                    chain_step(1, NBK - 1 - s_, cb)
                    if p_ahead and s_ + 1 < NBK:
                        cbn = 3 + ((s_ + 1) % 2)
                        o_ = chain_p(0, s_ + 1, cbn)
                        chain_p(1, NBK - 2 - s_, cbn, after=(o_,))
                for g in range(NG):
                    if ready[g] == s_ - 3:
                        og_d(g)
                for g in range(NG):
                    if ready[g] == s_ - 2:
                        og_c(g)
                for g in range(NG):
                    if ready[g] == s_ - 1:
                        og_b(g)
                pair_now = [g for g in range(NG) if ready[g] == s_ + 3]
                pair_prev = [g for g in range(NG) if ready[g] == s_ + 2]
                pair_prev2 = [g for g in range(NG) if ready[g] == s_ + 1]
                if pair_prev2:
                    og_a2(pair_prev2[1])
                if pair_prev:
                    og_a2(pair_prev[0])
                    og_a1(pair_prev[1])
                if pair_now:
                    og_a1(pair_now[0])
                yield

        heads = [int(v) for v in _os.environ.get('DBG_HEADS', '0,1,2,3,4,5,6,7').split(',') if v != '']
        assert heads == list(range(8))
        load_head_weights(0, dma=False)
        for _ in stageAB1(0):
            pass
        for _ in stageAB2(0):
            pass
        ilv = int(_os.environ.get('DBG_ILV', '2'))
        for hh in range(8):
            cgen = stageC(hh)
            abgen = stageAB1(hh + 1) if hh + 1 < 8 else None
            c_alive, ab_alive = True, abgen is not None
            step = 0
            while c_alive or ab_alive:
                if c_alive:
                    try:
                        next(cgen)
                    except StopIteration:
                        c_alive = False
                if ab_alive and (not c_alive or (ilv >= 1 and step % ilv == 0)):
                    try:
                        next(abgen)
                    except StopIteration:
                        ab_alive = False
                step += 1
            if hh + 1 < 8:
                for _ in stageAB2(hh + 1):
                    pass
        if debug:
            P.dma("sp", dbg["mergedT"], mergedT.t.rearrange("p k t -> p (k t)"), mT_tok, [], is_out=True)
        P.barrier()
        AR.reset(ph1_mark)
        X1 = AR.alloc("X1", [NB, D], F32)
        X1_tok = [Tok("X1_b%d" % b) for b in range(NB)]
        ph3_keep = AR.mark()
        MB[2] = AR.alloc("gate1_bc", [D], F32)
        MB[3] = AR.alloc("shift2_bc", [D], F32)
        MB[4] = AR.alloc("g2_bc", [D], F32)
        WO = AR.alloc("WO", [8, D], BF16)
        wstage = [AR.alloc("wstage%d" % i, [D], F32) for i in range(2)]
        junk = AR.alloc("junk", [D], BF16)
        tmpf = AR.alloc("tmpf", [D], F32)
        hb = [AR.alloc("hb%d" % i, [D], BF16) for i in range(2)]
        stt = [AR.alloc("stt%d" % i, [4], F32) for i in range(3)]
        dgt = [AR.alloc("dgt%d" % i, [128], F32) for i in range(2)]
        n2T = AR.alloc("n2T", [8], F32)
        g2T = AR.alloc("g2T", [8], F32)
        xb = {"n": 0}

        def expand(vec_ap, vec_toks, dst):
            for half in range(2):
                b = xb["n"] % 2
                xb["n"] += 1
                for q in range(4):
                    kc = half * 4 + q
                    dg = dgt[kc % 2]
                    P.op("dve", lambda e, kc=kc, dg=dg: e.tensor_scalar(dg.t, identF.t, vec_ap[:, kc:kc + 1], None, op0=ALU.mult), [identF] + vec_toks, [dg])
                    P.op("pe", lambda e, q=q, b=b, dg=dg: e.matmul(pst(b)[:, q * 128:(q + 1) * 128], onesF.t, dg.t, start=True, stop=True), [onesF, dg], PQ(b, 0, 512))
                copy_op(evac_eng(), dst.t[:, half * 512:(half + 1) * 512], pst(b)[:, :], PQ(b, 0, 512), [dst])

        P.dma("sp", n2T.t, norm2T_d, [], [n2T])
        P.op("dve", lambda e: e.scalar_tensor_tensor(g2T.t, modT.t[:, 2, :], 1.0, n2T.t, op0=ALU.add, op1=ALU.mult), [modT, n2T], [g2T])
        expand(modT.t[:, 0, :], [modT], MB[2])
        expand(modT.t[:, 1, :], [modT], MB[3])
        expand(g2T.t, [g2T], MB[4])
        for kc in range(8):
            ws = wstage[kc % 2]
            P.dma("sp", ws.t, wout_d[kc * 128:(kc + 1) * 128, :], [], [ws])
            gcol = 0 if kc < 4 else 1
            P.op("dve", lambda e, kc=kc, ws=ws, gcol=gcol: e.scalar_tensor_tensor(WO.t[:, kc, :], ws.t, GS[:, gcol:gcol + 1], MB[2].t, op0=ALU.mult, op1=ALU.mult),
                 [ws, smalls, MB[2]], [WO])
        for b in range(NB):
            P.dma("sp", X1.t[:, b, :], x_d[b * 128:(b + 1) * 128, :], [], [X1_tok[b]])
        ob = {"n": 0}
        for b in range(NB):
            for half in range(2):
                bk = 2 + ob["n"] % 2
                ob["n"] += 1
                for kc in range(8):
                    P.op("pe", lambda e, kc=kc, bk=bk, b=b, half=half: e.matmul(pst(bk)[:, :], mergedT.t[:, kc, b * 128:(b + 1) * 128], WO.t[:, kc, half * 512:(half + 1) * 512],
                                                                              start=(kc == 0), stop=(kc == 7)), [mT_tok[kc], WO], PQ(bk, 0, 512))
                hs = slice(half * 512, (half + 1) * 512)
                P.op("dve", lambda e, bk=bk, b=b, hs=hs: e.tensor_tensor(X1.t[:, b, hs], pst(bk)[:, :], X1.t[:, b, hs], ALU.add), PQ(bk, 0, 512) + [X1_tok[b]], [X1_tok[b]])
            norm_A(X1.t[:, b, :], [X1_tok[b]], stt[b % 3])
            if b >= 1:
                norm_B1(X1.t[:, b - 1, :], [X1_tok[b - 1]], MB[4], MB[3], b - 1, stt[(b - 1) % 3])
            if b >= 2:
                norm_B2(b - 2, 4 + ((b - 2) % 2))
            if debug:
                P.dma("sp", dbg["x1"][b * 128:(b + 1) * 128, :], X1.t[:, b, :], [X1_tok[b]], [], is_out=True)
        norm_B1(X1.t[:, NB - 1, :], [X1_tok[NB - 1]], MB[4], MB[3], NB - 1, stt[(NB - 1) % 3])
        norm_B2(NB - 2, 4 + ((NB - 2) % 2))
        norm_B2(NB - 1, 4 + ((NB - 1) % 2))
        if debug:
            dump("h2T", TT(hT.t, hT_tok[0]), 8 * T, BF16)

        P.barrier()
        AR.reset(ph3_keep)
        AR.end = top_keep
        MB[5] = AR.alloc("gate2_bc", [D], F32)
        FN = AR.alloc("fnorm_bc", [D], F32)
        dgt = [AR.alloc("dgt%d" % i, [128], F32) for i in range(2)]
        GMAX = max(j1 - j0 for j0, j1 in GROUPS)
        HT = AR.alloc("HT", [GMAX, T], BF16)
        WD = AR.alloc("WD", [GMAX, D], BF16)
        wdst = [AR.alloc("wdst%d" % i, [D], F32) for i in range(2)]
        UB = [AR.alloc("UB%d" % i, [UW], BF16) for i in range(2)]
        SG = AR.alloc("SG", [T], F32)
        DG = [AR.alloc("DG%d" % i, [11, 128], BF16) for i in range(2)]
        w11T = AR.alloc("w11T", [NCH * 11], F32)
        bconvT = AR.alloc("bconvT", [NCH], F32)
        ring["slots"] = [AR.alloc("wringC%d" % i, [8 * 256], BF16) for i in range(3)]
        ring["n"] = 0
        yst = [AR.alloc("yst%d" % i, [D], F32) for i in range(2)]
        DACC = AR.alloc("DACC", [T], BF16)
        ctmp = yst
        junk = AR.alloc("junk", [D], BF16)
        stt = [AR.alloc("stt%d" % i, [4], F32) for i in range(3)]
        expand(modT.t[:, 3, :], [modT], MB[5])
        P.dma("sp", FN.t, fnorm_d.partition_broadcast(128), [], [FN])
        P.dma("sp", w11T.t, w11T_d, [], [w11T])
        P.dma("sp", bconvT.t, bconvT_d, [], [bconvT])
        for u in range(2):
            P.op("pool", lambda e, u=u: e.memset(UB[u].t, 0.0), [], [UB[u]])
        P.op("pool", lambda e: e.memset(DACC.t, 0.0), [], [DACC])
        identB_b11 = bass.AP(identB.t.tensor, identB.t.offset, [list(identB.t.ap[0]), [0, 11], [1, 128]])
        ucnt = {"n": 0}
        pair_w = {}

        def load_pair(j):
            w = next_w()
            wv = w.t[:, 0:8 * 256].rearrange("p (k n) -> p k n", k=8)
            P.dma("pool", wv[:, :, 0:128], wup_d[:, j * 128:(j + 1) * 128].rearrange("(k p) n -> p k n", p=128), [], [w])
            P.dma("pool", wv[:, :, 128:256], wup_d[:, FFN_H + j * 128:FFN_H + (j + 1) * 128].rearrange("(k p) n -> p k n", p=128), [], [w])
            pair_w[j] = (w, wv)

        def up_proj(j, is_up):
            w, wv = pair_w[j]
            off = 128 if is_up else 0
            u = ucnt["n"] % 2
            ucnt["n"] += 1
            for tt in range(4):
                for kc in range(8):
                    P.op("pe", lambda e, kc=kc, tt=tt: e.matmul(pst(tt)[:, :], wv[:, kc, off:off + 128], hT.t[:, kc, tt * 512:(tt + 1) * 512], start=(kc == 0), stop=(kc == 7)),
                         [w] + hT_tok[4 * tt:4 * tt + 4], PQ(tt, 0, 512))
                P.op("act", lambda e, tt=tt: e.activation(UB[u].t[:, UPAD + tt * 512:UPAD + (tt + 1) * 512], pst(tt)[:, :], AF.Copy), PQ(tt, 0, 512), [UB[u]])
            return u

        ctn = {"n": 0}

        def conv(j, jj, is_up, u):
            cc = (NPAIR + j) if is_up else j
            dg = DG[cc % 2]
            wb = bass.AP(w11T.t.tensor, w11T.t.offset + cc * 11, [list(w11T.t.ap[0]), [1, 11], [0, 128]])
            P.op("pool", lambda e: e.tensor_tensor(dg.t, identB_b11, wb, ALU.mult), [identB, w11T], [dg])
            ub = UB[u].t
            acc3 = DACC.t.rearrange("p (r c) -> p r c", c=64)[:, :, 0:63]
            for k_, dy in enumerate((0, -1, 1)):
                wi = (dy + 1) * 3 + 2
                src3 = ub[:, UPAD + 64 * dy + 1:UPAD + 64 * dy + 1 + T].rearrange("p (r c) -> p r c", c=64)[:, :, 0:63]
                wsc = w11T.t[:, cc * 11 + wi:cc * 11 + wi + 1]
                if k_ == 0:
                    P.op("dve", lambda e, src3=src3, wsc=wsc: e.tensor_scalar(acc3, src3, wsc, None, op0=ALU.mult), [UB[u], w11T], [DACC])
                else:
                    P.op("dve", lambda e, src3=src3, wsc=wsc: e.scalar_tensor_tensor(acc3, src3, wsc, acc3, op0=ALU.mult, op1=ALU.add), [UB[u], w11T, DACC], [DACC])
            for tt in range(4):
                base = UPAD + tt * 512
                pt = pst(4 + tt)
                taps = []
                for dy in (0, -1, 1):
                    taps.append(((dy + 1) * 3 + 1, pt[:, 0:512], ub[:, base + 64 * dy:base + 64 * dy + 512]))
                for dy in (0, -1, 1):
                    o3 = pt[:, 0:512].rearrange("p (r c) -> p r c", c=64)[:, :, 1:64]
                    r3 = ub[:, base + 64 * dy - 1:base + 64 * dy - 1 + 512].rearrange("p (r c) -> p r c", c=64)[:, :, 1:64]
                    taps.append(((dy + 1) * 3 + 0, o3, r3))
                o4 = pt[:, 0:512].rearrange("p (a r c) -> p a r c", a=2, r=4)[:, :, 1:4, 0]
                r4 = ub[:, base - 1:base - 1 + 512].rearrange("p (a r c) -> p a r c", a=2, r=4)[:, :, 1:4, 0]
                taps.append((9, o4, r4))
                o4 = pt[:, 0:512].rearrange("p (a r c) -> p a r c", a=2, r=4)[:, :, 0:3, 63]
                r4 = ub[:, base + 1:base + 1 + 512].rearrange("p (a r c) -> p a r c", a=2, r=4)[:, :, 0:3, 63]
                taps.append((10, o4, r4))
                for ti, (wi, oap, rap) in enumerate(taps):
                    P.op("pe", lambda e, wi=wi, oap=oap, rap=rap, ti=ti: e.matmul(oap, dg.t[:, wi, :], rap, start=(ti == 0), stop=(ti == len(taps) - 1)),
                         [dg, UB[u]], PQ(4 + tt, 0, 512))
                ts_ = slice(tt * 512, (tt + 1) * 512)
                ct = ctmp[ctn["n"] % 2]
                ctn["n"] += 1
                P.op("dve", lambda e, pt=pt, ts_=ts_, ct=ct: e.tensor_tensor(ct.t[:, 0:512], pt[:, 0:512], DACC.t[:, ts_], ALU.add), PQ(4 + tt, 0, 512) + [DACC], [ct])
                if not is_up:
                    P.op("act", lambda e, ts_=ts_, ct=ct: e.activation(SG.t[:, ts_], ct.t[:, 0:512], AF.Silu, bias=bconvT.t[:, cc:cc + 1]), [ct, bconvT], [SG])
                else:
                    P.op("dve", lambda e, ts_=ts_, ct=ct: e.scalar_tensor_tensor(HT.t[:, jj, ts_], ct.t[:, 0:512], bconvT.t[:, cc:cc + 1], SG.t[:, ts_], op0=ALU.add, op1=ALU.mult),
                         [ct, bconvT, SG], [HT])

        fin_pending = []

        def final_out(b):
            st = stt[b % 3]
            ys = yst[b % 2]
            xap = X1.t[:, b, :]
            P.op("dve", lambda e: e.scalar_tensor_tensor(ys.t, xap, st.t[:, 2:3], FN.t, op0=ALU.mult, op1=ALU.mult), [X1_tok[b], st, FN], [ys])
            P.dma("sp", y_d[b * 128:(b + 1) * 128, :], ys.t, [ys], [], is_out=True)

        load_pair(0)
        load_pair(1)
        dbk = {"n": 0}

        def wd_load(gi):
            j0, j1 = GROUPS[gi]
            for jj, j in enumerate(range(j0, j1)):
                wq = wdst[j % 2]
                P.dma("sp", wq.t, wdn_d[j * 128:(j + 1) * 128, :], [], [wq])
                P.op("dve", lambda e, jj=jj, wq=wq: e.tensor_tensor(WD.t[:, jj, :], wq.t, MB[5].t, ALU.mult), [wq, MB[5]], [WD])

        def down(gi):
            j0, j1 = GROUPS[gi]
            last_group = gi == len(GROUPS) - 1
            ng = j1 - j0
            for b in range(NB):
                for half in range(2):
                    bk = dbk["n"] % 4
                    dbk["n"] += 1
                    hs = slice(half * 512, (half + 1) * 512)
                    for jj in range(ng):
                        P.op("pe", lambda e, jj=jj, bk=bk, b=b, hs=hs: e.matmul(pst(bk)[:, :], HT.t[:, jj, b * 128:(b + 1) * 128], WD.t[:, jj, hs], start=(jj == 0), stop=(jj == ng - 1)),
                             [HT, WD], PQ(bk, 0, 512))
                    P.op("dve", lambda e, bk=bk, b=b, hs=hs: e.tensor_tensor(X1.t[:, b, hs], pst(bk)[:, :], X1.t[:, b, hs], ALU.add), PQ(bk, 0, 512) + [X1_tok[b]], [X1_tok[b]])
                if last_group:
                    st = stt[b % 3]
                    xap = X1.t[:, b, :]
                    P.op("act", lambda e, xap=xap, st=st, jk=junk: e.activation(jk.t, xap, AF.Square, accum_out=st.t[:, 0:1]), [X1_tok[b]], [junk, st], multi=True)
                    P.op("act", lambda e, st=st: e.activation(st.t[:, 1:2], st.t[:, 0:1], AF.Ln, scale=1.0 / D, bias=EPS), [st], [st])
                    P.op("act", lambda e, st=st: e.activation(st.t[:, 2:3], st.t[:, 1:2], AF.Exp, scale=-0.5), [st], [st])
                    fin_pending.append(b)
                    if len(fin_pending) > 1:
                        final_out(fin_pending.pop(0))
            if last_group:
                while fin_pending:
                    final_out(fin_pending.pop(0))

        wd_load(0)
        pend = None
        deferred = None
        for gi, (j0, j1) in enumerate(GROUPS):
            for j in range(j0, j1):
                for is_up in (False, True):
                    if (not is_up) and (j + 2 < NPAIR):
                        load_pair(j + 2)
                    u = up_proj(j, is_up)
                    if pend is not None:
                        if deferred is not None and pend[4] == gi and pend[2]:
                            down(deferred)
                            wd_load(gi)
                            deferred = None
                        conv(*pend[:4])
                    pend = (j, j - j0, is_up, u, gi)
            deferred = gi
        if deferred is not None and pend[4] == deferred:
            conv(*pend[:4])
            down(deferred)
        P.finish()
    return nc


def _consts():
    ident = np.eye(128, dtype=np.float32)
    j = np.arange(128)[:, None]
    i = np.arange(128)[None, :]
    same = (j // 64) == (i // 64)
    maskT2 = np.concatenate([(j <= i) & same, (j >= i) & same], axis=1).astype(np.float32)
    scanmask = np.ones((1, T), np.float32)
    scanmask[0, ::64] = 0.0
    return ident, maskT2, scanmask


def prep_core_inputs(inp):
    f32 = lambda a: np.ascontiguousarray(np.asarray(a, dtype=np.float32))
    ident, maskT2, scanmask = _consts()
    shared = {
        "w_ada": f32(inp["w_ada"][0]), "b_ada": f32(inp["b_ada"][0]).reshape(1, -1),
        "norm1": f32(inp["norm1"][0]).reshape(1, -1), "norm2": f32(inp["norm2"][0]).reshape(1, -1),
        "b_adaT": f32(np.asarray(inp["b_ada"][0]).reshape(48, 128).T), "norm2T": f32(np.asarray(inp["norm2"][0]).reshape(8, 128).T),
        "fnorm": f32(inp["final_norm"]).reshape(1, -1),
        "w_in": f32(inp["w_in"][0]), "w_gla_up": f32(inp["w_gla_up"][0]),
        "b_glaT": f32(np.asarray(inp["b_gla"][0]).reshape(2, 2, 128).transpose(2, 0, 1).reshape(128, 4)),
        "lbT": f32(np.asarray(inp["hgrn_lb"]).reshape(2, 2, 4, 128).transpose(3, 0, 1, 2).reshape(128, 16)),
        "gnorm": f32(np.stack([np.asarray(inp["gla_norm"][0]), np.asarray(inp["hgrn_norm"][0])], axis=1)),
        "w_out": f32(inp["w_out"][0]), "w_ffn_up": f32(inp["w_ffn_up"][0]),
        "bconvT": f32(np.asarray(inp["b_ffn_conv"][0]).reshape(NCH, 128).T),
        "w_ffn_down": f32(inp["w_ffn_down"][0]),
        "identF": ident, "maskT2": maskT2, "scanmask": scanmask,
    }
    conv = np.asarray(inp["ffn_conv"][0], dtype=np.float32).reshape(9, 2 * FFN_H)
    zero_row = np.zeros((1, 2 * FFN_H), np.float32)
    rows_s = np.concatenate([conv, zero_row, zero_row], axis=0)
    rows_p = np.concatenate([zero_row] * 3 + [conv[3:6]] + [zero_row] * 3 + [conv[3:4], conv[5:6]], axis=0)
    w11 = lambda rows: f32(rows.reshape(11, NCH, 128).transpose(2, 1, 0).reshape(128, NCH * 11))
    x_prompt = np.asarray(inp["x_prompt"], dtype=np.float32)
    x_sample = np.asarray(inp["x_sample"], dtype=np.float32)
    maps = []
    for c in range(8):
        m = dict(shared)
        if c < 4:
            m["x"] = f32(x_sample[c])
            m["cvT"] = f32(np.asarray(inp["c"][c]).reshape(8, 128).T)
            m["sinit_g"] = f32(inp["state_gla"][c, 0])
            m["sinit_h"] = f32(inp["state_hgrn"][c, 0])
            mf = np.ones(32, np.float32)
            mb = np.ones(32, np.float32)
            m["w11T"] = w11(rows_s)
        else:
            p = c - 4
            m["x"] = f32(x_prompt[8 * p:8 * p + 8].reshape(T, D))
            m["cvT"] = f32(np.asarray(inp["c_ctx"]).reshape(8, 128).T)
            m["sinit_g"] = np.zeros((2, 4, 64, 128), np.float32)
            m["sinit_h"] = np.zeros((2, 4, 128, 128), np.float32)
            mf = (np.arange(32) % 4 != 0).astype(np.float32)
            mb = (np.arange(32) % 4 != 3).astype(np.float32)
            m["w11T"] = w11(rows_p)
        m["mchain"] = f32(np.tile(np.concatenate([mf, mb])[None, :], (128, 1)))
        maps.append(m)
    return maps


_PROGRAM = {}


def kernel(**inputs):
    if "nc" not in _PROGRAM:
        _PROGRAM["nc"] = build_program(debug=False)
    nc = _PROGRAM["nc"]
    in_maps = prep_core_inputs(inputs)
    res = run_bass_kernel_spmd(nc, in_maps, core_ids=list(range(8)))
    r = res.results
    y_sample = np.stack([np.asarray(r[c]["y"], dtype=np.float32) for c in range(4)], axis=0)
    y_prompt = np.concatenate([np.asarray(r[c]["y"], dtype=np.float32).reshape(8, 256, D) for c in range(4, 8)], axis=0)
    sg = np.concatenate([np.asarray(r[c]["snew_g"], dtype=np.float32) for c in range(4, 8)], axis=0)[:, None]
    sh = np.concatenate([np.asarray(r[c]["snew_h"], dtype=np.float32) for c in range(4, 8)], axis=0)[:, None]
    return (y_prompt, y_sample, sg, sh)
```

```python
import numpy as np
from contextlib import ExitStack
import concourse.bass as bass
import concourse.mybir as mybir
from concourse.bass_utils import run_bass_kernel_spmd

F32 = mybir.dt.float32
BF16 = mybir.dt.bfloat16
AF = mybir.ActivationFunctionType
ALU = mybir.AluOpType


class Tok:
    __slots__ = ("name", "w", "r", "rd", "excl", "acc")

    def __init__(self, name, excl=False):
        self.name = name
        self.w = None
        self.r = {}
        self.rd = []
        self.excl = excl
        self.acc = {}


class TT:
    def __init__(self, t, tok):
        self.t = t
        self.tok = tok


class _Op:
    __slots__ = ("idx", "eng", "fn", "deps", "is_dma", "sig", "semval", "slot", "is_out", "multi")


def _tok(x):
    return x.tok if isinstance(x, TT) else x


class Prog:
    NSLOT = {"sp": 24, "pool": 16, "act": 8}

    def __init__(self, nc, es):
        self.nc = nc
        self.es = es
        self.ops = []
        self.n_dma = {"sp": 0, "pool": 0, "act": 0}
        self._n = 0
        self.bar = set()

    def sb(self, name, shape, dtype):
        t = self.es.enter_context(self.nc.sbuf_tensor(name, list(shape), dtype))
        return TT(t, Tok(name))

    def ps(self, name):
        t = self.es.enter_context(self.nc.psum_tensor(name, [128, 512], F32))
        return TT(t, Tok(name))

    def tok(self, name):
        return Tok(name)

    def _record(self, eng, fn, reads, writes, is_dma, is_out=False, extra=(), multi=False):
        op = _Op()
        op.multi = multi
        op.idx = len(self.ops)
        op.eng = eng
        op.fn = fn
        op.is_dma = is_dma
        op.sig = False
        op.semval = 0
        op.slot = None
        op.is_out = is_out
        deps = set()
        reads = [_tok(x) for x in reads]
        writes = [_tok(x) for x in writes]

        def consider(pidx, kind):
            p = self.ops[pidx]
            if p.is_dma:
                deps.add(pidx)
                return
            if (not is_dma) and p.eng == eng:
                if eng == "pe":
                    return
                if kind != "raw":
                    return
            deps.add(pidx)

        for t in reads:
            if t.w is not None:
                consider(t.w, "raw")
        for t in writes:
            if t.w is not None:
                consider(t.w, "waw")
            for _, ridx in t.r.items():
                consider(ridx, "war")
            for ridx in t.rd:
                consider(ridx, "war")
        for t in reads + writes:
            if t.excl:
                for e2, aidx in t.acc.items():
                    if e2 != eng:
                        deps.add(aidx)
                t.acc[eng] = op.idx
        for x in extra:
            deps.add(x.idx)
        for pidx in self.bar:
            p = self.ops[pidx]
            if (not is_dma) and (not p.is_dma) and p.eng == eng:
                continue
            deps.add(pidx)
        op.deps = deps
        for t in reads:
            if is_dma:
                t.rd.append(op.idx)
            else:
                t.r[eng] = op.idx
        for t in writes:
            t.w = op.idx
            t.r = {}
            t.rd = []
        if is_dma:
            k = self.n_dma[eng]
            self.n_dma[eng] += 1
            ns = self.NSLOT[eng]
            op.slot = (eng, k % ns)
            op.semval = 16 * (k // ns + 1)
        self.ops.append(op)
        return op

    def op(self, eng, fn, reads, writes, extra=(), multi=False):
        return self._record(eng, fn, reads, writes, False, extra=extra, multi=multi)

    def barrier(self):
        last = {}
        for op in self.ops:
            if op.is_dma:
                last[("d",) + op.slot] = op.idx
            else:
                last[op.eng] = op.idx
        self.bar = set(last.values())

    def dma(self, eng, out_ap, in_ap, reads, writes, is_out=False, **kw):
        def fn(e, out_ap=out_ap, in_ap=in_ap, kw=kw):
            return e.dma_start(out=out_ap, in_=in_ap, **kw)
        return self._record(eng, fn, reads, writes, True, is_out)

    def finish(self):
        nc = self.nc
        es = self.es
        ops = self.ops
        for op in ops:
            for d in op.deps:
                ops[d].sig = True
        engs = ["pe", "act", "dve", "pool", "sp"]
        esem = {e: es.enter_context(nc.semaphore("s_" + e)) for e in engs}
        dsem = {}
        for e, ns in self.NSLOT.items():
            for i in range(min(ns, max(1, self.n_dma[e]))):
                dsem[(e, i)] = es.enter_context(nc.semaphore("d_%s%d" % (e, i)))
        cnt = {e: 0 for e in engs}
        for op in ops:
            if not op.is_dma and op.sig:
                cnt[op.eng] += 1
                op.semval = cnt[op.eng]
        slot_last = {}
        prev_on_slot = {}
        for op in ops:
            if op.is_dma:
                prev_on_slot[op.idx] = slot_last.get(op.slot)
                slot_last[op.slot] = op.idx
        by_eng = {e: [op for op in ops if op.eng == e] for e in engs}

        def sigof(p):
            if p.is_dma:
                return dsem[p.slot], p.semval
            return esem[p.eng], p.semval

        def emit(ename, e):
            waited = {}

            embed = ename in ("dve", "act", "pool")

            for op in by_eng[ename]:
                need = {}
                order = []
                cand = [sigof(ops[d]) for d in sorted(op.deps)]
                if op.is_dma:
                    pv = prev_on_slot[op.idx]
                    if pv is not None:
                        cand.append(sigof(ops[pv]))
                for s, v in cand:
                    key = id(s)
                    if waited.get(key, 0) < v and need.get(key, (None, 0))[1] < v:
                        if key not in need:
                            order.append(key)
                        need[key] = (s, v)
                pend = [need[k] for k in order]
                for s, v in pend:
                    waited[id(s)] = v
                fold = None
                if embed and pend and not op.is_dma and not op.multi:
                    fold = pend.pop()
                for s, v in pend:
                    e.wait_ge(s, v)
                ins = op.fn(e)
                if fold is not None:
                    ins._wait_ge(fold[0], fold[1])
                if op.is_dma:
                    ins.then_inc(dsem[op.slot], 16)
                elif op.sig:
                    ins.then_inc(esem[ename], 1)
            if ename == "sp":
                for slot, idx in slot_last.items():
                    s, v = sigof(ops[idx])
                    if waited.get(id(s), 0) < v:
                        e.wait_ge(s, v)
                        waited[id(s)] = v

        with nc.Block() as block:
            @block.tensor
            def _(e):
                emit("pe", e)

            @block.scalar
            def _(e):
                emit("act", e)

            @block.vector
            def _(e):
                emit("dve", e)

            @block.gpsimd
            def _(e):
                emit("pool", e)

            @block.sync
            def _(e):
                emit("sp", e)
        self.stats = {e: len(by_eng[e]) for e in engs}


D = 1024
T = 2048
NB = 16
EPS = 1e-6
IN_W = 4128
FFN_H = 2816
NCH = 44
NPAIR = 22
UPAD = 65
UW = UPAD + T + UPAD
GROUPS = [(0, 6), (6, 12), (12, 17), (17, 22)]
TS = 512
OFF_QA, OFF_KA, OFF_VA, OFF_GA, OFF_LR = 0, 256, 512, 1024, 1536
OFF_QB, OFF_FB, OFF_IB, OFF_GB = 1568, 2080, 3104, 3616


class Arena:
    def __init__(self, P, nf32):
        self.P = P
        self.tt = P.sb("arena", [128, nf32], F32)
        self.n = nf32
        self.top = 0
        self.end = nf32

    def alloc(self, name, free_shape, dtype, top=False):
        nel = int(np.prod(free_shape))
        nf = nel if dtype == F32 else (nel + 1) // 2
        nf = (nf + 3) // 4 * 4
        assert self.top + nf <= self.end, ("arena overflow", name, self.top, nf, self.end)
        if top:
            self.end -= nf
            ap = self.tt.t[:, self.end:self.end + nf]
        else:
            ap = self.tt.t[:, self.top:self.top + nf]
        if dtype != F32:
            ap = ap.bitcast(dtype)
        ap = ap[:, 0:nel]
        if len(free_shape) == 2:
            ap = ap.rearrange("p (a b) -> p a b", a=free_shape[0])
        elif len(free_shape) == 3:
            ap = ap.rearrange("p (a b c) -> p a b c", a=free_shape[0], b=free_shape[1])
        if not top:
            self.top += nf
        return TT(ap, Tok(name))

    def mark(self):
        return self.top

    def reset(self, m):
        self.top = m


def build_program(debug=False):
    import os as _os
    nc = bass.Bass("TRN2", target_bir_lowering=False)

    def din(name, shape):
        return nc.dram_tensor(name, list(shape), F32, kind="ExternalInput").ap()

    def dout(name, shape, dt=F32):
        return nc.dram_tensor(name, list(shape), dt, kind="ExternalOutput").ap()

    x_d = din("x", [T, D])
    cvT_d = din("cvT", [128, 8])
    wada_d = din("w_ada", [D, 6 * D])
    bada_d = din("b_ada", [1, 6 * D])
    badaT_d = din("b_adaT", [128, 48])
    norm2T_d = din("norm2T", [128, 8])
    norm1_d = din("norm1", [1, D])
    norm2_d = din("norm2", [1, D])
    fnorm_d = din("fnorm", [1, D])
    win_d = din("w_in", [D, IN_W])
    wgu_d = din("w_gla_up", [2, 16, 256])
    bglaT_d = din("b_glaT", [128, 4])
    lbT_d = din("lbT", [128, 16])
    gnorm_d = din("gnorm", [128, 2])
    wout_d = din("w_out", [D, D])
    wup_d = din("w_ffn_up", [D, 2 * FFN_H])
    w11T_d = din("w11T", [128, NCH * 11])
    bconvT_d = din("bconvT", [128, NCH])
    wdn_d = din("w_ffn_down", [FFN_H, D])
    sig_d = din("sinit_g", [2, 4, 64, 128])
    sih_d = din("sinit_h", [2, 4, 128, 128])
    mchain_d = din("mchain", [128, 64])
    identF_d = din("identF", [128, 128])
    maskT2_d = din("maskT2", [128, 256])
    scanmask_d = din("scanmask", [1, T])
    y_d = dout("y", [T, D])
    sog_d = dout("snew_g", [8, 2, 4, 64, 128])
    soh_d = dout("snew_h", [8, 2, 4, 128, 128])
    dbg = {}
    if debug:
        dbg["h1T"] = dout("dbg_h1T", [128, 8 * T], BF16)
        dbg["mod"] = dout("dbg_mod", [128, 6 * D])
        dbg["mergedT"] = dout("dbg_mergedT", [128, 8 * T], BF16)
        dbg["x1"] = dout("dbg_x1", [T, D])

    es = ExitStack()
    with es:
        P = Prog(nc, es)
        AR = Arena(P, 52800)
        dumps = {}

        def dump(name, tt, ncols, dt, parts=128):
            if not debug:
                return
            if name not in dumps:
                dumps[name] = nc.dram_tensor("dd_" + name, [128, ncols], dt, kind="ExternalOutput").ap()
            ap = tt.t
            if len(ap.shape) == 3:
                ap = ap.rearrange("p a b -> p (a b)")
            elif len(ap.shape) == 4:
                ap = ap.rearrange("p a b c -> p (a b c)")
            P.dma("sp", dumps[name][0:parts], ap[0:parts], [tt], [], is_out=True)
        psb = [P.ps("psum%d" % i) for i in range(8)]
        psq = [Tok("psbank%d" % b, excl=True) for b in range(8)]

        def PQ(b, c0, c1):
            return [psq[b]]

        def pst(b):
            return psb[b].t

        rr = {"n": 0}

        def evac_eng():
            rr["n"] += 1
            return "act" if rr["n"] % 2 else "dve"

        def copy_op(eng, out, in_, reads, writes, scale=None):
            if eng == "act":
                if scale is None:
                    P.op("act", lambda e: e.activation(out, in_, AF.Copy), reads, writes)
                else:
                    P.op("act", lambda e: e.activation(out, in_, AF.Copy, scale=scale), reads, writes)
            else:
                if scale is None:
                    P.op(eng, lambda e: e.tensor_copy(out, in_), reads, writes)
                else:
                    P.op(eng, lambda e: e.tensor_scalar(out, in_, scale, None, op0=ALU.mult), reads, writes)

        identF = AR.alloc("identF", [128], F32)
        identB = AR.alloc("identB", [128], BF16)
        maskT2 = AR.alloc("maskT2", [256], F32)
        scanmask = AR.alloc("scanmask", [TS], BF16)
        mchain = AR.alloc("mchain", [64], F32)
        smalls = AR.alloc("smalls", [64], F32)
        LB = smalls.t[:, 0:8]
        L1M = smalls.t[:, 8:16]
        NEGB = smalls.t[:, 16:24]
        GS = smalls.t[:, 24:26]
        SC = smalls.t[:, 32:40]
        TMPS = smalls.t[:, 40:64]
        MB = [None] * 6
        hT = AR.alloc("hT", [8, T], BF16)
        hT_tok = [Tok("hT_b%d" % b) for b in range(NB)]
        onesF = AR.alloc("onesF", [128], F32)
        ring = {"slots": [], "n": 0}

        def next_w():
            w = ring["slots"][ring["n"] % len(ring["slots"])]
            ring["n"] += 1
            return w

        ph1_mark = AR.mark()
        ringB = [AR.alloc("wringB%d" % i, [8 * 640], BF16) for i in range(2)]
        ring["slots"] = [TT(ringB[i].t[:, 0:8 * 512], ringB[i].tok) for i in range(2)]
        srep = AR.alloc("srep", [8, 128], BF16)
        brow = [AR.alloc("brow%d" % i, [512], F32) for i in range(2)]
        MB[0] = AR.alloc("mod0", [D], F32)
        MB[1] = AR.alloc("mod1", [D], F32)

        P.dma("sp", identF.t, identF_d, [], [identF])
        P.dma("sp", maskT2.t, maskT2_d, [], [maskT2])
        P.dma("pool", scanmask.t, scanmask_d[:, 0:TS].partition_broadcast(128), [], [scanmask])
        P.dma("sp", mchain.t, mchain_d, [], [mchain])
        P.op("dve", lambda e: e.tensor_copy(identB.t, identF.t), [identF], [identB])
        lbT = AR.alloc("lbT", [16], F32)
        cvT = AR.alloc("cvT", [8], F32)
        bgl = AR.alloc("bgl", [8], F32)
        gnm = AR.alloc("gnm", [2], F32)
        P.dma("sp", lbT.t, lbT_d, [], [lbT])
        P.dma("sp", cvT.t, cvT_d, [], [cvT])
        P.dma("sp", bgl.t[:, 0:4], bglaT_d, [], [bgl])
        P.dma("sp", gnm.t, gnorm_d, [], [gnm])
        DD = TMPS[:, 0:8]
        EE = TMPS[:, 8:16]
        P.op("dve", lambda e: e.tensor_tensor(DD, lbT.t[:, 8:16], lbT.t[:, 0:8], ALU.subtract), [lbT], [smalls])
        P.op("act", lambda e: e.activation(EE, DD, AF.Exp), [smalls], [smalls])
        P.op("act", lambda e: e.activation(EE, EE, AF.Ln, bias=1.0), [smalls], [smalls])
        P.op("act", lambda e: e.activation(LB, EE, AF.Exp, scale=-1.0), [smalls], [smalls])
        P.op("dve", lambda e: e.tensor_tensor(L1M, DD, EE, ALU.subtract), [smalls], [smalls])
        P.op("dve", lambda e: e.tensor_scalar(NEGB[:, 0:4], bgl.t[:, 0:4], -1.0, None, op0=ALU.mult), [bgl], [smalls])
        P.op("dve", lambda e: e.tensor_scalar(GS, gnm.t, float(np.sqrt(128.0)), None, op0=ALU.mult), [gnm], [smalls])
        E2 = TMPS[:, 16:24]
        P.op("act", lambda e: e.activation(E2, cvT.t, AF.Exp, scale=-1.0), [cvT, smalls], [smalls])
        P.op("dve", lambda e: e.tensor_scalar(E2, E2, 1.0, None, op0=ALU.add), [smalls], [smalls])
        P.op("dve", lambda e: e.reciprocal(E2, E2), [smalls], [smalls])
        P.op("dve", lambda e: e.tensor_tensor(SC, cvT.t, E2, ALU.mult), [cvT, smalls], [smalls])
        sc_b = bass.AP(smalls.t.tensor, smalls.t.offset + 32, [list(smalls.t.ap[0]), [1, 8], [0, 128]])
        P.op("dve", lambda e: e.tensor_copy(srep.t, sc_b), [smalls], [srep])
        srepF = AR.alloc("srepF", [8, 128], F32)
        P.op("dve", lambda e: e.tensor_copy(srepF.t, sc_b), [smalls], [srepF])
        wF32 = [AR.alloc("wF32_%d" % i, [8, 512], F32) for i in range(2)]
        nrm = AR.alloc("nrm_bc", [D], F32)
        P.op("pool", lambda e: e.memset(onesF.t, 1.0), [], [onesF])
        pbank = {"n": 0}

        def mod_compute(j):
            col = [0, 1, 2, 3, 4, 5][j]
            for n in range(2):
                c0 = col * D + n * 512
                if n == 0:
                    w = next_w()
                    wv = w.t[:, 0:8 * 512].rearrange("p (k n) -> p k n", k=8)
                    P.dma("pool", wv, wada_d[:, c0:c0 + 512].rearrange("(k p) n -> p k n", p=128), [], [w])
                    sr = srep
                else:
                    w = wF32[j % 2]
                    wv = w.t
                    P.dma("sp", wv, wada_d[:, c0:c0 + 512].rearrange("(k p) n -> p k n", p=128), [], [w])
                    sr = srepF
                br = brow[(2 * j + n) % 2]
                P.dma("sp", br.t[0:1, :], bada_d[:, c0:c0 + 512], [], [br])
                b = pbank["n"] % 2
                pbank["n"] += 1
                for kc in range(8):
                    P.op("pe", lambda e, kc=kc, b=b, wv=wv, sr=sr: e.matmul(pst(b)[:, :], sr.t[:, kc, :], wv[:, kc, :], start=(kc == 0), stop=False),
                         [sr, w], PQ(b, 0, 512))
                P.op("pe", lambda e, b=b, br=br: e.matmul(pst(b)[:, :], onesF.t[0:1, :], br.t[0:1, :], start=False, stop=True),
                     [onesF, br], PQ(b, 0, 512))
                copy_op(evac_eng(), MB[j].t[:, n * 512:(n + 1) * 512], pst(b)[:, :], PQ(b, 0, 512), [MB[j]])

        mod_compute(0)
        mod_compute(1)
        P.dma("sp", nrm.t, norm1_d.partition_broadcast(128), [], [nrm])
        P.op("dve", lambda e: e.scalar_tensor_tensor(MB[1].t, MB[1].t, 1.0, nrm.t, op0=ALU.add, op1=ALU.mult), [MB[1], nrm], [MB[1]])

        xring = [AR.alloc("xring%d" % i, [D], F32) for i in range(3)]
        junk = AR.alloc("junk", [D], BF16)
        tmpf = AR.alloc("tmpf", [D], F32)
        hb = [AR.alloc("hb%d" % i, [D], BF16) for i in range(2)]
        stt = [AR.alloc("stt%d" % i, [4], F32) for i in range(3)]

        def norm_A(src_ap, src_toks, st):
            jk = junk
            P.op("act", lambda e: e.activation(jk.t, src_ap, AF.Square, accum_out=st.t[:, 0:1]), src_toks, [jk, st], multi=True)
            P.op("act", lambda e: e.activation(st.t[:, 1:2], st.t[:, 0:1], AF.Ln, scale=1.0 / D, bias=EPS), [st], [st])
            P.op("act", lambda e: e.activation(st.t[:, 2:3], st.t[:, 1:2], AF.Exp, scale=-0.5), [st], [st])

        def norm_B1(src_ap, src_toks, g_t, s_t, b, st):
            tf, h = tmpf, hb[b % 2]
            P.op("dve", lambda e: e.scalar_tensor_tensor(tf.t, src_ap, st.t[:, 2:3], g_t.t, op0=ALU.mult, op1=ALU.mult),
                 src_toks + [st, g_t], [tf])
            P.op("dve", lambda e: e.tensor_tensor(h.t, tf.t, s_t.t, ALU.add), [tf, s_t], [h])

        def norm_B2(b, pbk):
            h = hb[b % 2]
            pv = pst(pbk).bitcast(BF16).rearrange("p (k t) -> p k t", k=8)
            for kc in range(8):
                P.op("pe", lambda e, kc=kc: e.transpose(pv[:, kc, :], h.t[:, kc * 128:(kc + 1) * 128], identB.t),
                     [h, identB], PQ(pbk, 0, 512))
            copy_op("act", hT.t[:, :, b * 128:(b + 1) * 128], pv, PQ(pbk, 0, 512), [hT_tok[b]])

        _w0 = ringB[0]
        _wv0 = _w0.t[:, 0:8 * 640].rearrange("p (k n) -> p k n", k=8)
        for (c0_, n_, o_) in [(OFF_QA, 128, 0), (OFF_KA, 128, 128), (OFF_VA, 128, 256), (OFF_GA, 128, 384)]:
            P.dma("pool", _wv0[:, :, o_:o_ + n_], win_d[:, c0_:c0_ + n_].rearrange("(k p) n -> p k n", p=128), [], [_w0])
        for b in range(NB + 2):
            if b < NB:
                xt = xring[b % 3]
                P.dma("sp", xt.t, x_d[b * 128:(b + 1) * 128, :], [], [xt])
                norm_A(xt.t, [xt], stt[b % 3])
            if 1 <= b <= NB:
                xp = xring[(b - 1) % 3]
                norm_B1(xp.t, [xp], MB[1], MB[0], b - 1, stt[(b - 1) % 3])
            if b >= 2:
                norm_B2(b - 2, 2 + ((b - 2) % 2))
        if debug:
            P.dma("sp", dbg["h1T"], hT.t.rearrange("p k t -> p (k t)"), hT_tok, [], is_out=True)
            for j in range(2):
                P.dma("sp", dbg["mod"][:, j * D:(j + 1) * D], MB[j].t, [MB[j]], [], is_out=True)

        P.barrier()
        AR.reset(ph1_mark)
        BS = 64
        NBK = T // BS
        NG = T // 128
        BPS = TS // BS
        NSPAN = T // TS
        modT = AR.alloc("modT", [4, 8], F32, top=True)
        badaT = AR.alloc("badaT", [48], F32, top=True)
        scb = AR.alloc("scb", [8], BF16, top=True)
        top_keep = AR.end
        mergedT = AR.alloc("mergedT", [8, T], BF16, top=True)
        mT_tok = [Tok("mT_h%d" % i) for i in range(8)]
        for i in range(2):
            AR.alloc("wringB_again%d" % i, [8 * 640], BF16)
        ring["slots"] = ringB
        QT = AR.alloc("QT", [T], F32)
        KT = AR.alloc("KT", [T], F32)
        QTt = [AR.alloc("QTt%d" % d, [T], BF16) for d in range(2)]
        KTt = [AR.alloc("KTt%d" % d, [T], BF16) for d in range(2)]
        KTM = [AR.alloc("KTM%d" % d, [NG, 128], BF16) for d in range(2)]
        LH = AR.alloc("LH", [2, NBK, 2], F32)
        EX = AR.alloc("EX", [2, NBK, 2], F32)
        SCL = AR.alloc("SCL", [2, NBK, 2], F32)
        Vt = [AR.alloc("Vt%d" % pb, [NG, 128], BF16) for pb in range(2)]
        GG = [AR.alloc("GG%d" % pb, [NG, 128], BF16) for pb in range(2)]
        NSET = 4
        SETS = [dict(X=TT(KT.t[:, i * TS:(i + 1) * TS], Tok("Xs%d" % i)), U=AR.alloc("Us%d" % i, [TS], F32), T2=AR.alloc("T2s%d" % i, [TS], F32),
                     B=AR.alloc("Bs%d" % i, [TS], F32), KH=AR.alloc("KHt%d" % i, [TS], BF16)) for i in range(NSET)]
        SQt = [AR.alloc("SQt%d" % d, [NBK, 128], BF16) for d in range(2)]
        ST = [[AR.alloc("ST%d_%d" % (d, i), [128], F32) for i in range(3)] for d in range(2)]
        ATt = [AR.alloc("AT%d" % i, [256], BF16) for i in range(4)]
        MTM = [AR.alloc("MTM%d" % i, [128], BF16) for i in range(2)]
        sto = [AR.alloc("sto%d" % i, [4], F32) for i in range(4)]
        junk2 = AR.alloc("junk2", [128], BF16)
        LRT = AR.alloc("LRT", [2, T], BF16)
        WLR = AR.alloc("WLR", [8, 32], BF16)
        WGU = AR.alloc("WGU", [2, 256], BF16)

        def bcol(Bap, parts, col, nblk):
            pstep = Bap.ap[0][0]
            return bass.AP(Bap.tensor, Bap.offset + col, [[pstep, parts], [BS, nblk], [0, BS]])

        P.dma("pool", WLR.t, win_d[:, OFF_LR:OFF_LR + 32].rearrange("(k p) n -> p k n", p=128), [], [WLR])
        P.dma("pool", WGU.t[0:16], wgu_d.rearrange("d r k -> r d k"), [], [WGU])
        P.dma("sp", badaT.t, badaT_d, [], [badaT])
        P.op("dve", lambda e: e.tensor_copy(scb.t, SC), [smalls], [scb])
        fmb = {"n": 0}

        def fm_bank():
            fmb["n"] += 1
            return fmb["n"] % 2

        def head_cfg(hh):
            gla = hh < 4
            h = hh if gla else hh - 4
            if gla:
                p = h // 2
                owner = (h % 2 == 0)
                if owner:
                    groups = [(OFF_QA + 128 * p, 128, 0), (OFF_KA + 128 * p, 128, 128), (OFF_VA + 128 * h, 128, 256), (OFF_GA + 128 * h, 128, 384)]
                    c_vg = 256
                else:
                    groups = [(OFF_VA + 128 * h, 128, 0), (OFF_GA + 128 * h, 128, 128)]
                    c_vg = 0
                return dict(gla=True, h=h, p=p, owner=owner, dk=64, po=64 * (h % 2), dscale=-1.0 / 16.0, groups=groups, c_q=0, c_k=128, c_vg=c_vg, c_f=None)
            groups = [(OFF_QB + 128 * h, 128, 0), (OFF_FB + 128 * h, 128, 128), (OFF_FB + 512 + 128 * h, 128, 256),
                      (OFF_IB + 128 * h, 128, 384), (OFF_GB + 128 * h, 128, 512)]
            return dict(gla=False, h=h, p=0, owner=True, dk=128, po=0, dscale=1.0, groups=groups, c_q=0, c_k=None, c_vg=384, c_f=(128, 256))

        head_w = {}

        def load_head_weights(hh, dma=True):
            cfg = head_cfg(hh)
            w = ring["slots"][hh % 2]
            wv = w.t[:, 0:8 * 640].rearrange("p (k n) -> p k n", k=8)
            if dma:
                for (c0, n, o) in cfg["groups"]:
                    P.dma("pool", wv[:, :, o:o + n], win_d[:, c0:c0 + n].rearrange("(k p) n -> p k n", p=128), [], [w])
            head_w[hh] = (w, wv)

        def mod_computeT(j, w):
            b = fm_bank()
            for n in range(4):
                c0 = j * D + n * 256
                half = n % 2
                wv = w.t[:, half * 2048:(half + 1) * 2048].rearrange("p (k n) -> p k n", k=8)
                P.dma("pool", wv, wada_d[:, c0:c0 + 256].rearrange("(k p) n -> p k n", p=128), [], [w])
                for mb in range(2):
                    for kc in range(8):
                        P.op("pe", lambda e, kc=kc, mb=mb, n=n, wv=wv: e.matmul(pst(b)[:, 2 * n + mb:2 * n + mb + 1], wv[:, kc, mb * 128:(mb + 1) * 128], scb.t[:, kc:kc + 1],
                                                                              start=(kc == 0), stop=(kc == 7)), [w, scb], PQ(b, 0, 512))
                yield
            copy_op("dve", modT.t[:, j - 2, :], pst(b)[:, 0:8], PQ(b, 0, 512), [modT])
            P.op("dve", lambda e: e.tensor_tensor(modT.t[:, j - 2, :], modT.t[:, j - 2, :], badaT.t[:, j * 8:(j + 1) * 8], ALU.add), [modT, badaT], [modT])

        def stageAB1(hh):
            cfg = head_cfg(hh)
            gla, h, dscale, owner = cfg["gla"], cfg["h"], cfg["dscale"], cfg["owner"]
            dk = 128
            c_q, c_k, c_vg, c_f = cfg["c_q"], cfg["c_k"], cfg["c_vg"], cfg["c_f"]
            pb = hh % 2
            w, wv = head_w[hh]
            if hh + 1 < 8:
                load_head_weights(hh + 1)
            myVt, myGG = Vt[pb], GG[pb]
            Us = SETS[0]["U"]

            def fm_proj(c0, M, tiles, dst_ap_fn, dst_toks, scale=None):
                for tt in tiles:
                    b = fm_bank()
                    for kc in range(8):
                        P.op("pe", lambda e, kc=kc, b=b, tt=tt: e.matmul(pst(b)[0:M, :], wv[:, kc, c0:c0 + M], hT.t[:, kc, tt * 512:(tt + 1) * 512],
                                                                       start=(kc == 0), stop=(kc == 7)),
                             [w] + hT_tok[4 * tt:4 * tt + 4], PQ(b, 0, 512))
                    yield
                    copy_op(evac_eng(), dst_ap_fn(tt), pst(b)[0:M, :], PQ(b, 0, 512), dst_toks, scale=scale)

            if owner:
                yield from fm_proj(c_q, dk, range(4), lambda tt: QT.t[0:dk, tt * 512:(tt + 1) * 512], [QT], scale=(0.125 if gla else None))
            if gla and owner:
                yield from fm_proj(c_k, dk, range(4), lambda tt: KT.t[0:dk, tt * 512:(tt + 1) * 512], [KT])
            if hh == 0:
                for d in range(2):
                    for tt in range(4):
                        b = fm_bank()
                        for kc in range(8):
                            P.op("pe", lambda e, kc=kc, b=b, tt=tt, d=d: e.matmul(pst(b)[0:16, :], WLR.t[:, kc, 16 * d:16 * d + 16], hT.t[:, kc, tt * 512:(tt + 1) * 512],
                                                                                  start=(kc == 0), stop=(kc == 7)),
                                 [WLR] + hT_tok[4 * tt:4 * tt + 4], PQ(b, 0, 512))
                        copy_op(evac_eng(), LRT.t[0:16, d, tt * 512:(tt + 1) * 512], pst(b)[0:16, :], PQ(b, 0, 512), [LRT])
                        yield
            for bp in range(NG // 2):
                bk = fm_bank()
                for i in range(2):
                    blk = 2 * bp + i
                    for kc in range(8):
                        P.op("pe", lambda e, kc=kc, bk=bk, i=i, blk=blk: e.matmul(pst(bk)[:, i * 256:(i + 1) * 256], hT.t[:, kc, blk * 128:(blk + 1) * 128],
                                                                                wv[:, kc, c_vg:c_vg + 256], start=(kc == 0), stop=(kc == 7)),
                             [w, hT_tok[blk]], PQ(bk, i * 256, (i + 1) * 256))
                pv = pst(bk).rearrange("p (b c) -> p b c", b=2)
                yield
                ce = evac_eng()
                copy_op(ce, myVt.t[:, 2 * bp:2 * bp + 2, :], pv[:, :, 0:128], PQ(bk, 0, 512), [myVt])
                copy_op(ce, myGG.t[:, 2 * bp:2 * bp + 2, :], pv[:, :, 128:256], PQ(bk, 0, 512), [myGG])
            GPS = TS // 128
            for s_ in range(NSPAN):
                gsp = myGG.t[:, s_ * GPS:(s_ + 1) * GPS, :].rearrange("p b c -> p (b c)")
                P.op("act", lambda e, gsp=gsp: e.activation(Us.t, gsp, AF.Exp, scale=-1.0), [myGG], [Us])
                P.op("act", lambda e: e.activation(Us.t, Us.t, AF.Ln, bias=1.0), [Us], [Us])
                P.op("act", lambda e: e.activation(Us.t, Us.t, AF.Exp, scale=-1.0), [Us], [Us])
                P.op("dve", lambda e, gsp=gsp: e.tensor_tensor(gsp, gsp, Us.t, ALU.mult), [myGG, Us], [myGG])
                yield

        def stageAB2(hh):
            cfg = head_cfg(hh)
            gla, h, dscale, owner, pr = cfg["gla"], cfg["h"], cfg["dscale"], cfg["owner"], cfg["p"]
            dk = 128
            c_f = cfg["c_f"]
            w, wv = head_w[hh]
            myQTt, myKTt, myKTM, myLH, myEX, mySCL = QTt, KTt, KTM, LH, EX, SCL
            if not owner:
                if hh < 4:
                    yield from mod_computeT(2 + hh, w)
                return

            def fm_proj(c0, M, tiles, dst_ap_fn, dst_toks, scale=None):
                for tt in tiles:
                    b = fm_bank()
                    for kc in range(8):
                        P.op("pe", lambda e, kc=kc, b=b, tt=tt: e.matmul(pst(b)[0:M, :], wv[:, kc, c0:c0 + M], hT.t[:, kc, tt * 512:(tt + 1) * 512],
                                                                       start=(kc == 0), stop=(kc == 7)),
                             [w] + hT_tok[4 * tt:4 * tt + 4], PQ(b, 0, 512))
                    copy_op("act", dst_ap_fn(tt), pst(b)[0:M, :], PQ(b, 0, 512), dst_toks, scale=scale)
                    yield

            v3 = lambda ap: ap.rearrange("p (b t) -> p b t", b=BPS)

            def decay(d, s_, tiles, sp0, sp1):
                st_ = SETS[(s_ % 2) * 2 + d]
                Xs, Us, T2s, Bs, KHt = st_["X"], st_["U"], st_["T2"], st_["B"], st_["KH"]
                X_, U_, T2_, B_, KH_ = Xs.t[0:dk], Us.t[0:dk], T2s.t[0:dk], Bs.t[0:dk], KHt.t[0:dk]
                col = (d * 2 + pr) if gla else (d * 4 + h)
                sel = d
                if gla:
                    for ti, tt in enumerate(tiles):
                        b = fm_bank()
                        P.op("pe", lambda e, b=b, tt=tt: e.matmul(pst(b)[0:128, :], WGU.t[0:16, d, 128 * pr:128 * pr + 128], LRT.t[0:16, d, tt * 512:(tt + 1) * 512],
                                                                start=True, stop=True), [WGU, LRT], PQ(b, 0, 512))
                        P.op("act", lambda e, b=b, ti=ti: e.activation(U_[:, ti * 512:(ti + 1) * 512], pst(b)[0:128, :], AF.Exp, scale=-1.0,
                                                                      bias=NEGB[:, col:col + 1]), PQ(b, 0, 512) + [smalls], [Us])
                    yield
                    P.op("act", lambda e: e.activation(T2_, U_, AF.Ln, bias=1.0), [Us], [T2s])
                    P.op("dve", lambda e: e.tensor_tensor_scan(B_, scanmask.t[0:dk, :], T2_, 0.0, ALU.mult, ALU.add), [scanmask, T2s], [Bs])
                else:
                    cf = c_f[d]
                    yield from fm_proj(cf, 128, tiles, lambda tt: X_[:, (tt - tiles[0]) * 512:(tt - tiles[0] + 1) * 512], [Xs])
                    P.op("act", lambda e: e.activation(U_, X_, AF.Exp, scale=-1.0), [Xs], [Us])
                    P.op("act", lambda e: e.activation(T2_, U_, AF.Ln, bias=1.0), [Us], [T2s])
                    P.op("act", lambda e: e.activation(U_, U_, AF.Ln, bias=1.0, scale=LB[:, col:col + 1]), [Us, smalls], [Us])
                    yield
                    P.op("dve", lambda e: e.tensor_tensor(U_, U_, T2_, ALU.subtract), [Us, T2s], [Us])
                    P.op("act", lambda e: e.activation(X_, X_, AF.Exp), [Xs], [Xs])
                    P.op("act", lambda e: e.activation(X_, X_, AF.Ln, bias=1.0), [Xs], [Xs])
                    P.op("dve", lambda e: e.tensor_tensor_scan(B_, scanmask.t[0:dk, :], U_, 0.0, ALU.mult, ALU.add), [scanmask, Us], [Bs])
                yield
                B3 = v3(B_)
                MID = BS // 2 - 1
                lh = myLH.t[0:dk, d, s_ * BPS:(s_ + 1) * BPS, :]
                ex = myEX.t[0:dk, d, s_ * BPS:(s_ + 1) * BPS, :]
                P.op("dve", lambda e: e.tensor_copy(lh[:, :, 0:1], B3[:, :, MID:MID + 1]), [Bs], [myLH])
                P.op("dve", lambda e: e.tensor_tensor(lh[:, :, 1:2], B3[:, :, BS - 1:BS], B3[:, :, MID:MID + 1], ALU.subtract), [Bs], [myLH])
                P.op("act", lambda e: e.activation(ex, lh, AF.Exp, scale=dscale), [myLH], [myEX])
                bmid = bcol(B_, dk, MID, BPS)
                ehb = bass.AP(myEX.t.tensor, myEX.t[0:dk, d, s_ * BPS:(s_ + 1) * BPS, 1 - sel].offset,
                              [[myEX.t.ap[0][0], dk], [2, BPS], [0, BS]])
                if gla:
                    if d == 0:
                        P.op("dve", lambda e: e.tensor_tensor(v3(U_), B3, bmid, ALU.subtract), [Bs], [Us])
                    else:
                        P.op("dve", lambda e: e.tensor_tensor(T2_, B_, T2_, ALU.subtract), [Bs, T2s], [T2s])
                        P.op("dve", lambda e: e.tensor_tensor(v3(U_), bmid, v3(T2_), ALU.subtract), [Bs, T2s], [Us])
                    P.op("dve", lambda e: e.tensor_scalar(U_, U_, 640.0, -640.0, op0=ALU.min, op1=ALU.max), [Us], [Us])
                    P.op("act", lambda e: e.activation(T2_, U_, AF.Exp, scale=dscale), [Us], [T2s])
                    P.op("pool", lambda e: e.tensor_tensor(myQTt[d].t[0:dk, sp0:sp1], QT.t[0:dk, sp0:sp1], T2_, ALU.mult), [QT, T2s], [myQTt[d]])
                    yield
                    P.op("act", lambda e: e.activation(T2_, U_, AF.Exp, scale=-dscale), [Us], [T2s])
                    P.op("dve", lambda e: e.tensor_tensor(myKTt[d].t[0:dk, sp0:sp1], KT.t[0:dk, sp0:sp1], T2_, ALU.mult), [KT, T2s], [myKTt[d]])
                else:
                    if d == 0:
                        P.op("dve", lambda e: e.tensor_tensor(v3(T2_), B3, bmid, ALU.subtract), [Bs], [T2s])
                    else:
                        P.op("dve", lambda e: e.tensor_tensor(U_, B_, U_, ALU.subtract), [Bs, Us], [Us])
                        P.op("dve", lambda e: e.tensor_tensor(v3(T2_), bmid, v3(U_), ALU.subtract), [Bs, Us], [T2s])
                    P.op("dve", lambda e: e.tensor_scalar(T2_, T2_, 40.0, -40.0, op0=ALU.min, op1=ALU.max), [T2s], [T2s])
                    P.op("act", lambda e: e.activation(U_, T2_, AF.Exp), [T2s], [Us])
                    P.op("pool", lambda e: e.tensor_tensor(myQTt[d].t[0:dk, sp0:sp1], QT.t[0:dk, sp0:sp1], U_, ALU.mult), [QT, Us], [myQTt[d]])
                    yield
                    P.op("dve", lambda e: e.tensor_tensor(X_, X_, T2_, ALU.add), [Xs, T2s], [Xs])
                    P.op("act", lambda e: e.activation(myKTt[d].t[0:dk, sp0:sp1], X_, AF.Exp, scale=-1.0, bias=L1M[:, col:col + 1]), [Xs, smalls], [myKTt[d]])
                yield
                P.op("dve", lambda e: e.tensor_tensor(v3(KH_), v3(myKTt[d].t[0:dk, sp0:sp1]), ehb, ALU.mult), [myKTt[d], myEX], [KHt])
                kb = fm_bank()
                pvk = pst(kb).bitcast(BF16).rearrange("p (b t) -> p b t", b=8)
                ng = TS // 128
                for i in range(ng):
                    P.op("pe", lambda e, i=i: e.transpose(pvk[:, i, 0:dk], KH_[:, i * 128:(i + 1) * 128], identB.t[0:dk, 0:dk]), [KHt, identB], PQ(kb, 0, 512))
                copy_op(evac_eng(), myKTM[d].t[:, s_ * ng:(s_ + 1) * ng, 0:dk], pvk[:, 0:ng, 0:dk], PQ(kb, 0, 512), [myKTM[d]])
                yield

            for s0_ in range(0, NSPAN, 2):
                gens = []
                for s_ in (s0_, s0_ + 1):
                    tiles = list(range(s_ * TS // 512, (s_ + 1) * TS // 512))
                    for d in range(2):
                        gens.append(decay(d, s_, tiles, s_ * TS, (s_ + 1) * TS))
                alive = [True] * len(gens)
                while any(alive):
                    for gi in range(len(gens)):
                        if alive[gi]:
                            try:
                                next(gens[gi])
                            except StopIteration:
                                alive[gi] = False
                    yield
            for d in range(2):
                sel = d
                mc = mchain.t[0:dk, d * NBK:(d + 1) * NBK]
                P.op("dve", lambda e, d=d, sel=sel, mc=mc: e.tensor_tensor(mySCL.t[0:dk, d, :, 0], myEX.t[0:dk, d, :, sel], mc, ALU.mult), [myEX, mchain], [mySCL])
                P.op("dve", lambda e, d=d, sel=sel: e.tensor_tensor(mySCL.t[0:dk, d, :, 1], mySCL.t[0:dk, d, :, 0], myEX.t[0:dk, d, :, 1 - sel], ALU.mult), [myEX, mySCL], [mySCL])
            yield
            if hh < 4:
                yield from mod_computeT(2 + hh, w)

        def stageC(hh):
            cfg = head_cfg(hh)
            gla, h, dk, po = cfg["gla"], cfg["h"], cfg["dk"], cfg["po"]
            pq = slice(po, po + dk)
            pb = hh % 2
            myQTt, myKTt, myKTM, myVt, myGG, mySCL = QTt, KTt, KTM, Vt[pb], GG[pb], SCL
            kmt = {"n": 0}
            pv7 = pst(7).bitcast(BF16).rearrange("p (r s t) -> p r s t", r=2, s=4)
            grp_done = {}
            for d in range(2):
                src = (sig_d if gla else sih_d)[d, h]
                P.dma("sp", ST[d][0].t[pq], src, [], [ST[d][0]])
            state = {0: ST[0][0], 1: ST[1][0]}
            nxt = [1, 1]

            def chain_p(d, n, cb, after=()):
                g, hf = n // 2, n % 2
                return P.op("pe", lambda e: e.matmul(pst(cb)[pq, d * 128:(d + 1) * 128], myKTM[d].t[hf * 64:(hf + 1) * 64, g, pq], myVt.t[hf * 64:(hf + 1) * 64, g, :], start=True, stop=True),
                            [myKTM[d], myVt], PQ(cb, 0, 256), extra=after)

            def chain_step(d, n, cb):
                prev = state[d]
                new = ST[d][nxt[d]]
                nxt[d] = (nxt[d] + 1) % int(_os.environ.get('DBG_TRI', '3'))
                P.op("act", lambda e: e.activation(SQt[d].t[pq, n, :], prev.t[pq], AF.Copy, scale=mySCL.t[pq, d, n, 0:1]), [prev, mySCL], [SQt[d]])
                P.op("dve", lambda e: e.scalar_tensor_tensor(new.t[pq], prev.t[pq], mySCL.t[pq, d, n, 1:2], pst(cb)[pq, d * 128:(d + 1) * 128], op0=ALU.mult, op1=ALU.add),
                     [prev, mySCL] + PQ(cb, 0, 256), [new])
                state[d] = new
                if (d == 0 and n % 4 == 3) or (d == 1 and n % 4 == 0):
                    dst = (sog_d if gla else soh_d)[n // 4, d, h]
                    P.dma("sp", dst, new.t[pq], [new], [], is_out=True)

            gctr = {"n": 0}
            ginfo = {}

            def og_a1(g):
                i = gctr["n"]
                gctr["n"] += 1
                ginfo[g] = i
                blk = slice(g * 128, (g + 1) * 128)
                for d in range(2):
                    P.op("pe", lambda e, d=d: e.matmul(pst(5)[:, d * 128:(d + 1) * 128], myKTt[d].t[pq, blk], myQTt[d].t[pq, blk], start=True, stop=True),
                         [myKTt[d], myQTt[d]], PQ(5, 0, 256))

            def og_a2(g):
                at = ATt[ginfo[g] % 4]
                P.op("dve", lambda e: e.tensor_tensor(at.t, pst(5)[:, 0:256], maskT2.t, ALU.mult), PQ(5, 0, 256) + [maskT2], [at])

            def og_b(g):
                i = ginfo[g]
                at = ATt[i % 4]
                ob_ = [2, 6][i % 2]
                og = pst(ob_)[:, 0:128]
                otok = PQ(ob_, 0, 128)
                P.op("pe", lambda e: e.matmul(og, at.t[:, 0:128], myVt.t[:, g, :], start=True, stop=False), [at, myVt], otok)
                P.op("pe", lambda e: e.matmul(og, at.t[:, 128:256], myVt.t[:, g, :], start=False, stop=False), [at, myVt], otok)
                for hf in range(2):
                    for d in range(2):
                        last = (hf == 1 and d == 1)
                        c0 = g * 128 + hf * 64
                        P.op("pe", lambda e, hf=hf, d=d, last=last, c0=c0: e.matmul(pst(ob_)[hf * 64:(hf + 1) * 64, 0:128], myQTt[d].t[pq, c0:c0 + 64], SQt[d].t[pq, 2 * g + hf, :],
                                                                                   start=False, stop=last), [myQTt[d], SQt[d]], otok)

            def og_c(g):
                i = ginfo[g]
                ob_ = [2, 6][i % 2]
                og = pst(ob_)[:, 0:128]
                otok = PQ(ob_, 0, 128)
                so = sto[i % 4]
                P.op("act", lambda e: e.activation(junk2.t, og, AF.Square, accum_out=so.t[:, 0:1]), otok, [junk2, so], multi=True)
                P.op("act", lambda e: e.activation(so.t[:, 1:2], so.t[:, 0:1], AF.Ln, bias=128.0 * EPS), [so], [so])
                P.op("act", lambda e: e.activation(so.t[:, 2:3], so.t[:, 1:2], AF.Exp, scale=-0.5), [so], [so])

            def og_d(g):
                i = ginfo[g]
                ob_ = [2, 6][i % 2]
                og = pst(ob_)[:, 0:128]
                otok = PQ(ob_, 0, 128)
                so = sto[i % 4]
                mt = MTM[i % 2]
                P.op("dve", lambda e: e.scalar_tensor_tensor(mt.t, og, so.t[:, 2:3], myGG.t[:, g, :], op0=ALU.mult, op1=ALU.mult), otok + [so, myGG], [mt])
                grp = g // 4
                r = grp % 2
                rtok = PQ(7, r * 256, (r + 1) * 256)
                P.op("pe", lambda e: e.transpose(pv7[:, r, g % 4, :], mt.t, identB.t), [mt, identB], rtok)
                grp_done[grp] = grp_done.get(grp, 0) + 1
                if grp_done[grp] == 4:
                    copy_op(evac_eng(), mergedT.t[:, hh, grp * 512:(grp + 1) * 512], pv7[:, r].rearrange("p s t -> p (s t)"), rtok, [mT_tok[hh]])

            ready = {g: max(2 * g + 1, NBK - 1 - 2 * g) for g in range(NG)}
            p_ahead = int(_os.environ.get('DBG_PAHEAD', '1'))
            if p_ahead:
                o_ = chain_p(0, 0, 3)
                chain_p(1, NBK - 1, 3, after=(o_,))
            for s_ in range(NBK + 4):
                if s_ < NBK:
                    cb = 3 + (s_ % 2)
                    if not p_ahead:
                        o_ = chain_p(0, s_, cb)
                        chain_p(1, NBK - 1 - s_, cb, after=(o_,))
                    chain_step(0, s_, cb)
                    chain_step(1, NBK - 1 - s_, cb)
                    if p_ahead and s_ + 1 < NBK:
                        cbn = 3 + ((s_ + 1) % 2)
                        o_ = chain_p(0, s_ + 1, cbn)
                        chain_p(1, NBK - 2 - s_, cbn, after=(o_,))
                for g in range(NG):
                    if ready[g] == s_ - 3:
                        og_d(g)
                for g in range(NG):
                    if ready[g] == s_ - 2:
                        og_c(g)
                for g in range(NG):
                    if ready[g] == s_ - 1:
                        og_b(g)
                pair_now = [g for g in range(NG) if ready[g] == s_ + 3]
                pair_prev = [g for g in range(NG) if ready[g] == s_ + 2]
                pair_prev2 = [g for g in range(NG) if ready[g] == s_ + 1]
                if pair_prev2:
                    og_a2(pair_prev2[1])
                if pair_prev:
                    og_a2(pair_prev[0])
                    og_a1(pair_prev[1])
                if pair_now:
                    og_a1(pair_now[0])
                yield

        heads = [int(v) for v in _os.environ.get('DBG_HEADS', '0,1,2,3,4,5,6,7').split(',') if v != '']
        assert heads == list(range(8))
        load_head_weights(0, dma=False)
        for _ in stageAB1(0):
            pass
        for _ in stageAB2(0):
            pass
        ilv = int(_os.environ.get('DBG_ILV', '2'))
        for hh in range(8):
            cgen = stageC(hh)
            abgen = stageAB1(hh + 1) if hh + 1 < 8 else None
            c_alive, ab_alive = True, abgen is not None
            step = 0
            while c_alive or ab_alive:
                if c_alive:
                    try:
                        next(cgen)
                    except StopIteration:
                        c_alive = False
                if ab_alive and (not c_alive or (ilv >= 1 and step % ilv == 0)):
                    try:
                        next(abgen)
                    except StopIteration:
                        ab_alive = False
                step += 1
            if hh + 1 < 8:
                for _ in stageAB2(hh + 1):
                    pass
        if debug:
            P.dma("sp", dbg["mergedT"], mergedT.t.rearrange("p k t -> p (k t)"), mT_tok, [], is_out=True)
        P.barrier()
        AR.reset(ph1_mark)
        X1 = AR.alloc("X1", [NB, D], F32)
        X1_tok = [Tok("X1_b%d" % b) for b in range(NB)]
        ph3_keep = AR.mark()
        ringC = [AR.alloc("wringC%d" % i, [8 * 256], BF16) for i in range(3)]
        ring["slots"] = ringC
        ring["n"] = 0
        pair_w = {}

        def load_pair(j):
            w = next_w()
            wv = w.t[:, 0:8 * 256].rearrange("p (k n) -> p k n", k=8)
            P.dma("pool", wv[:, :, 0:128], wup_d[:, j * 128:(j + 1) * 128].rearrange("(k p) n -> p k n", p=128), [], [w])
            P.dma("pool", wv[:, :, 128:256], wup_d[:, FFN_H + j * 128:FFN_H + (j + 1) * 128].rearrange("(k p) n -> p k n", p=128), [], [w])
            pair_w[j] = (w, wv)

        load_pair(0)
        load_pair(1)
        MB[2] = AR.alloc("gate1_bc", [D], F32)
        MB[3] = AR.alloc("shift2_bc", [D], F32)
        MB[4] = AR.alloc("g2_bc", [D], F32)
        WO = AR.alloc("WO", [8, D], BF16)
        wstage = [AR.alloc("wstage%d" % i, [D], F32) for i in range(2)]
        junk = AR.alloc("junk", [D], BF16)
        tmpf = AR.alloc("tmpf", [D], F32)
        hb = [AR.alloc("hb%d" % i, [D], BF16) for i in range(2)]
        stt = [AR.alloc("stt%d" % i, [4], F32) for i in range(3)]
        dgt = [AR.alloc("dgt%d" % i, [128], F32) for i in range(2)]
        n2T = AR.alloc("n2T", [8], F32)
        g2T = AR.alloc("g2T", [8], F32)
        xb = {"n": 0}

        def expand(vec_ap, vec_toks, dst):
            for half in range(2):
                b = xb["n"] % 2
                xb["n"] += 1
                for q in range(4):
                    kc = half * 4 + q
                    dg = dgt[kc % 2]
                    P.op("dve", lambda e, kc=kc, dg=dg: e.tensor_scalar(dg.t, identF.t, vec_ap[:, kc:kc + 1], None, op0=ALU.mult), [identF] + vec_toks, [dg])
                    P.op("pe", lambda e, q=q, b=b, dg=dg: e.matmul(pst(b)[:, q * 128:(q + 1) * 128], onesF.t, dg.t, start=True, stop=True), [onesF, dg], PQ(b, 0, 512))
                copy_op(evac_eng(), dst.t[:, half * 512:(half + 1) * 512], pst(b)[:, :], PQ(b, 0, 512), [dst])

        P.dma("sp", n2T.t, norm2T_d, [], [n2T])
        P.op("dve", lambda e: e.scalar_tensor_tensor(g2T.t, modT.t[:, 2, :], 1.0, n2T.t, op0=ALU.add, op1=ALU.mult), [modT, n2T], [g2T])
        expand(modT.t[:, 0, :], [modT], MB[2])
        expand(modT.t[:, 1, :], [modT], MB[3])
        expand(g2T.t, [g2T], MB[4])
        for kc in range(8):
            ws = wstage[kc % 2]
            P.dma("sp", ws.t, wout_d[kc * 128:(kc + 1) * 128, :], [], [ws])
            gcol = 0 if kc < 4 else 1
            P.op("dve", lambda e, kc=kc, ws=ws, gcol=gcol: e.scalar_tensor_tensor(WO.t[:, kc, :], ws.t, GS[:, gcol:gcol + 1], MB[2].t, op0=ALU.mult, op1=ALU.mult),
                 [ws, smalls, MB[2]], [WO])
        for b in range(NB):
            P.dma("sp", X1.t[:, b, :], x_d[b * 128:(b + 1) * 128, :], [], [X1_tok[b]])
        ob = {"n": 0}
        for b in range(NB):
            for half in range(2):
                bk = 2 + ob["n"] % 2
                ob["n"] += 1
                for kc in range(8):
                    P.op("pe", lambda e, kc=kc, bk=bk, b=b, half=half: e.matmul(pst(bk)[:, :], mergedT.t[:, kc, b * 128:(b + 1) * 128], WO.t[:, kc, half * 512:(half + 1) * 512],
                                                                              start=(kc == 0), stop=(kc == 7)), [mT_tok[kc], WO], PQ(bk, 0, 512))
                hs = slice(half * 512, (half + 1) * 512)
                P.op("dve", lambda e, bk=bk, b=b, hs=hs: e.tensor_tensor(X1.t[:, b, hs], pst(bk)[:, :], X1.t[:, b, hs], ALU.add), PQ(bk, 0, 512) + [X1_tok[b]], [X1_tok[b]])
            norm_A(X1.t[:, b, :], [X1_tok[b]], stt[b % 3])
            if b >= 1:
                norm_B1(X1.t[:, b - 1, :], [X1_tok[b - 1]], MB[4], MB[3], b - 1, stt[(b - 1) % 3])
            if b >= 2:
                norm_B2(b - 2, 4 + ((b - 2) % 2))
            if debug:
                P.dma("sp", dbg["x1"][b * 128:(b + 1) * 128, :], X1.t[:, b, :], [X1_tok[b]], [], is_out=True)
        norm_B1(X1.t[:, NB - 1, :], [X1_tok[NB - 1]], MB[4], MB[3], NB - 1, stt[(NB - 1) % 3])
        norm_B2(NB - 2, 4 + ((NB - 2) % 2))
        norm_B2(NB - 1, 4 + ((NB - 1) % 2))
        if debug:
            dump("h2T", TT(hT.t, hT_tok[0]), 8 * T, BF16)

        P.barrier()
        AR.reset(ph3_keep)
        for i in range(3):
            AR.alloc("wringC_again%d" % i, [8 * 256], BF16)
        AR.end = top_keep
        MB[5] = AR.alloc("gate2_bc", [D], F32)
        FN = AR.alloc("fnorm_bc", [D], F32)
        dgt = [AR.alloc("dgt%d" % i, [128], F32) for i in range(2)]
        GMAX = max(j1 - j0 for j0, j1 in GROUPS)
        HT = AR.alloc("HT", [GMAX, T], BF16)
        WD = AR.alloc("WD", [GMAX, D], BF16)
        wdst = [AR.alloc("wdst%d" % i, [D], F32) for i in range(2)]
        UB = [AR.alloc("UB%d" % i, [UW], BF16) for i in range(2)]
        SG = AR.alloc("SG", [T], F32)
        DG = [AR.alloc("DG%d" % i, [11, 128], BF16) for i in range(2)]
        w11T = AR.alloc("w11T", [NCH * 11], F32)
        bconvT = AR.alloc("bconvT", [NCH], F32)
        ring["slots"] = ringC
        yst = [AR.alloc("yst%d" % i, [D], F32) for i in range(2)]
        DACC = AR.alloc("DACC", [T], BF16)
        ctmp = yst
        junk = AR.alloc("junk", [D], BF16)
        stt = [AR.alloc("stt%d" % i, [4], F32) for i in range(3)]
        expand(modT.t[:, 3, :], [modT], MB[5])
        P.dma("sp", FN.t, fnorm_d.partition_broadcast(128), [], [FN])
        P.dma("sp", w11T.t, w11T_d, [], [w11T])
        P.dma("sp", bconvT.t, bconvT_d, [], [bconvT])
        for u in range(2):
            P.op("pool", lambda e, u=u: e.memset(UB[u].t, 0.0), [], [UB[u]])
        P.op("pool", lambda e: e.memset(DACC.t, 0.0), [], [DACC])
        identB_b11 = bass.AP(identB.t.tensor, identB.t.offset, [list(identB.t.ap[0]), [0, 11], [1, 128]])
        ucnt = {"n": 0}

        def up_proj(j, is_up):
            w, wv = pair_w[j]
            off = 128 if is_up else 0
            u = ucnt["n"] % 2
            ucnt["n"] += 1
            for tt in range(4):
                for kc in range(8):
                    P.op("pe", lambda e, kc=kc, tt=tt: e.matmul(pst(tt)[:, :], wv[:, kc, off:off + 128], hT.t[:, kc, tt * 512:(tt + 1) * 512], start=(kc == 0), stop=(kc == 7)),
                         [w] + hT_tok[4 * tt:4 * tt + 4], PQ(tt, 0, 512))
                P.op("act", lambda e, tt=tt: e.activation(UB[u].t[:, UPAD + tt * 512:UPAD + (tt + 1) * 512], pst(tt)[:, :], AF.Copy), PQ(tt, 0, 512), [UB[u]])
            return u

        ctn = {"n": 0}

        def conv(j, jj, is_up, u):
            cc = (NPAIR + j) if is_up else j
            dg = DG[cc % 2]
            wb = bass.AP(w11T.t.tensor, w11T.t.offset + cc * 11, [list(w11T.t.ap[0]), [1, 11], [0, 128]])
            P.op("pool", lambda e: e.tensor_tensor(dg.t, identB_b11, wb, ALU.mult), [identB, w11T], [dg])
            ub = UB[u].t
            acc3 = DACC.t.rearrange("p (r c) -> p r c", c=64)[:, :, 0:63]
            for k_, dy in enumerate((0, -1, 1)):
                wi = (dy + 1) * 3 + 2
                src3 = ub[:, UPAD + 64 * dy + 1:UPAD + 64 * dy + 1 + T].rearrange("p (r c) -> p r c", c=64)[:, :, 0:63]
                wsc = w11T.t[:, cc * 11 + wi:cc * 11 + wi + 1]
                if k_ == 0:
                    P.op("dve", lambda e, src3=src3, wsc=wsc: e.tensor_scalar(acc3, src3, wsc, None, op0=ALU.mult), [UB[u], w11T], [DACC])
                else:
                    P.op("dve", lambda e, src3=src3, wsc=wsc: e.scalar_tensor_tensor(acc3, src3, wsc, acc3, op0=ALU.mult, op1=ALU.add), [UB[u], w11T, DACC], [DACC])
            for tt in range(4):
                base = UPAD + tt * 512
                pt = pst(4 + tt)
                taps = []
                for dy in (0, -1, 1):
                    taps.append(((dy + 1) * 3 + 1, pt[:, 0:512], ub[:, base + 64 * dy:base + 64 * dy + 512]))
                for dy in (0, -1, 1):
                    o3 = pt[:, 0:512].rearrange("p (r c) -> p r c", c=64)[:, :, 1:64]
                    r3 = ub[:, base + 64 * dy - 1:base + 64 * dy - 1 + 512].rearrange("p (r c) -> p r c", c=64)[:, :, 1:64]
                    taps.append(((dy + 1) * 3 + 0, o3, r3))
                o4 = pt[:, 0:512].rearrange("p (a r c) -> p a r c", a=2, r=4)[:, :, 1:4, 0]
                r4 = ub[:, base - 1:base - 1 + 512].rearrange("p (a r c) -> p a r c", a=2, r=4)[:, :, 1:4, 0]
                taps.append((9, o4, r4))
                o4 = pt[:, 0:512].rearrange("p (a r c) -> p a r c", a=2, r=4)[:, :, 0:3, 63]
                r4 = ub[:, base + 1:base + 1 + 512].rearrange("p (a r c) -> p a r c", a=2, r=4)[:, :, 0:3, 63]
                taps.append((10, o4, r4))
                for ti, (wi, oap, rap) in enumerate(taps):
                    P.op("pe", lambda e, wi=wi, oap=oap, rap=rap, ti=ti: e.matmul(oap, dg.t[:, wi, :], rap, start=(ti == 0), stop=(ti == len(taps) - 1)),
                         [dg, UB[u]], PQ(4 + tt, 0, 512))
                ts_ = slice(tt * 512, (tt + 1) * 512)
                ct = ctmp[ctn["n"] % 2]
                ctn["n"] += 1
                P.op("dve", lambda e, pt=pt, ts_=ts_, ct=ct: e.tensor_tensor(ct.t[:, 0:512], pt[:, 0:512], DACC.t[:, ts_], ALU.add), PQ(4 + tt, 0, 512) + [DACC], [ct])
                if not is_up:
                    P.op("act", lambda e, ts_=ts_, ct=ct: e.activation(SG.t[:, ts_], ct.t[:, 0:512], AF.Silu, bias=bconvT.t[:, cc:cc + 1]), [ct, bconvT], [SG])
                else:
                    P.op("dve", lambda e, ts_=ts_, ct=ct: e.scalar_tensor_tensor(HT.t[:, jj, ts_], ct.t[:, 0:512], bconvT.t[:, cc:cc + 1], SG.t[:, ts_], op0=ALU.add, op1=ALU.mult),
                         [ct, bconvT, SG], [HT])

        fin_pending = []

        def final_out(b):
            st = stt[b % 3]
            ys = yst[b % 2]
            xap = X1.t[:, b, :]
            P.op("dve", lambda e: e.scalar_tensor_tensor(ys.t, xap, st.t[:, 2:3], FN.t, op0=ALU.mult, op1=ALU.mult), [X1_tok[b], st, FN], [ys])
            P.dma("sp", y_d[b * 128:(b + 1) * 128, :], ys.t, [ys], [], is_out=True)

        dbk = {"n": 0}

        def wd_load(gi):
            j0, j1 = GROUPS[gi]
            for jj, j in enumerate(range(j0, j1)):
                wq = wdst[j % 2]
                P.dma("sp", wq.t, wdn_d[j * 128:(j + 1) * 128, :], [], [wq])
                P.op("dve", lambda e, jj=jj, wq=wq: e.tensor_tensor(WD.t[:, jj, :], wq.t, MB[5].t, ALU.mult), [wq, MB[5]], [WD])

        def down(gi):
            j0, j1 = GROUPS[gi]
            last_group = gi == len(GROUPS) - 1
            ng = j1 - j0
            for b in range(NB):
                for half in range(2):
                    bk = dbk["n"] % 4
                    dbk["n"] += 1
                    hs = slice(half * 512, (half + 1) * 512)
                    for jj in range(ng):
                        P.op("pe", lambda e, jj=jj, bk=bk, b=b, hs=hs: e.matmul(pst(bk)[:, :], HT.t[:, jj, b * 128:(b + 1) * 128], WD.t[:, jj, hs], start=(jj == 0), stop=(jj == ng - 1)),
                             [HT, WD], PQ(bk, 0, 512))
                    P.op("dve", lambda e, bk=bk, b=b, hs=hs: e.tensor_tensor(X1.t[:, b, hs], pst(bk)[:, :], X1.t[:, b, hs], ALU.add), PQ(bk, 0, 512) + [X1_tok[b]], [X1_tok[b]])
                if last_group:
                    st = stt[b % 3]
                    xap = X1.t[:, b, :]
                    P.op("act", lambda e, xap=xap, st=st, jk=junk: e.activation(jk.t, xap, AF.Square, accum_out=st.t[:, 0:1]), [X1_tok[b]], [junk, st], multi=True)
                    P.op("act", lambda e, st=st: e.activation(st.t[:, 1:2], st.t[:, 0:1], AF.Ln, scale=1.0 / D, bias=EPS), [st], [st])
                    P.op("act", lambda e, st=st: e.activation(st.t[:, 2:3], st.t[:, 1:2], AF.Exp, scale=-0.5), [st], [st])
                    fin_pending.append(b)
                    if len(fin_pending) > 1:
                        final_out(fin_pending.pop(0))
            if last_group:
                while fin_pending:
                    final_out(fin_pending.pop(0))

        wd_load(0)
        pend = None
        deferred = None
        for gi, (j0, j1) in enumerate(GROUPS):
            for j in range(j0, j1):
                for is_up in (False, True):
                    if (not is_up) and (j + 2 < NPAIR):
                        load_pair(j + 2)
                    u = up_proj(j, is_up)
                    if pend is not None:
                        if deferred is not None and pend[4] == gi and pend[2]:
                            down(deferred)
                            wd_load(gi)
                            deferred = None
                        conv(*pend[:4])
                    pend = (j, j - j0, is_up, u, gi)
            deferred = gi
        if deferred is not None and pend[4] == deferred:
            conv(*pend[:4])
            down(deferred)
        P.finish()
    return nc


def _consts():
    ident = np.eye(128, dtype=np.float32)
    j = np.arange(128)[:, None]
    i = np.arange(128)[None, :]
    same = (j // 64) == (i // 64)
    maskT2 = np.concatenate([(j <= i) & same, (j >= i) & same], axis=1).astype(np.float32)
    scanmask = np.ones((1, T), np.float32)
    scanmask[0, ::64] = 0.0
    return ident, maskT2, scanmask


def prep_core_inputs(inp):
    f32 = lambda a: np.ascontiguousarray(np.asarray(a, dtype=np.float32))
    ident, maskT2, scanmask = _consts()
    shared = {
        "w_ada": f32(inp["w_ada"][0]), "b_ada": f32(inp["b_ada"][0]).reshape(1, -1),
        "norm1": f32(inp["norm1"][0]).reshape(1, -1), "norm2": f32(inp["norm2"][0]).reshape(1, -1),
        "b_adaT": f32(np.asarray(inp["b_ada"][0]).reshape(48, 128).T), "norm2T": f32(np.asarray(inp["norm2"][0]).reshape(8, 128).T),
        "fnorm": f32(inp["final_norm"]).reshape(1, -1),
        "w_in": f32(inp["w_in"][0]), "w_gla_up": f32(inp["w_gla_up"][0]),
        "b_glaT": f32(np.asarray(inp["b_gla"][0]).reshape(2, 2, 128).transpose(2, 0, 1).reshape(128, 4)),
        "lbT": f32(np.asarray(inp["hgrn_lb"]).reshape(2, 2, 4, 128).transpose(3, 0, 1, 2).reshape(128, 16)),
        "gnorm": f32(np.stack([np.asarray(inp["gla_norm"][0]), np.asarray(inp["hgrn_norm"][0])], axis=1)),
        "w_out": f32(inp["w_out"][0]), "w_ffn_up": f32(inp["w_ffn_up"][0]),
        "bconvT": f32(np.asarray(inp["b_ffn_conv"][0]).reshape(NCH, 128).T),
        "w_ffn_down": f32(inp["w_ffn_down"][0]),
        "identF": ident, "maskT2": maskT2, "scanmask": scanmask,
    }
    conv = np.asarray(inp["ffn_conv"][0], dtype=np.float32).reshape(9, 2 * FFN_H)
    zero_row = np.zeros((1, 2 * FFN_H), np.float32)
    rows_s = np.concatenate([conv, zero_row, zero_row], axis=0)
    rows_p = np.concatenate([zero_row] * 3 + [conv[3:6]] + [zero_row] * 3 + [conv[3:4], conv[5:6]], axis=0)
    w11 = lambda rows: f32(rows.reshape(11, NCH, 128).transpose(2, 1, 0).reshape(128, NCH * 11))
    x_prompt = np.asarray(inp["x_prompt"], dtype=np.float32)
    x_sample = np.asarray(inp["x_sample"], dtype=np.float32)
    maps = []
    for c in range(8):
        m = dict(shared)
        if c < 4:
            m["x"] = f32(x_sample[c])
            m["cvT"] = f32(np.asarray(inp["c"][c]).reshape(8, 128).T)
            m["sinit_g"] = f32(inp["state_gla"][c, 0])
            m["sinit_h"] = f32(inp["state_hgrn"][c, 0])
            mf = np.ones(32, np.float32)
            mb = np.ones(32, np.float32)
            m["w11T"] = w11(rows_s)
        else:
            p = c - 4
            m["x"] = f32(x_prompt[8 * p:8 * p + 8].reshape(T, D))
            m["cvT"] = f32(np.asarray(inp["c_ctx"]).reshape(8, 128).T)
            m["sinit_g"] = np.zeros((2, 4, 64, 128), np.float32)
            m["sinit_h"] = np.zeros((2, 4, 128, 128), np.float32)
            mf = (np.arange(32) % 4 != 0).astype(np.float32)
            mb = (np.arange(32) % 4 != 3).astype(np.float32)
            m["w11T"] = w11(rows_p)
        m["mchain"] = f32(np.tile(np.concatenate([mf, mb])[None, :], (128, 1)))
        maps.append(m)
    return maps


_PROGRAM = {}


def kernel(**inputs):
    if "nc" not in _PROGRAM:
        _PROGRAM["nc"] = build_program(debug=False)
    nc = _PROGRAM["nc"]
    in_maps = prep_core_inputs(inputs)
    res = run_bass_kernel_spmd(nc, in_maps, core_ids=list(range(8)))
    r = res.results
    y_sample = np.stack([np.asarray(r[c]["y"], dtype=np.float32) for c in range(4)], axis=0)
    y_prompt = np.concatenate([np.asarray(r[c]["y"], dtype=np.float32).reshape(8, 256, D) for c in range(4, 8)], axis=0)
    sg = np.concatenate([np.asarray(r[c]["snew_g"], dtype=np.float32) for c in range(4, 8)], axis=0)[:, None]
    sh = np.concatenate([np.asarray(r[c]["snew_h"], dtype=np.float32) for c in range(4, 8)], axis=0)[:, None]
    return (y_prompt, y_sample, sg, sh)
```

```python
import numpy as np
from contextlib import ExitStack
import concourse.bass as bass
import concourse.mybir as mybir
from concourse.bass_utils import run_bass_kernel_spmd

F32 = mybir.dt.float32
BF16 = mybir.dt.bfloat16
AF = mybir.ActivationFunctionType
ALU = mybir.AluOpType


class Tok:
    __slots__ = ("name", "w", "r", "rd", "excl", "acc")

    def __init__(self, name, excl=False):
        self.name = name
        self.w = None
        self.r = {}
        self.rd = []
        self.excl = excl
        self.acc = {}


class TT:
    def __init__(self, t, tok):
        self.t = t
        self.tok = tok


class _Op:
    __slots__ = ("idx", "eng", "fn", "deps", "is_dma", "sig", "semval", "slot", "is_out", "multi")


def _tok(x):
    return x.tok if isinstance(x, TT) else x


class Prog:
    NSLOT = {"sp": 24, "pool": 16, "act": 8}

    def __init__(self, nc, es):
        self.nc = nc
        self.es = es
        self.ops = []
        self.n_dma = {"sp": 0, "pool": 0, "act": 0}
        self._n = 0
        self.bar = set()

    def sb(self, name, shape, dtype):
        t = self.es.enter_context(self.nc.sbuf_tensor(name, list(shape), dtype))
        return TT(t, Tok(name))

    def ps(self, name):
        t = self.es.enter_context(self.nc.psum_tensor(name, [128, 512], F32))
        return TT(t, Tok(name))

    def tok(self, name):
        return Tok(name)

    def _record(self, eng, fn, reads, writes, is_dma, is_out=False, extra=(), multi=False):
        op = _Op()
        op.multi = multi
        op.idx = len(self.ops)
        op.eng = eng
        op.fn = fn
        op.is_dma = is_dma
        op.sig = False
        op.semval = 0
        op.slot = None
        op.is_out = is_out
        deps = set()
        reads = [_tok(x) for x in reads]
        writes = [_tok(x) for x in writes]

        def consider(pidx, kind):
            p = self.ops[pidx]
            if p.is_dma:
                deps.add(pidx)
                return
            if (not is_dma) and p.eng == eng:
                if eng == "pe":
                    return
                if kind != "raw":
                    return
            deps.add(pidx)

        for t in reads:
            if t.w is not None:
                consider(t.w, "raw")
        for t in writes:
            if t.w is not None:
                consider(t.w, "waw")
            for _, ridx in t.r.items():
                consider(ridx, "war")
            for ridx in t.rd:
                consider(ridx, "war")
        for t in reads + writes:
            if t.excl:
                for e2, aidx in t.acc.items():
                    if e2 != eng:
                        deps.add(aidx)
                t.acc[eng] = op.idx
        for x in extra:
            deps.add(x.idx)
        for pidx in self.bar:
            p = self.ops[pidx]
            if (not is_dma) and (not p.is_dma) and p.eng == eng:
                continue
            deps.add(pidx)
        op.deps = deps
        for t in reads:
            if is_dma:
                t.rd.append(op.idx)
            else:
                t.r[eng] = op.idx
        for t in writes:
            t.w = op.idx
            t.r = {}
            t.rd = []
        if is_dma:
            k = self.n_dma[eng]
            self.n_dma[eng] += 1
            ns = self.NSLOT[eng]
            op.slot = (eng, k % ns)
            op.semval = 16 * (k // ns + 1)
        self.ops.append(op)
        return op

    def op(self, eng, fn, reads, writes, extra=(), multi=False):
        return self._record(eng, fn, reads, writes, False, extra=extra, multi=multi)

    def barrier(self):
        last = {}
        for op in self.ops:
            if op.is_dma:
                last[("d",) + op.slot] = op.idx
            else:
                last[op.eng] = op.idx
        self.bar = set(last.values())

    def dma(self, eng, out_ap, in_ap, reads, writes, is_out=False, **kw):
        def fn(e, out_ap=out_ap, in_ap=in_ap, kw=kw):
            return e.dma_start(out=out_ap, in_=in_ap, **kw)
        return self._record(eng, fn, reads, writes, True, is_out)

    def finish(self):
        nc = self.nc
        es = self.es
        ops = self.ops
        for op in ops:
            for d in op.deps:
                ops[d].sig = True
        engs = ["pe", "act", "dve", "pool", "sp"]
        esem = {e: es.enter_context(nc.semaphore("s_" + e)) for e in engs}
        dsem = {}
        for e, ns in self.NSLOT.items():
            for i in range(min(ns, max(1, self.n_dma[e]))):
                dsem[(e, i)] = es.enter_context(nc.semaphore("d_%s%d" % (e, i)))
        cnt = {e: 0 for e in engs}
        for op in ops:
            if not op.is_dma and op.sig:
                cnt[op.eng] += 1
                op.semval = cnt[op.eng]
        slot_last = {}
        prev_on_slot = {}
        for op in ops:
            if op.is_dma:
                prev_on_slot[op.idx] = slot_last.get(op.slot)
                slot_last[op.slot] = op.idx
        by_eng = {e: [op for op in ops if op.eng == e] for e in engs}

        def sigof(p):
            if p.is_dma:
                return dsem[p.slot], p.semval
            return esem[p.eng], p.semval

        def emit(ename, e):
            waited = {}

            embed = ename in ("dve", "act", "pool")

            for op in by_eng[ename]:
                need = {}
                order = []
                cand = [sigof(ops[d]) for d in sorted(op.deps)]
                if op.is_dma:
                    pv = prev_on_slot[op.idx]
                    if pv is not None:
                        cand.append(sigof(ops[pv]))
                for s, v in cand:
                    key = id(s)
                    if waited.get(key, 0) < v and need.get(key, (None, 0))[1] < v:
                        if key not in need:
                            order.append(key)
                        need[key] = (s, v)
                pend = [need[k] for k in order]
                for s, v in pend:
                    waited[id(s)] = v
                fold = None
                if embed and pend and not op.is_dma and not op.multi:
                    fold = pend.pop()
                for s, v in pend:
                    e.wait_ge(s, v)
                ins = op.fn(e)
                if fold is not None:
                    ins._wait_ge(fold[0], fold[1])
                if op.is_dma:
                    ins.then_inc(dsem[op.slot], 16)
                elif op.sig:
                    ins.then_inc(esem[ename], 1)
            if ename == "sp":
                for slot, idx in slot_last.items():
                    s, v = sigof(ops[idx])
                    if waited.get(id(s), 0) < v:
                        e.wait_ge(s, v)
                        waited[id(s)] = v

        with nc.Block() as block:
            @block.tensor
            def _(e):
                emit("pe", e)

            @block.scalar
            def _(e):
                emit("act", e)

            @block.vector
            def _(e):
                emit("dve", e)

            @block.gpsimd
            def _(e):
                emit("pool", e)

            @block.sync
            def _(e):
                emit("sp", e)
        self.stats = {e: len(by_eng[e]) for e in engs}


D = 1024
T = 2048
NB = 16
EPS = 1e-6
IN_W = 4128
FFN_H = 2816
NCH = 44
NPAIR = 22
UPAD = 65
UW = UPAD + T + UPAD
GROUPS = [(0, 6), (6, 12), (12, 17), (17, 22)]
TS = 512
OFF_QA, OFF_KA, OFF_VA, OFF_GA, OFF_LR = 0, 256, 512, 1024, 1536
OFF_QB, OFF_FB, OFF_IB, OFF_GB = 1568, 2080, 3104, 3616


class Arena:
    def __init__(self, P, nf32):
        self.P = P
        self.tt = P.sb("arena", [128, nf32], F32)
        self.n = nf32
        self.top = 0
        self.end = nf32

    def alloc(self, name, free_shape, dtype, top=False):
        nel = int(np.prod(free_shape))
        nf = nel if dtype == F32 else (nel + 1) // 2
        nf = (nf + 3) // 4 * 4
        assert self.top + nf <= self.end, ("arena overflow", name, self.top, nf, self.end)
        if top:
            self.end -= nf
            ap = self.tt.t[:, self.end:self.end + nf]
        else:
            ap = self.tt.t[:, self.top:self.top + nf]
        if dtype != F32:
            ap = ap.bitcast(dtype)
        ap = ap[:, 0:nel]
        if len(free_shape) == 2:
            ap = ap.rearrange("p (a b) -> p a b", a=free_shape[0])
        elif len(free_shape) == 3:
            ap = ap.rearrange("p (a b c) -> p a b c", a=free_shape[0], b=free_shape[1])
        if not top:
            self.top += nf
        return TT(ap, Tok(name))

    def mark(self):
        return self.top

    def reset(self, m):
        self.top = m


def build_program(debug=False):
    import os as _os
    nc = bass.Bass("TRN2", target_bir_lowering=False)

    def din(name, shape):
        return nc.dram_tensor(name, list(shape), F32, kind="ExternalInput").ap()

    def dout(name, shape, dt=F32):
        return nc.dram_tensor(name, list(shape), dt, kind="ExternalOutput").ap()

    x_d = din("x", [T, D])
    cvT_d = din("cvT", [128, 8])
    wada_d = din("w_ada", [D, 6 * D])
    bada_d = din("b_ada", [1, 6 * D])
    badaT_d = din("b_adaT", [128, 48])
    norm2T_d = din("norm2T", [128, 8])
    norm1_d = din("norm1", [1, D])
    norm2_d = din("norm2", [1, D])
    fnorm_d = din("fnorm", [1, D])
    win_d = din("w_in", [D, IN_W])
    wgu_d = din("w_gla_up", [2, 16, 256])
    bglaT_d = din("b_glaT", [128, 4])
    lbT_d = din("lbT", [128, 16])
    gnorm_d = din("gnorm", [128, 2])
    wout_d = din("w_out", [D, D])
    wup_d = din("w_ffn_up", [D, 2 * FFN_H])
    w11T_d = din("w11T", [128, NCH * 11])
    bconvT_d = din("bconvT", [128, NCH])
    wdn_d = din("w_ffn_down", [FFN_H, D])
    sig_d = din("sinit_g", [2, 4, 64, 128])
    sih_d = din("sinit_h", [2, 4, 128, 128])
    mchain_d = din("mchain", [128, 64])
    identF_d = din("identF", [128, 128])
    maskT2_d = din("maskT2", [128, 256])
    scanmask_d = din("scanmask", [1, T])
    y_d = dout("y", [T, D])
    sog_d = dout("snew_g", [8, 2, 4, 64, 128])
    soh_d = dout("snew_h", [8, 2, 4, 128, 128])
    dbg = {}
    if debug:
        dbg["h1T"] = dout("dbg_h1T", [128, 8 * T], BF16)
        dbg["mod"] = dout("dbg_mod", [128, 6 * D])
        dbg["mergedT"] = dout("dbg_mergedT", [128, 8 * T], BF16)
        dbg["x1"] = dout("dbg_x1", [T, D])

    es = ExitStack()
    with es:
        P = Prog(nc, es)
        AR = Arena(P, 52800)
        dumps = {}

        def dump(name, tt, ncols, dt, parts=128):
            if not debug:
                return
            if name not in dumps:
                dumps[name] = nc.dram_tensor("dd_" + name, [128, ncols], dt, kind="ExternalOutput").ap()
            ap = tt.t
            if len(ap.shape) == 3:
                ap = ap.rearrange("p a b -> p (a b)")
            elif len(ap.shape) == 4:
                ap = ap.rearrange("p a b c -> p (a b c)")
            P.dma("sp", dumps[name][0:parts], ap[0:parts], [tt], [], is_out=True)
        psb = [P.ps("psum%d" % i) for i in range(8)]
        psq = [Tok("psbank%d" % b, excl=True) for b in range(8)]

        def PQ(b, c0, c1):
            return [psq[b]]

        def pst(b):
            return psb[b].t

        rr = {"n": 0}

        def evac_eng():
            rr["n"] += 1
            return "act" if rr["n"] % 2 else "dve"

        def copy_op(eng, out, in_, reads, writes, scale=None):
            if eng == "act":
                if scale is None:
                    P.op("act", lambda e: e.activation(out, in_, AF.Copy), reads, writes)
                else:
                    P.op("act", lambda e: e.activation(out, in_, AF.Copy, scale=scale), reads, writes)
            else:
                if scale is None:
                    P.op(eng, lambda e: e.tensor_copy(out, in_), reads, writes)
                else:
                    P.op(eng, lambda e: e.tensor_scalar(out, in_, scale, None, op0=ALU.mult), reads, writes)

        identF = AR.alloc("identF", [128], F32)
        identB = AR.alloc("identB", [128], BF16)
        maskT2 = AR.alloc("maskT2", [256], F32)
        scanmask = AR.alloc("scanmask", [TS], BF16)
        mchain = AR.alloc("mchain", [64], F32)
        smalls = AR.alloc("smalls", [64], F32)
        LB = smalls.t[:, 0:8]
        L1M = smalls.t[:, 8:16]
        NEGB = smalls.t[:, 16:24]
        GS = smalls.t[:, 24:26]
        SC = smalls.t[:, 32:40]
        TMPS = smalls.t[:, 40:64]
        MB = [None] * 6
        hT = AR.alloc("hT", [8, T], BF16)
        hT_tok = [Tok("hT_b%d" % b) for b in range(NB)]
        onesF = AR.alloc("onesF", [128], F32)
        ring = {"slots": [], "n": 0}

        def next_w():
            w = ring["slots"][ring["n"] % len(ring["slots"])]
            ring["n"] += 1
            return w

        ph1_mark = AR.mark()
        ringB = [AR.alloc("wringB%d" % i, [8 * 640], BF16) for i in range(2)]
        ring["slots"] = [TT(ringB[i].t[:, 0:8 * 512], ringB[i].tok) for i in range(2)]
        srep = AR.alloc("srep", [8, 128], BF16)
        brow = [AR.alloc("brow%d" % i, [512], F32) for i in range(2)]
        MB[0] = AR.alloc("mod0", [D], F32)
        MB[1] = AR.alloc("mod1", [D], F32)

        P.dma("sp", identF.t, identF_d, [], [identF])
        P.dma("sp", maskT2.t, maskT2_d, [], [maskT2])
        P.dma("pool", scanmask.t, scanmask_d[:, 0:TS].partition_broadcast(128), [], [scanmask])
        P.dma("sp", mchain.t, mchain_d, [], [mchain])
        P.op("dve", lambda e: e.tensor_copy(identB.t, identF.t), [identF], [identB])
        lbT = AR.alloc("lbT", [16], F32)
        cvT = AR.alloc("cvT", [8], F32)
        bgl = AR.alloc("bgl", [8], F32)
        gnm = AR.alloc("gnm", [2], F32)
        P.dma("sp", lbT.t, lbT_d, [], [lbT])
        P.dma("sp", cvT.t, cvT_d, [], [cvT])
        P.dma("sp", bgl.t[:, 0:4], bglaT_d, [], [bgl])
        P.dma("sp", gnm.t, gnorm_d, [], [gnm])
        DD = TMPS[:, 0:8]
        EE = TMPS[:, 8:16]
        P.op("dve", lambda e: e.tensor_tensor(DD, lbT.t[:, 8:16], lbT.t[:, 0:8], ALU.subtract), [lbT], [smalls])
        P.op("act", lambda e: e.activation(EE, DD, AF.Exp), [smalls], [smalls])
        P.op("act", lambda e: e.activation(EE, EE, AF.Ln, bias=1.0), [smalls], [smalls])
        P.op("act", lambda e: e.activation(LB, EE, AF.Exp, scale=-1.0), [smalls], [smalls])
        P.op("dve", lambda e: e.tensor_tensor(L1M, DD, EE, ALU.subtract), [smalls], [smalls])
        P.op("dve", lambda e: e.tensor_scalar(NEGB[:, 0:4], bgl.t[:, 0:4], -1.0, None, op0=ALU.mult), [bgl], [smalls])
        P.op("dve", lambda e: e.tensor_scalar(GS, gnm.t, float(np.sqrt(128.0)), None, op0=ALU.mult), [gnm], [smalls])
        E2 = TMPS[:, 16:24]
        P.op("act", lambda e: e.activation(E2, cvT.t, AF.Exp, scale=-1.0), [cvT, smalls], [smalls])
        P.op("dve", lambda e: e.tensor_scalar(E2, E2, 1.0, None, op0=ALU.add), [smalls], [smalls])
        P.op("dve", lambda e: e.reciprocal(E2, E2), [smalls], [smalls])
        P.op("dve", lambda e: e.tensor_tensor(SC, cvT.t, E2, ALU.mult), [cvT, smalls], [smalls])
        sc_b = bass.AP(smalls.t.tensor, smalls.t.offset + 32, [list(smalls.t.ap[0]), [1, 8], [0, 128]])
        P.op("dve", lambda e: e.tensor_copy(srep.t, sc_b), [smalls], [srep])
        srepF = AR.alloc("srepF", [8, 128], F32)
        P.op("dve", lambda e: e.tensor_copy(srepF.t, sc_b), [smalls], [srepF])
        wF32 = [AR.alloc("wF32_%d" % i, [8, 512], F32) for i in range(2)]
        nrm = AR.alloc("nrm_bc", [D], F32)
        P.op("pool", lambda e: e.memset(onesF.t, 1.0), [], [onesF])
        pbank = {"n": 0}

        def mod_compute(j):
            col = [0, 1, 2, 3, 4, 5][j]
            for n in range(2):
                c0 = col * D + n * 512
                if n == 0:
                    w = next_w()
                    wv = w.t[:, 0:8 * 512].rearrange("p (k n) -> p k n", k=8)
                    P.dma("pool", wv, wada_d[:, c0:c0 + 512].rearrange("(k p) n -> p k n", p=128), [], [w])
                    sr = srep
                else:
                    w = wF32[j % 2]
                    wv = w.t
                    P.dma("sp", wv, wada_d[:, c0:c0 + 512].rearrange("(k p) n -> p k n", p=128), [], [w])
                    sr = srepF
                br = brow[(2 * j + n) % 2]
                P.dma("sp", br.t[0:1, :], bada_d[:, c0:c0 + 512], [], [br])
                b = pbank["n"] % 2
                pbank["n"] += 1
                for kc in range(8):
                    P.op("pe", lambda e, kc=kc, b=b, wv=wv, sr=sr: e.matmul(pst(b)[:, :], sr.t[:, kc, :], wv[:, kc, :], start=(kc == 0), stop=False),
                         [sr, w], PQ(b, 0, 512))
                P.op("pe", lambda e, b=b, br=br: e.matmul(pst(b)[:, :], onesF.t[0:1, :], br.t[0:1, :], start=False, stop=True),
                     [onesF, br], PQ(b, 0, 512))
                copy_op(evac_eng(), MB[j].t[:, n * 512:(n + 1) * 512], pst(b)[:, :], PQ(b, 0, 512), [MB[j]])

        mod_compute(0)
        mod_compute(1)
        P.dma("sp", nrm.t, norm1_d.partition_broadcast(128), [], [nrm])
        P.op("dve", lambda e: e.scalar_tensor_tensor(MB[1].t, MB[1].t, 1.0, nrm.t, op0=ALU.add, op1=ALU.mult), [MB[1], nrm], [MB[1]])

        xring = [AR.alloc("xring%d" % i, [D], F32) for i in range(3)]
        junk = AR.alloc("junk", [D], BF16)
        tmpf = AR.alloc("tmpf", [D], F32)
        hb = [AR.alloc("hb%d" % i, [D], BF16) for i in range(2)]
        stt = [AR.alloc("stt%d" % i, [4], F32) for i in range(3)]

        def norm_A(src_ap, src_toks, st):
            jk = junk
            P.op("act", lambda e: e.activation(jk.t, src_ap, AF.Square, accum_out=st.t[:, 0:1]), src_toks, [jk, st], multi=True)
            P.op("act", lambda e: e.activation(st.t[:, 1:2], st.t[:, 0:1], AF.Ln, scale=1.0 / D, bias=EPS), [st], [st])
            P.op("act", lambda e: e.activation(st.t[:, 2:3], st.t[:, 1:2], AF.Exp, scale=-0.5), [st], [st])

        def norm_B1(src_ap, src_toks, g_t, s_t, b, st):
            tf, h = tmpf, hb[b % 2]
            P.op("dve", lambda e: e.scalar_tensor_tensor(tf.t, src_ap, st.t[:, 2:3], g_t.t, op0=ALU.mult, op1=ALU.mult),
                 src_toks + [st, g_t], [tf])
            P.op("dve", lambda e: e.tensor_tensor(h.t, tf.t, s_t.t, ALU.add), [tf, s_t], [h])

        def norm_B2(b, pbk):
            h = hb[b % 2]
            pv = pst(pbk).bitcast(BF16).rearrange("p (k t) -> p k t", k=8)
            for kc in range(8):
                P.op("pe", lambda e, kc=kc: e.transpose(pv[:, kc, :], h.t[:, kc * 128:(kc + 1) * 128], identB.t),
                     [h, identB], PQ(pbk, 0, 512))
            copy_op("act", hT.t[:, :, b * 128:(b + 1) * 128], pv, PQ(pbk, 0, 512), [hT_tok[b]])

        _w0 = ringB[0]
        _wv0 = _w0.t[:, 0:8 * 640].rearrange("p (k n) -> p k n", k=8)
        for (c0_, n_, o_) in [(OFF_QA, 128, 0), (OFF_KA, 128, 128), (OFF_VA, 128, 256), (OFF_GA, 128, 384)]:
            P.dma("pool", _wv0[:, :, o_:o_ + n_], win_d[:, c0_:c0_ + n_].rearrange("(k p) n -> p k n", p=128), [], [_w0])
        for b in range(NB + 2):
            if b < NB:
                xt = xring[b % 3]
                P.dma("sp", xt.t, x_d[b * 128:(b + 1) * 128, :], [], [xt])
                norm_A(xt.t, [xt], stt[b % 3])
            if 1 <= b <= NB:
                xp = xring[(b - 1) % 3]
                norm_B1(xp.t, [xp], MB[1], MB[0], b - 1, stt[(b - 1) % 3])
            if b >= 2:
                norm_B2(b - 2, 2 + ((b - 2) % 2))
        if debug:
            P.dma("sp", dbg["h1T"], hT.t.rearrange("p k t -> p (k t)"), hT_tok, [], is_out=True)
            for j in range(2):
                P.dma("sp", dbg["mod"][:, j * D:(j + 1) * D], MB[j].t, [MB[j]], [], is_out=True)

        P.barrier()
        AR.reset(ph1_mark)
        BS = 64
        NBK = T // BS
        NG = T // 128
        BPS = TS // BS
        NSPAN = T // TS
        modT = AR.alloc("modT", [4, 8], F32, top=True)
        badaT = AR.alloc("badaT", [48], F32, top=True)
        scb = AR.alloc("scb", [8], BF16, top=True)
        top_keep = AR.end
        mergedT = AR.alloc("mergedT", [8, T], BF16, top=True)
        mT_tok = [Tok("mT_h%d" % i) for i in range(8)]
        for i in range(2):
            AR.alloc("wringB_again%d" % i, [8 * 640], BF16)
        ring["slots"] = ringB
        QT = AR.alloc("QT", [T], F32)
        KT = AR.alloc("KT", [T], F32)
        QTt = [AR.alloc("QTt%d" % d, [T], BF16) for d in range(2)]
        KTt = [AR.alloc("KTt%d" % d, [T], BF16) for d in range(2)]
        KTM = [AR.alloc("KTM%d" % d, [NG, 128], BF16) for d in range(2)]
        LH = AR.alloc("LH", [2, NBK, 2], F32)
        EX = AR.alloc("EX", [2, NBK, 2], F32)
        SCL = AR.alloc("SCL", [2, NBK, 2], F32)
        Vt = [AR.alloc("Vt%d" % pb, [NG, 128], BF16) for pb in range(2)]
        GG = [AR.alloc("GG%d" % pb, [NG, 128], BF16) for pb in range(2)]
        NSET = 4
        SETS = [dict(X=TT(KT.t[:, i * TS:(i + 1) * TS], Tok("Xs%d" % i)), U=AR.alloc("Us%d" % i, [TS], F32), T2=AR.alloc("T2s%d" % i, [TS], F32),
                     B=AR.alloc("Bs%d" % i, [TS], F32), KH=AR.alloc("KHt%d" % i, [TS], BF16)) for i in range(NSET)]
        SQt = [AR.alloc("SQt%d" % d, [NBK, 128], BF16) for d in range(2)]
        ST = [[AR.alloc("ST%d_%d" % (d, i), [128], F32) for i in range(3)] for d in range(2)]
        ATt = [AR.alloc("AT%d" % i, [256], BF16) for i in range(4)]
        MTM = [AR.alloc("MTM%d" % i, [128], BF16) for i in range(2)]
        sto = [AR.alloc("sto%d" % i, [4], F32) for i in range(4)]
        junk2 = AR.alloc("junk2", [128], BF16)
        LRT = AR.alloc("LRT", [2, T], BF16)
        WLR = AR.alloc("WLR", [8, 32], BF16)
        WGU = AR.alloc("WGU", [2, 256], BF16)

        def bcol(Bap, parts, col, nblk):
            pstep = Bap.ap[0][0]
            return bass.AP(Bap.tensor, Bap.offset + col, [[pstep, parts], [BS, nblk], [0, BS]])

        P.dma("pool", WLR.t, win_d[:, OFF_LR:OFF_LR + 32].rearrange("(k p) n -> p k n", p=128), [], [WLR])
        P.dma("pool", WGU.t[0:16], wgu_d.rearrange("d r k -> r d k"), [], [WGU])
        P.dma("sp", badaT.t, badaT_d, [], [badaT])
        P.op("dve", lambda e: e.tensor_copy(scb.t, SC), [smalls], [scb])
        fmb = {"n": 0}

        def fm_bank():
            fmb["n"] += 1
            return fmb["n"] % 2

        def head_cfg(hh):
            gla = hh < 4
            h = hh if gla else hh - 4
            if gla:
                p = h // 2
                owner = (h % 2 == 0)
                if owner:
                    groups = [(OFF_QA + 128 * p, 128, 0), (OFF_KA + 128 * p, 128, 128), (OFF_VA + 128 * h, 128, 256), (OFF_GA + 128 * h, 128, 384)]
                    c_vg = 256
                else:
                    groups = [(OFF_VA + 128 * h, 128, 0), (OFF_GA + 128 * h, 128, 128)]
                    c_vg = 0
                return dict(gla=True, h=h, p=p, owner=owner, dk=64, po=64 * (h % 2), dscale=-1.0 / 16.0, groups=groups, c_q=0, c_k=128, c_vg=c_vg, c_f=None)
            groups = [(OFF_QB + 128 * h, 128, 0), (OFF_FB + 128 * h, 128, 128), (OFF_FB + 512 + 128 * h, 128, 256),
                      (OFF_IB + 128 * h, 128, 384), (OFF_GB + 128 * h, 128, 512)]
            return dict(gla=False, h=h, p=0, owner=True, dk=128, po=0, dscale=1.0, groups=groups, c_q=0, c_k=None, c_vg=384, c_f=(128, 256))

        head_w = {}

        def load_head_weights(hh, dma=True):
            cfg = head_cfg(hh)
            w = ring["slots"][hh % 2]
            wv = w.t[:, 0:8 * 640].rearrange("p (k n) -> p k n", k=8)
            if dma:
                for (c0, n, o) in cfg["groups"]:
                    P.dma("pool", wv[:, :, o:o + n], win_d[:, c0:c0 + n].rearrange("(k p) n -> p k n", p=128), [], [w])
            head_w[hh] = (w, wv)

        def mod_computeT(j, w):
            b = fm_bank()
            for n in range(4):
                c0 = j * D + n * 256
                half = n % 2
                wv = w.t[:, half * 2048:(half + 1) * 2048].rearrange("p (k n) -> p k n", k=8)
                P.dma("pool", wv, wada_d[:, c0:c0 + 256].rearrange("(k p) n -> p k n", p=128), [], [w])
                for mb in range(2):
                    for kc in range(8):
                        P.op("pe", lambda e, kc=kc, mb=mb, n=n, wv=wv: e.matmul(pst(b)[:, 2 * n + mb:2 * n + mb + 1], wv[:, kc, mb * 128:(mb + 1) * 128], scb.t[:, kc:kc + 1],
                                                                              start=(kc == 0), stop=(kc == 7)), [w, scb], PQ(b, 0, 512))
                yield
            copy_op("dve", modT.t[:, j - 2, :], pst(b)[:, 0:8], PQ(b, 0, 512), [modT])
            P.op("dve", lambda e: e.tensor_tensor(modT.t[:, j - 2, :], modT.t[:, j - 2, :], badaT.t[:, j * 8:(j + 1) * 8], ALU.add), [modT, badaT], [modT])

        def stageAB1(hh):
            cfg = head_cfg(hh)
            gla, h, dscale, owner = cfg["gla"], cfg["h"], cfg["dscale"], cfg["owner"]
            dk = 128
            c_q, c_k, c_vg, c_f = cfg["c_q"], cfg["c_k"], cfg["c_vg"], cfg["c_f"]
            pb = hh % 2
            w, wv = head_w[hh]
            if hh + 1 < 8:
                load_head_weights(hh + 1)
            myVt, myGG = Vt[pb], GG[pb]
            Us = SETS[0]["U"]

            def fm_proj(c0, M, tiles, dst_ap_fn, dst_toks, scale=None):
                for tt in tiles:
                    b = fm_bank()
                    for kc in range(8):
                        P.op("pe", lambda e, kc=kc, b=b, tt=tt: e.matmul(pst(b)[0:M, :], wv[:, kc, c0:c0 + M], hT.t[:, kc, tt * 512:(tt + 1) * 512],
                                                                       start=(kc == 0), stop=(kc == 7)),
                             [w] + hT_tok[4 * tt:4 * tt + 4], PQ(b, 0, 512))
                    yield
                    copy_op(evac_eng(), dst_ap_fn(tt), pst(b)[0:M, :], PQ(b, 0, 512), dst_toks, scale=scale)

            if owner:
                yield from fm_proj(c_q, dk, range(4), lambda tt: QT.t[0:dk, tt * 512:(tt + 1) * 512], [QT], scale=(0.125 if gla else None))
            if gla and owner:
                yield from fm_proj(c_k, dk, range(4), lambda tt: KT.t[0:dk, tt * 512:(tt + 1) * 512], [KT])
            if hh == 0:
                for d in range(2):
                    for tt in range(4):
                        b = fm_bank()
                        for kc in range(8):
                            P.op("pe", lambda e, kc=kc, b=b, tt=tt, d=d: e.matmul(pst(b)[0:16, :], WLR.t[:, kc, 16 * d:16 * d + 16], hT.t[:, kc, tt * 512:(tt + 1) * 512],
                                                                                  start=(kc == 0), stop=(kc == 7)),
                                 [WLR] + hT_tok[4 * tt:4 * tt + 4], PQ(b, 0, 512))
                        copy_op(evac_eng(), LRT.t[0:16, d, tt * 512:(tt + 1) * 512], pst(b)[0:16, :], PQ(b, 0, 512), [LRT])
                        yield
            for bp in range(NG // 2):
                bk = fm_bank()
                for i in range(2):
                    blk = 2 * bp + i
                    for kc in range(8):
                        P.op("pe", lambda e, kc=kc, bk=bk, i=i, blk=blk: e.matmul(pst(bk)[:, i * 256:(i + 1) * 256], hT.t[:, kc, blk * 128:(blk + 1) * 128],
                                                                                wv[:, kc, c_vg:c_vg + 256], start=(kc == 0), stop=(kc == 7)),
                             [w, hT_tok[blk]], PQ(bk, i * 256, (i + 1) * 256))
                pv = pst(bk).rearrange("p (b c) -> p b c", b=2)
                yield
                ce = evac_eng()
                copy_op(ce, myVt.t[:, 2 * bp:2 * bp + 2, :], pv[:, :, 0:128], PQ(bk, 0, 512), [myVt])
                copy_op(ce, myGG.t[:, 2 * bp:2 * bp + 2, :], pv[:, :, 128:256], PQ(bk, 0, 512), [myGG])
            GPS = TS // 128
            for s_ in range(NSPAN):
                gsp = myGG.t[:, s_ * GPS:(s_ + 1) * GPS, :].rearrange("p b c -> p (b c)")
                P.op("act", lambda e, gsp=gsp: e.activation(Us.t, gsp, AF.Exp, scale=-1.0), [myGG], [Us])
                P.op("act", lambda e: e.activation(Us.t, Us.t, AF.Ln, bias=1.0), [Us], [Us])
                P.op("act", lambda e: e.activation(Us.t, Us.t, AF.Exp, scale=-1.0), [Us], [Us])
                P.op("dve", lambda e, gsp=gsp: e.tensor_tensor(gsp, gsp, Us.t, ALU.mult), [myGG, Us], [myGG])
                yield

        def stageAB2(hh):
            cfg = head_cfg(hh)
            gla, h, dscale, owner, pr = cfg["gla"], cfg["h"], cfg["dscale"], cfg["owner"], cfg["p"]
            dk = 128
            c_f = cfg["c_f"]
            w, wv = head_w[hh]
            myQTt, myKTt, myKTM, myLH, myEX, mySCL = QTt, KTt, KTM, LH, EX, SCL
            if not owner:
                if hh < 4:
                    yield from mod_computeT(2 + hh, w)
                return

            def fm_proj(c0, M, tiles, dst_ap_fn, dst_toks, scale=None):
                for tt in tiles:
                    b = fm_bank()
                    for kc in range(8):
                        P.op("pe", lambda e, kc=kc, b=b, tt=tt: e.matmul(pst(b)[0:M, :], wv[:, kc, c0:c0 + M], hT.t[:, kc, tt * 512:(tt + 1) * 512],
                                                                       start=(kc == 0), stop=(kc == 7)),
                             [w] + hT_tok[4 * tt:4 * tt + 4], PQ(b, 0, 512))
                    copy_op("act", dst_ap_fn(tt), pst(b)[0:M, :], PQ(b, 0, 512), dst_toks, scale=scale)
                    yield

            v3 = lambda ap: ap.rearrange("p (b t) -> p b t", b=BPS)

            def decay(d, s_, tiles, sp0, sp1):
                st_ = SETS[(s_ % 2) * 2 + d]
                Xs, Us, T2s, Bs, KHt = st_["X"], st_["U"], st_["T2"], st_["B"], st_["KH"]
                X_, U_, T2_, B_, KH_ = Xs.t[0:dk], Us.t[0:dk], T2s.t[0:dk], Bs.t[0:dk], KHt.t[0:dk]
                col = (d * 2 + pr) if gla else (d * 4 + h)
                sel = d
                if gla:
                    for ti, tt in enumerate(tiles):
                        b = fm_bank()
                        P.op("pe", lambda e, b=b, tt=tt: e.matmul(pst(b)[0:128, :], WGU.t[0:16, d, 128 * pr:128 * pr + 128], LRT.t[0:16, d, tt * 512:(tt + 1) * 512],
                                                                start=True, stop=True), [WGU, LRT], PQ(b, 0, 512))
                        P.op("act", lambda e, b=b, ti=ti: e.activation(U_[:, ti * 512:(ti + 1) * 512], pst(b)[0:128, :], AF.Exp, scale=-1.0,
                                                                      bias=NEGB[:, col:col + 1]), PQ(b, 0, 512) + [smalls], [Us])
                    yield
                    P.op("act", lambda e: e.activation(T2_, U_, AF.Ln, bias=1.0), [Us], [T2s])
                    P.op("dve", lambda e: e.tensor_tensor_scan(B_, scanmask.t[0:dk, :], T2_, 0.0, ALU.mult, ALU.add), [scanmask, T2s], [Bs])
                else:
                    cf = c_f[d]
                    yield from fm_proj(cf, 128, tiles, lambda tt: X_[:, (tt - tiles[0]) * 512:(tt - tiles[0] + 1) * 512], [Xs])
                    P.op("act", lambda e: e.activation(U_, X_, AF.Exp, scale=-1.0), [Xs], [Us])
                    P.op("act", lambda e: e.activation(T2_, U_, AF.Ln, bias=1.0), [Us], [T2s])
                    P.op("act", lambda e: e.activation(U_, U_, AF.Ln, bias=1.0, scale=LB[:, col:col + 1]), [Us, smalls], [Us])
                    yield
                    P.op("dve", lambda e: e.tensor_tensor(U_, U_, T2_, ALU.subtract), [Us, T2s], [Us])
                    P.op("act", lambda e: e.activation(X_, X_, AF.Exp), [Xs], [Xs])
                    P.op("act", lambda e: e.activation(X_, X_, AF.Ln, bias=1.0), [Xs], [Xs])
                    P.op("dve", lambda e: e.tensor_tensor_scan(B_, scanmask.t[0:dk, :], U_, 0.0, ALU.mult, ALU.add), [scanmask, Us], [Bs])
                yield
                B3 = v3(B_)
                MID = BS // 2 - 1
                lh = myLH.t[0:dk, d, s_ * BPS:(s_ + 1) * BPS, :]
                ex = myEX.t[0:dk, d, s_ * BPS:(s_ + 1) * BPS, :]
                P.op("dve", lambda e: e.tensor_copy(lh[:, :, 0:1], B3[:, :, MID:MID + 1]), [Bs], [myLH])
                P.op("dve", lambda e: e.tensor_tensor(lh[:, :, 1:2], B3[:, :, BS - 1:BS], B3[:, :, MID:MID + 1], ALU.subtract), [Bs], [myLH])
                P.op("act", lambda e: e.activation(ex, lh, AF.Exp, scale=dscale), [myLH], [myEX])
                bmid = bcol(B_, dk, MID, BPS)
                ehb = bass.AP(myEX.t.tensor, myEX.t[0:dk, d, s_ * BPS:(s_ + 1) * BPS, 1 - sel].offset,
                              [[myEX.t.ap[0][0], dk], [2, BPS], [0, BS]])
                if gla:
                    if d == 0:
                        P.op("dve", lambda e: e.tensor_tensor(v3(U_), B3, bmid, ALU.subtract), [Bs], [Us])
                    else:
                        P.op("dve", lambda e: e.tensor_tensor(T2_, B_, T2_, ALU.subtract), [Bs, T2s], [T2s])
                        P.op("dve", lambda e: e.tensor_tensor(v3(U_), bmid, v3(T2_), ALU.subtract), [Bs, T2s], [Us])
                    P.op("dve", lambda e: e.tensor_scalar(U_, U_, 640.0, -640.0, op0=ALU.min, op1=ALU.max), [Us], [Us])
                    P.op("act", lambda e: e.activation(T2_, U_, AF.Exp, scale=dscale), [Us], [T2s])
                    P.op("pool", lambda e: e.tensor_tensor(myQTt[d].t[0:dk, sp0:sp1], QT.t[0:dk, sp0:sp1], T2_, ALU.mult), [QT, T2s], [myQTt[d]])
                    yield
                    P.op("act", lambda e: e.activation(T2_, U_, AF.Exp, scale=-dscale), [Us], [T2s])
                    P.op("dve", lambda e: e.tensor_tensor(myKTt[d].t[0:dk, sp0:sp1], KT.t[0:dk, sp0:sp1], T2_, ALU.mult), [KT, T2s], [myKTt[d]])
                else:
                    if d == 0:
                        P.op("dve", lambda e: e.tensor_tensor(v3(T2_), B3, bmid, ALU.subtract), [Bs], [T2s])
                    else:
                        P.op("dve", lambda e: e.tensor_tensor(U_, B_, U_, ALU.subtract), [Bs, Us], [Us])
                        P.op("dve", lambda e: e.tensor_tensor(v3(T2_), bmid, v3(U_), ALU.subtract), [Bs, Us], [T2s])
                    P.op("dve", lambda e: e.tensor_scalar(T2_, T2_, 40.0, -40.0, op0=ALU.min, op1=ALU.max), [T2s], [T2s])
                    P.op("act", lambda e: e.activation(U_, T2_, AF.Exp), [T2s], [Us])
                    P.op("pool", lambda e: e.tensor_tensor(myQTt[d].t[0:dk, sp0:sp1], QT.t[0:dk, sp0:sp1], U_, ALU.mult), [QT, Us], [myQTt[d]])
                    yield
                    P.op("dve", lambda e: e.tensor_tensor(X_, X_, T2_, ALU.add), [Xs, T2s], [Xs])
                    P.op("act", lambda e: e.activation(myKTt[d].t[0:dk, sp0:sp1], X_, AF.Exp, scale=-1.0, bias=L1M[:, col:col + 1]), [Xs, smalls], [myKTt[d]])
                yield
                P.op("dve", lambda e: e.tensor_tensor(v3(KH_), v3(myKTt[d].t[0:dk, sp0:sp1]), ehb, ALU.mult), [myKTt[d], myEX], [KHt])
                kb = fm_bank()
                pvk = pst(kb).bitcast(BF16).rearrange("p (b t) -> p b t", b=8)
                ng = TS // 128
                for i in range(ng):
                    P.op("pe", lambda e, i=i: e.transpose(pvk[:, i, 0:dk], KH_[:, i * 128:(i + 1) * 128], identB.t[0:dk, 0:dk]), [KHt, identB], PQ(kb, 0, 512))
                copy_op(evac_eng(), myKTM[d].t[:, s_ * ng:(s_ + 1) * ng, 0:dk], pvk[:, 0:ng, 0:dk], PQ(kb, 0, 512), [myKTM[d]])
                yield

            for s0_ in range(0, NSPAN, 2):
                gens = []
                for s_ in (s0_, s0_ + 1):
                    tiles = list(range(s_ * TS // 512, (s_ + 1) * TS // 512))
                    for d in range(2):
                        gens.append(decay(d, s_, tiles, s_ * TS, (s_ + 1) * TS))
                alive = [True] * len(gens)
                while any(alive):
                    for gi in range(len(gens)):
                        if alive[gi]:
                            try:
                                next(gens[gi])
                            except StopIteration:
                                alive[gi] = False
                    yield
            for d in range(2):
                sel = d
                mc = mchain.t[0:dk, d * NBK:(d + 1) * NBK]
                P.op("dve", lambda e, d=d, sel=sel, mc=mc: e.tensor_tensor(mySCL.t[0:dk, d, :, 0], myEX.t[0:dk, d, :, sel], mc, ALU.mult), [myEX, mchain], [mySCL])
                P.op("dve", lambda e, d=d, sel=sel: e.tensor_tensor(mySCL.t[0:dk, d, :, 1], mySCL.t[0:dk, d, :, 0], myEX.t[0:dk, d, :, 1 - sel], ALU.mult), [myEX, mySCL], [mySCL])
            yield
            if hh < 4:
                yield from mod_computeT(2 + hh, w)

        def stageC(hh):
            cfg = head_cfg(hh)
            gla, h, dk, po = cfg["gla"], cfg["h"], cfg["dk"], cfg["po"]
            pq = slice(po, po + dk)
            pb = hh % 2
            myQTt, myKTt, myKTM, myVt, myGG, mySCL = QTt, KTt, KTM, Vt[pb], GG[pb], SCL
            kmt = {"n": 0}
            pv7 = pst(7).bitcast(BF16).rearrange("p (r s t) -> p r s t", r=2, s=4)
            grp_done = {}
            for d in range(2):
                src = (sig_d if gla else sih_d)[d, h]
                P.dma("sp", ST[d][0].t[pq], src, [], [ST[d][0]])
            state = {0: ST[0][0], 1: ST[1][0]}
            nxt = [1, 1]

            def chain_p(d, n, cb, after=()):
                g, hf = n // 2, n % 2
                return P.op("pe", lambda e: e.matmul(pst(cb)[pq, d * 128:(d + 1) * 128], myKTM[d].t[hf * 64:(hf + 1) * 64, g, pq], myVt.t[hf * 64:(hf + 1) * 64, g, :], start=True, stop=True),
                            [myKTM[d], myVt], PQ(cb, 0, 256), extra=after)

            def chain_step(d, n, cb):
                prev = state[d]
                new = ST[d][nxt[d]]
                nxt[d] = (nxt[d] + 1) % int(_os.environ.get('DBG_TRI', '3'))
                P.op("act", lambda e: e.activation(SQt[d].t[pq, n, :], prev.t[pq], AF.Copy, scale=mySCL.t[pq, d, n, 0:1]), [prev, mySCL], [SQt[d]])
                P.op("dve", lambda e: e.scalar_tensor_tensor(new.t[pq], prev.t[pq], mySCL.t[pq, d, n, 1:2], pst(cb)[pq, d * 128:(d + 1) * 128], op0=ALU.mult, op1=ALU.add),
                     [prev, mySCL] + PQ(cb, 0, 256), [new])
                state[d] = new
                if (d == 0 and n % 4 == 3) or (d == 1 and n % 4 == 0):
                    dst = (sog_d if gla else soh_d)[n // 4, d, h]
                    P.dma("sp", dst, new.t[pq], [new], [], is_out=True)

            gctr = {"n": 0}
            ginfo = {}

            def og_a1(g):
                i = gctr["n"]
                gctr["n"] += 1
                ginfo[g] = i
                blk = slice(g * 128, (g + 1) * 128)
                for d in range(2):
                    P.op("pe", lambda e, d=d: e.matmul(pst(5)[:, d * 128:(d + 1) * 128], myKTt[d].t[pq, blk], myQTt[d].t[pq, blk], start=True, stop=True),
                         [myKTt[d], myQTt[d]], PQ(5, 0, 256))

            def og_a2(g):
                at = ATt[ginfo[g] % 4]
                P.op("dve", lambda e: e.tensor_tensor(at.t, pst(5)[:, 0:256], maskT2.t, ALU.mult), PQ(5, 0, 256) + [maskT2], [at])

            def og_b(g):
                i = ginfo[g]
                at = ATt[i % 4]
                ob_ = [2, 6][i % 2]
                og = pst(ob_)[:, 0:128]
                otok = PQ(ob_, 0, 128)
                P.op("pe", lambda e: e.matmul(og, at.t[:, 0:128], myVt.t[:, g, :], start=True, stop=False), [at, myVt], otok)
                P.op("pe", lambda e: e.matmul(og, at.t[:, 128:256], myVt.t[:, g, :], start=False, stop=False), [at, myVt], otok)
                for hf in range(2):
                    for d in range(2):
                        last = (hf == 1 and d == 1)
                        c0 = g * 128 + hf * 64
                        P.op("pe", lambda e, hf=hf, d=d, last=last, c0=c0: e.matmul(pst(ob_)[hf * 64:(hf + 1) * 64, 0:128], myQTt[d].t[pq, c0:c0 + 64], SQt[d].t[pq, 2 * g + hf, :],
                                                                                   start=False, stop=last), [myQTt[d], SQt[d]], otok)

            def og_c(g):
                i = ginfo[g]
                ob_ = [2, 6][i % 2]
                og = pst(ob_)[:, 0:128]
                otok = PQ(ob_, 0, 128)
                so = sto[i % 4]
                P.op("act", lambda e: e.activation(junk2.t, og, AF.Square, accum_out=so.t[:, 0:1]), otok, [junk2, so], multi=True)
                P.op("act", lambda e: e.activation(so.t[:, 1:2], so.t[:, 0:1], AF.Ln, bias=128.0 * EPS), [so], [so])
                P.op("act", lambda e: e.activation(so.t[:, 2:3], so.t[:, 1:2], AF.Exp, scale=-0.5), [so], [so])

            def og_d(g):
                i = ginfo[g]
                ob_ = [2, 6][i % 2]
                og = pst(ob_)[:, 0:128]
                otok = PQ(ob_, 0, 128)
                so = sto[i % 4]
                mt = MTM[i % 2]
                P.op("dve", lambda e: e.scalar_tensor_tensor(mt.t, og, so.t[:, 2:3], myGG.t[:, g, :], op0=ALU.mult, op1=ALU.mult), otok + [so, myGG], [mt])
                grp = g // 4
                r = grp % 2
                rtok = PQ(7, r * 256, (r + 1) * 256)
                P.op("pe", lambda e: e.transpose(pv7[:, r, g % 4, :], mt.t, identB.t), [mt, identB], rtok)
                grp_done[grp] = grp_done.get(grp, 0) + 1
                if grp_done[grp] == 4:
                    copy_op(evac_eng(), mergedT.t[:, hh, grp * 512:(grp + 1) * 512], pv7[:, r].rearrange("p s t -> p (s t)"), rtok, [mT_tok[hh]])

            ready = {g: max(2 * g + 1, NBK - 1 - 2 * g) for g in range(NG)}
            p_ahead = int(_os.environ.get('DBG_PAHEAD', '1'))
            if p_ahead:
                o_ = chain_p(0, 0, 3)
                chain_p(1, NBK - 1, 3, after=(o_,))
            for s_ in range(NBK + 4):
                if s_ < NBK:
                    cb = 3 + (s_ % 2)
                    if not p_ahead:
                        o_ = chain_p(0, s_, cb)
                        chain_p(1, NBK - 1 - s_, cb, after=(o_,))
                    chain_step(0, s_, cb)
                    chain_step(1, NBK - 1 - s_, cb)
                    if p_ahead and s_ + 1 < NBK:
                        cbn = 3 + ((s_ + 1) % 2)
                        o_ = chain_p(0, s_ + 1, cbn)
                        chain_p(1, NBK - 2 - s_, cbn, after=(o_,))
                for g in range(NG):
                    if ready[g] == s_ - 3:
                        og_d(g)
                for g in range(NG):
                    if ready[g] == s_ - 2:
                        og_c(g)
                for g in range(NG):
                    if ready[g] == s_ - 1:
                        og_b(g)
                pair_now = [g for g in range(NG) if ready[g] == s_ + 3]
                pair_prev = [g for g in range(NG) if ready[g] == s_ + 2]
                pair_prev2 = [g for g in range(NG) if ready[g] == s_ + 1]
                if pair_prev2:
                    og_a2(pair_prev2[1])
                if pair_prev:
                    og_a2(pair_prev[0])
                    og_a1(pair_prev[1])
                if pair_now:
                    og_a1(pair_now[0])
                yield

        heads = [int(v) for v in _os.environ.get('DBG_HEADS', '0,1,2,3,4,5,6,7').split(',') if v != '']
        assert heads == list(range(8))
        load_head_weights(0, dma=False)
        for _ in stageAB1(0):
            pass
        for _ in stageAB2(0):
            pass
        ilv = int(_os.environ.get('DBG_ILV', '2'))
        for hh in range(8):
            cgen = stageC(hh)
            abgen = stageAB1(hh + 1) if hh + 1 < 8 else None
            c_alive, ab_alive = True, abgen is not None
            step = 0
            while c_alive or ab_alive:
                if c_alive:
                    try:
                        next(cgen)
                    except StopIteration:
                        c_alive = False
                if ab_alive and (not c_alive or (ilv >= 1 and step % ilv == 0)):
                    try:
                        next(abgen)
                    except StopIteration:
                        ab_alive = False
                step += 1
            if hh + 1 < 8:
                for _ in stageAB2(hh + 1):
                    pass
        if debug:
            P.dma("sp", dbg["mergedT"], mergedT.t.rearrange("p k t -> p (k t)"), mT_tok, [], is_out=True)
        P.barrier()
        AR.reset(ph1_mark)
        X1 = AR.alloc("X1", [NB, D], F32)
        X1_tok = [Tok("X1_b%d" % b) for b in range(NB)]
        ph3_keep = AR.mark()
        ringC = [AR.alloc("wringC%d" % i, [8 * 256], BF16) for i in range(3)]
        ring["slots"] = ringC
        ring["n"] = 0
        pair_w = {}

        def load_pair(j):
            w = next_w()
            wv = w.t[:, 0:8 * 256].rearrange("p (k n) -> p k n", k=8)
            P.dma("pool", wv[:, :, 0:128], wup_d[:, j * 128:(j + 1) * 128].rearrange("(k p) n -> p k n", p=128), [], [w])
            P.dma("pool", wv[:, :, 128:256], wup_d[:, FFN_H + j * 128:FFN_H + (j + 1) * 128].rearrange("(k p) n -> p k n", p=128), [], [w])
            pair_w[j] = (w, wv)

        load_pair(0)
        load_pair(1)
        MB[2] = AR.alloc("gate1_bc", [D], F32)
        MB[3] = AR.alloc("shift2_bc", [D], F32)
        MB[4] = AR.alloc("g2_bc", [D], F32)
        WO = AR.alloc("WO", [8, D], BF16)
        wstage = [AR.alloc("wstage%d" % i, [D], F32) for i in range(4)]
        junk = AR.alloc("junk", [D], BF16)
        tmpf = AR.alloc("tmpf", [D], F32)
        hb = [AR.alloc("hb%d" % i, [D], BF16) for i in range(2)]
        stt = [AR.alloc("stt%d" % i, [4], F32) for i in range(3)]
        dgt = [AR.alloc("dgt%d" % i, [128], F32) for i in range(2)]
        n2T = AR.alloc("n2T", [8], F32)
        g2T = AR.alloc("g2T", [8], F32)
        xb = {"n": 0}

        def expand(vec_ap, vec_toks, dst):
            for half in range(2):
                b = xb["n"] % 2
                xb["n"] += 1
                for q in range(4):
                    kc = half * 4 + q
                    dg = dgt[kc % 2]
                    P.op("dve", lambda e, kc=kc, dg=dg: e.tensor_scalar(dg.t, identF.t, vec_ap[:, kc:kc + 1], None, op0=ALU.mult), [identF] + vec_toks, [dg])
                    P.op("pe", lambda e, q=q, b=b, dg=dg: e.matmul(pst(b)[:, q * 128:(q + 1) * 128], onesF.t, dg.t, start=True, stop=True), [onesF, dg], PQ(b, 0, 512))
                copy_op(evac_eng(), dst.t[:, half * 512:(half + 1) * 512], pst(b)[:, :], PQ(b, 0, 512), [dst])

        P.dma("sp", n2T.t, norm2T_d, [], [n2T])
        P.op("dve", lambda e: e.scalar_tensor_tensor(g2T.t, modT.t[:, 2, :], 1.0, n2T.t, op0=ALU.add, op1=ALU.mult), [modT, n2T], [g2T])
        for kc in range(4):
            P.dma("sp", wstage[kc].t, wout_d[kc * 128:(kc + 1) * 128, :], [], [wstage[kc]])
        expand(modT.t[:, 0, :], [modT], MB[2])
        for kc in range(8):
            ws = wstage[kc % 4]
            if kc >= 4:
                P.dma("sp", ws.t, wout_d[kc * 128:(kc + 1) * 128, :], [], [ws])
            gcol = 0 if kc < 4 else 1
            P.op("dve", lambda e, kc=kc, ws=ws, gcol=gcol: e.scalar_tensor_tensor(WO.t[:, kc, :], ws.t, GS[:, gcol:gcol + 1], MB[2].t, op0=ALU.mult, op1=ALU.mult),
                 [ws, smalls, MB[2]], [WO])
        for b in range(NB):
            P.dma("sp", X1.t[:, b, :], x_d[b * 128:(b + 1) * 128, :], [], [X1_tok[b]])
        expand(modT.t[:, 1, :], [modT], MB[3])
        expand(g2T.t, [g2T], MB[4])
        ob = {"n": 0}
        for b in range(NB):
            for half in range(2):
                bk = 2 + ob["n"] % 2
                ob["n"] += 1
                for kc in range(8):
                    P.op("pe", lambda e, kc=kc, bk=bk, b=b, half=half: e.matmul(pst(bk)[:, :], mergedT.t[:, kc, b * 128:(b + 1) * 128], WO.t[:, kc, half * 512:(half + 1) * 512],
                                                                              start=(kc == 0), stop=(kc == 7)), [mT_tok[kc], WO], PQ(bk, 0, 512))
                hs = slice(half * 512, (half + 1) * 512)
                P.op("dve", lambda e, bk=bk, b=b, hs=hs: e.tensor_tensor(X1.t[:, b, hs], pst(bk)[:, :], X1.t[:, b, hs], ALU.add), PQ(bk, 0, 512) + [X1_tok[b]], [X1_tok[b]])
            norm_A(X1.t[:, b, :], [X1_tok[b]], stt[b % 3])
            if b >= 1:
                norm_B1(X1.t[:, b - 1, :], [X1_tok[b - 1]], MB[4], MB[3], b - 1, stt[(b - 1) % 3])
            if b >= 2:
                norm_B2(b - 2, 4 + ((b - 2) % 2))
            if debug:
                P.dma("sp", dbg["x1"][b * 128:(b + 1) * 128, :], X1.t[:, b, :], [X1_tok[b]], [], is_out=True)
        norm_B1(X1.t[:, NB - 1, :], [X1_tok[NB - 1]], MB[4], MB[3], NB - 1, stt[(NB - 1) % 3])
        norm_B2(NB - 2, 4 + ((NB - 2) % 2))
        norm_B2(NB - 1, 4 + ((NB - 1) % 2))
        if debug:
            dump("h2T", TT(hT.t, hT_tok[0]), 8 * T, BF16)

        P.barrier()
        AR.reset(ph3_keep)
        for i in range(3):
            AR.alloc("wringC_again%d" % i, [8 * 256], BF16)
        AR.end = top_keep
        MB[5] = AR.alloc("gate2_bc", [D], F32)
        FN = AR.alloc("fnorm_bc", [D], F32)
        dgt = [AR.alloc("dgt%d" % i, [128], F32) for i in range(2)]
        GMAX = max(j1 - j0 for j0, j1 in GROUPS)
        HT = AR.alloc("HT", [GMAX, T], BF16)
        WD = AR.alloc("WD", [GMAX, D], BF16)
        wdst = [AR.alloc("wdst%d" % i, [D], F32) for i in range(2)]
        UB = [AR.alloc("UB%d" % i, [UW], BF16) for i in range(2)]
        SG = AR.alloc("SG", [T], F32)
        DG = [AR.alloc("DG%d" % i, [11, 128], BF16) for i in range(2)]
        w11T = AR.alloc("w11T", [NCH * 11], F32)
        bconvT = AR.alloc("bconvT", [NCH], F32)
        ring["slots"] = ringC
        yst = [AR.alloc("yst%d" % i, [D], F32) for i in range(2)]
        DACC = AR.alloc("DACC", [T], BF16)
        ctmp = yst
        junk = AR.alloc("junk", [D], BF16)
        stt = [AR.alloc("stt%d" % i, [4], F32) for i in range(3)]
        expand(modT.t[:, 3, :], [modT], MB[5])
        P.dma("sp", FN.t, fnorm_d.partition_broadcast(128), [], [FN])
        P.dma("sp", w11T.t, w11T_d, [], [w11T])
        P.dma("sp", bconvT.t, bconvT_d, [], [bconvT])
        for u in range(2):
            P.op("pool", lambda e, u=u: e.memset(UB[u].t, 0.0), [], [UB[u]])
        P.op("pool", lambda e: e.memset(DACC.t, 0.0), [], [DACC])
        identB_b11 = bass.AP(identB.t.tensor, identB.t.offset, [list(identB.t.ap[0]), [0, 11], [1, 128]])
        ucnt = {"n": 0}

        def up_proj(j, is_up):
            w, wv = pair_w[j]
            off = 128 if is_up else 0
            u = ucnt["n"] % 2
            ucnt["n"] += 1
            for tt in range(4):
                for kc in range(8):
                    P.op("pe", lambda e, kc=kc, tt=tt: e.matmul(pst(tt)[:, :], wv[:, kc, off:off + 128], hT.t[:, kc, tt * 512:(tt + 1) * 512], start=(kc == 0), stop=(kc == 7)),
                         [w] + hT_tok[4 * tt:4 * tt + 4], PQ(tt, 0, 512))
                P.op("act", lambda e, tt=tt: e.activation(UB[u].t[:, UPAD + tt * 512:UPAD + (tt + 1) * 512], pst(tt)[:, :], AF.Copy), PQ(tt, 0, 512), [UB[u]])
            return u

        ctn = {"n": 0}

        def conv(j, jj, is_up, u):
            cc = (NPAIR + j) if is_up else j
            dg = DG[cc % 2]
            wb = bass.AP(w11T.t.tensor, w11T.t.offset + cc * 11, [list(w11T.t.ap[0]), [1, 11], [0, 128]])
            P.op("pool", lambda e: e.tensor_tensor(dg.t, identB_b11, wb, ALU.mult), [identB, w11T], [dg])
            ub = UB[u].t
            acc3 = DACC.t.rearrange("p (r c) -> p r c", c=64)[:, :, 0:63]
            for k_, dy in enumerate((0, -1, 1)):
                wi = (dy + 1) * 3 + 2
                src3 = ub[:, UPAD + 64 * dy + 1:UPAD + 64 * dy + 1 + T].rearrange("p (r c) -> p r c", c=64)[:, :, 0:63]
                wsc = w11T.t[:, cc * 11 + wi:cc * 11 + wi + 1]
                if k_ == 0:
                    P.op("dve", lambda e, src3=src3, wsc=wsc: e.tensor_scalar(acc3, src3, wsc, None, op0=ALU.mult), [UB[u], w11T], [DACC])
                else:
                    P.op("dve", lambda e, src3=src3, wsc=wsc: e.scalar_tensor_tensor(acc3, src3, wsc, acc3, op0=ALU.mult, op1=ALU.add), [UB[u], w11T, DACC], [DACC])
            for tt in range(4):
                base = UPAD + tt * 512
                pt = pst(4 + tt)
                taps = []
                for dy in (0, -1, 1):
                    taps.append(((dy + 1) * 3 + 1, pt[:, 0:512], ub[:, base + 64 * dy:base + 64 * dy + 512]))
                for dy in (0, -1, 1):
                    o3 = pt[:, 0:512].rearrange("p (r c) -> p r c", c=64)[:, :, 1:64]
                    r3 = ub[:, base + 64 * dy - 1:base + 64 * dy - 1 + 512].rearrange("p (r c) -> p r c", c=64)[:, :, 1:64]
                    taps.append(((dy + 1) * 3 + 0, o3, r3))
                o4 = pt[:, 0:512].rearrange("p (a r c) -> p a r c", a=2, r=4)[:, :, 1:4, 0]
                r4 = ub[:, base - 1:base - 1 + 512].rearrange("p (a r c) -> p a r c", a=2, r=4)[:, :, 1:4, 0]
                taps.append((9, o4, r4))
                o4 = pt[:, 0:512].rearrange("p (a r c) -> p a r c", a=2, r=4)[:, :, 0:3, 63]
                r4 = ub[:, base + 1:base + 1 + 512].rearrange("p (a r c) -> p a r c", a=2, r=4)[:, :, 0:3, 63]
                taps.append((10, o4, r4))
                for ti, (wi, oap, rap) in enumerate(taps):
                    P.op("pe", lambda e, wi=wi, oap=oap, rap=rap, ti=ti: e.matmul(oap, dg.t[:, wi, :], rap, start=(ti == 0), stop=(ti == len(taps) - 1)),
                         [dg, UB[u]], PQ(4 + tt, 0, 512))
                ts_ = slice(tt * 512, (tt + 1) * 512)
                ct = ctmp[ctn["n"] % 2]
                ctn["n"] += 1
                P.op("dve", lambda e, pt=pt, ts_=ts_, ct=ct: e.tensor_tensor(ct.t[:, 0:512], pt[:, 0:512], DACC.t[:, ts_], ALU.add), PQ(4 + tt, 0, 512) + [DACC], [ct])
                if not is_up:
                    P.op("act", lambda e, ts_=ts_, ct=ct: e.activation(SG.t[:, ts_], ct.t[:, 0:512], AF.Silu, bias=bconvT.t[:, cc:cc + 1]), [ct, bconvT], [SG])
                else:
                    P.op("dve", lambda e, ts_=ts_, ct=ct: e.scalar_tensor_tensor(HT.t[:, jj, ts_], ct.t[:, 0:512], bconvT.t[:, cc:cc + 1], SG.t[:, ts_], op0=ALU.add, op1=ALU.mult),
                         [ct, bconvT, SG], [HT])

        fin_pending = []

        def final_out(b):
            st = stt[b % 3]
            ys = yst[b % 2]
            xap = X1.t[:, b, :]
            P.op("dve", lambda e: e.scalar_tensor_tensor(ys.t, xap, st.t[:, 2:3], FN.t, op0=ALU.mult, op1=ALU.mult), [X1_tok[b], st, FN], [ys])
            P.dma("sp", y_d[b * 128:(b + 1) * 128, :], ys.t, [ys], [], is_out=True)

        dbk = {"n": 0}

        def wd_load(gi):
            j0, j1 = GROUPS[gi]
            for jj, j in enumerate(range(j0, j1)):
                wq = wdst[j % 2]
                P.dma("sp", wq.t, wdn_d[j * 128:(j + 1) * 128, :], [], [wq])
                P.op("dve", lambda e, jj=jj, wq=wq: e.tensor_tensor(WD.t[:, jj, :], wq.t, MB[5].t, ALU.mult), [wq, MB[5]], [WD])

        def down(gi):
            j0, j1 = GROUPS[gi]
            last_group = gi == len(GROUPS) - 1
            ng = j1 - j0
            for b in range(NB):
                for half in range(2):
                    bk = dbk["n"] % 4
                    dbk["n"] += 1
                    hs = slice(half * 512, (half + 1) * 512)
                    for jj in range(ng):
                        P.op("pe", lambda e, jj=jj, bk=bk, b=b, hs=hs: e.matmul(pst(bk)[:, :], HT.t[:, jj, b * 128:(b + 1) * 128], WD.t[:, jj, hs], start=(jj == 0), stop=(jj == ng - 1)),
                             [HT, WD], PQ(bk, 0, 512))
                    P.op("dve", lambda e, bk=bk, b=b, hs=hs: e.tensor_tensor(X1.t[:, b, hs], pst(bk)[:, :], X1.t[:, b, hs], ALU.add), PQ(bk, 0, 512) + [X1_tok[b]], [X1_tok[b]])
                if last_group:
                    st = stt[b % 3]
                    xap = X1.t[:, b, :]
                    P.op("act", lambda e, xap=xap, st=st, jk=junk: e.activation(jk.t, xap, AF.Square, accum_out=st.t[:, 0:1]), [X1_tok[b]], [junk, st], multi=True)
                    P.op("act", lambda e, st=st: e.activation(st.t[:, 1:2], st.t[:, 0:1], AF.Ln, scale=1.0 / D, bias=EPS), [st], [st])
                    P.op("act", lambda e, st=st: e.activation(st.t[:, 2:3], st.t[:, 1:2], AF.Exp, scale=-0.5), [st], [st])
                    fin_pending.append(b)
                    if len(fin_pending) > 1:
                        final_out(fin_pending.pop(0))
            if last_group:
                while fin_pending:
                    final_out(fin_pending.pop(0))

        wd_load(0)
        pend = None
        deferred = None
        for gi, (j0, j1) in enumerate(GROUPS):
            for j in range(j0, j1):
                for is_up in (False, True):
                    if (not is_up) and (j + 2 < NPAIR):
                        load_pair(j + 2)
                    u = up_proj(j, is_up)
                    if pend is not None:
                        if deferred is not None and pend[4] == gi and pend[2]:
                            down(deferred)
                            wd_load(gi)
                            deferred = None
                        conv(*pend[:4])
                    pend = (j, j - j0, is_up, u, gi)
            deferred = gi
        if deferred is not None and pend[4] == deferred:
            conv(*pend[:4])
            down(deferred)
        P.finish()
    return nc


def _consts():
    ident = np.eye(128, dtype=np.float32)
    j = np.arange(128)[:, None]
    i = np.arange(128)[None, :]
    same = (j // 64) == (i // 64)
    maskT2 = np.concatenate([(j <= i) & same, (j >= i) & same], axis=1).astype(np.float32)
    scanmask = np.ones((1, T), np.float32)
    scanmask[0, ::64] = 0.0
    return ident, maskT2, scanmask


def prep_core_inputs(inp):
    f32 = lambda a: np.ascontiguousarray(np.asarray(a, dtype=np.float32))
    ident, maskT2, scanmask = _consts()
    shared = {
        "w_ada": f32(inp["w_ada"][0]), "b_ada": f32(inp["b_ada"][0]).reshape(1, -1),
        "norm1": f32(inp["norm1"][0]).reshape(1, -1), "norm2": f32(inp["norm2"][0]).reshape(1, -1),
        "b_adaT": f32(np.asarray(inp["b_ada"][0]).reshape(48, 128).T), "norm2T": f32(np.asarray(inp["norm2"][0]).reshape(8, 128).T),
        "fnorm": f32(inp["final_norm"]).reshape(1, -1),
        "w_in": f32(inp["w_in"][0]), "w_gla_up": f32(inp["w_gla_up"][0]),
        "b_glaT": f32(np.asarray(inp["b_gla"][0]).reshape(2, 2, 128).transpose(2, 0, 1).reshape(128, 4)),
        "lbT": f32(np.asarray(inp["hgrn_lb"]).reshape(2, 2, 4, 128).transpose(3, 0, 1, 2).reshape(128, 16)),
        "gnorm": f32(np.stack([np.asarray(inp["gla_norm"][0]), np.asarray(inp["hgrn_norm"][0])], axis=1)),
        "w_out": f32(inp["w_out"][0]), "w_ffn_up": f32(inp["w_ffn_up"][0]),
        "bconvT": f32(np.asarray(inp["b_ffn_conv"][0]).reshape(NCH, 128).T),
        "w_ffn_down": f32(inp["w_ffn_down"][0]),
        "identF": ident, "maskT2": maskT2, "scanmask": scanmask,
    }
    conv = np.asarray(inp["ffn_conv"][0], dtype=np.float32).reshape(9, 2 * FFN_H)
    zero_row = np.zeros((1, 2 * FFN_H), np.float32)
    rows_s = np.concatenate([conv, zero_row, zero_row], axis=0)
    rows_p = np.concatenate([zero_row] * 3 + [conv[3:6]] + [zero_row] * 3 + [conv[3:4], conv[5:6]], axis=0)
    w11 = lambda rows: f32(rows.reshape(11, NCH, 128).transpose(2, 1, 0).reshape(128, NCH * 11))
    x_prompt = np.asarray(inp["x_prompt"], dtype=np.float32)
    x_sample = np.asarray(inp["x_sample"], dtype=np.float32)
    maps = []
    for c in range(8):
        m = dict(shared)
        if c < 4:
            m["x"] = f32(x_sample[c])
            m["cvT"] = f32(np.asarray(inp["c"][c]).reshape(8, 128).T)
            m["sinit_g"] = f32(inp["state_gla"][c, 0])
            m["sinit_h"] = f32(inp["state_hgrn"][c, 0])
            mf = np.ones(32, np.float32)
            mb = np.ones(32, np.float32)
            m["w11T"] = w11(rows_s)
        else:
            p = c - 4
            m["x"] = f32(x_prompt[8 * p:8 * p + 8].reshape(T, D))
            m["cvT"] = f32(np.asarray(inp["c_ctx"]).reshape(8, 128).T)
            m["sinit_g"] = np.zeros((2, 4, 64, 128), np.float32)
            m["sinit_h"] = np.zeros((2, 4, 128, 128), np.float32)
            mf = (np.arange(32) % 4 != 0).astype(np.float32)
            mb = (np.arange(32) % 4 != 3).astype(np.float32)
            m["w11T"] = w11(rows_p)
        m["mchain"] = f32(np.tile(np.concatenate([mf, mb])[None, :], (128, 1)))
        maps.append(m)
    return maps


_PROGRAM = {}


def kernel(**inputs):
    if "nc" not in _PROGRAM:
        _PROGRAM["nc"] = build_program(debug=False)
    nc = _PROGRAM["nc"]
    in_maps = prep_core_inputs(inputs)
    res = run_bass_kernel_spmd(nc, in_maps, core_ids=list(range(8)))
    r = res.results
    y_sample = np.stack([np.asarray(r[c]["y"], dtype=np.float32) for c in range(4)], axis=0)
    y_prompt = np.concatenate([np.asarray(r[c]["y"], dtype=np.float32).reshape(8, 256, D) for c in range(4, 8)], axis=0)
    sg = np.concatenate([np.asarray(r[c]["snew_g"], dtype=np.float32) for c in range(4, 8)], axis=0)[:, None]
    sh = np.concatenate([np.asarray(r[c]["snew_h"], dtype=np.float32) for c in range(4, 8)], axis=0)[:, None]
    return (y_prompt, y_sample, sg, sh)
```

```python
import numpy as np
from contextlib import ExitStack
import concourse.bass as bass
import concourse.mybir as mybir
from concourse.bass_utils import run_bass_kernel_spmd

F32 = mybir.dt.float32
BF16 = mybir.dt.bfloat16
AF = mybir.ActivationFunctionType
ALU = mybir.AluOpType


class Tok:
    __slots__ = ("name", "w", "r", "rd", "excl", "acc")

    def __init__(self, name, excl=False):
        self.name = name
        self.w = None
        self.r = {}
        self.rd = []
        self.excl = excl
        self.acc = {}


class TT:
    def __init__(self, t, tok):
        self.t = t
        self.tok = tok


class _Op:
    __slots__ = ("idx", "eng", "fn", "deps", "is_dma", "sig", "semval", "slot", "is_out", "multi")


def _tok(x):
    return x.tok if isinstance(x, TT) else x


class Prog:
    NSLOT = {"sp": 24, "pool": 16, "act": 8}

    def __init__(self, nc, es):
        self.nc = nc
        self.es = es
        self.ops = []
        self.n_dma = {"sp": 0, "pool": 0, "act": 0}
        self._n = 0
        self.bar = set()

    def sb(self, name, shape, dtype):
        t = self.es.enter_context(self.nc.sbuf_tensor(name, list(shape), dtype))
        return TT(t, Tok(name))

    def ps(self, name):
        t = self.es.enter_context(self.nc.psum_tensor(name, [128, 512], F32))
        return TT(t, Tok(name))

    def tok(self, name):
        return Tok(name)

    def _record(self, eng, fn, reads, writes, is_dma, is_out=False, extra=(), multi=False):
        op = _Op()
        op.multi = multi
        op.idx = len(self.ops)
        op.eng = eng
        op.fn = fn
        op.is_dma = is_dma
        op.sig = False
        op.semval = 0
        op.slot = None
        op.is_out = is_out
        deps = set()
        reads = [_tok(x) for x in reads]
        writes = [_tok(x) for x in writes]

        def consider(pidx, kind):
            p = self.ops[pidx]
            if p.is_dma:
                deps.add(pidx)
                return
            if (not is_dma) and p.eng == eng:
                if eng == "pe":
                    return
                if kind != "raw":
                    return
            deps.add(pidx)

        for t in reads:
            if t.w is not None:
                consider(t.w, "raw")
        for t in writes:
            if t.w is not None:
                consider(t.w, "waw")
            for _, ridx in t.r.items():
                consider(ridx, "war")
            for ridx in t.rd:
                consider(ridx, "war")
        for t in reads + writes:
            if t.excl:
                for e2, aidx in t.acc.items():
                    if e2 != eng:
                        deps.add(aidx)
                t.acc[eng] = op.idx
        for x in extra:
            deps.add(x.idx)
        for pidx in self.bar:
            p = self.ops[pidx]
            if (not is_dma) and (not p.is_dma) and p.eng == eng:
                continue
            deps.add(pidx)
        op.deps = deps
        for t in reads:
            if is_dma:
                t.rd.append(op.idx)
            else:
                t.r[eng] = op.idx
        for t in writes:
            t.w = op.idx
            t.r = {}
            t.rd = []
        if is_dma:
            k = self.n_dma[eng]
            self.n_dma[eng] += 1
            ns = self.NSLOT[eng]
            op.slot = (eng, k % ns)
            op.semval = 16 * (k // ns + 1)
        self.ops.append(op)
        return op

    def op(self, eng, fn, reads, writes, extra=(), multi=False):
        return self._record(eng, fn, reads, writes, False, extra=extra, multi=multi)

    def barrier(self):
        last = {}
        for op in self.ops:
            if op.is_dma:
                last[("d",) + op.slot] = op.idx
            else:
                last[op.eng] = op.idx
        self.bar = set(last.values())

    def dma(self, eng, out_ap, in_ap, reads, writes, is_out=False, **kw):
        def fn(e, out_ap=out_ap, in_ap=in_ap, kw=kw):
            return e.dma_start(out=out_ap, in_=in_ap, **kw)
        return self._record(eng, fn, reads, writes, True, is_out)

    def finish(self):
        nc = self.nc
        es = self.es
        ops = self.ops
        for op in ops:
            for d in op.deps:
                ops[d].sig = True
        engs = ["pe", "act", "dve", "pool", "sp"]
        esem = {e: es.enter_context(nc.semaphore("s_" + e)) for e in engs}
        dsem = {}
        for e, ns in self.NSLOT.items():
            for i in range(min(ns, max(1, self.n_dma[e]))):
                dsem[(e, i)] = es.enter_context(nc.semaphore("d_%s%d" % (e, i)))
        cnt = {e: 0 for e in engs}
        for op in ops:
            if not op.is_dma and op.sig:
                cnt[op.eng] += 1
                op.semval = cnt[op.eng]
        slot_last = {}
        prev_on_slot = {}
        for op in ops:
            if op.is_dma:
                prev_on_slot[op.idx] = slot_last.get(op.slot)
                slot_last[op.slot] = op.idx
        by_eng = {e: [op for op in ops if op.eng == e] for e in engs}

        def sigof(p):
            if p.is_dma:
                return dsem[p.slot], p.semval
            return esem[p.eng], p.semval

        def emit(ename, e):
            waited = {}

            embed = ename in ("dve", "act", "pool")

            for op in by_eng[ename]:
                need = {}
                order = []
                cand = [sigof(ops[d]) for d in sorted(op.deps)]
                if op.is_dma:
                    pv = prev_on_slot[op.idx]
                    if pv is not None:
                        cand.append(sigof(ops[pv]))
                for s, v in cand:
                    key = id(s)
                    if waited.get(key, 0) < v and need.get(key, (None, 0))[1] < v:
                        if key not in need:
                            order.append(key)
                        need[key] = (s, v)
                pend = [need[k] for k in order]
                for s, v in pend:
                    waited[id(s)] = v
                fold = None
                if embed and pend and not op.is_dma and not op.multi:
                    fold = pend.pop()
                for s, v in pend:
                    e.wait_ge(s, v)
                ins = op.fn(e)
                if fold is not None:
                    ins._wait_ge(fold[0], fold[1])
                if op.is_dma:
                    ins.then_inc(dsem[op.slot], 16)
                elif op.sig:
                    ins.then_inc(esem[ename], 1)
            if ename == "sp":
                for slot, idx in slot_last.items():
                    s, v = sigof(ops[idx])
                    if waited.get(id(s), 0) < v:
                        e.wait_ge(s, v)
                        waited[id(s)] = v

        with nc.Block() as block:
            @block.tensor
            def _(e):
                emit("pe", e)

            @block.scalar
            def _(e):
                emit("act", e)

            @block.vector
            def _(e):
                emit("dve", e)

            @block.gpsimd
            def _(e):
                emit("pool", e)

            @block.sync
            def _(e):
                emit("sp", e)
        self.stats = {e: len(by_eng[e]) for e in engs}


D = 1024
T = 2048
NB = 16
EPS = 1e-6
IN_W = 4128
FFN_H = 2816
NCH = 44
NPAIR = 22
UPAD = 65
UW = UPAD + T + UPAD
GROUPS = [(0, 6), (6, 12), (12, 17), (17, 22)]
TS = 512
OFF_QA, OFF_KA, OFF_VA, OFF_GA, OFF_LR = 0, 256, 512, 1024, 1536
OFF_QB, OFF_FB, OFF_IB, OFF_GB = 1568, 2080, 3104, 3616


class Arena:
    def __init__(self, P, nf32):
        self.P = P
        self.tt = P.sb("arena", [128, nf32], F32)
        self.n = nf32
        self.top = 0
        self.end = nf32

    def alloc(self, name, free_shape, dtype, top=False):
        nel = int(np.prod(free_shape))
        nf = nel if dtype == F32 else (nel + 1) // 2
        nf = (nf + 3) // 4 * 4
        assert self.top + nf <= self.end, ("arena overflow", name, self.top, nf, self.end)
        if top:
            self.end -= nf
            ap = self.tt.t[:, self.end:self.end + nf]
        else:
            ap = self.tt.t[:, self.top:self.top + nf]
        if dtype != F32:
            ap = ap.bitcast(dtype)
        ap = ap[:, 0:nel]
        if len(free_shape) == 2:
            ap = ap.rearrange("p (a b) -> p a b", a=free_shape[0])
        elif len(free_shape) == 3:
            ap = ap.rearrange("p (a b c) -> p a b c", a=free_shape[0], b=free_shape[1])
        if not top:
            self.top += nf
        return TT(ap, Tok(name))

    def mark(self):
        return self.top

    def reset(self, m):
        self.top = m


def build_program(debug=False):
    import os as _os
    nc = bass.Bass("TRN2", target_bir_lowering=False)

    def din(name, shape):
        return nc.dram_tensor(name, list(shape), F32, kind="ExternalInput").ap()

    def dout(name, shape, dt=F32):
        return nc.dram_tensor(name, list(shape), dt, kind="ExternalOutput").ap()

    x_d = din("x", [T, D])
    cvT_d = din("cvT", [128, 8])
    wada_d = din("w_ada", [D, 6 * D])
    bada_d = din("b_ada", [1, 6 * D])
    badaT_d = din("b_adaT", [128, 48])
    norm2T_d = din("norm2T", [128, 8])
    norm1_d = din("norm1", [1, D])
    norm2_d = din("norm2", [1, D])
    fnorm_d = din("fnorm", [1, D])
    win_d = din("w_in", [D, IN_W])
    wgu_d = din("w_gla_up", [2, 16, 256])
    bglaT_d = din("b_glaT", [128, 4])
    lbT_d = din("lbT", [128, 16])
    gnorm_d = din("gnorm", [128, 2])
    wout_d = din("w_out", [D, D])
    wup_d = din("w_ffn_up", [D, 2 * FFN_H])
    w11T_d = din("w11T", [128, NCH * 11])
    bconvT_d = din("bconvT", [128, NCH])
    wdn_d = din("w_ffn_down", [FFN_H, D])
    sig_d = din("sinit_g", [2, 4, 64, 128])
    sih_d = din("sinit_h", [2, 4, 128, 128])
    mchain_d = din("mchain", [128, 64])
    identF_d = din("identF", [128, 128])
    maskT2_d = din("maskT2", [128, 256])
    scanmask_d = din("scanmask", [1, T])
    y_d = dout("y", [T, D])
    sog_d = dout("snew_g", [8, 2, 4, 64, 128])
    soh_d = dout("snew_h", [8, 2, 4, 128, 128])
    dbg = {}
    if debug:
        dbg["h1T"] = dout("dbg_h1T", [128, 8 * T], BF16)
        dbg["mod"] = dout("dbg_mod", [128, 6 * D])
        dbg["mergedT"] = dout("dbg_mergedT", [128, 8 * T], BF16)
        dbg["x1"] = dout("dbg_x1", [T, D])

    es = ExitStack()
    with es:
        P = Prog(nc, es)
        AR = Arena(P, 52800)
        dumps = {}

        def dump(name, tt, ncols, dt, parts=128):
            if not debug:
                return
            if name not in dumps:
                dumps[name] = nc.dram_tensor("dd_" + name, [128, ncols], dt, kind="ExternalOutput").ap()
            ap = tt.t
            if len(ap.shape) == 3:
                ap = ap.rearrange("p a b -> p (a b)")
            elif len(ap.shape) == 4:
                ap = ap.rearrange("p a b c -> p (a b c)")
            P.dma("sp", dumps[name][0:parts], ap[0:parts], [tt], [], is_out=True)
        psb = [P.ps("psum%d" % i) for i in range(8)]
        psq = [Tok("psbank%d" % b, excl=True) for b in range(8)]

        def PQ(b, c0, c1):
            return [psq[b]]

        def pst(b):
            return psb[b].t

        rr = {"n": 0}

        def evac_eng():
            rr["n"] += 1
            return "act" if rr["n"] % 2 else "dve"

        def copy_op(eng, out, in_, reads, writes, scale=None):
            if eng == "act":
                if scale is None:
                    P.op("act", lambda e: e.activation(out, in_, AF.Copy), reads, writes)
                else:
                    P.op("act", lambda e: e.activation(out, in_, AF.Copy, scale=scale), reads, writes)
            else:
                if scale is None:
                    P.op(eng, lambda e: e.tensor_copy(out, in_), reads, writes)
                else:
                    P.op(eng, lambda e: e.tensor_scalar(out, in_, scale, None, op0=ALU.mult), reads, writes)

        identF = AR.alloc("identF", [128], F32)
        identB = AR.alloc("identB", [128], BF16)
        maskT2 = AR.alloc("maskT2", [256], F32)
        scanmask = AR.alloc("scanmask", [TS], BF16)
        mchain = AR.alloc("mchain", [64], F32)
        smalls = AR.alloc("smalls", [64], F32)
        LB = smalls.t[:, 0:8]
        L1M = smalls.t[:, 8:16]
        NEGB = smalls.t[:, 16:24]
        GS = smalls.t[:, 24:26]
        SC = smalls.t[:, 32:40]
        TMPS = smalls.t[:, 40:64]
        MB = [None] * 6
        hT = AR.alloc("hT", [8, T], BF16)
        hT_tok = [Tok("hT_b%d" % b) for b in range(NB)]
        onesF = AR.alloc("onesF", [128], F32)
        ring = {"slots": [], "n": 0}

        def next_w():
            w = ring["slots"][ring["n"] % len(ring["slots"])]
            ring["n"] += 1
            return w

        ph1_mark = AR.mark()
        ringB = [AR.alloc("wringB%d" % i, [8 * 640], BF16) for i in range(2)]
        ring["slots"] = [TT(ringB[i].t[:, 0:8 * 512], ringB[i].tok) for i in range(2)]
        srep = AR.alloc("srep", [8, 128], BF16)
        brow = [AR.alloc("brow%d" % i, [512], F32) for i in range(2)]
        MB[0] = AR.alloc("mod0", [D], F32)
        MB[1] = AR.alloc("mod1", [D], F32)

        P.dma("sp", identF.t, identF_d, [], [identF])
        P.dma("sp", maskT2.t, maskT2_d, [], [maskT2])
        P.dma("pool", scanmask.t, scanmask_d[:, 0:TS].partition_broadcast(128), [], [scanmask])
        P.dma("sp", mchain.t, mchain_d, [], [mchain])
        P.op("dve", lambda e: e.tensor_copy(identB.t, identF.t), [identF], [identB])
        lbT = AR.alloc("lbT", [16], F32)
        cvT = AR.alloc("cvT", [8], F32)
        bgl = AR.alloc("bgl", [8], F32)
        gnm = AR.alloc("gnm", [2], F32)
        P.dma("sp", lbT.t, lbT_d, [], [lbT])
        P.dma("sp", cvT.t, cvT_d, [], [cvT])
        P.dma("sp", bgl.t[:, 0:4], bglaT_d, [], [bgl])
        P.dma("sp", gnm.t, gnorm_d, [], [gnm])
        DD = TMPS[:, 0:8]
        EE = TMPS[:, 8:16]
        P.op("dve", lambda e: e.tensor_tensor(DD, lbT.t[:, 8:16], lbT.t[:, 0:8], ALU.subtract), [lbT], [smalls])
        P.op("act", lambda e: e.activation(EE, DD, AF.Exp), [smalls], [smalls])
        P.op("act", lambda e: e.activation(EE, EE, AF.Ln, bias=1.0), [smalls], [smalls])
        P.op("act", lambda e: e.activation(LB, EE, AF.Exp, scale=-1.0), [smalls], [smalls])
        P.op("dve", lambda e: e.tensor_tensor(L1M, DD, EE, ALU.subtract), [smalls], [smalls])
        P.op("dve", lambda e: e.tensor_scalar(NEGB[:, 0:4], bgl.t[:, 0:4], -1.0, None, op0=ALU.mult), [bgl], [smalls])
        P.op("dve", lambda e: e.tensor_scalar(GS, gnm.t, float(np.sqrt(128.0)), None, op0=ALU.mult), [gnm], [smalls])
        E2 = TMPS[:, 16:24]
        P.op("act", lambda e: e.activation(E2, cvT.t, AF.Exp, scale=-1.0), [cvT, smalls], [smalls])
        P.op("dve", lambda e: e.tensor_scalar(E2, E2, 1.0, None, op0=ALU.add), [smalls], [smalls])
        P.op("dve", lambda e: e.reciprocal(E2, E2), [smalls], [smalls])
        P.op("dve", lambda e: e.tensor_tensor(SC, cvT.t, E2, ALU.mult), [cvT, smalls], [smalls])
        sc_b = bass.AP(smalls.t.tensor, smalls.t.offset + 32, [list(smalls.t.ap[0]), [1, 8], [0, 128]])
        P.op("dve", lambda e: e.tensor_copy(srep.t, sc_b), [smalls], [srep])
        srepF = AR.alloc("srepF", [8, 128], F32)
        P.op("dve", lambda e: e.tensor_copy(srepF.t, sc_b), [smalls], [srepF])
        wF32 = [AR.alloc("wF32_%d" % i, [8, 512], F32) for i in range(2)]
        nrm = AR.alloc("nrm_bc", [D], F32)
        P.op("pool", lambda e: e.memset(onesF.t, 1.0), [], [onesF])
        pbank = {"n": 0}

        def mod_compute(j):
            col = [0, 1, 2, 3, 4, 5][j]
            for n in range(2):
                c0 = col * D + n * 512
                if n == 0:
                    w = next_w()
                    wv = w.t[:, 0:8 * 512].rearrange("p (k n) -> p k n", k=8)
                    P.dma("pool", wv, wada_d[:, c0:c0 + 512].rearrange("(k p) n -> p k n", p=128), [], [w])
                    sr = srep
                else:
                    w = wF32[j % 2]
                    wv = w.t
                    P.dma("sp", wv, wada_d[:, c0:c0 + 512].rearrange("(k p) n -> p k n", p=128), [], [w])
                    sr = srepF
                br = brow[(2 * j + n) % 2]
                P.dma("sp", br.t[0:1, :], bada_d[:, c0:c0 + 512], [], [br])
                b = pbank["n"] % 2
                pbank["n"] += 1
                for kc in range(8):
                    P.op("pe", lambda e, kc=kc, b=b, wv=wv, sr=sr: e.matmul(pst(b)[:, :], sr.t[:, kc, :], wv[:, kc, :], start=(kc == 0), stop=False),
                         [sr, w], PQ(b, 0, 512))
                P.op("pe", lambda e, b=b, br=br: e.matmul(pst(b)[:, :], onesF.t[0:1, :], br.t[0:1, :], start=False, stop=True),
                     [onesF, br], PQ(b, 0, 512))
                copy_op(evac_eng(), MB[j].t[:, n * 512:(n + 1) * 512], pst(b)[:, :], PQ(b, 0, 512), [MB[j]])

        mod_compute(0)
        mod_compute(1)
        P.dma("sp", nrm.t, norm1_d.partition_broadcast(128), [], [nrm])
        P.op("dve", lambda e: e.scalar_tensor_tensor(MB[1].t, MB[1].t, 1.0, nrm.t, op0=ALU.add, op1=ALU.mult), [MB[1], nrm], [MB[1]])

        xring = [AR.alloc("xring%d" % i, [D], F32) for i in range(3)]
        junk = AR.alloc("junk", [D], BF16)
        tmpf = AR.alloc("tmpf", [D], F32)
        hb = [AR.alloc("hb%d" % i, [D], BF16) for i in range(2)]
        stt = [AR.alloc("stt%d" % i, [4], F32) for i in range(3)]

        def norm_A(src_ap, src_toks, st):
            jk = junk
            P.op("act", lambda e: e.activation(jk.t, src_ap, AF.Square, accum_out=st.t[:, 0:1]), src_toks, [jk, st], multi=True)
            P.op("act", lambda e: e.activation(st.t[:, 1:2], st.t[:, 0:1], AF.Ln, scale=1.0 / D, bias=EPS), [st], [st])
            P.op("act", lambda e: e.activation(st.t[:, 2:3], st.t[:, 1:2], AF.Exp, scale=-0.5), [st], [st])

        def norm_B1(src_ap, src_toks, g_t, s_t, b, st):
            tf, h = tmpf, hb[b % 2]
            P.op("dve", lambda e: e.scalar_tensor_tensor(tf.t, src_ap, st.t[:, 2:3], g_t.t, op0=ALU.mult, op1=ALU.mult),
                 src_toks + [st, g_t], [tf])
            P.op("dve", lambda e: e.tensor_tensor(h.t, tf.t, s_t.t, ALU.add), [tf, s_t], [h])

        def norm_B2(b, pbk):
            h = hb[b % 2]
            pv = pst(pbk).bitcast(BF16).rearrange("p (k t) -> p k t", k=8)
            for kc in range(8):
                P.op("pe", lambda e, kc=kc: e.transpose(pv[:, kc, :], h.t[:, kc * 128:(kc + 1) * 128], identB.t),
                     [h, identB], PQ(pbk, 0, 512))
            copy_op("act", hT.t[:, :, b * 128:(b + 1) * 128], pv, PQ(pbk, 0, 512), [hT_tok[b]])

        _w0 = ringB[0]
        _wv0 = _w0.t[:, 0:8 * 640].rearrange("p (k n) -> p k n", k=8)
        for (c0_, n_, o_) in [(OFF_QA, 128, 0), (OFF_KA, 128, 128), (OFF_VA, 128, 256), (OFF_GA, 128, 384)]:
            P.dma("pool", _wv0[:, :, o_:o_ + n_], win_d[:, c0_:c0_ + n_].rearrange("(k p) n -> p k n", p=128), [], [_w0])
        for b in range(NB + 2):
            if b < NB:
                xt = xring[b % 3]
                P.dma("sp", xt.t, x_d[b * 128:(b + 1) * 128, :], [], [xt])
                norm_A(xt.t, [xt], stt[b % 3])
            if 1 <= b <= NB:
                xp = xring[(b - 1) % 3]
                norm_B1(xp.t, [xp], MB[1], MB[0], b - 1, stt[(b - 1) % 3])
            if b >= 2:
                norm_B2(b - 2, 2 + ((b - 2) % 2))
        if debug:
            P.dma("sp", dbg["h1T"], hT.t.rearrange("p k t -> p (k t)"), hT_tok, [], is_out=True)
            for j in range(2):
                P.dma("sp", dbg["mod"][:, j * D:(j + 1) * D], MB[j].t, [MB[j]], [], is_out=True)

        P.barrier()
        AR.reset(ph1_mark)
        BS = 64
        NBK = T // BS
        NG = T // 128
        BPS = TS // BS
        NSPAN = T // TS
        modT = AR.alloc("modT", [4, 8], F32, top=True)
        badaT = AR.alloc("badaT", [48], F32, top=True)
        scb = AR.alloc("scb", [8], BF16, top=True)
        top_keep = AR.end
        mergedT = AR.alloc("mergedT", [8, T], BF16, top=True)
        mT_tok = [Tok("mT_h%d" % i) for i in range(8)]
        for i in range(2):
            AR.alloc("wringB_again%d" % i, [8 * 640], BF16)
        ring["slots"] = ringB
        QT = AR.alloc("QT", [T], F32)
        KT = AR.alloc("KT", [T], F32)
        QTt = [AR.alloc("QTt%d" % d, [T], BF16) for d in range(2)]
        KTt = [AR.alloc("KTt%d" % d, [T], BF16) for d in range(2)]
        KTM = [AR.alloc("KTM%d" % d, [NG, 128], BF16) for d in range(2)]
        LH = AR.alloc("LH", [2, NBK, 2], F32)
        EX = AR.alloc("EX", [2, NBK, 2], F32)
        SCL = AR.alloc("SCL", [2, NBK, 2], F32)
        Vt = [AR.alloc("Vt%d" % pb, [NG, 128], BF16) for pb in range(2)]
        GG = [AR.alloc("GG%d" % pb, [NG, 128], BF16) for pb in range(2)]
        NSET = 4
        SETS = [dict(X=TT(KT.t[:, i * TS:(i + 1) * TS], Tok("Xs%d" % i)), U=AR.alloc("Us%d" % i, [TS], F32), T2=AR.alloc("T2s%d" % i, [TS], F32),
                     B=AR.alloc("Bs%d" % i, [TS], F32), KH=AR.alloc("KHt%d" % i, [TS], BF16)) for i in range(NSET)]
        SQt = [AR.alloc("SQt%d" % d, [NBK, 128], BF16) for d in range(2)]
        ST = [[AR.alloc("ST%d_%d" % (d, i), [128], F32) for i in range(3)] for d in range(2)]
        ATt = [AR.alloc("AT%d" % i, [256], BF16) for i in range(4)]
        MTM = [AR.alloc("MTM%d" % i, [128], BF16) for i in range(2)]
        sto = [AR.alloc("sto%d" % i, [4], F32) for i in range(4)]
        junk2 = AR.alloc("junk2", [128], BF16)
        LRT = AR.alloc("LRT", [2, T], BF16)
        WLR = AR.alloc("WLR", [8, 32], BF16)
        WGU = AR.alloc("WGU", [2, 256], BF16)

        def bcol(Bap, parts, col, nblk):
            pstep = Bap.ap[0][0]
            return bass.AP(Bap.tensor, Bap.offset + col, [[pstep, parts], [BS, nblk], [0, BS]])

        P.dma("pool", WLR.t, win_d[:, OFF_LR:OFF_LR + 32].rearrange("(k p) n -> p k n", p=128), [], [WLR])
        P.dma("pool", WGU.t[0:16], wgu_d.rearrange("d r k -> r d k"), [], [WGU])
        P.dma("sp", badaT.t, badaT_d, [], [badaT])
        P.op("dve", lambda e: e.tensor_copy(scb.t, SC), [smalls], [scb])
        fmb = {"n": 0}

        def fm_bank():
            fmb["n"] += 1
            return fmb["n"] % 2

        def head_cfg(hh):
            gla = hh < 4
            h = hh if gla else hh - 4
            if gla:
                p = h // 2
                owner = (h % 2 == 0)
                if owner:
                    groups = [(OFF_QA + 128 * p, 128, 0), (OFF_KA + 128 * p, 128, 128), (OFF_VA + 128 * h, 128, 256), (OFF_GA + 128 * h, 128, 384)]
                    c_vg = 256
                else:
                    groups = [(OFF_VA + 128 * h, 128, 0), (OFF_GA + 128 * h, 128, 128)]
                    c_vg = 0
                return dict(gla=True, h=h, p=p, owner=owner, dk=64, po=64 * (h % 2), dscale=-1.0 / 16.0, groups=groups, c_q=0, c_k=128, c_vg=c_vg, c_f=None)
            groups = [(OFF_QB + 128 * h, 128, 0), (OFF_FB + 128 * h, 128, 128), (OFF_FB + 512 + 128 * h, 128, 256),
                      (OFF_IB + 128 * h, 128, 384), (OFF_GB + 128 * h, 128, 512)]
            return dict(gla=False, h=h, p=0, owner=True, dk=128, po=0, dscale=1.0, groups=groups, c_q=0, c_k=None, c_vg=384, c_f=(128, 256))

        head_w = {}

        def load_head_weights(hh, dma=True):
            cfg = head_cfg(hh)
            w = ring["slots"][hh % 2]
            wv = w.t[:, 0:8 * 640].rearrange("p (k n) -> p k n", k=8)
            if dma:
                for (c0, n, o) in cfg["groups"]:
                    P.dma("pool", wv[:, :, o:o + n], win_d[:, c0:c0 + n].rearrange("(k p) n -> p k n", p=128), [], [w])
            head_w[hh] = (w, wv)

        def mod_pipeline(js, w):
            htok = [Tok(w.tok.name + "_h0"), Tok(w.tok.name + "_h1")]
            chunks = [(j, n) for j in js for n in range(4)]

            def dma(i):
                j, n = chunks[i]
                half = i % 2
                wv_ = w.t[:, half * 2048:(half + 1) * 2048].rearrange("p (k n) -> p k n", k=8)
                c0 = j * D + n * 256
                P.dma("pool", wv_, wada_d[:, c0:c0 + 256].rearrange("(k p) n -> p k n", p=128), [], [htok[half]] + ([w] if i < 2 else []))

            dma(0)
            dma(1)
            yield
            for i, (j, n) in enumerate(chunks):
                half = i % 2
                wv_ = w.t[:, half * 2048:(half + 1) * 2048].rearrange("p (k n) -> p k n", k=8)
                ji = js.index(j)
                last2 = i >= len(chunks) - 2
                for mb in range(2):
                    col = 8 * ji + 2 * n + mb
                    for kc in range(8):
                        P.op("pe", lambda e, kc=kc, mb=mb, col=col, wv_=wv_: e.matmul(pst(7)[:, col:col + 1], wv_[:, kc, mb * 128:(mb + 1) * 128], scb.t[:, kc:kc + 1],
                                                                                 start=(kc == 0), stop=(kc == 7)), [htok[half], scb] + ([w] if last2 else []), PQ(7, 0, 512))
                if i + 2 < len(chunks):
                    dma(i + 2)
                if n == 3:
                    copy_op("dve", modT.t[:, j - 2, :], pst(7)[:, 8 * ji:8 * ji + 8], PQ(7, 0, 512), [modT])
                    P.op("dve", lambda e, j=j: e.tensor_tensor(modT.t[:, j - 2, :], modT.t[:, j - 2, :], badaT.t[:, j * 8:(j + 1) * 8], ALU.add), [modT, badaT], [modT])
                yield

        def mod_computeT(j, w):
            b = fm_bank()
            for n in range(4):
                c0 = j * D + n * 256
                half = n % 2
                wv = w.t[:, half * 2048:(half + 1) * 2048].rearrange("p (k n) -> p k n", k=8)
                P.dma("pool", wv, wada_d[:, c0:c0 + 256].rearrange("(k p) n -> p k n", p=128), [], [w])
                for mb in range(2):
                    for kc in range(8):
                        P.op("pe", lambda e, kc=kc, mb=mb, n=n, wv=wv: e.matmul(pst(b)[:, 2 * n + mb:2 * n + mb + 1], wv[:, kc, mb * 128:(mb + 1) * 128], scb.t[:, kc:kc + 1],
                                                                              start=(kc == 0), stop=(kc == 7)), [w, scb], PQ(b, 0, 512))
                yield
            copy_op("dve", modT.t[:, j - 2, :], pst(b)[:, 0:8], PQ(b, 0, 512), [modT])
            P.op("dve", lambda e: e.tensor_tensor(modT.t[:, j - 2, :], modT.t[:, j - 2, :], badaT.t[:, j * 8:(j + 1) * 8], ALU.add), [modT, badaT], [modT])

        def stageAB1(hh):
            cfg = head_cfg(hh)
            gla, h, dscale, owner = cfg["gla"], cfg["h"], cfg["dscale"], cfg["owner"]
            dk = 128
            c_q, c_k, c_vg, c_f = cfg["c_q"], cfg["c_k"], cfg["c_vg"], cfg["c_f"]
            pb = hh % 2
            w, wv = head_w[hh]
            if hh + 1 < 8:
                load_head_weights(hh + 1)
            myVt, myGG = Vt[pb], GG[pb]
            Us = SETS[0]["U"]

            def fm_proj(c0, M, tiles, dst_ap_fn, dst_toks, scale=None):
                for tt in tiles:
                    b = fm_bank()
                    for kc in range(8):
                        P.op("pe", lambda e, kc=kc, b=b, tt=tt: e.matmul(pst(b)[0:M, :], wv[:, kc, c0:c0 + M], hT.t[:, kc, tt * 512:(tt + 1) * 512],
                                                                       start=(kc == 0), stop=(kc == 7)),
                             [w] + hT_tok[4 * tt:4 * tt + 4], PQ(b, 0, 512))
                    yield
                    copy_op(evac_eng(), dst_ap_fn(tt), pst(b)[0:M, :], PQ(b, 0, 512), dst_toks, scale=scale)

            if owner:
                yield from fm_proj(c_q, dk, range(4), lambda tt: QT.t[0:dk, tt * 512:(tt + 1) * 512], [QT], scale=(0.125 if gla else None))
            if gla and owner:
                yield from fm_proj(c_k, dk, range(4), lambda tt: KT.t[0:dk, tt * 512:(tt + 1) * 512], [KT])
            if hh == 0:
                for d in range(2):
                    for tt in range(4):
                        b = fm_bank()
                        for kc in range(8):
                            P.op("pe", lambda e, kc=kc, b=b, tt=tt, d=d: e.matmul(pst(b)[0:16, :], WLR.t[:, kc, 16 * d:16 * d + 16], hT.t[:, kc, tt * 512:(tt + 1) * 512],
                                                                                  start=(kc == 0), stop=(kc == 7)),
                                 [WLR] + hT_tok[4 * tt:4 * tt + 4], PQ(b, 0, 512))
                        copy_op(evac_eng(), LRT.t[0:16, d, tt * 512:(tt + 1) * 512], pst(b)[0:16, :], PQ(b, 0, 512), [LRT])
                        yield
            for bp in range(NG // 2):
                bk = fm_bank()
                for i in range(2):
                    blk = 2 * bp + i
                    for kc in range(8):
                        P.op("pe", lambda e, kc=kc, bk=bk, i=i, blk=blk: e.matmul(pst(bk)[:, i * 256:(i + 1) * 256], hT.t[:, kc, blk * 128:(blk + 1) * 128],
                                                                                wv[:, kc, c_vg:c_vg + 256], start=(kc == 0), stop=(kc == 7)),
                             [w, hT_tok[blk]], PQ(bk, i * 256, (i + 1) * 256))
                pv = pst(bk).rearrange("p (b c) -> p b c", b=2)
                yield
                ce = evac_eng()
                copy_op(ce, myVt.t[:, 2 * bp:2 * bp + 2, :], pv[:, :, 0:128], PQ(bk, 0, 512), [myVt])
                copy_op(ce, myGG.t[:, 2 * bp:2 * bp + 2, :], pv[:, :, 128:256], PQ(bk, 0, 512), [myGG])
            GPS = TS // 128
            for s_ in range(NSPAN):
                gsp = myGG.t[:, s_ * GPS:(s_ + 1) * GPS, :].rearrange("p b c -> p (b c)")
                P.op("act", lambda e, gsp=gsp: e.activation(Us.t, gsp, AF.Exp, scale=-1.0), [myGG], [Us])
                P.op("act", lambda e: e.activation(Us.t, Us.t, AF.Ln, bias=1.0), [Us], [Us])
                P.op("act", lambda e: e.activation(Us.t, Us.t, AF.Exp, scale=-1.0), [Us], [Us])
                P.op("dve", lambda e, gsp=gsp: e.tensor_tensor(gsp, gsp, Us.t, ALU.mult), [myGG, Us], [myGG])
                yield

        def stageAB2(hh):
            cfg = head_cfg(hh)
            gla, h, dscale, owner, pr = cfg["gla"], cfg["h"], cfg["dscale"], cfg["owner"], cfg["p"]
            dk = 128
            c_f = cfg["c_f"]
            w, wv = head_w[hh]
            myQTt, myKTt, myKTM, myLH, myEX, mySCL = QTt, KTt, KTM, LH, EX, SCL
            if not owner:
                return
            modgen = mod_pipeline([2, 3] if hh == 0 else [4, 5], w) if (gla and owner) else None

            def fm_proj(c0, M, tiles, dst_ap_fn, dst_toks, scale=None):
                for tt in tiles:
                    b = fm_bank()
                    for kc in range(8):
                        P.op("pe", lambda e, kc=kc, b=b, tt=tt: e.matmul(pst(b)[0:M, :], wv[:, kc, c0:c0 + M], hT.t[:, kc, tt * 512:(tt + 1) * 512],
                                                                       start=(kc == 0), stop=(kc == 7)),
                             [w] + hT_tok[4 * tt:4 * tt + 4], PQ(b, 0, 512))
                    copy_op("act", dst_ap_fn(tt), pst(b)[0:M, :], PQ(b, 0, 512), dst_toks, scale=scale)
                    yield

            v3 = lambda ap: ap.rearrange("p (b t) -> p b t", b=BPS)

            def decay(d, s_, tiles, sp0, sp1):
                st_ = SETS[(s_ % 2) * 2 + d]
                Xs, Us, T2s, Bs, KHt = st_["X"], st_["U"], st_["T2"], st_["B"], st_["KH"]
                X_, U_, T2_, B_, KH_ = Xs.t[0:dk], Us.t[0:dk], T2s.t[0:dk], Bs.t[0:dk], KHt.t[0:dk]
                col = (d * 2 + pr) if gla else (d * 4 + h)
                sel = d
                if gla:
                    for ti, tt in enumerate(tiles):
                        b = fm_bank()
                        P.op("pe", lambda e, b=b, tt=tt: e.matmul(pst(b)[0:128, :], WGU.t[0:16, d, 128 * pr:128 * pr + 128], LRT.t[0:16, d, tt * 512:(tt + 1) * 512],
                                                                start=True, stop=True), [WGU, LRT], PQ(b, 0, 512))
                        P.op("act", lambda e, b=b, ti=ti: e.activation(U_[:, ti * 512:(ti + 1) * 512], pst(b)[0:128, :], AF.Exp, scale=-1.0,
                                                                      bias=NEGB[:, col:col + 1]), PQ(b, 0, 512) + [smalls], [Us])
                    yield
                    P.op("act", lambda e: e.activation(T2_, U_, AF.Ln, bias=1.0), [Us], [T2s])
                    P.op("dve", lambda e: e.tensor_tensor_scan(B_, scanmask.t[0:dk, :], T2_, 0.0, ALU.mult, ALU.add), [scanmask, T2s], [Bs])
                else:
                    cf = c_f[d]
                    yield from fm_proj(cf, 128, tiles, lambda tt: X_[:, (tt - tiles[0]) * 512:(tt - tiles[0] + 1) * 512], [Xs])
                    P.op("act", lambda e: e.activation(U_, X_, AF.Exp, scale=-1.0), [Xs], [Us])
                    P.op("act", lambda e: e.activation(T2_, U_, AF.Ln, bias=1.0), [Us], [T2s])
                    P.op("act", lambda e: e.activation(U_, U_, AF.Ln, bias=1.0, scale=LB[:, col:col + 1]), [Us, smalls], [Us])
                    yield
                    P.op("dve", lambda e: e.tensor_tensor(U_, U_, T2_, ALU.subtract), [Us, T2s], [Us])
                    P.op("act", lambda e: e.activation(X_, X_, AF.Exp), [Xs], [Xs])
                    P.op("act", lambda e: e.activation(X_, X_, AF.Ln, bias=1.0), [Xs], [Xs])
                    P.op("dve", lambda e: e.tensor_tensor_scan(B_, scanmask.t[0:dk, :], U_, 0.0, ALU.mult, ALU.add), [scanmask, Us], [Bs])
                yield
                B3 = v3(B_)
                MID = BS // 2 - 1
                lh = myLH.t[0:dk, d, s_ * BPS:(s_ + 1) * BPS, :]
                ex = myEX.t[0:dk, d, s_ * BPS:(s_ + 1) * BPS, :]
                P.op("dve", lambda e: e.tensor_copy(lh[:, :, 0:1], B3[:, :, MID:MID + 1]), [Bs], [myLH])
                P.op("dve", lambda e: e.tensor_tensor(lh[:, :, 1:2], B3[:, :, BS - 1:BS], B3[:, :, MID:MID + 1], ALU.subtract), [Bs], [myLH])
                P.op("act", lambda e: e.activation(ex, lh, AF.Exp, scale=dscale), [myLH], [myEX])
                bmid = bcol(B_, dk, MID, BPS)
                ehb = bass.AP(myEX.t.tensor, myEX.t[0:dk, d, s_ * BPS:(s_ + 1) * BPS, 1 - sel].offset,
                              [[myEX.t.ap[0][0], dk], [2, BPS], [0, BS]])
                if gla:
                    if d == 0:
                        P.op("dve", lambda e: e.tensor_tensor(v3(U_), B3, bmid, ALU.subtract), [Bs], [Us])
                    else:
                        P.op("dve", lambda e: e.tensor_tensor(T2_, B_, T2_, ALU.subtract), [Bs, T2s], [T2s])
                        P.op("dve", lambda e: e.tensor_tensor(v3(U_), bmid, v3(T2_), ALU.subtract), [Bs, T2s], [Us])
                    P.op("dve", lambda e: e.tensor_scalar(U_, U_, 640.0, -640.0, op0=ALU.min, op1=ALU.max), [Us], [Us])
                    P.op("act", lambda e: e.activation(T2_, U_, AF.Exp, scale=dscale), [Us], [T2s])
                    P.op("pool", lambda e: e.tensor_tensor(myQTt[d].t[0:dk, sp0:sp1], QT.t[0:dk, sp0:sp1], T2_, ALU.mult), [QT, T2s], [myQTt[d]])
                    yield
                    P.op("act", lambda e: e.activation(T2_, U_, AF.Exp, scale=-dscale), [Us], [T2s])
                    P.op("dve", lambda e: e.tensor_tensor(myKTt[d].t[0:dk, sp0:sp1], KT.t[0:dk, sp0:sp1], T2_, ALU.mult), [KT, T2s], [myKTt[d]])
                else:
                    if d == 0:
                        P.op("dve", lambda e: e.tensor_tensor(v3(T2_), B3, bmid, ALU.subtract), [Bs], [T2s])
                    else:
                        P.op("dve", lambda e: e.tensor_tensor(U_, B_, U_, ALU.subtract), [Bs, Us], [Us])
                        P.op("dve", lambda e: e.tensor_tensor(v3(T2_), bmid, v3(U_), ALU.subtract), [Bs, Us], [T2s])
                    P.op("dve", lambda e: e.tensor_scalar(T2_, T2_, 40.0, -40.0, op0=ALU.min, op1=ALU.max), [T2s], [T2s])
                    P.op("act", lambda e: e.activation(U_, T2_, AF.Exp), [T2s], [Us])
                    P.op("pool", lambda e: e.tensor_tensor(myQTt[d].t[0:dk, sp0:sp1], QT.t[0:dk, sp0:sp1], U_, ALU.mult), [QT, Us], [myQTt[d]])
                    yield
                    P.op("dve", lambda e: e.tensor_tensor(X_, X_, T2_, ALU.add), [Xs, T2s], [Xs])
                    P.op("act", lambda e: e.activation(myKTt[d].t[0:dk, sp0:sp1], X_, AF.Exp, scale=-1.0, bias=L1M[:, col:col + 1]), [Xs, smalls], [myKTt[d]])
                yield
                P.op("dve", lambda e: e.tensor_tensor(v3(KH_), v3(myKTt[d].t[0:dk, sp0:sp1]), ehb, ALU.mult), [myKTt[d], myEX], [KHt])
                kb = fm_bank()
                pvk = pst(kb).bitcast(BF16).rearrange("p (b t) -> p b t", b=8)
                ng = TS // 128
                for i in range(ng):
                    P.op("pe", lambda e, i=i: e.transpose(pvk[:, i, 0:dk], KH_[:, i * 128:(i + 1) * 128], identB.t[0:dk, 0:dk]), [KHt, identB], PQ(kb, 0, 512))
                copy_op(evac_eng(), myKTM[d].t[:, s_ * ng:(s_ + 1) * ng, 0:dk], pvk[:, 0:ng, 0:dk], PQ(kb, 0, 512), [myKTM[d]])
                yield

            for s0_ in range(0, NSPAN, 2):
                gens = []
                for s_ in (s0_, s0_ + 1):
                    tiles = list(range(s_ * TS // 512, (s_ + 1) * TS // 512))
                    for d in range(2):
                        gens.append(decay(d, s_, tiles, s_ * TS, (s_ + 1) * TS))
                alive = [True] * len(gens)
                while any(alive):
                    for gi in range(len(gens)):
                        if alive[gi]:
                            try:
                                next(gens[gi])
                            except StopIteration:
                                alive[gi] = False
                    if modgen is not None:
                        try:
                            next(modgen)
                        except StopIteration:
                            modgen = None
                    yield
            for d in range(2):
                sel = d
                mc = mchain.t[0:dk, d * NBK:(d + 1) * NBK]
                P.op("dve", lambda e, d=d, sel=sel, mc=mc: e.tensor_tensor(mySCL.t[0:dk, d, :, 0], myEX.t[0:dk, d, :, sel], mc, ALU.mult), [myEX, mchain], [mySCL])
                P.op("dve", lambda e, d=d, sel=sel: e.tensor_tensor(mySCL.t[0:dk, d, :, 1], mySCL.t[0:dk, d, :, 0], myEX.t[0:dk, d, :, 1 - sel], ALU.mult), [myEX, mySCL], [mySCL])
            yield
            if modgen is not None:
                for _ in modgen:
                    pass

        def stageC(hh):
            cfg = head_cfg(hh)
            gla, h, dk, po = cfg["gla"], cfg["h"], cfg["dk"], cfg["po"]
            pq = slice(po, po + dk)
            pb = hh % 2
            myQTt, myKTt, myKTM, myVt, myGG, mySCL = QTt, KTt, KTM, Vt[pb], GG[pb], SCL
            kmt = {"n": 0}
            pv7 = pst(7).bitcast(BF16).rearrange("p (r s t) -> p r s t", r=2, s=4)
            grp_done = {}
            for d in range(2):
                src = (sig_d if gla else sih_d)[d, h]
                P.dma("sp", ST[d][0].t[pq], src, [], [ST[d][0]])
            state = {0: ST[0][0], 1: ST[1][0]}
            nxt = [1, 1]

            def chain_p(d, n, cb, after=()):
                g, hf = n // 2, n % 2
                return P.op("pe", lambda e: e.matmul(pst(cb)[pq, d * 128:(d + 1) * 128], myKTM[d].t[hf * 64:(hf + 1) * 64, g, pq], myVt.t[hf * 64:(hf + 1) * 64, g, :], start=True, stop=True),
                            [myKTM[d], myVt], PQ(cb, 0, 256), extra=after)

            def chain_step(d, n, cb):
                prev = state[d]
                new = ST[d][nxt[d]]
                nxt[d] = (nxt[d] + 1) % int(_os.environ.get('DBG_TRI', '3'))
                P.op("act", lambda e: e.activation(SQt[d].t[pq, n, :], prev.t[pq], AF.Copy, scale=mySCL.t[pq, d, n, 0:1]), [prev, mySCL], [SQt[d]])
                P.op("dve", lambda e: e.scalar_tensor_tensor(new.t[pq], prev.t[pq], mySCL.t[pq, d, n, 1:2], pst(cb)[pq, d * 128:(d + 1) * 128], op0=ALU.mult, op1=ALU.add),
                     [prev, mySCL] + PQ(cb, 0, 256), [new])
                state[d] = new
                if (d == 0 and n % 4 == 3) or (d == 1 and n % 4 == 0):
                    dst = (sog_d if gla else soh_d)[n // 4, d, h]
                    P.dma("sp", dst, new.t[pq], [new], [], is_out=True)

            gctr = {"n": 0}
            ginfo = {}

            def og_a1(g):
                i = gctr["n"]
                gctr["n"] += 1
                ginfo[g] = i
                blk = slice(g * 128, (g + 1) * 128)
                for d in range(2):
                    P.op("pe", lambda e, d=d: e.matmul(pst(5)[:, d * 128:(d + 1) * 128], myKTt[d].t[pq, blk], myQTt[d].t[pq, blk], start=True, stop=True),
                         [myKTt[d], myQTt[d]], PQ(5, 0, 256))

            def og_a2(g):
                at = ATt[ginfo[g] % 4]
                P.op("dve", lambda e: e.tensor_tensor(at.t, pst(5)[:, 0:256], maskT2.t, ALU.mult), PQ(5, 0, 256) + [maskT2], [at])

            def og_b(g):
                i = ginfo[g]
                at = ATt[i % 4]
                ob_ = [2, 6][i % 2]
                og = pst(ob_)[:, 0:128]
                otok = PQ(ob_, 0, 128)
                P.op("pe", lambda e: e.matmul(og, at.t[:, 0:128], myVt.t[:, g, :], start=True, stop=False), [at, myVt], otok)
                P.op("pe", lambda e: e.matmul(og, at.t[:, 128:256], myVt.t[:, g, :], start=False, stop=False), [at, myVt], otok)
                for hf in range(2):
                    for d in range(2):
                        last = (hf == 1 and d == 1)
                        c0 = g * 128 + hf * 64
                        P.op("pe", lambda e, hf=hf, d=d, last=last, c0=c0: e.matmul(pst(ob_)[hf * 64:(hf + 1) * 64, 0:128], myQTt[d].t[pq, c0:c0 + 64], SQt[d].t[pq, 2 * g + hf, :],
                                                                                   start=False, stop=last), [myQTt[d], SQt[d]], otok)

            def og_c(g):
                i = ginfo[g]
                ob_ = [2, 6][i % 2]
                og = pst(ob_)[:, 0:128]
                otok = PQ(ob_, 0, 128)
                so = sto[i % 4]
                P.op("act", lambda e: e.activation(junk2.t, og, AF.Square, accum_out=so.t[:, 0:1]), otok, [junk2, so], multi=True)
                P.op("act", lambda e: e.activation(so.t[:, 1:2], so.t[:, 0:1], AF.Ln, bias=128.0 * EPS), [so], [so])
                P.op("act", lambda e: e.activation(so.t[:, 2:3], so.t[:, 1:2], AF.Exp, scale=-0.5), [so], [so])

            def og_d(g):
                i = ginfo[g]
                ob_ = [2, 6][i % 2]
                og = pst(ob_)[:, 0:128]
                otok = PQ(ob_, 0, 128)
                so = sto[i % 4]
                mt = MTM[i % 2]
                P.op("dve", lambda e: e.scalar_tensor_tensor(mt.t, og, so.t[:, 2:3], myGG.t[:, g, :], op0=ALU.mult, op1=ALU.mult), otok + [so, myGG], [mt])
                grp = g // 4
                r = grp % 2
                rtok = PQ(7, r * 256, (r + 1) * 256)
                P.op("pe", lambda e: e.transpose(pv7[:, r, g % 4, :], mt.t, identB.t), [mt, identB], rtok)
                grp_done[grp] = grp_done.get(grp, 0) + 1
                if grp_done[grp] == 4:
                    copy_op(evac_eng(), mergedT.t[:, hh, grp * 512:(grp + 1) * 512], pv7[:, r].rearrange("p s t -> p (s t)"), rtok, [mT_tok[hh]])

            ready = {g: max(2 * g + 1, NBK - 1 - 2 * g) for g in range(NG)}
            p_ahead = int(_os.environ.get('DBG_PAHEAD', '1'))
            if p_ahead:
                o_ = chain_p(0, 0, 3)
                chain_p(1, NBK - 1, 3, after=(o_,))
            for s_ in range(NBK + 4):
                if s_ < NBK:
                    cb = 3 + (s_ % 2)
                    if not p_ahead:
                        o_ = chain_p(0, s_, cb)
                        chain_p(1, NBK - 1 - s_, cb, after=(o_,))
                    chain_step(0, s_, cb)
                    chain_step(1, NBK - 1 - s_, cb)
                    if p_ahead and s_ + 1 < NBK:
                        cbn = 3 + ((s_ + 1) % 2)
                        o_ = chain_p(0, s_ + 1, cbn)
                        chain_p(1, NBK - 2 - s_, cbn, after=(o_,))
                for g in range(NG):
                    if ready[g] == s_ - 3:
                        og_d(g)
                for g in range(NG):
                    if ready[g] == s_ - 2:
                        og_c(g)
                for g in range(NG):
                    if ready[g] == s_ - 1:
                        og_b(g)
                pair_now = [g for g in range(NG) if ready[g] == s_ + 3]
                pair_prev = [g for g in range(NG) if ready[g] == s_ + 2]
                pair_prev2 = [g for g in range(NG) if ready[g] == s_ + 1]
                if pair_prev2:
                    og_a2(pair_prev2[1])
                if pair_prev:
                    og_a2(pair_prev[0])
                    og_a1(pair_prev[1])
                if pair_now:
                    og_a1(pair_now[0])
                yield

        heads = [int(v) for v in _os.environ.get('DBG_HEADS', '0,1,2,3,4,5,6,7').split(',') if v != '']
        assert heads == list(range(8))
        load_head_weights(0, dma=False)
        for _ in stageAB1(0):
            pass
        for _ in stageAB2(0):
            pass
        ilv = int(_os.environ.get('DBG_ILV', '2'))
        for hh in range(8):
            cgen = stageC(hh)
            abgen = stageAB1(hh + 1) if hh + 1 < 8 else None
            c_alive, ab_alive = True, abgen is not None
            step = 0
            while c_alive or ab_alive:
                if c_alive:
                    try:
                        next(cgen)
                    except StopIteration:
                        c_alive = False
                if ab_alive and (not c_alive or (ilv >= 1 and step % ilv == 0)):
                    try:
                        next(abgen)
                    except StopIteration:
                        ab_alive = False
                step += 1
            if hh + 1 < 8:
                for _ in stageAB2(hh + 1):
                    pass
        if debug:
            P.dma("sp", dbg["mergedT"], mergedT.t.rearrange("p k t -> p (k t)"), mT_tok, [], is_out=True)
        P.barrier()
        AR.reset(ph1_mark)
        X1 = AR.alloc("X1", [NB, D], F32)
        X1_tok = [Tok("X1_b%d" % b) for b in range(NB)]
        ph3_keep = AR.mark()
        ringC = [AR.alloc("wringC%d" % i, [8 * 256], BF16) for i in range(3)]
        ring["slots"] = ringC
        ring["n"] = 0
        pair_w = {}

        def load_pair(j):
            w = next_w()
            wv = w.t[:, 0:8 * 256].rearrange("p (k n) -> p k n", k=8)
            P.dma("pool", wv[:, :, 0:128], wup_d[:, j * 128:(j + 1) * 128].rearrange("(k p) n -> p k n", p=128), [], [w])
            P.dma("pool", wv[:, :, 128:256], wup_d[:, FFN_H + j * 128:FFN_H + (j + 1) * 128].rearrange("(k p) n -> p k n", p=128), [], [w])
            pair_w[j] = (w, wv)

        load_pair(0)
        load_pair(1)
        MB[2] = AR.alloc("gate1_bc", [D], F32)
        MB[3] = AR.alloc("shift2_bc", [D], F32)
        MB[4] = AR.alloc("g2_bc", [D], F32)
        WO = AR.alloc("WO", [8, D], BF16)
        wstage = [AR.alloc("wstage%d" % i, [D], F32) for i in range(4)]
        junk = AR.alloc("junk", [D], BF16)
        tmpf = AR.alloc("tmpf", [D], F32)
        hb = [AR.alloc("hb%d" % i, [D], BF16) for i in range(2)]
        stt = [AR.alloc("stt%d" % i, [4], F32) for i in range(3)]
        dgt = [AR.alloc("dgt%d" % i, [128], F32) for i in range(2)]
        n2T = AR.alloc("n2T", [8], F32)
        g2T = AR.alloc("g2T", [8], F32)
        xb = {"n": 0}

        def expand(vec_ap, vec_toks, dst):
            for half in range(2):
                b = xb["n"] % 2
                xb["n"] += 1
                for q in range(4):
                    kc = half * 4 + q
                    dg = dgt[kc % 2]
                    P.op("dve", lambda e, kc=kc, dg=dg: e.tensor_scalar(dg.t, identF.t, vec_ap[:, kc:kc + 1], None, op0=ALU.mult), [identF] + vec_toks, [dg])
                    P.op("pe", lambda e, q=q, b=b, dg=dg: e.matmul(pst(b)[:, q * 128:(q + 1) * 128], onesF.t, dg.t, start=True, stop=True), [onesF, dg], PQ(b, 0, 512))
                copy_op(evac_eng(), dst.t[:, half * 512:(half + 1) * 512], pst(b)[:, :], PQ(b, 0, 512), [dst])

        P.dma("sp", n2T.t, norm2T_d, [], [n2T])
        P.op("dve", lambda e: e.scalar_tensor_tensor(g2T.t, modT.t[:, 2, :], 1.0, n2T.t, op0=ALU.add, op1=ALU.mult), [modT, n2T], [g2T])
        for kc in range(4):
            P.dma("sp", wstage[kc].t, wout_d[kc * 128:(kc + 1) * 128, :], [], [wstage[kc]])
        expand(modT.t[:, 0, :], [modT], MB[2])
        for kc in range(8):
            ws = wstage[kc % 4]
            if kc >= 4:
                P.dma("sp", ws.t, wout_d[kc * 128:(kc + 1) * 128, :], [], [ws])
            gcol = 0 if kc < 4 else 1
            P.op("dve", lambda e, kc=kc, ws=ws, gcol=gcol: e.scalar_tensor_tensor(WO.t[:, kc, :], ws.t, GS[:, gcol:gcol + 1], MB[2].t, op0=ALU.mult, op1=ALU.mult),
                 [ws, smalls, MB[2]], [WO])
        for b in range(NB):
            P.dma("sp", X1.t[:, b, :], x_d[b * 128:(b + 1) * 128, :], [], [X1_tok[b]])
        expand(modT.t[:, 1, :], [modT], MB[3])
        expand(g2T.t, [g2T], MB[4])
        ob = {"n": 0}
        for b in range(NB):
            for half in range(2):
                bk = 2 + ob["n"] % 2
                ob["n"] += 1
                for kc in range(8):
                    P.op("pe", lambda e, kc=kc, bk=bk, b=b, half=half: e.matmul(pst(bk)[:, :], mergedT.t[:, kc, b * 128:(b + 1) * 128], WO.t[:, kc, half * 512:(half + 1) * 512],
                                                                              start=(kc == 0), stop=(kc == 7)), [mT_tok[kc], WO], PQ(bk, 0, 512))
                hs = slice(half * 512, (half + 1) * 512)
                P.op("dve", lambda e, bk=bk, b=b, hs=hs: e.tensor_tensor(X1.t[:, b, hs], pst(bk)[:, :], X1.t[:, b, hs], ALU.add), PQ(bk, 0, 512) + [X1_tok[b]], [X1_tok[b]])
            norm_A(X1.t[:, b, :], [X1_tok[b]], stt[b % 3])
            if b >= 1:
                norm_B1(X1.t[:, b - 1, :], [X1_tok[b - 1]], MB[4], MB[3], b - 1, stt[(b - 1) % 3])
            if b >= 2:
                norm_B2(b - 2, 4 + ((b - 2) % 2))
            if debug:
                P.dma("sp", dbg["x1"][b * 128:(b + 1) * 128, :], X1.t[:, b, :], [X1_tok[b]], [], is_out=True)
        norm_B1(X1.t[:, NB - 1, :], [X1_tok[NB - 1]], MB[4], MB[3], NB - 1, stt[(NB - 1) % 3])
        norm_B2(NB - 2, 4 + ((NB - 2) % 2))
        norm_B2(NB - 1, 4 + ((NB - 1) % 2))
        if debug:
            dump("h2T", TT(hT.t, hT_tok[0]), 8 * T, BF16)

        P.barrier()
        AR.reset(ph3_keep)
        for i in range(3):
            AR.alloc("wringC_again%d" % i, [8 * 256], BF16)
        AR.end = top_keep
        MB[5] = AR.alloc("gate2_bc", [D], F32)
        FN = AR.alloc("fnorm_bc", [D], F32)
        dgt = [AR.alloc("dgt%d" % i, [128], F32) for i in range(2)]
        GMAX = max(j1 - j0 for j0, j1 in GROUPS)
        HT = AR.alloc("HT", [GMAX, T], BF16)
        WD = AR.alloc("WD", [GMAX, D], BF16)
        wdst = [AR.alloc("wdst%d" % i, [D], F32) for i in range(2)]
        UB = [AR.alloc("UB%d" % i, [UW], BF16) for i in range(2)]
        SG = AR.alloc("SG", [T], F32)
        DG = [AR.alloc("DG%d" % i, [11, 128], BF16) for i in range(2)]
        w11T = AR.alloc("w11T", [NCH * 11], F32)
        bconvT = AR.alloc("bconvT", [NCH], F32)
        ring["slots"] = ringC
        yst = [AR.alloc("yst%d" % i, [D], F32) for i in range(2)]
        DACC = AR.alloc("DACC", [T], BF16)
        ctmp = yst
        junk = AR.alloc("junk", [D], BF16)
        stt = [AR.alloc("stt%d" % i, [4], F32) for i in range(3)]
        expand(modT.t[:, 3, :], [modT], MB[5])
        P.dma("sp", FN.t, fnorm_d.partition_broadcast(128), [], [FN])
        P.dma("sp", w11T.t, w11T_d, [], [w11T])
        P.dma("sp", bconvT.t, bconvT_d, [], [bconvT])
        for u in range(2):
            P.op("pool", lambda e, u=u: e.memset(UB[u].t, 0.0), [], [UB[u]])
        P.op("pool", lambda e: e.memset(DACC.t, 0.0), [], [DACC])
        identB_b11 = bass.AP(identB.t.tensor, identB.t.offset, [list(identB.t.ap[0]), [0, 11], [1, 128]])
        ucnt = {"n": 0}

        def up_proj(j, is_up):
            w, wv = pair_w[j]
            off = 128 if is_up else 0
            u = ucnt["n"] % 2
            ucnt["n"] += 1
            for tt in range(4):
                for kc in range(8):
                    P.op("pe", lambda e, kc=kc, tt=tt: e.matmul(pst(tt)[:, :], wv[:, kc, off:off + 128], hT.t[:, kc, tt * 512:(tt + 1) * 512], start=(kc == 0), stop=(kc == 7)),
                         [w] + hT_tok[4 * tt:4 * tt + 4], PQ(tt, 0, 512))
                P.op("act", lambda e, tt=tt: e.activation(UB[u].t[:, UPAD + tt * 512:UPAD + (tt + 1) * 512], pst(tt)[:, :], AF.Copy), PQ(tt, 0, 512), [UB[u]])
            return u

        ctn = {"n": 0}

        def conv(j, jj, is_up, u):
            cc = (NPAIR + j) if is_up else j
            dg = DG[cc % 2]
            wb = bass.AP(w11T.t.tensor, w11T.t.offset + cc * 11, [list(w11T.t.ap[0]), [1, 11], [0, 128]])
            P.op("pool", lambda e: e.tensor_tensor(dg.t, identB_b11, wb, ALU.mult), [identB, w11T], [dg])
            ub = UB[u].t
            acc3 = DACC.t.rearrange("p (r c) -> p r c", c=64)[:, :, 0:63]
            for k_, dy in enumerate((0, -1, 1)):
                wi = (dy + 1) * 3 + 2
                src3 = ub[:, UPAD + 64 * dy + 1:UPAD + 64 * dy + 1 + T].rearrange("p (r c) -> p r c", c=64)[:, :, 0:63]
                wsc = w11T.t[:, cc * 11 + wi:cc * 11 + wi + 1]
                if k_ == 0:
                    P.op("dve", lambda e, src3=src3, wsc=wsc: e.tensor_scalar(acc3, src3, wsc, None, op0=ALU.mult), [UB[u], w11T], [DACC])
                else:
                    P.op("dve", lambda e, src3=src3, wsc=wsc: e.scalar_tensor_tensor(acc3, src3, wsc, acc3, op0=ALU.mult, op1=ALU.add), [UB[u], w11T, DACC], [DACC])
            for tt in range(4):
                base = UPAD + tt * 512
                pt = pst(4 + tt)
                taps = []
                for dy in (0, -1, 1):
                    taps.append(((dy + 1) * 3 + 1, pt[:, 0:512], ub[:, base + 64 * dy:base + 64 * dy + 512]))
                for dy in (0, -1, 1):
                    o3 = pt[:, 0:512].rearrange("p (r c) -> p r c", c=64)[:, :, 1:64]
                    r3 = ub[:, base + 64 * dy - 1:base + 64 * dy - 1 + 512].rearrange("p (r c) -> p r c", c=64)[:, :, 1:64]
                    taps.append(((dy + 1) * 3 + 0, o3, r3))
                o4 = pt[:, 0:512].rearrange("p (a r c) -> p a r c", a=2, r=4)[:, :, 1:4, 0]
                r4 = ub[:, base - 1:base - 1 + 512].rearrange("p (a r c) -> p a r c", a=2, r=4)[:, :, 1:4, 0]
                taps.append((9, o4, r4))
                o4 = pt[:, 0:512].rearrange("p (a r c) -> p a r c", a=2, r=4)[:, :, 0:3, 63]
                r4 = ub[:, base + 1:base + 1 + 512].rearrange("p (a r c) -> p a r c", a=2, r=4)[:, :, 0:3, 63]
                taps.append((10, o4, r4))
                for ti, (wi, oap, rap) in enumerate(taps):
                    P.op("pe", lambda e, wi=wi, oap=oap, rap=rap, ti=ti: e.matmul(oap, dg.t[:, wi, :], rap, start=(ti == 0), stop=(ti == len(taps) - 1)),
                         [dg, UB[u]], PQ(4 + tt, 0, 512))
                ts_ = slice(tt * 512, (tt + 1) * 512)
                ct = ctmp[ctn["n"] % 2]
                ctn["n"] += 1
                P.op("dve", lambda e, pt=pt, ts_=ts_, ct=ct: e.tensor_tensor(ct.t[:, 0:512], pt[:, 0:512], DACC.t[:, ts_], ALU.add), PQ(4 + tt, 0, 512) + [DACC], [ct])
                if not is_up:
                    P.op("act", lambda e, ts_=ts_, ct=ct: e.activation(SG.t[:, ts_], ct.t[:, 0:512], AF.Silu, bias=bconvT.t[:, cc:cc + 1]), [ct, bconvT], [SG])
                else:
                    P.op("dve", lambda e, ts_=ts_, ct=ct: e.scalar_tensor_tensor(HT.t[:, jj, ts_], ct.t[:, 0:512], bconvT.t[:, cc:cc + 1], SG.t[:, ts_], op0=ALU.add, op1=ALU.mult),
                         [ct, bconvT, SG], [HT])

        fin_pending = []

        def final_out(b):
            st = stt[b % 3]
            ys = yst[b % 2]
            xap = X1.t[:, b, :]
            P.op("dve", lambda e: e.scalar_tensor_tensor(ys.t, xap, st.t[:, 2:3], FN.t, op0=ALU.mult, op1=ALU.mult), [X1_tok[b], st, FN], [ys])
            P.dma("sp", y_d[b * 128:(b + 1) * 128, :], ys.t, [ys], [], is_out=True)

        dbk = {"n": 0}

        def wd_load(gi):
            j0, j1 = GROUPS[gi]
            for jj, j in enumerate(range(j0, j1)):
                wq = wdst[j % 2]
                P.dma("sp", wq.t, wdn_d[j * 128:(j + 1) * 128, :], [], [wq])
                P.op("dve", lambda e, jj=jj, wq=wq: e.tensor_tensor(WD.t[:, jj, :], wq.t, MB[5].t, ALU.mult), [wq, MB[5]], [WD])

        def down(gi):
            j0, j1 = GROUPS[gi]
            last_group = gi == len(GROUPS) - 1
            ng = j1 - j0
            for b in range(NB):
                for half in range(2):
                    bk = dbk["n"] % 4
                    dbk["n"] += 1
                    hs = slice(half * 512, (half + 1) * 512)
                    for jj in range(ng):
                        P.op("pe", lambda e, jj=jj, bk=bk, b=b, hs=hs: e.matmul(pst(bk)[:, :], HT.t[:, jj, b * 128:(b + 1) * 128], WD.t[:, jj, hs], start=(jj == 0), stop=(jj == ng - 1)),
                             [HT, WD], PQ(bk, 0, 512))
                    P.op("dve", lambda e, bk=bk, b=b, hs=hs: e.tensor_tensor(X1.t[:, b, hs], pst(bk)[:, :], X1.t[:, b, hs], ALU.add), PQ(bk, 0, 512) + [X1_tok[b]], [X1_tok[b]])
                if last_group:
                    st = stt[b % 3]
                    xap = X1.t[:, b, :]
                    P.op("act", lambda e, xap=xap, st=st, jk=junk: e.activation(jk.t, xap, AF.Square, accum_out=st.t[:, 0:1]), [X1_tok[b]], [junk, st], multi=True)
                    P.op("act", lambda e, st=st: e.activation(st.t[:, 1:2], st.t[:, 0:1], AF.Ln, scale=1.0 / D, bias=EPS), [st], [st])
                    P.op("act", lambda e, st=st: e.activation(st.t[:, 2:3], st.t[:, 1:2], AF.Exp, scale=-0.5), [st], [st])
                    fin_pending.append(b)
                    if len(fin_pending) > 1:
                        final_out(fin_pending.pop(0))
            if last_group:
                while fin_pending:
                    final_out(fin_pending.pop(0))

        wd_load(0)
        pend = None
        deferred = None
        for gi, (j0, j1) in enumerate(GROUPS):
            for j in range(j0, j1):
                for is_up in (False, True):
                    if (not is_up) and (j + 2 < NPAIR):
                        load_pair(j + 2)
                    u = up_proj(j, is_up)
                    if pend is not None:
                        if deferred is not None and pend[4] == gi and pend[2]:
                            down(deferred)
                            wd_load(gi)
                            deferred = None
                        conv(*pend[:4])
                    pend = (j, j - j0, is_up, u, gi)
            deferred = gi
        if deferred is not None and pend[4] == deferred:
            conv(*pend[:4])
            down(deferred)
        P.finish()
    return nc


def _consts():
    ident = np.eye(128, dtype=np.float32)
    j = np.arange(128)[:, None]
    i = np.arange(128)[None, :]
    same = (j // 64) == (i // 64)
    maskT2 = np.concatenate([(j <= i) & same, (j >= i) & same], axis=1).astype(np.float32)
    scanmask = np.ones((1, T), np.float32)
    scanmask[0, ::64] = 0.0
    return ident, maskT2, scanmask


def prep_core_inputs(inp):
    f32 = lambda a: np.ascontiguousarray(np.asarray(a, dtype=np.float32))
    ident, maskT2, scanmask = _consts()
    shared = {
        "w_ada": f32(inp["w_ada"][0]), "b_ada": f32(inp["b_ada"][0]).reshape(1, -1),
        "norm1": f32(inp["norm1"][0]).reshape(1, -1), "norm2": f32(inp["norm2"][0]).reshape(1, -1),
        "b_adaT": f32(np.asarray(inp["b_ada"][0]).reshape(48, 128).T), "norm2T": f32(np.asarray(inp["norm2"][0]).reshape(8, 128).T),
        "fnorm": f32(inp["final_norm"]).reshape(1, -1),
        "w_in": f32(inp["w_in"][0]), "w_gla_up": f32(inp["w_gla_up"][0]),
        "b_glaT": f32(np.asarray(inp["b_gla"][0]).reshape(2, 2, 128).transpose(2, 0, 1).reshape(128, 4)),
        "lbT": f32(np.asarray(inp["hgrn_lb"]).reshape(2, 2, 4, 128).transpose(3, 0, 1, 2).reshape(128, 16)),
        "gnorm": f32(np.stack([np.asarray(inp["gla_norm"][0]), np.asarray(inp["hgrn_norm"][0])], axis=1)),
        "w_out": f32(inp["w_out"][0]), "w_ffn_up": f32(inp["w_ffn_up"][0]),
        "bconvT": f32(np.asarray(inp["b_ffn_conv"][0]).reshape(NCH, 128).T),
        "w_ffn_down": f32(inp["w_ffn_down"][0]),
        "identF": ident, "maskT2": maskT2, "scanmask": scanmask,
    }
    conv = np.asarray(inp["ffn_conv"][0], dtype=np.float32).reshape(9, 2 * FFN_H)
    zero_row = np.zeros((1, 2 * FFN_H), np.float32)
    rows_s = np.concatenate([conv, zero_row, zero_row], axis=0)
    rows_p = np.concatenate([zero_row] * 3 + [conv[3:6]] + [zero_row] * 3 + [conv[3:4], conv[5:6]], axis=0)
    w11 = lambda rows: f32(rows.reshape(11, NCH, 128).transpose(2, 1, 0).reshape(128, NCH * 11))
    x_prompt = np.asarray(inp["x_prompt"], dtype=np.float32)
    x_sample = np.asarray(inp["x_sample"], dtype=np.float32)
    maps = []
    for c in range(8):
        m = dict(shared)
        if c < 4:
            m["x"] = f32(x_sample[c])
            m["cvT"] = f32(np.asarray(inp["c"][c]).reshape(8, 128).T)
            m["sinit_g"] = f32(inp["state_gla"][c, 0])
            m["sinit_h"] = f32(inp["state_hgrn"][c, 0])
            mf = np.ones(32, np.float32)
            mb = np.ones(32, np.float32)
            m["w11T"] = w11(rows_s)
        else:
            p = c - 4
            m["x"] = f32(x_prompt[8 * p:8 * p + 8].reshape(T, D))
            m["cvT"] = f32(np.asarray(inp["c_ctx"]).reshape(8, 128).T)
            m["sinit_g"] = np.zeros((2, 4, 64, 128), np.float32)
            m["sinit_h"] = np.zeros((2, 4, 128, 128), np.float32)
            mf = (np.arange(32) % 4 != 0).astype(np.float32)
            mb = (np.arange(32) % 4 != 3).astype(np.float32)
            m["w11T"] = w11(rows_p)
        m["mchain"] = f32(np.tile(np.concatenate([mf, mb])[None, :], (128, 1)))
        maps.append(m)
    return maps


_PROGRAM = {}


def kernel(**inputs):
    if "nc" not in _PROGRAM:
        _PROGRAM["nc"] = build_program(debug=False)
    nc = _PROGRAM["nc"]
    in_maps = prep_core_inputs(inputs)
    res = run_bass_kernel_spmd(nc, in_maps, core_ids=list(range(8)))
    r = res.results
    y_sample = np.stack([np.asarray(r[c]["y"], dtype=np.float32) for c in range(4)], axis=0)
    y_prompt = np.concatenate([np.asarray(r[c]["y"], dtype=np.float32).reshape(8, 256, D) for c in range(4, 8)], axis=0)
    sg = np.concatenate([np.asarray(r[c]["snew_g"], dtype=np.float32) for c in range(4, 8)], axis=0)[:, None]
    sh = np.concatenate([np.asarray(r[c]["snew_h"], dtype=np.float32) for c in range(4, 8)], axis=0)[:, None]
    return (y_prompt, y_sample, sg, sh)
```

```python
import numpy as np
from contextlib import ExitStack
import concourse.bass as bass
import concourse.mybir as mybir
from concourse.bass_utils import run_bass_kernel_spmd

F32 = mybir.dt.float32
BF16 = mybir.dt.bfloat16
AF = mybir.ActivationFunctionType
ALU = mybir.AluOpType


class Tok:
    __slots__ = ("name", "w", "r", "rd", "excl", "acc")

    def __init__(self, name, excl=False):
        self.name = name
        self.w = None
        self.r = {}
        self.rd = []
        self.excl = excl
        self.acc = {}


class TT:
    def __init__(self, t, tok):
        self.t = t
        self.tok = tok


class _Op:
    __slots__ = ("idx", "eng", "fn", "deps", "is_dma", "sig", "semval", "slot", "is_out", "multi")


def _tok(x):
    return x.tok if isinstance(x, TT) else x


class Prog:
    NSLOT = {"sp": 24, "pool": 16, "act": 8}

    def __init__(self, nc, es):
        self.nc = nc
        self.es = es
        self.ops = []
        self.n_dma = {"sp": 0, "pool": 0, "act": 0}
        self._n = 0
        self.bar = set()

    def sb(self, name, shape, dtype):
        t = self.es.enter_context(self.nc.sbuf_tensor(name, list(shape), dtype))
        return TT(t, Tok(name))

    def ps(self, name):
        t = self.es.enter_context(self.nc.psum_tensor(name, [128, 512], F32))
        return TT(t, Tok(name))

    def tok(self, name):
        return Tok(name)

    def _record(self, eng, fn, reads, writes, is_dma, is_out=False, extra=(), multi=False):
        op = _Op()
        op.multi = multi
        op.idx = len(self.ops)
        op.eng = eng
        op.fn = fn
        op.is_dma = is_dma
        op.sig = False
        op.semval = 0
        op.slot = None
        op.is_out = is_out
        deps = set()
        reads = [_tok(x) for x in reads]
        writes = [_tok(x) for x in writes]

        def consider(pidx, kind):
            p = self.ops[pidx]
            if p.is_dma:
                deps.add(pidx)
                return
            if (not is_dma) and p.eng == eng:
                if eng == "pe":
                    return
                if kind != "raw":
                    return
            deps.add(pidx)

        for t in reads:
            if t.w is not None:
                consider(t.w, "raw")
        for t in writes:
            if t.w is not None:
                consider(t.w, "waw")
            for _, ridx in t.r.items():
                consider(ridx, "war")
            for ridx in t.rd:
                consider(ridx, "war")
        for t in reads + writes:
            if t.excl:
                for e2, aidx in t.acc.items():
                    if e2 != eng:
                        deps.add(aidx)
                t.acc[eng] = op.idx
        for x in extra:
            deps.add(x.idx)
        for pidx in self.bar:
            p = self.ops[pidx]
            if (not is_dma) and (not p.is_dma) and p.eng == eng:
                continue
            deps.add(pidx)
        op.deps = deps
        for t in reads:
            if is_dma:
                t.rd.append(op.idx)
            else:
                t.r[eng] = op.idx
        for t in writes:
            t.w = op.idx
            t.r = {}
            t.rd = []
        if is_dma:
            k = self.n_dma[eng]
            self.n_dma[eng] += 1
            ns = self.NSLOT[eng]
            op.slot = (eng, k % ns)
            op.semval = 16 * (k // ns + 1)
        self.ops.append(op)
        return op

    def op(self, eng, fn, reads, writes, extra=(), multi=False):
        return self._record(eng, fn, reads, writes, False, extra=extra, multi=multi)

    def barrier(self):
        last = {}
        for op in self.ops:
            if op.is_dma:
                last[("d",) + op.slot] = op.idx
            else:
                last[op.eng] = op.idx
        self.bar = set(last.values())

    def dma(self, eng, out_ap, in_ap, reads, writes, is_out=False, **kw):
        def fn(e, out_ap=out_ap, in_ap=in_ap, kw=kw):
            return e.dma_start(out=out_ap, in_=in_ap, **kw)
        return self._record(eng, fn, reads, writes, True, is_out)

    def finish(self):
        nc = self.nc
        es = self.es
        ops = self.ops
        for op in ops:
            for d in op.deps:
                ops[d].sig = True
        engs = ["pe", "act", "dve", "pool", "sp"]
        esem = {e: es.enter_context(nc.semaphore("s_" + e)) for e in engs}
        dsem = {}
        for e, ns in self.NSLOT.items():
            for i in range(min(ns, max(1, self.n_dma[e]))):
                dsem[(e, i)] = es.enter_context(nc.semaphore("d_%s%d" % (e, i)))
        cnt = {e: 0 for e in engs}
        for op in ops:
            if not op.is_dma and op.sig:
                cnt[op.eng] += 1
                op.semval = cnt[op.eng]
        slot_last = {}
        prev_on_slot = {}
        for op in ops:
            if op.is_dma:
                prev_on_slot[op.idx] = slot_last.get(op.slot)
                slot_last[op.slot] = op.idx
        by_eng = {e: [op for op in ops if op.eng == e] for e in engs}

        def sigof(p):
            if p.is_dma:
                return dsem[p.slot], p.semval
            return esem[p.eng], p.semval

        def emit(ename, e):
            waited = {}

            embed = ename in ("dve", "act", "pool")

            for op in by_eng[ename]:
                need = {}
                order = []
                cand = [sigof(ops[d]) for d in sorted(op.deps)]
                if op.is_dma:
                    pv = prev_on_slot[op.idx]
                    if pv is not None:
                        cand.append(sigof(ops[pv]))
                for s, v in cand:
                    key = id(s)
                    if waited.get(key, 0) < v and need.get(key, (None, 0))[1] < v:
                        if key not in need:
                            order.append(key)
                        need[key] = (s, v)
                pend = [need[k] for k in order]
                for s, v in pend:
                    waited[id(s)] = v
                fold = None
                if embed and pend and not op.is_dma and not op.multi:
                    fold = pend.pop()
                for s, v in pend:
                    e.wait_ge(s, v)
                ins = op.fn(e)
                if fold is not None:
                    ins._wait_ge(fold[0], fold[1])
                if op.is_dma:
                    ins.then_inc(dsem[op.slot], 16)
                elif op.sig:
                    ins.then_inc(esem[ename], 1)
            if ename == "sp":
                for slot, idx in slot_last.items():
                    s, v = sigof(ops[idx])
                    if waited.get(id(s), 0) < v:
                        e.wait_ge(s, v)
                        waited[id(s)] = v

        with nc.Block() as block:
            @block.tensor
            def _(e):
                emit("pe", e)

            @block.scalar
            def _(e):
                emit("act", e)

            @block.vector
            def _(e):
                emit("dve", e)

            @block.gpsimd
            def _(e):
                emit("pool", e)

            @block.sync
            def _(e):
                emit("sp", e)
        self.stats = {e: len(by_eng[e]) for e in engs}


D = 1024
T = 2048
NB = 16
EPS = 1e-6
IN_W = 4128
FFN_H = 2816
NCH = 44
NPAIR = 22
UPAD = 65
UW = UPAD + T + UPAD
GROUPS = [(0, 6), (6, 12), (12, 17), (17, 22)]
TS = 512
OFF_QA, OFF_KA, OFF_VA, OFF_GA, OFF_LR = 0, 256, 512, 1024, 1536
OFF_QB, OFF_FB, OFF_IB, OFF_GB = 1568, 2080, 3104, 3616


class Arena:
    def __init__(self, P, nf32):
        self.P = P
        self.tt = P.sb("arena", [128, nf32], F32)
        self.n = nf32
        self.top = 0
        self.end = nf32

    def alloc(self, name, free_shape, dtype, top=False):
        nel = int(np.prod(free_shape))
        nf = nel if dtype == F32 else (nel + 1) // 2
        nf = (nf + 3) // 4 * 4
        assert self.top + nf <= self.end, ("arena overflow", name, self.top, nf, self.end)
        if top:
            self.end -= nf
            ap = self.tt.t[:, self.end:self.end + nf]
        else:
            ap = self.tt.t[:, self.top:self.top + nf]
        if dtype != F32:
            ap = ap.bitcast(dtype)
        ap = ap[:, 0:nel]
        if len(free_shape) == 2:
            ap = ap.rearrange("p (a b) -> p a b", a=free_shape[0])
        elif len(free_shape) == 3:
            ap = ap.rearrange("p (a b c) -> p a b c", a=free_shape[0], b=free_shape[1])
        if not top:
            self.top += nf
        return TT(ap, Tok(name))

    def mark(self):
        return self.top

    def reset(self, m):
        self.top = m


def build_program(debug=False):
    import os as _os
    nc = bass.Bass("TRN2", target_bir_lowering=False)

    def din(name, shape):
        return nc.dram_tensor(name, list(shape), F32, kind="ExternalInput").ap()

    def dout(name, shape, dt=F32):
        return nc.dram_tensor(name, list(shape), dt, kind="ExternalOutput").ap()

    x_d = din("x", [T, D])
    cvT_d = din("cvT", [128, 8])
    wada_d = din("w_ada", [D, 6 * D])
    bada_d = din("b_ada", [1, 6 * D])
    badaT_d = din("b_adaT", [128, 48])
    norm2T_d = din("norm2T", [128, 8])
    norm1_d = din("norm1", [1, D])
    norm2_d = din("norm2", [1, D])
    fnorm_d = din("fnorm", [1, D])
    win_d = din("w_in", [D, IN_W])
    wgu_d = din("w_gla_up", [2, 16, 256])
    bglaT_d = din("b_glaT", [128, 4])
    lbT_d = din("lbT", [128, 16])
    gnorm_d = din("gnorm", [128, 2])
    wout_d = din("w_out", [D, D])
    wup_d = din("w_ffn_up", [D, 2 * FFN_H])
    w11T_d = din("w11T", [128, NCH * 11])
    bconvT_d = din("bconvT", [128, NCH])
    wdn_d = din("w_ffn_down", [FFN_H, D])
    sig_d = din("sinit_g", [2, 4, 64, 128])
    sih_d = din("sinit_h", [2, 4, 128, 128])
    mchain_d = din("mchain", [128, 64])
    identF_d = din("identF", [128, 128])
    maskT2_d = din("maskT2", [128, 256])
    scanmask_d = din("scanmask", [1, T])
    y_d = dout("y", [T, D])
    sog_d = dout("snew_g", [8, 2, 4, 64, 128])
    soh_d = dout("snew_h", [8, 2, 4, 128, 128])
    dbg = {}
    if debug:
        dbg["h1T"] = dout("dbg_h1T", [128, 8 * T], BF16)
        dbg["mod"] = dout("dbg_mod", [128, 6 * D])
        dbg["mergedT"] = dout("dbg_mergedT", [128, 8 * T], BF16)
        dbg["x1"] = dout("dbg_x1", [T, D])

    es = ExitStack()
    with es:
        P = Prog(nc, es)
        AR = Arena(P, 52800)
        dumps = {}

        def dump(name, tt, ncols, dt, parts=128):
            if not debug:
                return
            if name not in dumps:
                dumps[name] = nc.dram_tensor("dd_" + name, [128, ncols], dt, kind="ExternalOutput").ap()
            ap = tt.t
            if len(ap.shape) == 3:
                ap = ap.rearrange("p a b -> p (a b)")
            elif len(ap.shape) == 4:
                ap = ap.rearrange("p a b c -> p (a b c)")
            P.dma("sp", dumps[name][0:parts], ap[0:parts], [tt], [], is_out=True)
        psb = [P.ps("psum%d" % i) for i in range(8)]
        psq = [Tok("psbank%d" % b, excl=True) for b in range(8)]

        def PQ(b, c0, c1):
            return [psq[b]]

        def pst(b):
            return psb[b].t

        rr = {"n": 0}

        def evac_eng():
            rr["n"] += 1
            return "act" if rr["n"] % 2 else "dve"

        def copy_op(eng, out, in_, reads, writes, scale=None):
            if eng == "act":
                if scale is None:
                    P.op("act", lambda e: e.activation(out, in_, AF.Copy), reads, writes)
                else:
                    P.op("act", lambda e: e.activation(out, in_, AF.Copy, scale=scale), reads, writes)
            else:
                if scale is None:
                    P.op(eng, lambda e: e.tensor_copy(out, in_), reads, writes)
                else:
                    P.op(eng, lambda e: e.tensor_scalar(out, in_, scale, None, op0=ALU.mult), reads, writes)

        identF = AR.alloc("identF", [128], F32)
        identB = AR.alloc("identB", [128], BF16)
        maskT2 = AR.alloc("maskT2", [256], F32)
        scanmask = AR.alloc("scanmask", [TS], BF16)
        mchain = AR.alloc("mchain", [64], F32)
        smalls = AR.alloc("smalls", [64], F32)
        LB = smalls.t[:, 0:8]
        L1M = smalls.t[:, 8:16]
        NEGB = smalls.t[:, 16:24]
        GS = smalls.t[:, 24:26]
        SC = smalls.t[:, 32:40]
        TMPS = smalls.t[:, 40:64]
        MB = [None] * 6
        hT = AR.alloc("hT", [8, T], BF16)
        hT_tok = [Tok("hT_b%d" % b) for b in range(NB)]
        onesF = AR.alloc("onesF", [128], F32)
        ring = {"slots": [], "n": 0}

        def next_w():
            w = ring["slots"][ring["n"] % len(ring["slots"])]
            ring["n"] += 1
            return w

        ph1_mark = AR.mark()
        ringB = [AR.alloc("wringB%d" % i, [8 * 640], BF16) for i in range(2)]
        ring["slots"] = [TT(ringB[i].t[:, 0:8 * 512], ringB[i].tok) for i in range(2)]
        srep = AR.alloc("srep", [8, 128], BF16)
        brow = [AR.alloc("brow%d" % i, [512], F32) for i in range(2)]
        MB[0] = AR.alloc("mod0", [D], F32)
        MB[1] = AR.alloc("mod1", [D], F32)

        P.dma("sp", identF.t, identF_d, [], [identF])
        P.dma("sp", maskT2.t, maskT2_d, [], [maskT2])
        P.dma("pool", scanmask.t, scanmask_d[:, 0:TS].partition_broadcast(128), [], [scanmask])
        P.dma("sp", mchain.t, mchain_d, [], [mchain])
        P.op("dve", lambda e: e.tensor_copy(identB.t, identF.t), [identF], [identB])
        lbT = AR.alloc("lbT", [16], F32)
        cvT = AR.alloc("cvT", [8], F32)
        bgl = AR.alloc("bgl", [8], F32)
        gnm = AR.alloc("gnm", [2], F32)
        P.dma("sp", lbT.t, lbT_d, [], [lbT])
        P.dma("sp", cvT.t, cvT_d, [], [cvT])
        P.dma("sp", bgl.t[:, 0:4], bglaT_d, [], [bgl])
        P.dma("sp", gnm.t, gnorm_d, [], [gnm])
        DD = TMPS[:, 0:8]
        EE = TMPS[:, 8:16]
        P.op("dve", lambda e: e.tensor_tensor(DD, lbT.t[:, 8:16], lbT.t[:, 0:8], ALU.subtract), [lbT], [smalls])
        P.op("act", lambda e: e.activation(EE, DD, AF.Exp), [smalls], [smalls])
        P.op("act", lambda e: e.activation(EE, EE, AF.Ln, bias=1.0), [smalls], [smalls])
        P.op("act", lambda e: e.activation(LB, EE, AF.Exp, scale=-1.0), [smalls], [smalls])
        P.op("dve", lambda e: e.tensor_tensor(L1M, DD, EE, ALU.subtract), [smalls], [smalls])
        P.op("dve", lambda e: e.tensor_scalar(NEGB[:, 0:4], bgl.t[:, 0:4], -1.0, None, op0=ALU.mult), [bgl], [smalls])
        P.op("dve", lambda e: e.tensor_scalar(GS, gnm.t, float(np.sqrt(128.0)), None, op0=ALU.mult), [gnm], [smalls])
        E2 = TMPS[:, 16:24]
        P.op("act", lambda e: e.activation(E2, cvT.t, AF.Exp, scale=-1.0), [cvT, smalls], [smalls])
        P.op("dve", lambda e: e.tensor_scalar(E2, E2, 1.0, None, op0=ALU.add), [smalls], [smalls])
        P.op("dve", lambda e: e.reciprocal(E2, E2), [smalls], [smalls])
        P.op("dve", lambda e: e.tensor_tensor(SC, cvT.t, E2, ALU.mult), [cvT, smalls], [smalls])
        sc_b = bass.AP(smalls.t.tensor, smalls.t.offset + 32, [list(smalls.t.ap[0]), [1, 8], [0, 128]])
        P.op("dve", lambda e: e.tensor_copy(srep.t, sc_b), [smalls], [srep])
        srepF = AR.alloc("srepF", [8, 128], F32)
        P.op("dve", lambda e: e.tensor_copy(srepF.t, sc_b), [smalls], [srepF])
        wF32 = [AR.alloc("wF32_%d" % i, [8, 512], F32) for i in range(2)]
        nrm = AR.alloc("nrm_bc", [D], F32)
        P.op("pool", lambda e: e.memset(onesF.t, 1.0), [], [onesF])
        pbank = {"n": 0}

        def mod_compute(j):
            col = [0, 1, 2, 3, 4, 5][j]
            for n in range(2):
                c0 = col * D + n * 512
                if n == 0:
                    w = next_w()
                    wv = w.t[:, 0:8 * 512].rearrange("p (k n) -> p k n", k=8)
                    P.dma("pool", wv, wada_d[:, c0:c0 + 512].rearrange("(k p) n -> p k n", p=128), [], [w])
                    sr = srep
                else:
                    w = wF32[j % 2]
                    wv = w.t
                    P.dma("sp", wv, wada_d[:, c0:c0 + 512].rearrange("(k p) n -> p k n", p=128), [], [w])
                    sr = srepF
                br = brow[(2 * j + n) % 2]
                P.dma("sp", br.t[0:1, :], bada_d[:, c0:c0 + 512], [], [br])
                b = pbank["n"] % 2
                pbank["n"] += 1
                for kc in range(8):
                    P.op("pe", lambda e, kc=kc, b=b, wv=wv, sr=sr: e.matmul(pst(b)[:, :], sr.t[:, kc, :], wv[:, kc, :], start=(kc == 0), stop=False),
                         [sr, w], PQ(b, 0, 512))
                P.op("pe", lambda e, b=b, br=br: e.matmul(pst(b)[:, :], onesF.t[0:1, :], br.t[0:1, :], start=False, stop=True),
                     [onesF, br], PQ(b, 0, 512))
                copy_op(evac_eng(), MB[j].t[:, n * 512:(n + 1) * 512], pst(b)[:, :], PQ(b, 0, 512), [MB[j]])

        mod_compute(0)
        mod_compute(1)
        P.dma("sp", nrm.t, norm1_d.partition_broadcast(128), [], [nrm])
        P.op("dve", lambda e: e.scalar_tensor_tensor(MB[1].t, MB[1].t, 1.0, nrm.t, op0=ALU.add, op1=ALU.mult), [MB[1], nrm], [MB[1]])

        xring = [AR.alloc("xring%d" % i, [D], F32) for i in range(3)]
        junk = AR.alloc("junk", [D], BF16)
        tmpf = AR.alloc("tmpf", [D], F32)
        hb = [AR.alloc("hb%d" % i, [D], BF16) for i in range(2)]
        stt = [AR.alloc("stt%d" % i, [4], F32) for i in range(3)]

        def norm_A(src_ap, src_toks, st):
            jk = junk
            P.op("act", lambda e: e.activation(jk.t, src_ap, AF.Square, accum_out=st.t[:, 0:1]), src_toks, [jk, st], multi=True)
            P.op("act", lambda e: e.activation(st.t[:, 1:2], st.t[:, 0:1], AF.Ln, scale=1.0 / D, bias=EPS), [st], [st])
            P.op("act", lambda e: e.activation(st.t[:, 2:3], st.t[:, 1:2], AF.Exp, scale=-0.5), [st], [st])

        def norm_B1(src_ap, src_toks, g_t, s_t, b, st):
            tf, h = tmpf, hb[b % 2]
            P.op("dve", lambda e: e.scalar_tensor_tensor(tf.t, src_ap, st.t[:, 2:3], g_t.t, op0=ALU.mult, op1=ALU.mult),
                 src_toks + [st, g_t], [tf])
            P.op("dve", lambda e: e.tensor_tensor(h.t, tf.t, s_t.t, ALU.add), [tf, s_t], [h])

        def norm_B2(b, pbk):
            h = hb[b % 2]
            pv = pst(pbk).bitcast(BF16).rearrange("p (k t) -> p k t", k=8)
            for kc in range(8):
                P.op("pe", lambda e, kc=kc: e.transpose(pv[:, kc, :], h.t[:, kc * 128:(kc + 1) * 128], identB.t),
                     [h, identB], PQ(pbk, 0, 512))
            copy_op("act", hT.t[:, :, b * 128:(b + 1) * 128], pv, PQ(pbk, 0, 512), [hT_tok[b]])

        _w0 = ringB[0]
        _wv0 = _w0.t[:, 0:8 * 640].rearrange("p (k n) -> p k n", k=8)
        for (c0_, n_, o_) in [(OFF_QA, 128, 0), (OFF_KA, 128, 128), (OFF_VA, 128, 256), (OFF_GA, 128, 384)]:
            P.dma("pool", _wv0[:, :, o_:o_ + n_], win_d[:, c0_:c0_ + n_].rearrange("(k p) n -> p k n", p=128), [], [_w0])
        for b in range(NB + 2):
            if b < NB:
                xt = xring[b % 3]
                P.dma("sp", xt.t, x_d[b * 128:(b + 1) * 128, :], [], [xt])
                norm_A(xt.t, [xt], stt[b % 3])
            if 1 <= b <= NB:
                xp = xring[(b - 1) % 3]
                norm_B1(xp.t, [xp], MB[1], MB[0], b - 1, stt[(b - 1) % 3])
            if b >= 2:
                norm_B2(b - 2, 2 + ((b - 2) % 2))
        if debug:
            P.dma("sp", dbg["h1T"], hT.t.rearrange("p k t -> p (k t)"), hT_tok, [], is_out=True)
            for j in range(2):
                P.dma("sp", dbg["mod"][:, j * D:(j + 1) * D], MB[j].t, [MB[j]], [], is_out=True)

        P.barrier()
        AR.reset(ph1_mark)
        BS = 64
        NBK = T // BS
        NG = T // 128
        BPS = TS // BS
        NSPAN = T // TS
        modT = AR.alloc("modT", [4, 8], F32, top=True)
        badaT = AR.alloc("badaT", [48], F32, top=True)
        scb = AR.alloc("scb", [8], BF16, top=True)
        top_keep = AR.end
        mergedT = AR.alloc("mergedT", [8, T], BF16, top=True)
        mT_tok = [Tok("mT_h%d" % i) for i in range(8)]
        for i in range(2):
            AR.alloc("wringB_again%d" % i, [8 * 640], BF16)
        ring["slots"] = ringB
        QT = AR.alloc("QT", [T], F32)
        KT = AR.alloc("KT", [T], F32)
        QTt = [AR.alloc("QTt%d" % d, [T], BF16) for d in range(2)]
        KTt = [AR.alloc("KTt%d" % d, [T], BF16) for d in range(2)]
        KTM = [AR.alloc("KTM%d" % d, [NG, 128], BF16) for d in range(2)]
        LH = AR.alloc("LH", [2, NBK, 2], F32)
        EX = AR.alloc("EX", [2, NBK, 2], F32)
        SCL = AR.alloc("SCL", [2, NBK, 2], F32)
        Vt = [AR.alloc("Vt%d" % pb, [NG, 128], BF16) for pb in range(2)]
        GG = [AR.alloc("GG%d" % pb, [NG, 128], BF16) for pb in range(2)]
        NSET = 4
        SETS = [dict(X=TT(KT.t[:, i * TS:(i + 1) * TS], Tok("Xs%d" % i)), U=AR.alloc("Us%d" % i, [TS], F32), T2=AR.alloc("T2s%d" % i, [TS], F32),
                     B=AR.alloc("Bs%d" % i, [TS], F32), KH=AR.alloc("KHt%d" % i, [TS], BF16)) for i in range(NSET)]
        SQt = [AR.alloc("SQt%d" % d, [NBK, 128], BF16) for d in range(2)]
        ST = [[AR.alloc("ST%d_%d" % (d, i), [128], F32) for i in range(3)] for d in range(2)]
        ATt = [AR.alloc("AT%d" % i, [256], BF16) for i in range(4)]
        MTM = [AR.alloc("MTM%d" % i, [128], BF16) for i in range(2)]
        sto = [AR.alloc("sto%d" % i, [4], F32) for i in range(4)]
        junk2 = AR.alloc("junk2", [128], BF16)
        LRT = AR.alloc("LRT", [2, T], BF16)
        WLR = AR.alloc("WLR", [8, 32], BF16)
        WGU = AR.alloc("WGU", [2, 256], BF16)

        def bcol(Bap, parts, col, nblk):
            pstep = Bap.ap[0][0]
            return bass.AP(Bap.tensor, Bap.offset + col, [[pstep, parts], [BS, nblk], [0, BS]])

        P.dma("pool", WLR.t, win_d[:, OFF_LR:OFF_LR + 32].rearrange("(k p) n -> p k n", p=128), [], [WLR])
        P.dma("pool", WGU.t[0:16], wgu_d.rearrange("d r k -> r d k"), [], [WGU])
        P.dma("sp", badaT.t, badaT_d, [], [badaT])
        P.op("dve", lambda e: e.tensor_copy(scb.t, SC), [smalls], [scb])
        fmb = {"n": 0}

        def fm_bank():
            fmb["n"] += 1
            return fmb["n"] % 2

        def head_cfg(hh):
            gla = hh < 4
            h = hh if gla else hh - 4
            if gla:
                p = h // 2
                owner = (h % 2 == 0)
                if owner:
                    groups = [(OFF_QA + 128 * p, 128, 0), (OFF_KA + 128 * p, 128, 128), (OFF_VA + 128 * h, 128, 256), (OFF_GA + 128 * h, 128, 384)]
                    c_vg = 256
                else:
                    groups = [(OFF_VA + 128 * h, 128, 0), (OFF_GA + 128 * h, 128, 128)]
                    c_vg = 0
                return dict(gla=True, h=h, p=p, owner=owner, dk=64, po=64 * (h % 2), dscale=-1.0 / 16.0, groups=groups, c_q=0, c_k=128, c_vg=c_vg, c_f=None)
            groups = [(OFF_QB + 128 * h, 128, 0), (OFF_FB + 128 * h, 128, 128), (OFF_FB + 512 + 128 * h, 128, 256),
                      (OFF_IB + 128 * h, 128, 384), (OFF_GB + 128 * h, 128, 512)]
            return dict(gla=False, h=h, p=0, owner=True, dk=128, po=0, dscale=1.0, groups=groups, c_q=0, c_k=None, c_vg=384, c_f=(128, 256))

        head_w = {}

        def load_head_weights(hh, dma=True):
            cfg = head_cfg(hh)
            w = ring["slots"][hh % 2]
            wv = w.t[:, 0:8 * 640].rearrange("p (k n) -> p k n", k=8)
            if dma:
                for (c0, n, o) in cfg["groups"]:
                    P.dma("pool", wv[:, :, o:o + n], win_d[:, c0:c0 + n].rearrange("(k p) n -> p k n", p=128), [], [w])
            head_w[hh] = (w, wv)

        def mod_pipeline(js, w):
            htok = [Tok(w.tok.name + "_h0"), Tok(w.tok.name + "_h1")]
            chunks = [(j, n) for j in js for n in range(4)]

            def dma(i):
                j, n = chunks[i]
                half = i % 2
                wv_ = w.t[:, half * 2048:(half + 1) * 2048].rearrange("p (k n) -> p k n", k=8)
                c0 = j * D + n * 256
                P.dma("pool", wv_, wada_d[:, c0:c0 + 256].rearrange("(k p) n -> p k n", p=128), [], [htok[half]] + ([w] if i < 2 else []))

            dma(0)
            dma(1)
            yield
            for i, (j, n) in enumerate(chunks):
                half = i % 2
                wv_ = w.t[:, half * 2048:(half + 1) * 2048].rearrange("p (k n) -> p k n", k=8)
                ji = js.index(j)
                last2 = i >= len(chunks) - 2
                for mb in range(2):
                    col = 8 * ji + 2 * n + mb
                    for kc in range(8):
                        P.op("pe", lambda e, kc=kc, mb=mb, col=col, wv_=wv_: e.matmul(pst(7)[:, col:col + 1], wv_[:, kc, mb * 128:(mb + 1) * 128], scb.t[:, kc:kc + 1],
                                                                                 start=(kc == 0), stop=(kc == 7)), [htok[half], scb] + ([w] if last2 else []), PQ(7, 0, 512))
                if i + 2 < len(chunks):
                    dma(i + 2)
                if n == 3:
                    copy_op("dve", modT.t[:, j - 2, :], pst(7)[:, 8 * ji:8 * ji + 8], PQ(7, 0, 512), [modT])
                    P.op("dve", lambda e, j=j: e.tensor_tensor(modT.t[:, j - 2, :], modT.t[:, j - 2, :], badaT.t[:, j * 8:(j + 1) * 8], ALU.add), [modT, badaT], [modT])
                yield

        def mod_computeT(j, w):
            b = fm_bank()
            for n in range(4):
                c0 = j * D + n * 256
                half = n % 2
                wv = w.t[:, half * 2048:(half + 1) * 2048].rearrange("p (k n) -> p k n", k=8)
                P.dma("pool", wv, wada_d[:, c0:c0 + 256].rearrange("(k p) n -> p k n", p=128), [], [w])
                for mb in range(2):
                    for kc in range(8):
                        P.op("pe", lambda e, kc=kc, mb=mb, n=n, wv=wv: e.matmul(pst(b)[:, 2 * n + mb:2 * n + mb + 1], wv[:, kc, mb * 128:(mb + 1) * 128], scb.t[:, kc:kc + 1],
                                                                              start=(kc == 0), stop=(kc == 7)), [w, scb], PQ(b, 0, 512))
                yield
            copy_op("dve", modT.t[:, j - 2, :], pst(b)[:, 0:8], PQ(b, 0, 512), [modT])
            P.op("dve", lambda e: e.tensor_tensor(modT.t[:, j - 2, :], modT.t[:, j - 2, :], badaT.t[:, j * 8:(j + 1) * 8], ALU.add), [modT, badaT], [modT])

        def stageAB1(hh):
            cfg = head_cfg(hh)
            gla, h, dscale, owner = cfg["gla"], cfg["h"], cfg["dscale"], cfg["owner"]
            dk = 128
            c_q, c_k, c_vg, c_f = cfg["c_q"], cfg["c_k"], cfg["c_vg"], cfg["c_f"]
            pb = hh % 2
            w, wv = head_w[hh]
            if hh + 1 < 8:
                load_head_weights(hh + 1)
            myVt, myGG = Vt[pb], GG[pb]
            Us = SETS[0]["U"]

            def fm_proj(c0, M, tiles, dst_ap_fn, dst_toks, scale=None):
                for tt in tiles:
                    b = fm_bank()
                    for kc in range(8):
                        P.op("pe", lambda e, kc=kc, b=b, tt=tt: e.matmul(pst(b)[0:M, :], wv[:, kc, c0:c0 + M], hT.t[:, kc, tt * 512:(tt + 1) * 512],
                                                                       start=(kc == 0), stop=(kc == 7)),
                             [w] + hT_tok[4 * tt:4 * tt + 4], PQ(b, 0, 512))
                    yield
                    copy_op(evac_eng(), dst_ap_fn(tt), pst(b)[0:M, :], PQ(b, 0, 512), dst_toks, scale=scale)

            if owner:
                yield from fm_proj(c_q, dk, range(4), lambda tt: QT.t[0:dk, tt * 512:(tt + 1) * 512], [QT], scale=(0.125 if gla else None))
            if gla and owner:
                yield from fm_proj(c_k, dk, range(4), lambda tt: KT.t[0:dk, tt * 512:(tt + 1) * 512], [KT])
            if hh == 0:
                for d in range(2):
                    for tt in range(4):
                        b = fm_bank()
                        for kc in range(8):
                            P.op("pe", lambda e, kc=kc, b=b, tt=tt, d=d: e.matmul(pst(b)[0:16, :], WLR.t[:, kc, 16 * d:16 * d + 16], hT.t[:, kc, tt * 512:(tt + 1) * 512],
                                                                                  start=(kc == 0), stop=(kc == 7)),
                                 [WLR] + hT_tok[4 * tt:4 * tt + 4], PQ(b, 0, 512))
                        copy_op(evac_eng(), LRT.t[0:16, d, tt * 512:(tt + 1) * 512], pst(b)[0:16, :], PQ(b, 0, 512), [LRT])
                        yield
            for bp in range(NG // 2):
                bk = fm_bank()
                for i in range(2):
                    blk = 2 * bp + i
                    for kc in range(8):
                        P.op("pe", lambda e, kc=kc, bk=bk, i=i, blk=blk: e.matmul(pst(bk)[:, i * 256:(i + 1) * 256], hT.t[:, kc, blk * 128:(blk + 1) * 128],
                                                                                wv[:, kc, c_vg:c_vg + 256], start=(kc == 0), stop=(kc == 7)),
                             [w, hT_tok[blk]], PQ(bk, i * 256, (i + 1) * 256))
                pv = pst(bk).rearrange("p (b c) -> p b c", b=2)
                yield
                ce = evac_eng()
                copy_op(ce, myVt.t[:, 2 * bp:2 * bp + 2, :], pv[:, :, 0:128], PQ(bk, 0, 512), [myVt])
                copy_op(ce, myGG.t[:, 2 * bp:2 * bp + 2, :], pv[:, :, 128:256], PQ(bk, 0, 512), [myGG])
            GPS = TS // 128
            for s_ in range(NSPAN):
                gsp = myGG.t[:, s_ * GPS:(s_ + 1) * GPS, :].rearrange("p b c -> p (b c)")
                P.op("act", lambda e, gsp=gsp: e.activation(Us.t, gsp, AF.Exp, scale=-1.0), [myGG], [Us])
                P.op("act", lambda e: e.activation(Us.t, Us.t, AF.Ln, bias=1.0), [Us], [Us])
                P.op("act", lambda e: e.activation(Us.t, Us.t, AF.Exp, scale=-1.0), [Us], [Us])
                P.op("dve", lambda e, gsp=gsp: e.tensor_tensor(gsp, gsp, Us.t, ALU.mult), [myGG, Us], [myGG])
                yield

        def stageAB2(hh):
            cfg = head_cfg(hh)
            gla, h, dscale, owner, pr = cfg["gla"], cfg["h"], cfg["dscale"], cfg["owner"], cfg["p"]
            dk = 128
            c_f = cfg["c_f"]
            w, wv = head_w[hh]
            myQTt, myKTt, myKTM, myLH, myEX, mySCL = QTt, KTt, KTM, LH, EX, SCL
            if not owner:
                return
            modgen = mod_pipeline([2, 3] if hh == 0 else [4, 5], w) if (gla and owner) else None

            def fm_proj(c0, M, tiles, dst_ap_fn, dst_toks, scale=None):
                for tt in tiles:
                    b = fm_bank()
                    for kc in range(8):
                        P.op("pe", lambda e, kc=kc, b=b, tt=tt: e.matmul(pst(b)[0:M, :], wv[:, kc, c0:c0 + M], hT.t[:, kc, tt * 512:(tt + 1) * 512],
                                                                       start=(kc == 0), stop=(kc == 7)),
                             [w] + hT_tok[4 * tt:4 * tt + 4], PQ(b, 0, 512))
                    copy_op("act", dst_ap_fn(tt), pst(b)[0:M, :], PQ(b, 0, 512), dst_toks, scale=scale)
                    yield

            v3 = lambda ap: ap.rearrange("p (b t) -> p b t", b=BPS)

            def decay(d, s_, tiles, sp0, sp1):
                st_ = SETS[(s_ % 2) * 2 + d]
                Xs, Us, T2s, Bs, KHt = st_["X"], st_["U"], st_["T2"], st_["B"], st_["KH"]
                X_, U_, T2_, B_, KH_ = Xs.t[0:dk], Us.t[0:dk], T2s.t[0:dk], Bs.t[0:dk], KHt.t[0:dk]
                col = (d * 2 + pr) if gla else (d * 4 + h)
                sel = d
                if gla:
                    for ti, tt in enumerate(tiles):
                        b = fm_bank()
                        P.op("pe", lambda e, b=b, tt=tt: e.matmul(pst(b)[0:128, :], WGU.t[0:16, d, 128 * pr:128 * pr + 128], LRT.t[0:16, d, tt * 512:(tt + 1) * 512],
                                                                start=True, stop=True), [WGU, LRT], PQ(b, 0, 512))
                        P.op("act", lambda e, b=b, ti=ti: e.activation(U_[:, ti * 512:(ti + 1) * 512], pst(b)[0:128, :], AF.Exp, scale=-1.0,
                                                                      bias=NEGB[:, col:col + 1]), PQ(b, 0, 512) + [smalls], [Us])
                    yield
                    P.op("act", lambda e: e.activation(T2_, U_, AF.Ln, bias=1.0), [Us], [T2s])
                    P.op("dve", lambda e: e.tensor_tensor_scan(B_, scanmask.t[0:dk, :], T2_, 0.0, ALU.mult, ALU.add), [scanmask, T2s], [Bs])
                else:
                    cf = c_f[d]
                    yield from fm_proj(cf, 128, tiles, lambda tt: X_[:, (tt - tiles[0]) * 512:(tt - tiles[0] + 1) * 512], [Xs])
                    P.op("act", lambda e: e.activation(U_, X_, AF.Exp, scale=-1.0), [Xs], [Us])
                    P.op("act", lambda e: e.activation(T2_, U_, AF.Ln, bias=1.0), [Us], [T2s])
                    P.op("act", lambda e: e.activation(U_, U_, AF.Ln, bias=1.0, scale=LB[:, col:col + 1]), [Us, smalls], [Us])
                    yield
                    P.op("dve", lambda e: e.tensor_tensor(U_, U_, T2_, ALU.subtract), [Us, T2s], [Us])
                    P.op("act", lambda e: e.activation(X_, X_, AF.Exp), [Xs], [Xs])
                    P.op("act", lambda e: e.activation(X_, X_, AF.Ln, bias=1.0), [Xs], [Xs])
                    P.op("dve", lambda e: e.tensor_tensor_scan(B_, scanmask.t[0:dk, :], U_, 0.0, ALU.mult, ALU.add), [scanmask, Us], [Bs])
                yield
                B3 = v3(B_)
                MID = BS // 2 - 1
                lh = myLH.t[0:dk, d, s_ * BPS:(s_ + 1) * BPS, :]
                ex = myEX.t[0:dk, d, s_ * BPS:(s_ + 1) * BPS, :]
                P.op("dve", lambda e: e.tensor_copy(lh[:, :, 0:1], B3[:, :, MID:MID + 1]), [Bs], [myLH])
                P.op("dve", lambda e: e.tensor_tensor(lh[:, :, 1:2], B3[:, :, BS - 1:BS], B3[:, :, MID:MID + 1], ALU.subtract), [Bs], [myLH])
                P.op("act", lambda e: e.activation(ex, lh, AF.Exp, scale=dscale), [myLH], [myEX])
                bmid = bcol(B_, dk, MID, BPS)
                ehb = bass.AP(myEX.t.tensor, myEX.t[0:dk, d, s_ * BPS:(s_ + 1) * BPS, 1 - sel].offset,
                              [[myEX.t.ap[0][0], dk], [2, BPS], [0, BS]])
                if gla:
                    if d == 0:
                        P.op("dve", lambda e: e.tensor_tensor(v3(U_), B3, bmid, ALU.subtract), [Bs], [Us])
                    else:
                        P.op("dve", lambda e: e.tensor_tensor(T2_, B_, T2_, ALU.subtract), [Bs, T2s], [T2s])
                        P.op("dve", lambda e: e.tensor_tensor(v3(U_), bmid, v3(T2_), ALU.subtract), [Bs, T2s], [Us])
                    P.op("dve", lambda e: e.tensor_scalar(U_, U_, 640.0, -640.0, op0=ALU.min, op1=ALU.max), [Us], [Us])
                    P.op("act", lambda e: e.activation(T2_, U_, AF.Exp, scale=dscale), [Us], [T2s])
                    P.op("pool", lambda e: e.tensor_tensor(myQTt[d].t[0:dk, sp0:sp1], QT.t[0:dk, sp0:sp1], T2_, ALU.mult), [QT, T2s], [myQTt[d]])
                    yield
                    P.op("act", lambda e: e.activation(T2_, U_, AF.Exp, scale=-dscale), [Us], [T2s])
                    P.op("dve", lambda e: e.tensor_tensor(myKTt[d].t[0:dk, sp0:sp1], KT.t[0:dk, sp0:sp1], T2_, ALU.mult), [KT, T2s], [myKTt[d]])
                else:
                    if d == 0:
                        P.op("dve", lambda e: e.tensor_tensor(v3(T2_), B3, bmid, ALU.subtract), [Bs], [T2s])
                    else:
                        P.op("dve", lambda e: e.tensor_tensor(U_, B_, U_, ALU.subtract), [Bs, Us], [Us])
                        P.op("dve", lambda e: e.tensor_tensor(v3(T2_), bmid, v3(U_), ALU.subtract), [Bs, Us], [T2s])
                    P.op("dve", lambda e: e.tensor_scalar(T2_, T2_, 40.0, -40.0, op0=ALU.min, op1=ALU.max), [T2s], [T2s])
                    P.op("act", lambda e: e.activation(U_, T2_, AF.Exp), [T2s], [Us])
                    P.op("pool", lambda e: e.tensor_tensor(myQTt[d].t[0:dk, sp0:sp1], QT.t[0:dk, sp0:sp1], U_, ALU.mult), [QT, Us], [myQTt[d]])
                    yield
                    P.op("dve", lambda e: e.tensor_tensor(X_, X_, T2_, ALU.add), [Xs, T2s], [Xs])
                    P.op("act", lambda e: e.activation(myKTt[d].t[0:dk, sp0:sp1], X_, AF.Exp, scale=-1.0, bias=L1M[:, col:col + 1]), [Xs, smalls], [myKTt[d]])
                yield
                P.op("dve", lambda e: e.tensor_tensor(v3(KH_), v3(myKTt[d].t[0:dk, sp0:sp1]), ehb, ALU.mult), [myKTt[d], myEX], [KHt])
                kb = fm_bank()
                pvk = pst(kb).bitcast(BF16).rearrange("p (b t) -> p b t", b=8)
                ng = TS // 128
                for i in range(ng):
                    P.op("pe", lambda e, i=i: e.transpose(pvk[:, i, 0:dk], KH_[:, i * 128:(i + 1) * 128], identB.t[0:dk, 0:dk]), [KHt, identB], PQ(kb, 0, 512))
                copy_op(evac_eng(), myKTM[d].t[:, s_ * ng:(s_ + 1) * ng, 0:dk], pvk[:, 0:ng, 0:dk], PQ(kb, 0, 512), [myKTM[d]])
                yield

            for s0_ in range(0, NSPAN, 2):
                gens = []
                for s_ in (s0_, s0_ + 1):
                    tiles = list(range(s_ * TS // 512, (s_ + 1) * TS // 512))
                    for d in range(2):
                        gens.append(decay(d, s_, tiles, s_ * TS, (s_ + 1) * TS))
                alive = [True] * len(gens)
                while any(alive):
                    for gi in range(len(gens)):
                        if alive[gi]:
                            try:
                                next(gens[gi])
                            except StopIteration:
                                alive[gi] = False
                    if modgen is not None:
                        try:
                            next(modgen)
                        except StopIteration:
                            modgen = None
                    yield
            for d in range(2):
                sel = d
                mc = mchain.t[0:dk, d * NBK:(d + 1) * NBK]
                P.op("dve", lambda e, d=d, sel=sel, mc=mc: e.tensor_tensor(mySCL.t[0:dk, d, :, 0], myEX.t[0:dk, d, :, sel], mc, ALU.mult), [myEX, mchain], [mySCL])
                P.op("dve", lambda e, d=d, sel=sel: e.tensor_tensor(mySCL.t[0:dk, d, :, 1], mySCL.t[0:dk, d, :, 0], myEX.t[0:dk, d, :, 1 - sel], ALU.mult), [myEX, mySCL], [mySCL])
            yield
            if modgen is not None:
                for _ in modgen:
                    pass

        def stageC(hh):
            cfg = head_cfg(hh)
            gla, h, dk, po = cfg["gla"], cfg["h"], cfg["dk"], cfg["po"]
            pq = slice(po, po + dk)
            pb = hh % 2
            myQTt, myKTt, myKTM, myVt, myGG, mySCL = QTt, KTt, KTM, Vt[pb], GG[pb], SCL
            kmt = {"n": 0}
            pv7 = pst(7).bitcast(BF16).rearrange("p (r s t) -> p r s t", r=2, s=4)
            grp_done = {}
            state = {0: ST[0][0], 1: ST[1][0]}
            nxt = [1, 1]

            def chain_p(d, n, cb, after=()):
                g, hf = n // 2, n % 2
                return P.op("pe", lambda e: e.matmul(pst(cb)[pq, d * 128:(d + 1) * 128], myKTM[d].t[hf * 64:(hf + 1) * 64, g, pq], myVt.t[hf * 64:(hf + 1) * 64, g, :], start=True, stop=True),
                            [myKTM[d], myVt], PQ(cb, 0, 256), extra=after)

            def chain_step(d, n, cb):
                prev = state[d]
                new = ST[d][nxt[d]]
                nxt[d] = (nxt[d] + 1) % int(_os.environ.get('DBG_TRI', '3'))
                P.op("act", lambda e: e.activation(SQt[d].t[pq, n, :], prev.t[pq], AF.Copy, scale=mySCL.t[pq, d, n, 0:1]), [prev, mySCL], [SQt[d]])
                P.op("dve", lambda e: e.scalar_tensor_tensor(new.t[pq], prev.t[pq], mySCL.t[pq, d, n, 1:2], pst(cb)[pq, d * 128:(d + 1) * 128], op0=ALU.mult, op1=ALU.add),
                     [prev, mySCL] + PQ(cb, 0, 256), [new])
                state[d] = new
                if (d == 0 and n % 4 == 3) or (d == 1 and n % 4 == 0):
                    dst = (sog_d if gla else soh_d)[n // 4, d, h]
                    P.dma("sp", dst, new.t[pq], [new], [], is_out=True)

            gctr = {"n": 0}
            ginfo = {}

            def og_a1(g):
                i = gctr["n"]
                gctr["n"] += 1
                ginfo[g] = i
                blk = slice(g * 128, (g + 1) * 128)
                for d in range(2):
                    P.op("pe", lambda e, d=d: e.matmul(pst(5)[:, d * 128:(d + 1) * 128], myKTt[d].t[pq, blk], myQTt[d].t[pq, blk], start=True, stop=True),
                         [myKTt[d], myQTt[d]], PQ(5, 0, 256))

            def og_a2(g):
                at = ATt[ginfo[g] % 4]
                P.op("dve", lambda e: e.tensor_tensor(at.t, pst(5)[:, 0:256], maskT2.t, ALU.mult), PQ(5, 0, 256) + [maskT2], [at])

            def og_b(g):
                i = ginfo[g]
                at = ATt[i % 4]
                ob_ = [2, 6][i % 2]
                og = pst(ob_)[:, 0:128]
                otok = PQ(ob_, 0, 128)
                P.op("pe", lambda e: e.matmul(og, at.t[:, 0:128], myVt.t[:, g, :], start=True, stop=False), [at, myVt], otok)
                P.op("pe", lambda e: e.matmul(og, at.t[:, 128:256], myVt.t[:, g, :], start=False, stop=False), [at, myVt], otok)
                for hf in range(2):
                    for d in range(2):
                        last = (hf == 1 and d == 1)
                        c0 = g * 128 + hf * 64
                        P.op("pe", lambda e, hf=hf, d=d, last=last, c0=c0: e.matmul(pst(ob_)[hf * 64:(hf + 1) * 64, 0:128], myQTt[d].t[pq, c0:c0 + 64], SQt[d].t[pq, 2 * g + hf, :],
                                                                                   start=False, stop=last), [myQTt[d], SQt[d]], otok)

            def og_c(g):
                i = ginfo[g]
                ob_ = [2, 6][i % 2]
                og = pst(ob_)[:, 0:128]
                otok = PQ(ob_, 0, 128)
                so = sto[i % 4]
                P.op("act", lambda e: e.activation(junk2.t, og, AF.Square, accum_out=so.t[:, 0:1]), otok, [junk2, so], multi=True)
                P.op("act", lambda e: e.activation(so.t[:, 1:2], so.t[:, 0:1], AF.Ln, bias=128.0 * EPS), [so], [so])
                P.op("act", lambda e: e.activation(so.t[:, 2:3], so.t[:, 1:2], AF.Exp, scale=-0.5), [so], [so])

            def og_d(g):
                i = ginfo[g]
                ob_ = [2, 6][i % 2]
                og = pst(ob_)[:, 0:128]
                otok = PQ(ob_, 0, 128)
                so = sto[i % 4]
                mt = MTM[i % 2]
                P.op("dve", lambda e: e.scalar_tensor_tensor(mt.t, og, so.t[:, 2:3], myGG.t[:, g, :], op0=ALU.mult, op1=ALU.mult), otok + [so, myGG], [mt])
                grp = g // 4
                r = grp % 2
                rtok = PQ(7, r * 256, (r + 1) * 256)
                P.op("pe", lambda e: e.transpose(pv7[:, r, g % 4, :], mt.t, identB.t), [mt, identB], rtok)
                grp_done[grp] = grp_done.get(grp, 0) + 1
                if grp_done[grp] == 4:
                    copy_op(evac_eng(), mergedT.t[:, hh, grp * 512:(grp + 1) * 512], pv7[:, r].rearrange("p s t -> p (s t)"), rtok, [mT_tok[hh]])

            ready = {g: max(2 * g + 1, NBK - 1 - 2 * g) for g in range(NG)}
            p_ahead = int(_os.environ.get('DBG_PAHEAD', '1'))
            if p_ahead:
                o_ = chain_p(0, 0, 3)
                chain_p(1, NBK - 1, 3, after=(o_,))
            for s_ in range(NBK + 4):
                if s_ < NBK:
                    cb = 3 + (s_ % 2)
                    if not p_ahead:
                        o_ = chain_p(0, s_, cb)
                        chain_p(1, NBK - 1 - s_, cb, after=(o_,))
                    chain_step(0, s_, cb)
                    chain_step(1, NBK - 1 - s_, cb)
                    if p_ahead and s_ + 1 < NBK:
                        cbn = 3 + ((s_ + 1) % 2)
                        o_ = chain_p(0, s_ + 1, cbn)
                        chain_p(1, NBK - 2 - s_, cbn, after=(o_,))
                for g in range(NG):
                    if ready[g] == s_ - 3:
                        og_d(g)
                for g in range(NG):
                    if ready[g] == s_ - 2:
                        og_c(g)
                for g in range(NG):
                    if ready[g] == s_ - 1:
                        og_b(g)
                pair_now = [g for g in range(NG) if ready[g] == s_ + 3]
                pair_prev = [g for g in range(NG) if ready[g] == s_ + 2]
                pair_prev2 = [g for g in range(NG) if ready[g] == s_ + 1]
                if pair_prev2:
                    og_a2(pair_prev2[1])
                if pair_prev:
                    og_a2(pair_prev[0])
                    og_a1(pair_prev[1])
                if pair_now:
                    og_a1(pair_now[0])
                yield

        heads = [int(v) for v in _os.environ.get('DBG_HEADS', '0,1,2,3,4,5,6,7').split(',') if v != '']
        assert heads == list(range(8))
        load_head_weights(0, dma=False)
        for _ in stageAB1(0):
            pass
        def load_init_state(hh):
            cfg = head_cfg(hh)
            pq_ = slice(cfg["po"], cfg["po"] + cfg["dk"])
            for d in range(2):
                src = (sig_d if cfg["gla"] else sih_d)[d, cfg["h"]]
                P.dma("sp", ST[d][0].t[pq_], src, [], [ST[d][0]])

        load_init_state(0)
        for _ in stageAB2(0):
            pass
        ilv = int(_os.environ.get('DBG_ILV', '2'))
        for hh in range(8):
            cgen = stageC(hh)
            abgen = stageAB1(hh + 1) if hh + 1 < 8 else None
            c_alive, ab_alive = True, abgen is not None
            step = 0
            while c_alive or ab_alive:
                if c_alive:
                    try:
                        next(cgen)
                    except StopIteration:
                        c_alive = False
                if ab_alive and (not c_alive or (ilv >= 1 and step % ilv == 0)):
                    try:
                        next(abgen)
                    except StopIteration:
                        ab_alive = False
                step += 1
            if hh + 1 < 8:
                load_init_state(hh + 1)
                for _ in stageAB2(hh + 1):
                    pass
        if debug:
            P.dma("sp", dbg["mergedT"], mergedT.t.rearrange("p k t -> p (k t)"), mT_tok, [], is_out=True)
        P.barrier()
        AR.reset(ph1_mark)
        X1 = AR.alloc("X1", [NB, D], F32)
        X1_tok = [Tok("X1_b%d" % b) for b in range(NB)]
        ph3_keep = AR.mark()
        ringC = [AR.alloc("wringC%d" % i, [8 * 256], BF16) for i in range(3)]
        ring["slots"] = ringC
        ring["n"] = 0
        pair_w = {}

        def load_pair(j):
            w = next_w()
            wv = w.t[:, 0:8 * 256].rearrange("p (k n) -> p k n", k=8)
            P.dma("pool", wv[:, :, 0:128], wup_d[:, j * 128:(j + 1) * 128].rearrange("(k p) n -> p k n", p=128), [], [w])
            P.dma("pool", wv[:, :, 128:256], wup_d[:, FFN_H + j * 128:FFN_H + (j + 1) * 128].rearrange("(k p) n -> p k n", p=128), [], [w])
            pair_w[j] = (w, wv)

        load_pair(0)
        load_pair(1)
        MB[2] = AR.alloc("gate1_bc", [D], F32)
        MB[3] = AR.alloc("shift2_bc", [D], F32)
        MB[4] = AR.alloc("g2_bc", [D], F32)
        WO = AR.alloc("WO", [8, D], BF16)
        wstage = [AR.alloc("wstage%d" % i, [D], F32) for i in range(4)]
        junk = AR.alloc("junk", [D], BF16)
        tmpf = AR.alloc("tmpf", [D], F32)
        hb = [AR.alloc("hb%d" % i, [D], BF16) for i in range(2)]
        stt = [AR.alloc("stt%d" % i, [4], F32) for i in range(3)]
        dgt = [AR.alloc("dgt%d" % i, [128], F32) for i in range(2)]
        n2T = AR.alloc("n2T", [8], F32)
        g2T = AR.alloc("g2T", [8], F32)
        xb = {"n": 0}

        def expand(vec_ap, vec_toks, dst):
            for half in range(2):
                b = xb["n"] % 2
                xb["n"] += 1
                for q in range(4):
                    kc = half * 4 + q
                    dg = dgt[kc % 2]
                    P.op("dve", lambda e, kc=kc, dg=dg: e.tensor_scalar(dg.t, identF.t, vec_ap[:, kc:kc + 1], None, op0=ALU.mult), [identF] + vec_toks, [dg])
                    P.op("pe", lambda e, q=q, b=b, dg=dg: e.matmul(pst(b)[:, q * 128:(q + 1) * 128], onesF.t, dg.t, start=True, stop=True), [onesF, dg], PQ(b, 0, 512))
                copy_op(evac_eng(), dst.t[:, half * 512:(half + 1) * 512], pst(b)[:, :], PQ(b, 0, 512), [dst])

        P.dma("sp", n2T.t, norm2T_d, [], [n2T])
        P.op("dve", lambda e: e.scalar_tensor_tensor(g2T.t, modT.t[:, 2, :], 1.0, n2T.t, op0=ALU.add, op1=ALU.mult), [modT, n2T], [g2T])
        for kc in range(4):
            P.dma("sp", wstage[kc].t, wout_d[kc * 128:(kc + 1) * 128, :], [], [wstage[kc]])
        expand(modT.t[:, 0, :], [modT], MB[2])
        for kc in range(8):
            ws = wstage[kc % 4]
            if kc >= 4:
                P.dma("sp", ws.t, wout_d[kc * 128:(kc + 1) * 128, :], [], [ws])
            gcol = 0 if kc < 4 else 1
            P.op("dve", lambda e, kc=kc, ws=ws, gcol=gcol: e.scalar_tensor_tensor(WO.t[:, kc, :], ws.t, GS[:, gcol:gcol + 1], MB[2].t, op0=ALU.mult, op1=ALU.mult),
                 [ws, smalls, MB[2]], [WO])
        for b in range(NB):
            P.dma("sp", X1.t[:, b, :], x_d[b * 128:(b + 1) * 128, :], [], [X1_tok[b]])
        expand(modT.t[:, 1, :], [modT], MB[3])
        expand(g2T.t, [g2T], MB[4])
        ob = {"n": 0}
        for b in range(NB):
            for half in range(2):
                bk = 2 + ob["n"] % 2
                ob["n"] += 1
                for kc in range(8):
                    P.op("pe", lambda e, kc=kc, bk=bk, b=b, half=half: e.matmul(pst(bk)[:, :], mergedT.t[:, kc, b * 128:(b + 1) * 128], WO.t[:, kc, half * 512:(half + 1) * 512],
                                                                              start=(kc == 0), stop=(kc == 7)), [mT_tok[kc], WO], PQ(bk, 0, 512))
                hs = slice(half * 512, (half + 1) * 512)
                P.op("dve", lambda e, bk=bk, b=b, hs=hs: e.tensor_tensor(X1.t[:, b, hs], pst(bk)[:, :], X1.t[:, b, hs], ALU.add), PQ(bk, 0, 512) + [X1_tok[b]], [X1_tok[b]])
            norm_A(X1.t[:, b, :], [X1_tok[b]], stt[b % 3])
            if b >= 1:
                norm_B1(X1.t[:, b - 1, :], [X1_tok[b - 1]], MB[4], MB[3], b - 1, stt[(b - 1) % 3])
            if b >= 2:
                norm_B2(b - 2, 4 + ((b - 2) % 2))
            if debug:
                P.dma("sp", dbg["x1"][b * 128:(b + 1) * 128, :], X1.t[:, b, :], [X1_tok[b]], [], is_out=True)
        norm_B1(X1.t[:, NB - 1, :], [X1_tok[NB - 1]], MB[4], MB[3], NB - 1, stt[(NB - 1) % 3])
        norm_B2(NB - 2, 4 + ((NB - 2) % 2))
        norm_B2(NB - 1, 4 + ((NB - 1) % 2))
        if debug:
            dump("h2T", TT(hT.t, hT_tok[0]), 8 * T, BF16)

        P.barrier()
        AR.reset(ph3_keep)
        for i in range(3):
            AR.alloc("wringC_again%d" % i, [8 * 256], BF16)
        AR.end = top_keep
        MB[5] = AR.alloc("gate2_bc", [D], F32)
        FN = AR.alloc("fnorm_bc", [D], F32)
        dgt = [AR.alloc("dgt%d" % i, [128], F32) for i in range(2)]
        GMAX = max(j1 - j0 for j0, j1 in GROUPS)
        HT = AR.alloc("HT", [GMAX, T], BF16)
        WD = AR.alloc("WD", [GMAX, D], BF16)
        wdst = [AR.alloc("wdst%d" % i, [D], F32) for i in range(2)]
        UB = [AR.alloc("UB%d" % i, [UW], BF16) for i in range(2)]
        SG = AR.alloc("SG", [T], F32)
        DG = [AR.alloc("DG%d" % i, [11, 128], BF16) for i in range(2)]
        w11T = AR.alloc("w11T", [NCH * 11], F32)
        bconvT = AR.alloc("bconvT", [NCH], F32)
        ring["slots"] = ringC
        yst = [AR.alloc("yst%d" % i, [D], F32) for i in range(2)]
        DACC = AR.alloc("DACC", [T], BF16)
        ctmp = yst
        junk = AR.alloc("junk", [D], BF16)
        stt = [AR.alloc("stt%d" % i, [4], F32) for i in range(3)]
        expand(modT.t[:, 3, :], [modT], MB[5])
        P.dma("sp", FN.t, fnorm_d.partition_broadcast(128), [], [FN])
        P.dma("sp", w11T.t, w11T_d, [], [w11T])
        P.dma("sp", bconvT.t, bconvT_d, [], [bconvT])
        for u in range(2):
            P.op("pool", lambda e, u=u: e.memset(UB[u].t, 0.0), [], [UB[u]])
        P.op("pool", lambda e: e.memset(DACC.t, 0.0), [], [DACC])
        identB_b11 = bass.AP(identB.t.tensor, identB.t.offset, [list(identB.t.ap[0]), [0, 11], [1, 128]])
        ucnt = {"n": 0}

        def up_proj(j, is_up):
            w, wv = pair_w[j]
            off = 128 if is_up else 0
            u = ucnt["n"] % 2
            ucnt["n"] += 1
            for tt in range(4):
                for kc in range(8):
                    P.op("pe", lambda e, kc=kc, tt=tt: e.matmul(pst(tt)[:, :], wv[:, kc, off:off + 128], hT.t[:, kc, tt * 512:(tt + 1) * 512], start=(kc == 0), stop=(kc == 7)),
                         [w] + hT_tok[4 * tt:4 * tt + 4], PQ(tt, 0, 512))
                P.op("act", lambda e, tt=tt: e.activation(UB[u].t[:, UPAD + tt * 512:UPAD + (tt + 1) * 512], pst(tt)[:, :], AF.Copy), PQ(tt, 0, 512), [UB[u]])
            return u

        ctn = {"n": 0}

        def conv(j, jj, is_up, u):
            cc = (NPAIR + j) if is_up else j
            dg = DG[cc % 2]
            wb = bass.AP(w11T.t.tensor, w11T.t.offset + cc * 11, [list(w11T.t.ap[0]), [1, 11], [0, 128]])
            P.op("pool", lambda e: e.tensor_tensor(dg.t, identB_b11, wb, ALU.mult), [identB, w11T], [dg])
            ub = UB[u].t
            acc3 = DACC.t.rearrange("p (r c) -> p r c", c=64)[:, :, 0:63]
            for k_, dy in enumerate((0, -1, 1)):
                wi = (dy + 1) * 3 + 2
                src3 = ub[:, UPAD + 64 * dy + 1:UPAD + 64 * dy + 1 + T].rearrange("p (r c) -> p r c", c=64)[:, :, 0:63]
                wsc = w11T.t[:, cc * 11 + wi:cc * 11 + wi + 1]
                if k_ == 0:
                    P.op("dve", lambda e, src3=src3, wsc=wsc: e.tensor_scalar(acc3, src3, wsc, None, op0=ALU.mult), [UB[u], w11T], [DACC])
                else:
                    P.op("dve", lambda e, src3=src3, wsc=wsc: e.scalar_tensor_tensor(acc3, src3, wsc, acc3, op0=ALU.mult, op1=ALU.add), [UB[u], w11T, DACC], [DACC])
            for tt in range(4):
                base = UPAD + tt * 512
                pt = pst(4 + tt)
                taps = []
                for dy in (0, -1, 1):
                    taps.append(((dy + 1) * 3 + 1, pt[:, 0:512], ub[:, base + 64 * dy:base + 64 * dy + 512]))
                for dy in (0, -1, 1):
                    o3 = pt[:, 0:512].rearrange("p (r c) -> p r c", c=64)[:, :, 1:64]
                    r3 = ub[:, base + 64 * dy - 1:base + 64 * dy - 1 + 512].rearrange("p (r c) -> p r c", c=64)[:, :, 1:64]
                    taps.append(((dy + 1) * 3 + 0, o3, r3))
                o4 = pt[:, 0:512].rearrange("p (a r c) -> p a r c", a=2, r=4)[:, :, 1:4, 0]
                r4 = ub[:, base - 1:base - 1 + 512].rearrange("p (a r c) -> p a r c", a=2, r=4)[:, :, 1:4, 0]
                taps.append((9, o4, r4))
                o4 = pt[:, 0:512].rearrange("p (a r c) -> p a r c", a=2, r=4)[:, :, 0:3, 63]
                r4 = ub[:, base + 1:base + 1 + 512].rearrange("p (a r c) -> p a r c", a=2, r=4)[:, :, 0:3, 63]
                taps.append((10, o4, r4))
                for ti, (wi, oap, rap) in enumerate(taps):
                    P.op("pe", lambda e, wi=wi, oap=oap, rap=rap, ti=ti: e.matmul(oap, dg.t[:, wi, :], rap, start=(ti == 0), stop=(ti == len(taps) - 1)),
                         [dg, UB[u]], PQ(4 + tt, 0, 512))
                ts_ = slice(tt * 512, (tt + 1) * 512)
                ct = ctmp[ctn["n"] % 2]
                ctn["n"] += 1
                P.op("dve", lambda e, pt=pt, ts_=ts_, ct=ct: e.tensor_tensor(ct.t[:, 0:512], pt[:, 0:512], DACC.t[:, ts_], ALU.add), PQ(4 + tt, 0, 512) + [DACC], [ct])
                if not is_up:
                    P.op("act", lambda e, ts_=ts_, ct=ct: e.activation(SG.t[:, ts_], ct.t[:, 0:512], AF.Silu, bias=bconvT.t[:, cc:cc + 1]), [ct, bconvT], [SG])
                else:
                    P.op("dve", lambda e, ts_=ts_, ct=ct: e.scalar_tensor_tensor(HT.t[:, jj, ts_], ct.t[:, 0:512], bconvT.t[:, cc:cc + 1], SG.t[:, ts_], op0=ALU.add, op1=ALU.mult),
                         [ct, bconvT, SG], [HT])

        fin_pending = []

        def final_out(b):
            st = stt[b % 3]
            ys = yst[b % 2]
            xap = X1.t[:, b, :]
            P.op("dve", lambda e: e.scalar_tensor_tensor(ys.t, xap, st.t[:, 2:3], FN.t, op0=ALU.mult, op1=ALU.mult), [X1_tok[b], st, FN], [ys])
            P.dma("sp", y_d[b * 128:(b + 1) * 128, :], ys.t, [ys], [], is_out=True)

        dbk = {"n": 0}

        def wd_load(gi):
            j0, j1 = GROUPS[gi]
            for jj, j in enumerate(range(j0, j1)):
                wq = wdst[j % 2]
                P.dma("sp", wq.t, wdn_d[j * 128:(j + 1) * 128, :], [], [wq])
                P.op("dve", lambda e, jj=jj, wq=wq: e.tensor_tensor(WD.t[:, jj, :], wq.t, MB[5].t, ALU.mult), [wq, MB[5]], [WD])

        def down(gi):
            j0, j1 = GROUPS[gi]
            last_group = gi == len(GROUPS) - 1
            ng = j1 - j0
            for b in range(NB):
                for half in range(2):
                    bk = dbk["n"] % 4
                    dbk["n"] += 1
                    hs = slice(half * 512, (half + 1) * 512)
                    for jj in range(ng):
                        P.op("pe", lambda e, jj=jj, bk=bk, b=b, hs=hs: e.matmul(pst(bk)[:, :], HT.t[:, jj, b * 128:(b + 1) * 128], WD.t[:, jj, hs], start=(jj == 0), stop=(jj == ng - 1)),
                             [HT, WD], PQ(bk, 0, 512))
                    P.op("dve", lambda e, bk=bk, b=b, hs=hs: e.tensor_tensor(X1.t[:, b, hs], pst(bk)[:, :], X1.t[:, b, hs], ALU.add), PQ(bk, 0, 512) + [X1_tok[b]], [X1_tok[b]])
                if last_group:
                    st = stt[b % 3]
                    xap = X1.t[:, b, :]
                    P.op("act", lambda e, xap=xap, st=st, jk=junk: e.activation(jk.t, xap, AF.Square, accum_out=st.t[:, 0:1]), [X1_tok[b]], [junk, st], multi=True)
                    P.op("act", lambda e, st=st: e.activation(st.t[:, 1:2], st.t[:, 0:1], AF.Ln, scale=1.0 / D, bias=EPS), [st], [st])
                    P.op("act", lambda e, st=st: e.activation(st.t[:, 2:3], st.t[:, 1:2], AF.Exp, scale=-0.5), [st], [st])
                    fin_pending.append(b)
                    if len(fin_pending) > 1:
                        final_out(fin_pending.pop(0))
            if last_group:
                while fin_pending:
                    final_out(fin_pending.pop(0))

        wd_load(0)
        pend = None
        deferred = None
        for gi, (j0, j1) in enumerate(GROUPS):
            for j in range(j0, j1):
                for is_up in (False, True):
                    if (not is_up) and (j + 2 < NPAIR):
                        load_pair(j + 2)
                    u = up_proj(j, is_up)
                    if pend is not None:
                        if deferred is not None and pend[4] == gi and pend[2]:
                            down(deferred)
                            wd_load(gi)
                            deferred = None
                        conv(*pend[:4])
                    pend = (j, j - j0, is_up, u, gi)
            deferred = gi
        if deferred is not None and pend[4] == deferred:
            conv(*pend[:4])
            down(deferred)
        P.finish()
    return nc


def _consts():
    ident = np.eye(128, dtype=np.float32)
    j = np.arange(128)[:, None]
    i = np.arange(128)[None, :]
    same = (j // 64) == (i // 64)
    maskT2 = np.concatenate([(j <= i) & same, (j >= i) & same], axis=1).astype(np.float32)
    scanmask = np.ones((1, T), np.float32)
    scanmask[0, ::64] = 0.0
    return ident, maskT2, scanmask


def prep_core_inputs(inp):
    f32 = lambda a: np.ascontiguousarray(np.asarray(a, dtype=np.float32))
    ident, maskT2, scanmask = _consts()
    shared = {
        "w_ada": f32(inp["w_ada"][0]), "b_ada": f32(inp["b_ada"][0]).reshape(1, -1),
        "norm1": f32(inp["norm1"][0]).reshape(1, -1), "norm2": f32(inp["norm2"][0]).reshape(1, -1),
        "b_adaT": f32(np.asarray(inp["b_ada"][0]).reshape(48, 128).T), "norm2T": f32(np.asarray(inp["norm2"][0]).reshape(8, 128).T),
        "fnorm": f32(inp["final_norm"]).reshape(1, -1),
        "w_in": f32(inp["w_in"][0]), "w_gla_up": f32(inp["w_gla_up"][0]),
        "b_glaT": f32(np.asarray(inp["b_gla"][0]).reshape(2, 2, 128).transpose(2, 0, 1).reshape(128, 4)),
        "lbT": f32(np.asarray(inp["hgrn_lb"]).reshape(2, 2, 4, 128).transpose(3, 0, 1, 2).reshape(128, 16)),
        "gnorm": f32(np.stack([np.asarray(inp["gla_norm"][0]), np.asarray(inp["hgrn_norm"][0])], axis=1)),
        "w_out": f32(inp["w_out"][0]), "w_ffn_up": f32(inp["w_ffn_up"][0]),
        "bconvT": f32(np.asarray(inp["b_ffn_conv"][0]).reshape(NCH, 128).T),
        "w_ffn_down": f32(inp["w_ffn_down"][0]),
        "identF": ident, "maskT2": maskT2, "scanmask": scanmask,
    }
    conv = np.asarray(inp["ffn_conv"][0], dtype=np.float32).reshape(9, 2 * FFN_H)
    zero_row = np.zeros((1, 2 * FFN_H), np.float32)
    rows_s = np.concatenate([conv, zero_row, zero_row], axis=0)
    rows_p = np.concatenate([zero_row] * 3 + [conv[3:6]] + [zero_row] * 3 + [conv[3:4], conv[5:6]], axis=0)
    w11 = lambda rows: f32(rows.reshape(11, NCH, 128).transpose(2, 1, 0).reshape(128, NCH * 11))
    x_prompt = np.asarray(inp["x_prompt"], dtype=np.float32)
    x_sample = np.asarray(inp["x_sample"], dtype=np.float32)
    maps = []
    for c in range(8):
        m = dict(shared)
        if c < 4:
            m["x"] = f32(x_sample[c])
            m["cvT"] = f32(np.asarray(inp["c"][c]).reshape(8, 128).T)
            m["sinit_g"] = f32(inp["state_gla"][c, 0])
            m["sinit_h"] = f32(inp["state_hgrn"][c, 0])
            mf = np.ones(32, np.float32)
            mb = np.ones(32, np.float32)
            m["w11T"] = w11(rows_s)
        else:
            p = c - 4
            m["x"] = f32(x_prompt[8 * p:8 * p + 8].reshape(T, D))
            m["cvT"] = f32(np.asarray(inp["c_ctx"]).reshape(8, 128).T)
            m["sinit_g"] = np.zeros((2, 4, 64, 128), np.float32)
            m["sinit_h"] = np.zeros((2, 4, 128, 128), np.float32)
            mf = (np.arange(32) % 4 != 0).astype(np.float32)
            mb = (np.arange(32) % 4 != 3).astype(np.float32)
            m["w11T"] = w11(rows_p)
        m["mchain"] = f32(np.tile(np.concatenate([mf, mb])[None, :], (128, 1)))
        maps.append(m)
    return maps


_PROGRAM = {}


def kernel(**inputs):
    if "nc" not in _PROGRAM:
        _PROGRAM["nc"] = build_program(debug=False)
    nc = _PROGRAM["nc"]
    in_maps = prep_core_inputs(inputs)
    res = run_bass_kernel_spmd(nc, in_maps, core_ids=list(range(8)))
    r = res.results
    y_sample = np.stack([np.asarray(r[c]["y"], dtype=np.float32) for c in range(4)], axis=0)
    y_prompt = np.concatenate([np.asarray(r[c]["y"], dtype=np.float32).reshape(8, 256, D) for c in range(4, 8)], axis=0)
    sg = np.concatenate([np.asarray(r[c]["snew_g"], dtype=np.float32) for c in range(4, 8)], axis=0)[:, None]
    sh = np.concatenate([np.asarray(r[c]["snew_h"], dtype=np.float32) for c in range(4, 8)], axis=0)[:, None]
    return (y_prompt, y_sample, sg, sh)
```
